# Optimizing a Trainium2 kernel written in Bass

```python
import jax, jax.numpy as jnp
from jax import lax
import numpy as np

D_MODEL = 1024
BATCH = 32
SEQ = 256
DEPTH = 4
DEC_BATCH = 4
DEC_SEQ = 1024
PAST_LEN = 512

GRID_W = 64
N_EVEN = (DEPTH + 1) // 2
N_ODD = DEPTH // 2
D_HALF = D_MODEL // 2
HD_A = 64
H_A = D_HALF // HD_A
KV_A = H_A // 4
Q_BLOCK = 128
ROPE_BASE = 10000.0
HS_B = 64
H_B = D_HALF // HS_B
W_LORA = 64
A_LORA = 64
RWKV_DECAY_SCALE = 0.606531
GN_EPS = 64e-5
B_SHIFT = 3 * D_HALF + 2 * W_LORA + 2 * A_LORA
H_C = 4
DK_C = D_MODEL // 2 // H_C
DV_C = D_MODEL // H_C
G_LORA = 16
GLA_TAU = 16.0
GLA_CHUNK = 16
EPS = 1e-6

EV_SPLITS = [H_A * HD_A, KV_A * HD_A, KV_A * HD_A, D_HALF, B_SHIFT, D_HALF]
EV_COLS = sum(EV_SPLITS)
OD_SPLITS = [H_C * DK_C, H_C * DK_C, D_MODEL, D_MODEL, G_LORA, G_LORA]
OD_COLS = sum(OD_SPLITS)

kernel_name = 'hybrid_dit_gqa_rwkv7_gla_step'


def split_cols(z, sizes):
    return jnp.split(z, np.cumsum(sizes)[:-1].tolist(), axis=-1)


def rms_norm(x, g):
    xf = x.astype(jnp.float32)
    y = xf * lax.rsqrt(jnp.mean(xf * xf, axis=-1, keepdims=True) + EPS)
    return (y * g.astype(jnp.float32)).astype(x.dtype)


def rope_2d(x):
    B, T, H, D = x.shape
    n_rows = T // GRID_W
    row = jnp.repeat(jnp.arange(n_rows), GRID_W).astype(jnp.float32)
    col = jnp.tile(jnp.arange(GRID_W), n_rows).astype(jnp.float32)
    n_freq = D // 4
    inv = ROPE_BASE ** (-jnp.arange(n_freq, dtype=jnp.float32) / n_freq)
    ang = jnp.stack([row[:, None] * inv, col[:, None] * inv], axis=1)
    cos = jnp.cos(ang)[None, :, None]
    sin = jnp.sin(ang)[None, :, None]
    xr = x.astype(jnp.float32).reshape(B, T, H, 2, 2, n_freq)
    x1, x2 = xr[..., 0, :], xr[..., 1, :]
    out = jnp.stack([x1 * cos - x2 * sin, x2 * cos + x1 * sin], axis=-2)
    return out.reshape(B, T, H, D).astype(x.dtype)


def block_attention(q, k, v):
    B, T, H, D = q.shape
    kvh = k.shape[2]
    G = H // kvh
    nblk = T // Q_BLOCK
    qb = q.reshape(B, nblk, Q_BLOCK, kvh, G, D).transpose(1, 0, 2, 3, 4, 5)
    scale = D ** -0.5

    def attend(qi):
        s = jnp.einsum('bqkgd,bskd->bkgqs', qi, k).astype(jnp.float32) * scale
        p = jax.nn.softmax(s, axis=-1).astype(v.dtype)
        return jnp.einsum('bkgqs,bskd->bqkgd', p, v)

    o = lax.map(attend, qb)
    return o.transpose(1, 0, 2, 3, 4, 5).reshape(B, T, H * D)


def token_shift(z, mu):
    zp = jnp.pad(z[:, :-1], ((0, 0), (1, 0), (0, 0)))
    zn = jnp.pad(z[:, 1:], ((0, 0), (0, 1), (0, 0)))
    return z + mu * (0.5 * (zp + zn) - z)


def rwkv7_scan(r, w, k, v, kk, a, s0):
    def step(s, inp):
        r_t, w_t, k_t, v_t, kk_t, a_t = inp
        sa = jnp.einsum('bhvk,bhk->bhv', s, -kk_t)
        s = (s * w_t[:, :, None, :] + sa[..., None] * (kk_t * a_t)[:, :, None, :]
             + v_t[..., None] * k_t[:, :, None, :])
        return s, jnp.einsum('bhvk,bhk->bhv', s, r_t)

    xs = tuple(jnp.swapaxes(t, 0, 1) for t in (r, w, k, v, kk, a))
    s, ys = lax.scan(step, s0, xs)
    return jnp.swapaxes(ys, 0, 1), s


def gla_chunked(q, k, v, log_a, s0):
    B, T, H, K = q.shape
    n_chunks = T // GLA_CHUNK

    def to_chunks(t):
        return t.reshape(B, n_chunks, GLA_CHUNK, H, t.shape[-1]).transpose(1, 0, 3, 2, 4)

    qc, kc, vc = to_chunks(q), to_chunks(k), to_chunks(v)
    bc = jnp.cumsum(to_chunks(log_a), axis=3)
    mask = jnp.tril(jnp.ones((GLA_CHUNK, GLA_CHUNK), dtype=bool))

    def step(s, inp):
        q_, k_, v_, b_ = inp
        inter = jnp.einsum('bhck,bhkv->bhcv', q_ * jnp.exp(b_), s)
        diff = b_[:, :, :, None, :] - b_[:, :, None, :, :]
        decay = jnp.exp(jnp.where(mask[:, :, None], diff, -jnp.inf))
        att = jnp.einsum('bhtk,bhsk,bhtsk->bhts', q_, k_, decay)
        o = inter + jnp.einsum('bhts,bhsv->bhtv', att, v_)
        b_last = b_[:, :, -1:, :]
        s = (jnp.exp(b_last[:, :, 0, :])[..., None] * s
             + jnp.einsum('bhsk,bhsv->bhkv', k_ * jnp.exp(b_last - b_), v_))
        return s, o

    s, o = lax.scan(step, s0, (qc, kc, vc, bc))
    return o.transpose(1, 0, 3, 2, 4).reshape(B, T, H, v.shape[-1]), s


def even_mixer(h, w_in, w_out, qn_g, kn_g, shift_mu, w0, w2, a0, a2, k_k, k_a, r_k, ln_g, ln_b, ctx):
    B, T, _ = h.shape
    f32 = jnp.float32
    z = h @ w_in
    qa, ka, va, ga, zb, gb = split_cols(z, EV_SPLITS)
    q = rms_norm(qa.reshape(B, T, H_A, HD_A), qn_g)
    k = rms_norm(ka.reshape(B, T, KV_A, HD_A), kn_g)
    v = va.reshape(B, T, KV_A, HD_A)
    if ctx is None:
        keys, vals = k, v
        zero = jnp.zeros((B, H_B, HS_B, HS_B), f32)
        s_init = (zero, zero)
    else:
        ctx_k, ctx_v, s_f, s_b = ctx
        q = rope_2d(q)
        keys = jnp.concatenate([ctx_k.astype(k.dtype), rope_2d(k)], axis=1)
        vals = jnp.concatenate([ctx_v.astype(v.dtype), v], axis=1)
        s_init = (s_f.astype(f32), s_b.astype(f32))
    o_a = block_attention(q, keys, vals) * jax.nn.silu(ga)
    zb = token_shift(zb, shift_mu).astype(f32)
    rb, kb, vb, wl_f, wl_b, al_f, al_b = split_cols(zb, [D_HALF] * 3 + [W_LORA] * 2 + [A_LORA] * 2)
    heads = lambda t: t.reshape(B, T, H_B, HS_B)
    r, k_raw, vv = heads(rb), heads(kb), heads(vb)
    kk = k_raw * k_k.reshape(H_B, HS_B)
    kk = kk * lax.rsqrt(jnp.sum(kk * kk, axis=-1, keepdims=True) + 1e-12)
    y = jnp.zeros_like(r)
    bonus = jnp.zeros_like(r)
    finals = []
    for d, (wl, al) in enumerate(((wl_f, al_f), (wl_b, al_b))):
        w = heads(jnp.exp(-RWKV_DECAY_SCALE * jax.nn.sigmoid(w0[d] + jnp.tanh(wl) @ w2[d])))
        a = heads(jax.nn.sigmoid(a0[d] + al @ a2[d]))
        kd = k_raw * (1.0 + (a - 1.0) * k_a.reshape(H_B, HS_B))
        seqs = (r, w, kd, vv, kk, a)
        if d == 1:
            seqs = tuple(jnp.flip(t, axis=1) for t in seqs)
        yd, sd = rwkv7_scan(*seqs, s_init[d])
        if d == 1:
            yd = jnp.flip(yd, axis=1)
        y = y + yd
        bonus = bonus + jnp.sum(r * kd * r_k, axis=-1, keepdims=True) * vv
        finals.append(sd)
    mu = jnp.mean(y, axis=-1, keepdims=True)
    var = jnp.mean(jnp.square(y - mu), axis=-1, keepdims=True)
    yn = ((y - mu) * lax.rsqrt(var + GN_EPS)).reshape(B, T, D_HALF) * ln_g + ln_b
    o_b = (yn + bonus.reshape(B, T, D_HALF)).astype(h.dtype) * jax.nn.silu(gb)
    out = jnp.concatenate([o_a, o_b], axis=-1) @ w_out
    return out, (k, v, finals[0], finals[1])


def odd_mixer(h, w_in, w_out, gw2, gbias, ln_g, ctx):
    B, T, _ = h.shape
    f32 = jnp.float32
    z = h @ w_in
    q, k, v, g, gl_f, gl_b = split_cols(z, OD_SPLITS)
    q = q.reshape(B, T, H_C, DK_C).astype(f32) * (DK_C ** -0.5)
    k = k.reshape(B, T, H_C, DK_C).astype(f32)
    v = v.reshape(B, T, H_C, DV_C).astype(f32)
    if ctx is None:
        zero = jnp.zeros((B, H_C, DK_C, DV_C), f32)
        s_init = (zero, zero)
    else:
        s_init = (ctx[0].astype(f32), ctx[1].astype(f32))
    o = jnp.zeros_like(v)
    finals = []
    for d, gl in enumerate((gl_f, gl_b)):
        log_a = jax.nn.log_sigmoid(gl.astype(f32) @ gw2[d] + gbias[d]).reshape(B, T, H_C, DK_C) / GLA_TAU
        seqs = (q, k, v, log_a)
        if d == 1:
            seqs = tuple(jnp.flip(t, axis=1) for t in seqs)
        od, sd = gla_chunked(*seqs, s_init[d])
        if d == 1:
            od = jnp.flip(od, axis=1)
        o = o + od
        finals.append(sd)
    o = rms_norm(o, ln_g).reshape(B, T, D_MODEL).astype(h.dtype) * jax.nn.silu(g)
    return o @ w_out, (finals[0], finals[1])


def setup_inputs(seed: int = 0) -> dict:
    key = jax.random.key(seed)
    ks = iter(jax.random.split(key, 40))
    f32 = jnp.float32
    nrm = lambda shape, scale: scale * jax.random.normal(next(ks), shape, f32)
    return {
        'x_prompt': nrm((BATCH, SEQ, D_MODEL), 1.0),
        'x_sample': nrm((DEC_BATCH, DEC_SEQ, D_MODEL), 1.0),
        'c': nrm((DEC_BATCH, D_MODEL), 1.0),
        'cache_attn_k': nrm((DEC_BATCH, N_EVEN, PAST_LEN, KV_A, HD_A), 1.0),
        'cache_attn_v': nrm((DEC_BATCH, N_EVEN, PAST_LEN, KV_A, HD_A), 1.0),
        'state_rwkv_fwd': nrm((DEC_BATCH, N_EVEN, H_B, HS_B, HS_B), 0.3),
        'state_rwkv_bwd': nrm((DEC_BATCH, N_EVEN, H_B, HS_B, HS_B), 0.3),
        'state_gla_fwd': nrm((DEC_BATCH, N_ODD, H_C, DK_C, DV_C), 0.3),
        'state_gla_bwd': nrm((DEC_BATCH, N_ODD, H_C, DK_C, DV_C), 0.3),
        'c_ctx': nrm((D_MODEL,), 1.0),
        'norm_g': 1.0 + nrm((DEPTH, D_MODEL), 0.05),
        'mod_w': nrm((DEPTH, D_MODEL, 3 * D_MODEL), 0.5 * D_MODEL ** -0.5),
        'mod_b': nrm((DEPTH, 3 * D_MODEL), 0.02),
        'ev_w_in': nrm((N_EVEN, D_MODEL, EV_COLS), D_MODEL ** -0.5),
        'ev_w_out': nrm((N_EVEN, 2 * D_HALF, D_MODEL), (2 * D_HALF) ** -0.5),
        'ev_qn_g': 1.0 + nrm((N_EVEN, HD_A), 0.05),
        'ev_kn_g': 1.0 + nrm((N_EVEN, HD_A), 0.05),
        'ev_shift_mu': jax.random.uniform(next(ks), (N_EVEN, B_SHIFT), f32),
        'rw_w0': nrm((N_EVEN, 2, D_HALF), 0.5),
        'rw_w2': nrm((N_EVEN, 2, W_LORA, D_HALF), 0.5 * W_LORA ** -0.5),
        'rw_a0': nrm((N_EVEN, 2, D_HALF), 0.5),
        'rw_a2': nrm((N_EVEN, 2, A_LORA, D_HALF), 0.5 * A_LORA ** -0.5),
        'rw_kk': 0.85 + nrm((N_EVEN, D_HALF), 0.05),
        'rw_ka': 1.0 + nrm((N_EVEN, D_HALF), 0.05),
        'rw_rk': nrm((N_EVEN, H_B, HS_B), 0.1),
        'rw_ln_g': 1.0 + nrm((N_EVEN, D_HALF), 0.05),
        'rw_ln_b': nrm((N_EVEN, D_HALF), 0.02),
        'od_w_in': nrm((N_ODD, D_MODEL, OD_COLS), D_MODEL ** -0.5),
        'od_w_out': nrm((N_ODD, D_MODEL, D_MODEL), D_MODEL ** -0.5),
        'gla_w2': nrm((N_ODD, 2, G_LORA, H_C * DK_C), G_LORA ** -0.5),
        'gla_b': nrm((N_ODD, 2, H_C * DK_C), 0.5),
        'gla_ln_g': 1.0 + nrm((N_ODD, DV_C), 0.05),
        'final_g': 1.0 + nrm((D_MODEL,), 0.05),
    }


def reference(x_prompt, x_sample, c, cache_attn_k, cache_attn_v, state_rwkv_fwd, state_rwkv_bwd,
              state_gla_fwd, state_gla_bwd, c_ctx, norm_g, mod_w, mod_b, ev_w_in, ev_w_out, ev_qn_g,
              ev_kn_g, ev_shift_mu, rw_w0, rw_w2, rw_a0, rw_a2, rw_kk, rw_ka, rw_rk, rw_ln_g, rw_ln_b,
              od_w_in, od_w_out, gla_w2, gla_b, gla_ln_g, final_g):

    def trunk_layer(x, i, cond, ctx_tensors):
        m = jax.nn.silu(cond.astype(jnp.float32)) @ mod_w[i] + mod_b[i]
        shift, scale, gate = jnp.split(m.reshape(-1, 1, 3 * D_MODEL), 3, axis=-1)
        h = (rms_norm(x, norm_g[i]).astype(jnp.float32) * (1.0 + scale) + shift).astype(x.dtype)
        j = i // 2
        if i % 2 == 0:
            out, new = even_mixer(h, ev_w_in[j], ev_w_out[j], ev_qn_g[j], ev_kn_g[j], ev_shift_mu[j],
                                  rw_w0[j], rw_w2[j], rw_a0[j], rw_a2[j], rw_kk[j], rw_ka[j], rw_rk[j],
                                  rw_ln_g[j], rw_ln_b[j], ctx_tensors)
        else:
            out, new = odd_mixer(h, od_w_in[j], od_w_out[j], gla_w2[j], gla_b[j], gla_ln_g[j], ctx_tensors)
        return x + gate.astype(x.dtype) * out.astype(x.dtype), new

    new_k, new_v, new_rf, new_rb, new_gf, new_gb = [], [], [], [], [], []
    x = x_prompt
    for i in range(DEPTH):
        x, new = trunk_layer(x, i, c_ctx, None)
        if i % 2 == 0:
            new_k.append(new[0]); new_v.append(new[1]); new_rf.append(new[2]); new_rb.append(new[3])
        else:
            new_gf.append(new[0]); new_gb.append(new[1])
    y_prompt = rms_norm(x, final_g)

    x = x_sample
    for i in range(DEPTH):
        j = i // 2
        if i % 2 == 0:
            ctx = (cache_attn_k[:, j], cache_attn_v[:, j], state_rwkv_fwd[:, j], state_rwkv_bwd[:, j])
        else:
            ctx = (state_gla_fwd[:, j], state_gla_bwd[:, j])
        x, _ = trunk_layer(x, i, c, ctx)
    y_sample = rms_norm(x, final_g)

    return (y_prompt, y_sample, jnp.stack(new_k, axis=1), jnp.stack(new_v, axis=1),
            jnp.stack(new_rf, axis=1), jnp.stack(new_rb, axis=1),
            jnp.stack(new_gf, axis=1), jnp.stack(new_gb, axis=1))
```

```python
import contextlib
import os
STOP = int(os.environ.get('KSTOP', '9'))
SUB = float(os.environ.get('KSUB', '9'))
KNS = int(os.environ.get('KNS', '99'))
KNC = int(os.environ.get('KNC', '99'))
KND = int(os.environ.get('KND', '2'))
import numpy as np
import concourse.bass as bass
import concourse.mybir as mybir
from concourse.bass_utils import run_bass_kernel_spmd

F32 = mybir.dt.float32
BF16 = mybir.dt.bfloat16
AF = mybir.ActivationFunctionType
ALU = mybir.AluOpType
AX = mybir.AxisListType

D = 1024
NCH = 8
DEPTH = 4
GT = 1024
EV_COLS = 3584
OD_COLS = 3104
EPS = 1e-6
GN_EPS = 64e-5
RWKV_DECAY_SCALE = 0.606531
WB = 256


class Buf:
    __slots__ = ("name", "w", "r")

    def __init__(self, name):
        self.name = name
        self.w = None
        self.r = {}


class KB:
    SEM_WRAP = 20000

    def __init__(self, nc):
        self.nc = nc
        self.es = contextlib.ExitStack()
        self.engs = {"pe": nc.tensor, "act": nc.scalar, "dve": nc.vector, "pool": nc.gpsimd, "sp": nc.sync}
        self.cnt = {e: 0 for e in self.engs}
        self.esems = {e: [] for e in self.engs}
        self.seen = {e: {} for e in self.engs}
        self.semobj = {}
        self.ndma_sems = {"sp": 12, "pool": 6, "act": 4}
        self.dma_sems = {}
        self.dma_rr = {q: 0 for q in self.ndma_sems}
        self.dma_cnt = {}
        self.nsem = 0
        self.ninstr = 0

    def new_sem(self, name):
        s = self.es.enter_context(self.nc.semaphore(name))
        self.semobj[name] = s
        return name

    def sb(self, name, shape, dtype):
        return self.es.enter_context(self.nc.sbuf_tensor(name, list(shape), dtype))

    def ps(self, name, shape, dtype=F32):
        return self.es.enter_context(self.nc.psum_tensor(name, list(shape), dtype))

    def _cur_sem(self, e):
        idx = self.cnt[e] // self.SEM_WRAP
        while len(self.esems[e]) <= idx:
            self.esems[e].append(self.new_sem(f"s_{e}_{len(self.esems[e])}"))
        return self.esems[e][idx]

    def _wait(self, e, tok):
        if tok is None:
            return
        key, val = tok
        if self.seen[e].get(key, 0) >= val:
            return
        self.engs[e].wait_ge(self.semobj[key], val)
        self.seen[e][key] = val

    def _deps(self, e, reads, writes, pe_acc=False):
        for b in reads:
            if b.w is not None:
                self._wait(e, b.w)
        for b in writes:
            if b.w is not None and not (pe_acc and e == "pe"):
                self._wait(e, b.w)
            for e2, tok in b.r.items():
                if e2 == e and e == "pe":
                    continue
                self._wait(e, tok)

    def _mark(self, e, tok, reads, writes):
        for b in reads:
            b.r[e] = tok
        for b in writes:
            b.w = tok
            b.r = {}

    def op(self, e, reads, writes, fn, pe_acc=False):
        self._deps(e, reads, writes, pe_acc)
        sem = self._cur_sem(e)
        ins = fn(self.engs[e])
        ins.then_inc(self.semobj[sem], 1)
        self.cnt[e] += 1
        self.ninstr += 1
        val = self.cnt[e] - (self.cnt[e] - 1) // self.SEM_WRAP * self.SEM_WRAP
        tok = (sem, val)
        self._mark(e, tok, reads, writes)
        return tok

    def dma(self, q, out, in_, reads, writes):
        if q not in self.dma_sems:
            self.dma_sems[q] = [self.new_sem(f"d_{q}_{i}") for i in range(self.ndma_sems[q])]
            for s in self.dma_sems[q]:
                self.dma_cnt[s] = 0
        sems = self.dma_sems[q]
        s = sems[self.dma_rr[q] % len(sems)]
        self.dma_rr[q] += 1
        if self.dma_cnt[s] > 0:
            self._wait(q, (s, 16 * self.dma_cnt[s]))
        self._deps(q, reads, writes)
        self.engs[q].dma_start(out=out, in_=in_).then_inc(self.semobj[s], 16)
        self.dma_cnt[s] += 1
        self.ninstr += 1
        tok = (s, 16 * self.dma_cnt[s])
        self._mark(s, tok, reads, writes)
        return tok

    def finish(self, bufs):
        for b in bufs:
            if b.w is not None:
                self._wait("sp", b.w)
        for q, sems in self.dma_sems.items():
            for s in sems:
                if self.dma_cnt[s] > 0:
                    self._wait("sp", (s, 16 * self.dma_cnt[s]))


    def barrier(self):
        toks = []
        for e in self.engs:
            if self.cnt[e] > 0:
                sem = self.esems[e][(self.cnt[e] - 1) // self.SEM_WRAP]
                val = self.cnt[e] - (self.cnt[e] - 1) // self.SEM_WRAP * self.SEM_WRAP
                toks.append((sem, val))
        for q, sems in self.dma_sems.items():
            for s in sems:
                if self.dma_cnt[s] > 0:
                    toks.append((s, 16 * self.dma_cnt[s]))
        for e in self.engs:
            for t in toks:
                self._wait(e, t)

    @contextlib.contextmanager
    def scope(self):
        sc = _Scope(self)
        try:
            yield sc
        finally:
            self.barrier()
            sc.es.close()

    def _pe_rows(self, ap):
        base = ap.base_partition()
        n = ap.shape[0]
        grp = set(range(base // 32, (base + n - 1) // 32 + 1))
        last = getattr(self, "_pe_last_grp", None)
        if last is not None and not (grp & last) and self.cnt["pe"] > 0:
            for sem_i in range(max(0, (self.cnt["pe"] - 1) // self.SEM_WRAP - 1), (self.cnt["pe"] - 1) // self.SEM_WRAP + 1):
                sem = self.esems["pe"][sem_i]
                if sem_i == (self.cnt["pe"] - 1) // self.SEM_WRAP:
                    val = self.cnt["pe"] - sem_i * self.SEM_WRAP
                else:
                    val = self.SEM_WRAP
                self._wait("pe", (sem, val))
        self._pe_last_grp = grp

    def mm(self, out, lhsT, rhs, R, W, start=True, stop=True):
        self._pe_rows(lhsT)
        return self.op("pe", R, W, lambda e: e.matmul(out, lhsT, rhs, start=start, stop=stop), pe_acc=True)

    def tr(self, out, in_, ident, R, W):
        self._pe_rows(in_)
        return self.op("pe", R, W, lambda e: e.transpose(out, in_, ident), pe_acc=True)

    def tt(self, out, in0, in1, op, R, W, e="dve"):
        return self.op(e, R, W, lambda g: g.tensor_tensor(out=out, in0=in0, in1=in1, op=op))

    def ts(self, out, in0, s1, op0, R, W, s2=None, op1=None, e="dve"):
        if op1 is None:
            return self.op(e, R, W, lambda g: g.tensor_scalar(out=out, in0=in0, scalar1=s1, scalar2=None, op0=op0))
        return self.op(e, R, W, lambda g: g.tensor_scalar(out=out, in0=in0, scalar1=s1, scalar2=s2, op0=op0, op1=op1))

    def stt(self, out, in0, scalar, in1, op0, op1, R, W, e="dve"):
        return self.op(e, R, W, lambda g: g.scalar_tensor_tensor(out=out, in0=in0, scalar=scalar, in1=in1, op0=op0, op1=op1))

    def act(self, out, in_, func, R, W, **kw):
        return self.op("act", R, W, lambda g: g.activation(out=out, in_=in_, func=func, **kw))

    def cp(self, out, in_, R, W, e="dve"):
        return self.op(e, R, W, lambda g: g.tensor_copy(out=out, in_=in_))

    def red(self, out, in_, R, W, op=None):
        return self.op("dve", R, W, lambda g: g.tensor_reduce(out=out, in_=in_, axis=AX.X, op=(op or ALU.add)))

    def recip(self, out, in_, R, W):
        return self.op("dve", R, W, lambda g: g.reciprocal(out=out, in_=in_))

    def ms(self, ap, val, W, e="dve"):
        return self.op(e, [], W, lambda g: g.memset(ap, val))


class _Scope:
    def __init__(self, k):
        self.k = k
        self.es = contextlib.ExitStack()

    def sb(self, name, shape, dtype):
        return self.es.enter_context(self.k.nc.sbuf_tensor(name, list(shape), dtype))


def build_program(nlayers=DEPTH, groups=(0, 1)):
    nc = bass.Bass("TRN2", target_bir_lowering=False)
    k = KB(nc)
    uid = [0]

    def un(name):
        uid[0] += 1
        return f"{name}_{uid[0]}"

    def din(name, shape):
        return nc.dram_tensor(name, list(shape), F32, kind="ExternalInput").ap()

    def dout(name, shape):
        return nc.dram_tensor(name, list(shape), F32, kind="ExternalOutput").ap()

    xg = din("xg", [2, GT, D])
    condT = din("condT", [128, NCH, 2])
    modw = din("modw", [DEPTH * 12, 128, NCH, WB])
    modbT = din("modbT", [128, DEPTH, 24])
    normgT = din("normgT", [128, DEPTH, NCH])
    finalgT = din("finalgT", [128, NCH])
    ident_d = din("ident", [128, 128])
    evw = din("evw", [2, 14, 128, NCH, WB])
    evo = din("evo", [2, 4, 128, NCH, WB])
    odw = din("odw", [2, 13, 128, NCH, WB])
    odo = din("odo", [2, 4, 128, NCH, WB])
    tri64_d = din("tri64", [64, 2, 2, 64])
    tri128_d = din("tri128", [128, 2, 128])
    mask64_d = din("mask64", [64, 2, 128])
    maskN_d = din("maskN", [64, 2, 64])
    mask128_d = din("mask128", [128, 2, 128])
    bd_d = din("bdones", [128, 128])
    cos_d = din("ropecos", [128, 8, 64])
    sin_d = din("ropesin", [128, 8, 64])
    w2aug_d = din("w2aug", [2, 65, 2, 512])
    a2aug_d = din("a2aug", [2, 65, 2, 512])
    gw2aug_d = din("gw2aug", [2, 17, 2, 512])
    ln256_d = din("ln256", [2, 128, 256])
    qkg_d = din("qkg", [2, 128, 2, 64])
    evs_d = din("evs", [2, 128, 34])
    cak = din("cak", [2, 512, 128])
    cav = din("cav", [2, 512, 128])
    srs_d = [din("srf", [2, 8, 64, 64]), din("srb", [2, 8, 64, 64])]
    sgs_d = [din("sgf", [2, 4, 128, 256]), din("sgb", [2, 4, 128, 256])]
    yg = dout("yg", [2, GT, D])
    nk_o = dout("nk", [4, 2, 256, 128])
    nv_o = dout("nv", [4, 2, 256, 128])
    nrs_o = [dout("nrf", [4, 2, 8, 64, 64]), dout("nrb", [4, 2, 8, 64, 64])]
    ngs_o = [dout("ngf", [4, 2, 4, 128, 256]), dout("ngb", [4, 2, 4, 128, 256])]

    with k.es:
        bC = Buf("const")

        def cload(name, src, shape, dtype=F32):
            t = k.sb(name, shape, dtype)
            k.dma("sp", t[:], src, [], [bC])
            return t

        ident = cload("ident_sb", ident_d[:, :], [128, 128])
        tri64 = cload("tri64_sb", tri64_d[:, :, :, :], [64, 2, 2, 64])
        tri128 = cload("tri128_sb", tri128_d[:, :, :], [128, 2, 128])
        mask64 = cload("mask64_sb", mask64_d[:, :, :], [64, 2, 128])
        maskN = cload("maskN_sb", maskN_d[:, :, :], [64, 2, 64])
        mask128 = cload("mask128_sb", mask128_d[:, :, :], [128, 2, 128])
        bdones = cload("bd_sb", bd_d[:, :], [128, 128])
        rope_tabs = {}
        cond_sb = cload("cond_sb", condT[:, :, :], [128, NCH, 2])
        modb_sb = cload("modb_sb", modbT[:, :, :], [128, DEPTH, 24])
        normg_sb = cload("normg_sb", normgT[:, :, :], [128, DEPTH, NCH])
        finalg_sb = cload("finalg_sb", finalgT[:, :], [128, NCH])
        ones_f = k.sb("ones_f", [128, 128], F32)
        k.ms(ones_f[:], 1.0 / D, [bC])
        ones_bf = k.sb("ones_bf", [128, 64], BF16)
        k.ms(ones_bf[:], 1.0, [bC])
        bP = Buf("params")
        w2aug = k.sb("w2aug_sb", [65, 2, 512], F32)
        a2aug = k.sb("a2aug_sb", [65, 2, 512], F32)
        gw2aug = k.sb("gw2aug_sb", [17, 2, 512], F32)
        ln256 = k.sb("ln256_sb", [128, 256], F32)
        qkg = k.sb("qkg_sb", [128, 2, 64], F32)
        evs = k.sb("evs_sb", [128, 34], F32)
        evx = k.sb("evx_sb", [128, 32], F32)

        PT = [k.ps(f"P{i}", [128, 1024], F32) for i in range(4)]
        bB = [Buf(f"bank{i}") for i in range(8)]
        rr = [0]

        def bank(i):
            return PT[i // 2][:, (i % 2) * 512:(i % 2) * 512 + 512]

        def nb(lo=0, hi=8):
            i = lo + rr[0] % (hi - lo)
            rr[0] += 1
            return i

        def nb2():
            i = (rr[0] % 8 + 1) // 2 * 2 % 8
            rr[0] += (i - rr[0] % 8) % 8 + 2
            return i

        scond = k.sb("scond", [128, NCH, 2], F32)
        k.act(scond[:], cond_sb[:], AF.Silu, [bC], [bC])
        modT = k.sb("modT", [128, DEPTH, 24, 2], F32)
        bM = Buf("mod")
        xT = k.sb("xT", [128, NCH, GT], F32)
        bX = Buf("xT")
        oT = k.sb("oT", [128, NCH, GT], BF16)
        bO = Buf("oT")
        bR = Buf("rstd")
        hcol = k.sb("hcol", [128, 3, NCH], F32)
        bH = Buf("hcol")

        with k.scope() as sc:
            wst = [sc.sb(un("mwst"), [128, NCH, WB], F32) for i in range(4)]
            bws = [Buf("mwst%d" % i) for i in range(4)]
            wi = 0
            for L in range(nlayers):
                for blk in range(12):
                    w = wst[wi % 4]; bw = bws[wi % 4]
                    k.dma("sp" if wi % 2 == 0 else "pool", w[:], modw[L * 12 + blk], [], [bw])
                    bi = nb(); pm = bank(bi); bp = bB[bi]
                    for sub in range(2):
                        for kc in range(NCH):
                            k.mm(pm[:, sub * 2:sub * 2 + 2], w[:, kc, sub * 128:(sub + 1) * 128], scond[:, kc, :],
                                 [bw, bC], [bp], start=(kc == 0), stop=(kc == NCH - 1))
                    for sub in range(2):
                        ch = blk * 2 + sub
                        k.ts(modT[:, L, ch, :], pm[:, sub * 2:sub * 2 + 2], modb_sb[:, L, ch:ch + 1], ALU.add, [bp, bC], [bM])
                    wi += 1

        def compute_rstd(sc):
            rstd = sc.sb(un("rstd"), [128, GT], F32)
            sq = [sc.sb(un("sq"), [128, 512], F32) for i in range(2)]
            bsq = [Buf("sq0"), Buf("sq1")]
            for tt in range(GT // 512):
                bi = nb(); pn = bank(bi); bp = bB[bi]
                for c in range(NCH):
                    s_ = sq[c % 2]; bs_ = bsq[c % 2]
                    k.act(s_[:], xT[:, c, tt * 512:(tt + 1) * 512], AF.Square, [bX], [bs_])
                    k.mm(pn, ones_f[:], s_[:], [bs_, bC], [bp], start=(c == 0), stop=(c == NCH - 1))
                sl = rstd[:, tt * 512:(tt + 1) * 512]
                k.ts(sl, pn, EPS, ALU.add, [bp], [bR])
                k.act(sl, sl, AF.Sqrt, [bR], [bR])
                k.recip(sl, sl, [bR], [bR])
            return rstd

        def wblock_factory(sc):
            wst2 = [sc.sb(un("wst"), [128, NCH, WB], F32) for i in range(2)]
            bws2 = [Buf("wst0"), Buf("wst1")]
            wbf = [sc.sb(un("wbf"), [128, NCH, WB], BF16) for i in range(2)]
            bwb = [Buf("wbf0"), Buf("wbf1")]
            cnt = [0]

            def wblock(src):
                i = cnt[0] % 2
                cnt[0] += 1
                wst = wst2[i]; bws = bws2[i]
                k.dma("sp", wst[:], src, [], [bws])
                k.cp(wbf[i][:], wst[:], [bws], [bwb[i]], e="pool")
                return wbf[i], bwb[i]
            return wblock

        def proj_F(wblock, wsrc, blocks, hT, bHT, dest, msub=128, nsub=None):
            nsub = nsub or WB // msub
            for bi_, blk in enumerate(blocks):
                w, bw = wblock(wsrc[blk])
                for sub in range(nsub):
                    for tt in range(GT // 512):
                        bi = nb(); pb = bank(bi); bp = bB[bi]
                        for kc in range(NCH):
                            k.mm(pb[0:msub, :], w[:, kc, sub * msub:(sub + 1) * msub], hT[:, kc, tt * 512:(tt + 1) * 512],
                                 [bw, bHT], [bp], start=(kc == 0), stop=(kc == NCH - 1))
                        dest(bi_ * nsub + sub, tt, pb, bp)

        def proj_T(wblock, wsrc, blocks, hT, bHT, dest):
            for bi_, blk in enumerate(blocks):
                w, bw = wblock(wsrc[blk])
                for t8 in range(GT // 128):
                    bi = nb(); pb = bank(bi); bp = bB[bi]
                    for kc in range(NCH):
                        k.mm(pb[:, 0:WB], hT[:, kc, t8 * 128:(t8 + 1) * 128], w[:, kc, :],
                             [bw, bHT], [bp], start=(kc == 0), stop=(kc == NCH - 1))
                    dest(bi_, t8, pb, bp)

        def make_h(sc, L, g):
            hT = sc.sb(un("hT"), [128, NCH, GT], BF16)
            bHT = Buf("hT")
            k.stt(hcol[:, 0, :], modT[:, L, 8:16, g], 1.0, normg_sb[:, L, :], ALU.add, ALU.mult, [bM, bC], [bH])
            k.cp(hcol[:, 1, :], modT[:, L, 0:8, g], [bM], [bH])
            k.cp(hcol[:, 2, :], modT[:, L, 16:24, g], [bM], [bH])
            rstd = compute_rstd(sc)
            tmp = [sc.sb(un("htmp"), [128, 512], F32) for i in range(2)]
            btmp = [Buf("htmp0"), Buf("htmp1")]
            i = 0
            for tt in range(GT // 512):
                for c in range(NCH):
                    t_ = tmp[i % 2]; bt_ = btmp[i % 2]; i += 1
                    k.tt(t_[:], xT[:, c, tt * 512:(tt + 1) * 512], rstd[:, tt * 512:(tt + 1) * 512], ALU.mult, [bX, bR], [bt_])
                    k.ts(hT[:, c, tt * 512:(tt + 1) * 512], t_[:], hcol[:, 0, c:c + 1], ALU.mult, [bt_, bH], [bHT],
                         s2=hcol[:, 1, c:c + 1], op1=ALU.add)
            return hT, bHT

        def out_proj(wsrc):
            with k.scope() as sc:
                wblock = wblock_factory(sc)

                def dest(cc, tt, pb, bp):
                    sl = xT[:, cc, tt * 512:(tt + 1) * 512]
                    k.stt(sl, pb, hcol[:, 2, cc:cc + 1], sl, ALU.mult, ALU.add, [bp, bH, bX], [bX])
                proj_F(wblock, wsrc, range(4), oT, bO, dest)

        def headnorm(sc_t, pb_view, nh, gidx, out3, R, W, bT=None):
            bT = bT or bT0
            sqt, ssq = sc_t
            k.act(sqt[:, 0:nh * 64], pb_view.rearrange("p h d -> p (h d)"), AF.Square, R, [bT])
            k.red(ssq[:, 0:nh], sqt[:, 0:nh * 64].rearrange("p (h d) -> p h d", h=nh), [bT], [bT])
            k.ts(ssq[:, 0:nh], ssq[:, 0:nh], 1.0 / 64, ALU.mult, [bT], [bT], s2=EPS, op1=ALU.add)
            k.act(ssq[:, 0:nh], ssq[:, 0:nh], AF.Sqrt, [bT], [bT])
            k.recip(ssq[:, 0:nh], ssq[:, 0:nh], [bT], [bT])
            k.tt(out3, pb_view, ssq[:, 0:nh].unsqueeze(2).to_broadcast([128, nh, 64]), ALU.mult, R + [bT], W)
            k.tt(out3, out3, qkg[:, gidx, :].unsqueeze(1).to_broadcast([128, nh, 64]), ALU.mult, W + [bP], W)

        bT0 = Buf("tmpT")

        def rope(x3, nh, t8, t1, t2, R, bT=None):
            bT = bT or bT0
            cosb = rope_tabs["cos"][:, t8, :].unsqueeze(1).to_broadcast([128, nh, 64])
            k.tt(t1[:, 0:nh, :], x3, cosb, ALU.mult, R + [bC], [bT])
            x5 = x3.rearrange("p h (a q f) -> p h a q f", a=2, q=2)
            t5 = t2[:, 0:nh, :].rearrange("p h (a q f) -> p h a q f", a=2, q=2)
            s4 = rope_tabs["sin"][:, t8, :].rearrange("p (a q f) -> p a q f", a=2, q=2)
            for q_ in range(2):
                k.tt(t5[:, :, :, q_, :], x5[:, :, :, 1 - q_, :],
                     s4[:, :, q_, :].unsqueeze(1).to_broadcast([128, nh, 2, 16]), ALU.mult, R + [bC], [bT])
            k.tt(x3, t1[:, 0:nh, :], t2[:, 0:nh, :], ALU.add, [bT], R)

        def even_layer(g, j, L):
            nseq, T = (4, 256) if g == 0 else (1, 1024)
            TP = T + 2
            koff = 0 if g == 0 else 512
            SK = GT + koff
            k.dma("sp", w2aug[:], w2aug_d[j], [], [bP])
            k.dma("sp", a2aug[:], a2aug_d[j], [], [bP])
            k.dma("sp", qkg[:], qkg_d[j], [], [bP])
            k.dma("sp", evs[:], evs_d[j], [], [bP])
            k.ts(evx[:, 0:14], evs[:, 0:14], 0.5, ALU.mult, [bP], [bP])
            k.ts(evx[:, 14:28], evs[:, 0:14], -1.0, ALU.mult, [bP], [bP], s2=1.0, op1=ALU.add)
            k.ts(evx[:, 28:32], evs[:, 18:22], -1.0, ALU.mult, [bP], [bP], s2=1.0, op1=ALU.add)
            with k.scope() as scL:
                gbT = scL.sb(un("gbT"), [128, 4, GT], BF16); bGB = Buf("gbT")
                zraw = scL.sb(un("zraw"), [128, 14, nseq * TP], BF16); bZ = Buf("zraw")
                k.ms(zraw[:], 0.0, [bZ])
                zr4 = zraw[:].rearrange("p c (s t) -> p c s t", s=nseq)
                with k.scope() as scA:
                    gaT = scA.sb(un("gaT"), [128, 4, GT], BF16); bGA = Buf("gaT")
                    qT = scA.sb(un("qT"), [64, 8, GT], BF16); bQ = Buf("qT")
                    kT = scA.sb(un("kT"), [64, 2, SK], BF16); bK = Buf("kT")
                    vtok = scA.sb(un("vtok"), [128, SK // 128, 128], BF16); bV = Buf("vtok")
                    if g == 1:
                      with k.scope() as sc:
                          ck = sc.sb(un("ck"), [128, 4, 128], F32); bCK = Buf("ck")
                          k.dma("sp", ck[:], cak[j].rearrange("(i p) f -> p i f", p=128), [], [bCK])
                          for i in range(4):
                              b2 = nb(); p2 = bank(b2)
                              for hh in range(2):
                                  k.tr(p2[0:64, hh * 128:(hh + 1) * 128], ck[:, i, hh * 64:(hh + 1) * 64], ident[:], [bCK, bC], [bB[b2]])
                              k.cp(kT[:, :, i * 128:(i + 1) * 128],
                                   p2[0:64, 0:256].rearrange("p (h t) -> p h t", h=2), [bB[b2]], [bK])
                          cv = sc.sb(un("cv"), [128, 4, 128], F32); bCV = Buf("cv")
                          k.dma("sp", cv[:], cav[j].rearrange("(i p) f -> p i f", p=128), [], [bCV])
                          k.cp(vtok[:, 0:4, :], cv[:], [bCV], [bV])

                    with k.scope() as sc:
                        hT, bHT = make_h(sc, L, g)
                        wblock = wblock_factory(sc)
                        if g == 1:
                            rope_tabs["cos"] = sc.sb(un("cos_sb"), [128, 8, 64], F32)
                            rope_tabs["sin"] = sc.sb(un("sin_sb"), [128, 8, 64], F32)
                            k.dma("sp", rope_tabs["cos"][:], cos_d[:, :, :], [], [bC])
                            k.dma("sp", rope_tabs["sin"][:], sin_d[:, :, :], [], [bC])
                        print("S1 sbuf remaining", nc.sbuf_bytes_remaining)
                        sqtL = [sc.sb(un("sqt"), [128, 256], F32) for _ in range(2)]
                        ssqL = [sc.sb(un("ssq"), [128, 4], F32) for _ in range(2)]
                        qnL = [sc.sb(un("qn"), [128, 4, 64], F32) for _ in range(2)]; bQNL = [Buf("qn0"), Buf("qn1")]
                        r1L = [sc.sb(un("r1"), [128, 4, 64], F32) for _ in range(2)]
                        r2L = [sc.sb(un("r2"), [128, 4, 64], F32) for _ in range(2)]
                        bTL = [Buf("tmpT0"), Buf("tmpT1")]
                        kvo = [sc.sb(un("kvo"), [128, 256], F32) for i in range(2)]
                        bKVO = [Buf("kvo0"), Buf("kvo1")]

                        def dest_q(bi_, t8, pb, bp):
                            ix = t8 % 2
                            sqt, ssq, qn, r1, r2, bQN, bTx = sqtL[ix], ssqL[ix], qnL[ix], r1L[ix], r2L[ix], bQNL[ix], bTL[ix]
                            headnorm((sqt, ssq), pb[:, 0:256].rearrange("p (h d) -> p h d", h=4), 4, 0, qn[:], [bp], [bQN], bT=bTx)
                            if g == 1:
                                rope(qn[:], 4, t8, r1, r2, [bQN], bT=bTx)
                            b2 = nb(); p2 = bank(b2)
                            for hh in range(4):
                                k.tr(p2[0:64, hh * 128:(hh + 1) * 128], qn[:, hh, :], ident[:], [bQN, bC], [bB[b2]])
                            k.cp(qT[:, bi_ * 4:bi_ * 4 + 4, t8 * 128:(t8 + 1) * 128],
                                 p2[0:64, :].rearrange("p (h t) -> p h t", h=4), [bB[b2]], [bQ])
                        proj_T(wblock, evw[j], [0, 1], hT, bHT, dest_q)

                        def dest_kv(bi_, t8, pb, bp):
                            ko = kvo[t8 % 2]; bko = bKVO[t8 % 2]
                            kn3 = ko[:, 0:128].rearrange("p (h d) -> p h d", h=2)
                            ix = t8 % 2
                            sqt, ssq, r1, r2, bTx = sqtL[ix], ssqL[ix], r1L[ix], r2L[ix], bTL[ix]
                            headnorm((sqt, ssq), pb[:, 0:128].rearrange("p (h d) -> p h d", h=2), 2, 1, kn3, [bp], [bko], bT=bTx)
                            k.cp(ko[:, 128:256], pb[:, 128:256], [bp], [bko])
                            k.cp(vtok[:, koff // 128 + t8, :], pb[:, 128:256], [bp], [bV])
                            if g == 0:
                                b_ = t8 // 2; t0 = (t8 % 2) * 128
                                k.dma("pool", nk_o[b_, j, t0:t0 + 128, :], ko[:, 0:128], [bko], [])
                                k.dma("pool", nv_o[b_, j, t0:t0 + 128, :], ko[:, 128:256], [bko], [])
                            else:
                                rope(kn3, 2, t8, r1, r2, [bko], bT=bTx)
                            b2 = nb(); p2 = bank(b2)
                            for hh in range(2):
                                k.tr(p2[0:64, hh * 128:(hh + 1) * 128], kn3[:, hh, :], ident[:], [bko, bC], [bB[b2]])
                            k.cp(kT[:, :, koff + t8 * 128:koff + (t8 + 1) * 128],
                                 p2[0:64, 0:256].rearrange("p (h t) -> p h t", h=2), [bB[b2]], [bK])
                        proj_T(wblock, evw[j], [2], hT, bHT, dest_kv)

                        def dest_ga(cc, tt, pb, bp):
                            k.act(gaT[:, cc, tt * 512:(tt + 1) * 512], pb, AF.Silu, [bp], [bGA])
                        proj_F(wblock, evw[j], [3, 4], hT, bHT, dest_ga)

                        def dest_zb(cc, tt, pb, bp):
                            if g == 0:
                                k.cp(zr4[:, cc, 2 * tt:2 * tt + 2, 1:T + 1], pb.rearrange("p (s t) -> p s t", s=2), [bp], [bZ])
                            else:
                                k.cp(zr4[:, cc, 0, 1 + tt * 512:1 + (tt + 1) * 512], pb, [bp], [bZ])
                        proj_F(wblock, evw[j], range(5, 12), hT, bHT, dest_zb)

                        def dest_gb(cc, tt, pb, bp):
                            k.act(gbT[:, cc, tt * 512:(tt + 1) * 512], pb, AF.Silu, [bp], [bGB])
                        proj_F(wblock, evw[j], [12, 13], hT, bHT, dest_gb)

                    with (k.scope() if STOP >= 2 else contextlib.nullcontext()) as sc:
                      if STOP >= 2:
                            pexp = [sc.sb(un("pexp"), [128, 512], BF16) for i in range(2)]
                            bPE = [Buf("pexp0"), Buf("pexp1")]
                            rec = sc.sb(un("rec"), [64, 512], F32); bRec = Buf("rec")
                            oa = sc.sb(un("oa"), [128, 512], F32); bOA = Buf("oa")
                            QB = min(T, 512)
                            po = bank(6); psm = bank(7)
                            ie = 0
                            for s in range(nseq):
                                kbase = s * T if g == 0 else 0
                                nsc = (T + koff) // 128
                                for h in range(8):
                                    kv = h // 4
                                    hb = (h % 2) * 64
                                    for qb in range(T // QB):
                                        q0 = s * T + qb * QB
                                        for sc_ in range(nsc):
                                            kpos = kbase + sc_ * 128
                                            bi = nb(0, 6); pb = bank(bi); bp = bB[bi]
                                            k.mm(pb[:, 0:QB], kT[:, kv, kpos:kpos + 128], qT[:, h, q0:q0 + QB], [bK, bQ], [bp])
                                            pe_ = pexp[ie % 2]; bpe = bPE[ie % 2]; ie += 1
                                            k.act(pe_[:, 0:QB], pb[:, 0:QB], AF.Exp, [bp], [bpe], scale=0.125)
                                            k.mm(po[0:64, 0:QB], vtok[:, kpos // 128, kv * 64:(kv + 1) * 64], pe_[:, 0:QB],
                                                 [bV, bpe], [bB[6]], start=(sc_ == 0), stop=(sc_ == nsc - 1))
                                            k.mm(psm[0:64, 0:QB], ones_bf[:, :], pe_[:, 0:QB],
                                                 [bC, bpe], [bB[7]], start=(sc_ == 0), stop=(sc_ == nsc - 1))
                                        k.recip(rec[:, 0:QB], psm[0:64, 0:QB], [bB[7]], [bRec])
                                        k.tt(oa[hb:hb + 64, 0:QB], po[0:64, 0:QB], rec[:, 0:QB], ALU.mult, [bB[6], bRec], [bOA])
                                        k.tt(oT[hb:hb + 64, h // 2, q0:q0 + QB], oa[hb:hb + 64, 0:QB],
                                             gaT[hb:hb + 64, h // 2, q0:q0 + QB], ALU.mult, [bOA, bGA], [bO])

                with k.scope() as sc:
                    C = 64
                    if STOP < 3:
                        raise_skip = True
                    else:
                        raise_skip = False
                    nchunk = T // C
                    f = lambda name, shape, dt=F32: sc.sb(un(name), shape, dt)
                    yf = f("yf", [128, nseq * nchunk // 2, 512], BF16); bYF = Buf("yf")
                    ST = [f("ST", [128, 4, 64]) for d in range(2)]; bST = [Buf("ST0"), Buf("ST1")]
                    zsP = [f("zs", [128, 14, C]) for p_ in range(2)]
                    zt1 = f("zt1", [128, 14, C])
                    twa = [f("tw", [65, C]) for d in range(2)]
                    ala = [f("al", [65, C]) for d in range(2)]
                    bW = Buf("rwtmp")
                    Ls = f("Ls", [64, 512])
                    gam = f("gam", [128, 4, C]); gamp = f("gamp", [128, 4, C]); ginv = f("ginv", [128, 4, C])
                    av = [f("av", [128, 4, C]) for d in range(2)]
                    kkr = f("kkr", [128, 4, C]); ksq = f("ksq", [128, 4, C]); kk = f("kk", [128, 4, C])
                    kdP = [[f("kd", [128, 4, C]) for d in range(2)] for p_ in range(2)]
                    tmp4 = f("tmp4", [128, 4, C]); tmp5 = f("tmp5", [128, 4, C])
                    Bn = {n_: Buf(n_) for n_ in ["zs", "zt1", "ala0", "ala1", "twa0", "twa1", "av0", "av1", "kkr", "ksq", "kk", "kd0", "kd1", "tmp4", "tmp5", "Ls", "gam", "gamp", "ginv", "AR", "KBt", "Gm1", "Gm2", "Nm0", "Nm1", "Am0", "Am1", "Vt", "Us0", "Us1", "KBT", "ysum", "ysq", "yst", "bon", "obt", "tmp6"] + [x + str(p_) for p_ in range(2) for x in ["zs", "AR", "KBt", "Vt", "Gm1", "Gm2", "Nmi", "gl", "kd0_", "kd1_"]]}
                    ARP = [f("AR", [128, 4, 2, C]) for p_ in range(2)]; KBtP = [f("KBt", [128, 4, 2, C]) for p_ in range(2)]
                    Gm1P = [f("Gm1", [64, 8, 128]) for p_ in range(2)]; Gm2P = [f("Gm2", [64, 8, 128]) for p_ in range(2)]; Nm = [f("Nm", [64, 8, 64]) for i in range(2)]
                    NmiP = [f("Nmi", [64, 8, 64]) for p_ in range(2)]; glP = [f("gl", [128, 4, 1]) for p_ in range(2)]; tmp6 = f("tmp6", [128, 4, C])
                    Am = [f("Am", [64, 8, 64]) for i in range(2)]
                    VtP = [f("Vt", [64, 512]) for p_ in range(2)]; Us = [f("Us", [64, 512]) for i in range(2)]
                    KBT = f("KBT", [64, 4, 2, 128])
                    ysum = f("ysum", [64, 8, 64]); ysq = f("ysq", [64, 8, 64]); yst = f("yst", [64, 16])
                    bon = f("bon", [128, 4, C]); obt = f("obt", [128, 4, C])
                    sto = f("sto", [64, 4, 128]); bSTO = Buf("sto")
                    sld = f("sld", [64, 8, 64]); bSLD = Buf("sld")
                    for d in range(2):
                        k.ms(twa[d][:], 1.0, [Bn["twa%d" % d]])
                        k.ms(ala[d][:], 1.0, [Bn["ala%d" % d]])

                    units = []
                    for s in range(min(nseq, KNS) if STOP >= 3 else 0):
                        for d in range(KND):
                            order = list(range(nchunk) if d == 0 else range(nchunk - 1, -1, -1))[:KNC]
                            for ci_, c in enumerate(order):
                                units.append((s, d, c, ci_ == 0, ci_ == len(order) - 1))
                    HO = [0, 2, 4, 6, 1, 3, 5, 7]
                    HOr = [1, 3, 5, 7, 0, 2, 4, 6]
                    rrA = [0]; rrB = [0]

                    def nbA():
                        rrA[0] += 1
                        return rrA[0] % 5

                    def nbB():
                        rrB[0] += 1
                        return 5 + rrB[0] % 3

                    def stageA(u, p):
                        s, d, c, first, last = u
                        tcol = s * TP + c * C
                        gt0 = s * T + c * C
                        yield
                        k.tt(zt1[:], zraw[:, :, tcol:tcol + C], zraw[:, :, tcol + 2:tcol + 2 + C], ALU.add, [bZ], [Bn["zt1"]], e="pool")
                        k.tt(zt1[:], zt1[:], evx[:, 0:14].unsqueeze(2).to_broadcast([128, 14, C]), ALU.mult, [Bn["zt1"], bP], [Bn["zt1"]], e="pool")
                        k.tt(zsP[p][:], zraw[:, :, tcol + 1:tcol + 1 + C], evx[:, 14:28].unsqueeze(2).to_broadcast([128, 14, C]),
                             ALU.mult, [bZ, bP], [Bn["zs%d" % p]])
                        k.tt(zsP[p][:], zsP[p][:], zt1[:], ALU.add, [Bn["zs%d" % p], Bn["zt1"]], [Bn["zs%d" % p]])
                        r_ = zsP[p][:, 0:4, :]; kraw = zsP[p][:, 4:8, :]; vv = zsP[p][:, 8:12, :]
                        dirs = [d] if d == 0 else [0, 1]
                        yield
                        for dd in dirs:
                            bal = Bn["ala%d" % dd]; bav = Bn["av%d" % dd]; bkd = Bn["kd%d_%d" % (dd, p)]
                            k.cp(ala[dd][0:64, :], zsP[p][dd * 64:(dd + 1) * 64, 13, :], [Bn["zs%d" % p]], [bal])
                            b2 = nbA(); p2 = bank(b2)
                            for cp in range(4):
                                k.mm(p2[:, cp * C:(cp + 1) * C], a2aug[:, dd, cp * 128:(cp + 1) * 128], ala[dd][:, :],
                                     [bal, bP], [bB[b2]])
                            k.act(av[dd][:].rearrange("p c t -> p (c t)"), p2[:, 0:4 * C], AF.Sigmoid, [bB[b2]], [bav])
                            k.tt(tmp4[:], av[dd][:], evs[:, 18:22].unsqueeze(2).to_broadcast([128, 4, C]), ALU.mult, [bav, bP], [Bn["tmp4"]], e="pool")
                            k.tt(tmp4[:], tmp4[:], evx[:, 28:32].unsqueeze(2).to_broadcast([128, 4, C]), ALU.add, [Bn["tmp4"], bP], [Bn["tmp4"]], e="pool")
                            k.tt(kdP[p][dd][:], kraw, tmp4[:], ALU.mult, [Bn["zs%d" % p], Bn["tmp4"]], [bkd], e="pool")
                        yield
                        k.tt(kkr[:], kraw, evs[:, 14:18].unsqueeze(2).to_broadcast([128, 4, C]), ALU.mult, [Bn["zs%d" % p], bP], [Bn["kkr"]])
                        k.tt(ksq[:], kkr[:], kkr[:], ALU.mult, [Bn["kkr"]], [Bn["ksq"]])
                        b2 = nbA(); p2 = bank(b2)
                        k.mm(p2[:, 0:4 * C], bdones[:, :], ksq[:].rearrange("p c t -> p (c t)"), [Bn["ksq"], bC], [bB[b2]])
                        k.ts(ksq[:].rearrange("p c t -> p (c t)"), p2[:, 0:4 * C], 1e-12, ALU.add, [bB[b2]], [Bn["ksq"]])
                        k.act(ksq[:], ksq[:], AF.Sqrt, [Bn["ksq"]], [Bn["ksq"]])
                        k.recip(ksq[:], ksq[:], [Bn["ksq"]], [Bn["ksq"]])
                        k.tt(kk[:], kkr[:], ksq[:], ALU.mult, [Bn["kkr"], Bn["ksq"]], [Bn["kk"]])
                        yield
                        btw = Bn["twa%d" % d]
                        k.act(twa[d][0:64, :], zsP[p][d * 64:(d + 1) * 64, 12, :], AF.Tanh, [Bn["zs%d" % p]], [btw])
                        b2 = nbA(); p2 = bank(b2)
                        k.mm(p2[0:64, :], twa[d][:, :], w2aug[:, d, :], [btw, bP], [bB[b2]])
                        k.act(Ls[:], p2[0:64, :], AF.Sigmoid, [bB[b2]], [Bn["Ls"]])
                        b2 = nbA(); p2 = bank(b2)
                        for cp in range(4):
                            k.mm(p2[:, cp * 128:(cp + 1) * 128], Ls[:, cp * 128:(cp + 1) * 128],
                                 tri64[:, d, :, :].rearrange("p a t -> p (a t)"), [Bn["Ls"], bC], [bB[b2]])
                        p4 = p2.rearrange("p (c a t) -> p c a t", c=4, a=2)
                        k.act(gamp[:], p4[:, :, 1, :], AF.Exp, [bB[b2]], [Bn["gamp"]])
                        k.act(ginv[:], p4[:, :, 0, :], AF.Exp, [bB[b2]], [Bn["ginv"]], scale=-1.0)
                        k.act(gam[:], p4[:, :, 0, :], AF.Exp, [bB[b2]], [Bn["gam"]])
                        k.cp(glP[p][:], (gam[:, :, C - 1:C] if d == 0 else gam[:, :, 0:1]), [Bn["gam"]], [Bn["gl%d" % p]])
                        yield
                        bav = Bn["av%d" % d]; bkd = Bn["kd%d_%d" % (d, p)]
                        k.stt(ARP[p][:, :, 0, :], kk[:], -1.0, gamp[:], ALU.mult, ALU.mult, [Bn["kk"], Bn["gamp"]], [Bn["AR%d" % p]])
                        k.tt(KBtP[p][:, :, 0, :], kdP[p][d][:], ginv[:], ALU.mult, [bkd, Bn["ginv"]], [Bn["KBt%d" % p]])
                        k.tt(tmp5[:], kk[:], av[d][:], ALU.mult, [Bn["kk"], bav], [Bn["tmp5"]])
                        k.tt(KBtP[p][:, :, 1, :], tmp5[:], ginv[:], ALU.mult, [Bn["tmp5"], Bn["ginv"]], [Bn["KBt%d" % p]])
                        k.tt(ARP[p][:, :, 1, :], r_, gam[:], ALU.mult, [Bn["zs%d" % p], Bn["gam"]], [Bn["AR%d" % p]])
                        yield
                        b2 = nbA(); p2 = bank(b2)
                        for cp in range(4):
                            k.tr(p2[0:64, cp * 128:(cp + 1) * 128], zsP[p][:, 8 + cp, :], ident[:], [Bn["zs%d" % p], bC], [bB[b2]])
                        k.act(VtP[p][:], p2[0:64, :], AF.Copy, [bB[b2]], [Bn["Vt%d" % p]])
                        yield
                        g1, g2, gn = 0, 2, 4
                        for h in HO:
                            cp = h // 2; hb = (h % 2) * 64
                            arh = ARP[p][hb:hb + 64, cp, :, :].rearrange("p a t -> p (a t)")
                            k.mm(bank(g1 + h // 4)[0:64, (h % 4) * 128:(h % 4 + 1) * 128], KBtP[p][hb:hb + 64, cp, 0, :], arh,
                                 [Bn["KBt%d" % p], Bn["AR%d" % p]], [bB[g1 + h // 4]])
                            k.mm(bank(g2 + h // 4)[0:64, (h % 4) * 128:(h % 4 + 1) * 128], KBtP[p][hb:hb + 64, cp, 1, :], arh,
                                 [Bn["KBt%d" % p], Bn["AR%d" % p]], [bB[g2 + h // 4]])
                            k.mm(bank(gn)[0:64, h * 64:(h + 1) * 64], ARP[p][hb:hb + 64, cp, 0, :], KBtP[p][hb:hb + 64, cp, 1, :],
                                 [Bn["KBt%d" % p], Bn["AR%d" % p]], [bB[gn]])
                        yield
                        m64b = mask64[:, d, :].unsqueeze(1).to_broadcast([64, 4, 128])
                        for hf in range(2):
                            k.tt(Gm1P[p][:, hf * 4:hf * 4 + 4, :], bank(g1 + hf)[0:64, :].rearrange("p (h t) -> p h t", h=4), m64b,
                                 ALU.mult, [bB[g1 + hf], bC], [Bn["Gm1%d" % p]])
                        for hf in range(2):
                            k.tt(Gm2P[p][:, hf * 4:hf * 4 + 4, :], bank(g2 + hf)[0:64, :].rearrange("p (h t) -> p h t", h=4), m64b,
                                 ALU.mult, [bB[g2 + hf], bC], [Bn["Gm2%d" % p]])
                        k.tt(NmiP[p][:], bank(gn)[0:64, :].rearrange("p (h t) -> p h t", h=8),
                             maskN[:, d, :].unsqueeze(1).to_broadcast([64, 8, 64]), ALU.mult, [bB[gn], bC], [Bn["Nmi%d" % p]])

                        yield

                    def stageB(u, p, pull):
                        s, d, c, first, last = u
                        tcol = s * TP + c * C
                        gt0 = s * T + c * C
                        r_ = zsP[p][:, 0:4, :]; vv = zsP[p][:, 8:12, :]
                        if first:
                            if g == 0:
                                k.ms(ST[d][:], 0.0, [bST[d]])
                            else:
                                k.dma("sp", sld[:], srs_d[d][j].rearrange("h v k -> v h k"), [], [bSLD])
                                b2 = nbB(); p2 = bank(b2)
                                for cp in range(4):
                                    k.tr(p2[:, cp * 64:(cp + 1) * 64], sld[:, 2 * cp:2 * cp + 2, :].rearrange("v h k -> v (h k)"),
                                         ident[0:64, 0:64], [bSLD, bC], [bB[b2]])
                                k.cp(ST[d][:], p2[:, 0:256].rearrange("p (c v) -> p c v", c=4), [bB[b2]], [bST[d]])

                        b2 = nbB(); p2 = bank(b2)
                        for h in HOr:
                            cp = h // 2; hb = (h % 2) * 64
                            k.mm(p2[0:64, h * 64:(h + 1) * 64], ARP[p][hb:hb + 64, cp, 0, :], ST[d][hb:hb + 64, cp, :],
                                 [Bn["AR%d" % p], bST[d]], [bB[b2]])
                        b2b = nbB(); p2b = bank(b2b)
                        for h in range(8):
                            k.mm(p2b[0:64, h * 64:(h + 1) * 64], Gm1P[p][:, h, 0:64], VtP[p][:, h * 64:(h + 1) * 64],
                                 [Bn["Gm1%d" % p], Bn["Vt%d" % p]], [bB[b2b]])
                        k.act(Us[0][:], p2[0:64, :], AF.Copy, [bB[b2]], [Bn["Us0"]])
                        k.tt(Us[0][:], Us[0][:], p2b[0:64, :], ALU.add, [Bn["Us0"], bB[b2b]], [Bn["Us0"]])
                        ui = 0; ai = 0
                        for jn in range(6):
                            bA = Bn["Am%d" % ai] if jn else Bn["Gm2%d" % p]; bN = Bn["Nm%d" % ai] if jn else Bn["Nmi%d" % p]
                            bA2 = Bn["Am%d" % (1 - ai)]; bN2 = Bn["Nm%d" % (1 - ai)]
                            Acur = (lambda h, ai=ai: Am[ai][:, h, :]) if jn else (lambda h: Gm2P[p][:, h, 0:64])
                            Ncur = (lambda h, ai=ai: Nm[ai][:, h, :]) if jn else (lambda h: NmiP[p][:, h, :])
                            pull()
                            bU = Bn["Us%d" % ui]; bU2 = Bn["Us%d" % (1 - ui)]
                            b2 = nbB(); p2 = bank(b2)
                            for h in range(8):
                                k.mm(p2[0:64, h * 64:(h + 1) * 64], Acur(h), Us[ui][:, h * 64:(h + 1) * 64], [bA, bU], [bB[b2]])
                            if jn < 5:
                                b3 = nbB(); p3 = bank(b3)
                                for h in range(8):
                                    k.mm(p3[0:64, h * 64:(h + 1) * 64], Ncur(h), Acur(h), [bN, bA], [bB[b3]])
                                if jn < 4:
                                    b4 = nbB(); p4_ = bank(b4)
                                    for h in range(8):
                                        k.mm(p4_[0:64, h * 64:(h + 1) * 64], Acur(h), Ncur(h), [bA, bN], [bB[b4]])
                            k.tt(Us[1 - ui][:], Us[ui][:], p2[0:64, :], ALU.add, [bU, bB[b2]], [bU2])
                            ui = 1 - ui
                            pull()
                            if jn < 5:
                                k.act(Am[1 - ai][:].rearrange("p h t -> p (h t)"), p3[0:64, :], AF.Copy, [bB[b3]], [bA2])
                                if jn < 4:
                                    k.act(Nm[1 - ai][:].rearrange("p h t -> p (h t)"), p4_[0:64, :], AF.Copy, [bB[b4]], [bN2])
                                ai = 1 - ai
                        U = Us[ui]; bU = Bn["Us%d" % ui]
                        by = nbB(); py = bank(by)
                        by0 = nbB(); py0 = bank(by0)
                        for h in range(8):
                            o_ = py[0:64, h * 64:(h + 1) * 64]
                            k.mm(o_, Gm2P[p][:, h, 64:128], U[:, h * 64:(h + 1) * 64], [Bn["Gm2%d" % p], bU], [bB[by]], start=True, stop=False)
                            k.mm(o_, Gm1P[p][:, h, 64:128], VtP[p][:, h * 64:(h + 1) * 64], [Bn["Gm1%d" % p], Bn["Vt%d" % p]], [bB[by]], start=False, stop=True)
                        for h in HO:
                            cp = h // 2; hb = (h % 2) * 64
                            k.mm(py0[0:64, h * 64:(h + 1) * 64], ARP[p][hb:hb + 64, cp, 1, :], ST[d][hb:hb + 64, cp, :], [Bn["AR%d" % p], bST[d]], [bB[by0]])
                        ci = s * nchunk + c
                        ys2 = ysum[:].rearrange("p h v -> p (h v)")
                        yfs = yf[(ci % 2) * 64:(ci % 2) * 64 + 64, ci // 2, :]
                        bys = Bn["ysum"]
                        if d == 0:
                            k.act(ys2, py0[0:64, :], AF.Copy, [bB[by0]], [bys])
                            k.tt(ys2, ys2, py[0:64, :], ALU.add, [bys, bB[by]], [bys])
                            k.cp(yfs, ys2, [bys], [bYF])
                        else:
                            k.tt(ys2, py[0:64, :], yfs, ALU.add, [bB[by], bYF], [bys])
                            k.tt(ys2, ys2, py0[0:64, :], ALU.add, [bys, bB[by0]], [bys])
                        pull()
                        bt1 = 6
                        for cp in range(4):
                            for a_ in range(2):
                                idx = cp * 2 + a_
                                k.tr(bank(bt1 + idx // 4)[0:64, (idx % 4) * 128:(idx % 4 + 1) * 128], KBtP[p][:, cp, a_, :], ident[:],
                                     [Bn["KBt%d" % p], bC], [bB[bt1 + idx // 4]])
                        k.act(KBT[:, 0:2, :, :].rearrange("p c a k -> p (c a k)"), bank(bt1)[0:64, :], AF.Copy, [bB[bt1]], [Bn["KBT"]])
                        k.cp(KBT[:, 2:4, :, :].rearrange("p c a k -> p (c a k)"), bank(bt1 + 1)[0:64, :], [bB[bt1 + 1]], [Bn["KBT"]])
                        bs = 5; psu = bank(bs)
                        for cp in range(4):
                            k.mm(psu[:, cp * 128:(cp + 1) * 128], KBT[:, cp, 0, :], VtP[p][:, cp * 128:(cp + 1) * 128], [Bn["KBT"], Bn["Vt%d" % p]], [bB[bs]],
                                 start=True, stop=False)
                            k.mm(psu[:, cp * 128:(cp + 1) * 128], KBT[:, cp, 1, :], U[:, cp * 128:(cp + 1) * 128], [Bn["KBT"], bU], [bB[bs]],
                                 start=False, stop=True)
                        ps4 = psu.rearrange("p (c x) -> p c x", c=4)
                        for hh in range(2):
                            hb = hh * 64
                            k.tt(ST[d][hb:hb + 64, :, :], ST[d][hb:hb + 64, :, :], ps4[hb:hb + 64, :, hb:hb + 64], ALU.add,
                                 [bST[d], bB[bs]], [bST[d]])
                            k.tt(ST[d][hb:hb + 64, :, :], ST[d][hb:hb + 64, :, :],
                                 glP[p][hb:hb + 64, :, :].to_broadcast([64, 4, 64]), ALU.mult, [bST[d], Bn["gl%d" % p]], [bST[d]])
                        pull()
                        if d == 1:
                            k.red(yst[:, 0:8], ysum[:], [bys], [Bn["yst"]])
                            k.ts(yst[:, 0:8], yst[:, 0:8], 1.0 / 64, ALU.mult, [Bn["yst"]], [Bn["yst"]])
                            k.tt(ysum[:], ysum[:], yst[:, 0:8].unsqueeze(2).to_broadcast([64, 8, 64]), ALU.subtract, [bys, Bn["yst"]], [bys])
                            k.tt(ysq[:], ysum[:], ysum[:], ALU.mult, [bys], [Bn["ysq"]])
                            k.red(yst[:, 8:16], ysq[:], [Bn["ysq"]], [Bn["yst"]])
                            k.ts(yst[:, 8:16], yst[:, 8:16], 1.0 / 64, ALU.mult, [Bn["yst"]], [Bn["yst"]], s2=GN_EPS, op1=ALU.add)
                            k.act(yst[:, 8:16], yst[:, 8:16], AF.Sqrt, [Bn["yst"]], [Bn["yst"]])
                            k.recip(yst[:, 8:16], yst[:, 8:16], [Bn["yst"]], [Bn["yst"]])
                            k.tt(ysum[:], ysum[:], yst[:, 8:16].unsqueeze(2).to_broadcast([64, 8, 64]), ALU.mult, [bys, Bn["yst"]], [bys])
                            k.tt(tmp6[:], kdP[p][0][:], kdP[p][1][:], ALU.add, [Bn["kd0_%d" % p], Bn["kd1_%d" % p]], [Bn["tmp6"]], e="pool")
                            k.tt(tmp6[:], tmp6[:], r_, ALU.mult, [Bn["tmp6"], Bn["zs%d" % p]], [Bn["tmp6"]], e="pool")
                            k.tt(tmp6[:], tmp6[:], evs[:, 22:26].unsqueeze(2).to_broadcast([128, 4, C]), ALU.mult, [Bn["tmp6"], bP], [Bn["tmp6"]], e="pool")
                            b2 = nbB(); p2 = bank(b2)
                            k.mm(p2[:, 0:4 * C], bdones[:, :], tmp6[:].rearrange("p c t -> p (c t)"), [Bn["tmp6"], bC], [bB[b2]])
                            k.tt(bon[:], p2[:, 0:4 * C].rearrange("p (c t) -> p c t", c=4), vv, ALU.mult, [bB[b2], Bn["zs%d" % p]], [Bn["bon"]])
                            b3 = nbB(); p3 = bank(b3)
                            for cp in range(4):
                                k.tr(p3[:, cp * C:(cp + 1) * C], ysum[:, 2 * cp:2 * cp + 2, :].rearrange("p h v -> p (h v)"),
                                     ident[0:64, 0:64], [bys, bC], [bB[b3]])
                            p3v = p3[:, 0:4 * C].rearrange("p (c t) -> p c t", c=4)
                            k.tt(obt[:], p3v, evs[:, 26:30].unsqueeze(2).to_broadcast([128, 4, C]), ALU.mult, [bB[b3], bP], [Bn["obt"]])
                            k.tt(obt[:], obt[:], evs[:, 30:34].unsqueeze(2).to_broadcast([128, 4, C]), ALU.add, [Bn["obt"], bP], [Bn["obt"]])
                            k.tt(obt[:], obt[:], bon[:], ALU.add, [Bn["obt"], Bn["bon"]], [Bn["obt"]])
                            k.tt(oT[:, 4:8, gt0:gt0 + C], obt[:], gbT[:, :, gt0:gt0 + C], ALU.mult, [Bn["obt"], bGB], [bO])

                        if last:
                            if g == 0:
                                b2 = nbB(); p2 = bank(b2)
                                for cp in range(4):
                                    k.tr(p2[0:64, cp * 128:(cp + 1) * 128], ST[d][:, cp, :], ident[:], [bST[d], bC], [bB[b2]])
                                k.cp(sto[:].rearrange("p c x -> p (c x)"), p2[0:64, :], [bB[b2]], [bSTO])
                                k.dma("pool", nrs_o[d][s, j].rearrange("h v k -> v h k"),
                                      sto[:].rearrange("p c (h k) -> p (c h) k", h=2), [bSTO], [])

                    gens = {}
                    if units:
                        for _ in stageA(units[0], 0):
                            pass
                    for ui_, u in enumerate(units):
                        nxt = stageA(units[ui_ + 1], (ui_ + 1) % 2) if ui_ + 1 < len(units) else None

                        def pull(nxt=nxt, n=2):
                            if nxt is None:
                                return
                            for _ in range(n):
                                try:
                                    next(nxt)
                                except StopIteration:
                                    return
                        stageB(u, ui_ % 2, pull)
                        if nxt is not None:
                            for _ in nxt:
                                pass
            if STOP >= 4:
                out_proj(evo[j])

        def odd_layer(g, j, L):
            nseq, T = (4, 256) if g == 0 else (1, 1024)
            C = 128
            nchunk = T // C
            k.dma("sp", gw2aug[:], gw2aug_d[j], [], [bP])
            k.dma("sp", ln256[:], ln256_d[j], [], [bP])
            with k.scope() as scL:
                qT = scL.sb(un("gqT"), [128, 4, GT], BF16); bQ = Buf("gqT")
                kT = scL.sb(un("gkT"), [128, 4, GT], BF16); bK = Buf("gkT")
                vt = scL.sb(un("gvt"), [128, 8, 1024], BF16); bV = Buf("gvt")
                gs = scL.sb(un("ggs"), [128, 8, GT], BF16); bG = Buf("ggs")
                glT = scL.sb(un("glT"), [17, 2, GT], F32); bGL = Buf("glT")
                of = scL.sb(un("gof"), [128, 8, 1024], BF16); bOF = Buf("gof")
                k.ms(glT[:], 1.0, [bGL])
                with k.scope() as sc:
                    hT, bHT = make_h(sc, L, g)
                    wblock = wblock_factory(sc)

                    def dest_q(cc, tt, pb, bp):
                        k.ts(qT[:, cc, tt * 512:(tt + 1) * 512], pb, float(128 ** -0.5), ALU.mult, [bp], [bQ])
                    proj_F(wblock, odw[j], [0, 1], hT, bHT, dest_q)

                    def dest_k(cc, tt, pb, bp):
                        k.cp(kT[:, cc, tt * 512:(tt + 1) * 512], pb, [bp], [bK])
                    proj_F(wblock, odw[j], [2, 3], hT, bHT, dest_k)

                    def dest_v(bi_, t8, pb, bp):
                        k.cp(vt[:, t8, bi_ * 256:(bi_ + 1) * 256], pb[:, 0:256], [bp], [bV])
                    proj_T(wblock, odw[j], [4, 5, 6, 7], hT, bHT, dest_v)

                    def dest_g(cc, tt, pb, bp):
                        k.act(gs[:, cc, tt * 512:(tt + 1) * 512], pb, AF.Silu, [bp], [bG])
                    proj_F(wblock, odw[j], [8, 9, 10, 11], hT, bHT, dest_g)

                    def dest_gl(cc, tt, pb, bp):
                        if cc < 2:
                            k.cp(glT[0:16, cc, tt * 512:(tt + 1) * 512], pb[0:16, :], [bp], [bGL])
                    proj_F(wblock, odw[j], [12], hT, bHT, dest_gl, msub=16, nsub=2)

                with k.scope() as sc:
                    f = lambda name, shape, dt=F32: sc.sb(un(name), shape, dt)
                    S = [f("gS", [128, 4, 256]) for d in range(2)]; bS = [Buf("gS0"), Buf("gS1")]
                    bW = Buf("glatmp")
                    Lg = f("Lg", [128, 512])
                    gam = f("ggam", [128, 4, C]); ginv = f("gginv", [128, 4, C])
                    qs = f("gqs", [128, 4, C]); ks = f("gks", [128, 4, C])
                    Am = f("gAm", [128, 4, C], BF16); kTt = f("gkTt", [128, 4, C], BF16)
                    osum = f("gosum", [128, 4, 256]); osq = f("gosq", [128, 4, 256]); ost = f("gost", [128, 4])
                    print("GLA sbuf remaining", nc.sbuf_bytes_remaining)
                    for s in range(nseq):
                        for d in range(2):
                            if g == 0:
                                k.ms(S[d][:], 0.0, [bS[d]])
                            else:
                                k.dma("sp", S[d][:], sgs_d[d][j].rearrange("h k v -> k h v"), [], [bS[d]])
                            order = range(nchunk) if d == 0 else range(nchunk - 1, -1, -1)
                            for c in order:
                                gt0 = s * T + c * C
                                t8 = gt0 // 128
                                b2 = nb(); p2 = bank(b2)
                                k.mm(p2, glT[:, d, gt0:gt0 + C], gw2aug[:, d, :], [bGL, bP], [bB[b2]])
                                k.act(Lg[:], p2, AF.Sigmoid, [bB[b2]], [bW])
                                k.act(Lg[:], Lg[:], AF.Ln, [bW], [bW])
                                b2 = nb(); p2 = bank(b2)
                                for h in range(4):
                                    k.mm(p2[:, h * C:(h + 1) * C], Lg[:, h * 128:(h + 1) * 128], tri128[:, d, :], [bW, bC], [bB[b2]])
                                k.act(gam[:].rearrange("p h t -> p (h t)"), p2, AF.Exp, [bB[b2]], [bW])
                                k.act(ginv[:].rearrange("p h t -> p (h t)"), p2, AF.Exp, [bB[b2]], [bW], scale=-1.0)
                                glast = gam[:, :, C - 1:C] if d == 0 else gam[:, :, 0:1]
                                k.tt(qs[:], qT[:, :, gt0:gt0 + C], gam[:], ALU.mult, [bQ, bW], [bW])
                                k.tt(ks[:], kT[:, :, gt0:gt0 + C], ginv[:], ALU.mult, [bK, bW], [bW])
                                b2 = nb(); p2 = bank(b2)
                                for h in range(4):
                                    k.mm(p2[:, h * C:(h + 1) * C], ks[:, h, :], qs[:, h, :], [bW], [bB[b2]])
                                k.tt(Am[:], p2.rearrange("p (h t) -> p h t", h=4), mask128[:, d, :].unsqueeze(1).to_broadcast([128, 4, C]),
                                     ALU.mult, [bB[b2], bC], [bW])
                                b2 = nb(); p2 = bank(b2)
                                for h in range(4):
                                    k.tr(p2[:, h * C:(h + 1) * C], ks[:, h, :], ident[:], [bW, bC], [bB[b2]])
                                k.cp(kTt[:].rearrange("p h t -> p (h t)"), p2, [bB[b2]], [bW])
                                by = nb2()
                                for h in range(4):
                                    o_ = bank(by + h // 2)[:, (h % 2) * 256:(h % 2 + 1) * 256]
                                    k.mm(o_, qs[:, h, :], S[d][:, h, :], [bW, bS[d]], [bB[by + h // 2]], start=True, stop=False)
                                    k.mm(o_, Am[:, h, :], vt[:, t8, h * 256:(h + 1) * 256], [bW, bV], [bB[by + h // 2]], start=False, stop=True)
                                bs = nb2()
                                for h in range(4):
                                    k.mm(bank(bs + h // 2)[:, (h % 2) * 256:(h % 2 + 1) * 256], kTt[:, h, :], vt[:, t8, h * 256:(h + 1) * 256],
                                         [bW, bV], [bB[bs + h // 2]])
                                for hf in range(2):
                                    sl = S[d][:, 2 * hf:2 * hf + 2, :]
                                    k.tt(sl, sl, bank(bs + hf).rearrange("p (h v) -> p h v", h=2), ALU.add, [bS[d], bB[bs + hf]], [bS[d]])
                                k.tt(S[d][:], S[d][:], glast.to_broadcast([128, 4, 256]), ALU.mult, [bS[d], bW], [bS[d]])
                                if d == 0:
                                    for hf in range(2):
                                        k.cp(of[:, t8, hf * 512:(hf + 1) * 512], bank(by + hf), [bB[by + hf]], [bOF])
                                else:
                                    for hf in range(2):
                                        k.tt(osum[:, 2 * hf:2 * hf + 2, :].rearrange("p h v -> p (h v)"), bank(by + hf),
                                             of[:, t8, hf * 512:(hf + 1) * 512], ALU.add, [bB[by + hf], bOF], [bW])
                                    k.act(osq[:], osum[:], AF.Square, [bW], [bW])
                                    k.red(ost[:], osq[:], [bW], [bW])
                                    k.ts(ost[:], ost[:], 1.0 / 256, ALU.mult, [bW], [bW], s2=EPS, op1=ALU.add)
                                    k.act(ost[:], ost[:], AF.Sqrt, [bW], [bW])
                                    k.recip(ost[:], ost[:], [bW], [bW])
                                    k.tt(osum[:], osum[:], ost[:].unsqueeze(2).to_broadcast([128, 4, 256]), ALU.mult, [bW], [bW])
                                    k.tt(osum[:], osum[:], ln256[:, :].unsqueeze(1).to_broadcast([128, 4, 256]), ALU.mult, [bW, bP], [bW])
                                    bt = nb2()
                                    o2 = osum[:].rearrange("p h v -> p (h v)")
                                    for cc in range(8):
                                        k.tr(bank(bt + cc // 4)[:, (cc % 4) * 128:(cc % 4 + 1) * 128], o2[:, cc * 128:(cc + 1) * 128], ident[:],
                                             [bW, bC], [bB[bt + cc // 4]])
                                    for hf in range(2):
                                        k.tt(oT[:, 4 * hf:4 * hf + 4, gt0:gt0 + C], bank(bt + hf).rearrange("p (c t) -> p c t", c=4),
                                             gs[:, 4 * hf:4 * hf + 4, gt0:gt0 + C], ALU.mult, [bB[bt + hf], bG], [bO])
                            if g == 0:
                                k.dma("pool", ngs_o[d][s, j].rearrange("h k v -> k h v"), S[d][:], [bS[d]], [])
            out_proj(odo[j])

        for g in groups:
            with k.scope() as sc:
                xin = [sc.sb(un("xin"), [128, D], F32) for i in range(2)]
                bxin = [Buf("xin0"), Buf("xin1")]
                for t8 in range(GT // 128):
                    xi = xin[t8 % 2]; bxi = bxin[t8 % 2]
                    k.dma("sp", xi[:], xg[g, t8 * 128:(t8 + 1) * 128, :], [], [bxi])
                    for half in range(2):
                        b2 = nb(); p2 = bank(b2)
                        for jj in range(4):
                            cch = half * 4 + jj
                            k.tr(p2[:, jj * 128:(jj + 1) * 128], xi[:, cch * 128:(cch + 1) * 128], ident[:], [bxi, bC], [bB[b2]])
                        k.cp(xT[:, half * 4:(half + 1) * 4, t8 * 128:(t8 + 1) * 128], p2.rearrange("p (j t) -> p j t", j=4), [bB[b2]], [bX])
            for L in range(nlayers):
                if L % 2 == 0:
                    even_layer(g, L // 2, L)
                else:
                    odd_layer(g, L // 2, L)
            with k.scope() as sc:
                rstd = compute_rstd(sc)
                yout = [sc.sb(un("yout"), [128, D], F32) for i in range(2)]
                byo = [Buf("yout0"), Buf("yout1")]
                tmpn = [sc.sb(un("tmpn"), [128, 512], F32) for i in range(2)]
                btn = [Buf("tmpn0"), Buf("tmpn1")]
                it = 0
                for t8 in range(GT // 128):
                    yo = yout[t8 % 2]; by_ = byo[t8 % 2]
                    for half in range(2):
                        b2 = nb(); p2 = bank(b2)
                        tn = tmpn[it % 2]; bt_ = btn[it % 2]; it += 1
                        for jj in range(4):
                            cch = half * 4 + jj
                            k.stt(tn[:, jj * 128:(jj + 1) * 128], xT[:, cch, t8 * 128:(t8 + 1) * 128], finalg_sb[:, cch:cch + 1],
                                  rstd[:, t8 * 128:(t8 + 1) * 128], ALU.mult, ALU.mult, [bX, bR, bC], [bt_])
                            k.tr(p2[:, jj * 128:(jj + 1) * 128], tn[:, jj * 128:(jj + 1) * 128], ident[:], [bt_, bC], [bB[b2]])
                        k.act(yo[:, half * 512:(half + 1) * 512], p2, AF.Copy, [bB[b2]], [by_])
                    k.dma("pool", yg[g, t8 * 128:(t8 + 1) * 128, :], yo[:], [by_], [])
        k.barrier()
    print("instructions:", k.ninstr, {e: c for e, c in k.cnt.items()})
    return nc


_CACHE = {}


def _consts():
    f32 = np.float32
    c = {}
    c["ident"] = np.eye(128, dtype=f32)
    s = np.arange(64)[:, None]; t = np.arange(64)[None, :]
    tri = np.zeros((64, 2, 2, 64), f32)
    tri[:, 0, 0] = (s <= t); tri[:, 0, 1] = (s < t); tri[:, 1, 0] = (s >= t); tri[:, 1, 1] = (s > t)
    c["tri64"] = tri * f32(-RWKV_DECAY_SCALE)
    s1 = np.arange(128)[:, None]; t1 = np.arange(128)[None, :]
    tri128 = np.zeros((128, 2, 128), f32)
    tri128[:, 0] = (s1 <= t1); tri128[:, 1] = (s1 >= t1)
    c["tri128"] = tri128 / f32(16.0)
    c["mask128"] = tri128.copy()
    m64 = np.zeros((64, 2, 128), f32)
    m64[:, 0, 0:64] = (s < t); m64[:, 0, 64:128] = (s <= t); m64[:, 1, 0:64] = (s > t); m64[:, 1, 64:128] = (s >= t)
    c["mask64"] = m64
    mN = np.zeros((64, 2, 64), f32)
    mN[:, 0] = (t < s); mN[:, 1] = (t > s)
    c["maskN"] = mN
    bd = np.zeros((128, 128), f32); bd[0:64, 0:64] = 1; bd[64:128, 64:128] = 1
    c["bdones"] = bd
    T = 1024
    row = np.repeat(np.arange(T // 64), 64).astype(f32); col = np.tile(np.arange(64), T // 64).astype(f32)
    inv = (f32(10000.0) ** (-np.arange(16, dtype=f32) / f32(16))).astype(f32)
    ang = np.stack([row[:, None] * inv, col[:, None] * inv], axis=1).astype(f32)
    cos = np.cos(ang).astype(f32); sin = np.sin(ang).astype(f32)
    cos64 = np.stack([cos, cos], axis=2).reshape(T, 64)
    sin64 = np.stack([-sin, sin], axis=2).reshape(T, 64)
    c["ropecos"] = np.ascontiguousarray(cos64.reshape(8, 128, 64).transpose(1, 0, 2))
    c["ropesin"] = np.ascontiguousarray(sin64.reshape(8, 128, 64).transpose(1, 0, 2))
    return c


def kernel(**inp):
    f32 = np.float32
    n = 8
    A = lambda name: np.asarray(inp[name], f32)
    x_prompt = A("x_prompt"); x_sample = A("x_sample"); c = A("c"); c_ctx = A("c_ctx")

    def pc(v, nchunk):
        v = np.asarray(v, f32)
        lead = v.shape[:-1]
        v = v.reshape(lead + (nchunk, 128))
        return np.ascontiguousarray(np.moveaxis(v, -1, 0))

    def wblk(w, nblk):
        Ln, rows, cols = w.shape
        if cols < nblk * WB:
            w = np.concatenate([w, np.zeros((Ln, rows, nblk * WB - cols), f32)], axis=2)
        return np.ascontiguousarray(w.reshape(Ln, NCH, 128, nblk, WB).transpose(0, 3, 2, 1, 4))

    shared = dict(_consts())
    shared["modw"] = wblk(A("mod_w"), 12).reshape(DEPTH * 12, 128, NCH, WB)
    shared["modbT"] = pc(A("mod_b"), 24)
    shared["normgT"] = pc(A("norm_g"), NCH)
    shared["finalgT"] = pc(A("final_g"), NCH)
    shared["evw"] = wblk(A("ev_w_in"), 14)
    shared["evo"] = wblk(A("ev_w_out"), 4)
    shared["odw"] = wblk(A("od_w_in"), 13)
    shared["odo"] = wblk(A("od_w_out"), 4)
    shared["w2aug"] = np.ascontiguousarray(np.concatenate([A("rw_w2"), A("rw_w0")[:, :, None, :]], axis=2).transpose(0, 2, 1, 3))
    shared["a2aug"] = np.ascontiguousarray(np.concatenate([A("rw_a2"), A("rw_a0")[:, :, None, :]], axis=2).transpose(0, 2, 1, 3))
    shared["gw2aug"] = np.ascontiguousarray(np.concatenate([A("gla_w2"), A("gla_b")[:, :, None, :]], axis=2).transpose(0, 2, 1, 3))
    shared["ln256"] = np.ascontiguousarray(np.broadcast_to(A("gla_ln_g")[:, None, :], (2, 128, 256)))
    qk = np.stack([A("ev_qn_g"), A("ev_kn_g")], axis=1)
    shared["qkg"] = np.ascontiguousarray(np.broadcast_to(qk[:, None], (2, 128, 2, 64)))
    evs = np.concatenate([pc(A("ev_shift_mu"), 14), pc(A("rw_kk"), 4), pc(A("rw_ka"), 4), pc(A("rw_rk").reshape(2, 512), 4),
                          pc(A("rw_ln_g"), 4), pc(A("rw_ln_b"), 4)], axis=2)
    shared["evs"] = np.ascontiguousarray(evs.transpose(1, 0, 2))
    in_maps = []
    for core in range(n):
        sb = core % 4
        m = dict(shared)
        m["xg"] = np.ascontiguousarray(np.stack([x_prompt[4 * core:4 * core + 4].reshape(GT, D), x_sample[sb]], axis=0))
        m["condT"] = np.ascontiguousarray(np.stack([pc(c_ctx, NCH), pc(c[sb], NCH)], axis=-1))
        m["cak"] = np.ascontiguousarray(A("cache_attn_k")[sb].reshape(2, 512, 128))
        m["cav"] = np.ascontiguousarray(A("cache_attn_v")[sb].reshape(2, 512, 128))
        m["srf"] = np.ascontiguousarray(A("state_rwkv_fwd")[sb]); m["srb"] = np.ascontiguousarray(A("state_rwkv_bwd")[sb])
        m["sgf"] = np.ascontiguousarray(A("state_gla_fwd")[sb]); m["sgb"] = np.ascontiguousarray(A("state_gla_bwd")[sb])
        in_maps.append(m)
    if "nc" not in _CACHE:
        _CACHE["nc"] = build_program()
    res = run_bass_kernel_spmd(_CACHE["nc"], in_maps, core_ids=list(range(n)))
    R = res.results
    cat = lambda name: np.concatenate([R[i][name] for i in range(n)], axis=0)
    y_prompt = np.concatenate([R[i]["yg"][0].reshape(4, 256, D) for i in range(n)], axis=0)
    y_sample = np.stack([R[i]["yg"][1] for i in range(4)], axis=0)
    new_k = cat("nk").reshape(32, 2, 256, 2, 64)
    new_v = cat("nv").reshape(32, 2, 256, 2, 64)
    return (y_prompt, y_sample, new_k, new_v, cat("nrf"), cat("nrb"), cat("ngf"), cat("ngb"))
```

```python
import contextlib
import os
STOP = int(os.environ.get('KSTOP', '9'))
SUB = float(os.environ.get('KSUB', '9'))
KNS = int(os.environ.get('KNS', '99'))
KNC = int(os.environ.get('KNC', '99'))
KND = int(os.environ.get('KND', '2'))
import numpy as np
import concourse.bass as bass
import concourse.mybir as mybir
from concourse.bass_utils import run_bass_kernel_spmd

F32 = mybir.dt.float32
BF16 = mybir.dt.bfloat16
AF = mybir.ActivationFunctionType
ALU = mybir.AluOpType
AX = mybir.AxisListType

D = 1024
NCH = 8
DEPTH = 4
GT = 1024
EV_COLS = 3584
OD_COLS = 3104
EPS = 1e-6
GN_EPS = 64e-5
RWKV_DECAY_SCALE = 0.606531
WB = 256


class Buf:
    __slots__ = ("name", "w", "r")

    def __init__(self, name):
        self.name = name
        self.w = None
        self.r = {}


class KB:
    SEM_WRAP = 20000

    def __init__(self, nc):
        self.nc = nc
        self.es = contextlib.ExitStack()
        self.engs = {"pe": nc.tensor, "act": nc.scalar, "dve": nc.vector, "pool": nc.gpsimd, "sp": nc.sync}
        self.cnt = {e: 0 for e in self.engs}
        self.esems = {e: [] for e in self.engs}
        self.seen = {e: {} for e in self.engs}
        self.semobj = {}
        self.ndma_sems = {"sp": 12, "pool": 6, "act": 4}
        self.dma_sems = {}
        self.dma_rr = {q: 0 for q in self.ndma_sems}
        self.dma_cnt = {}
        self.nsem = 0
        self.ninstr = 0

    def new_sem(self, name):
        s = self.es.enter_context(self.nc.semaphore(name))
        self.semobj[name] = s
        return name

    def sb(self, name, shape, dtype):
        return self.es.enter_context(self.nc.sbuf_tensor(name, list(shape), dtype))

    def ps(self, name, shape, dtype=F32):
        return self.es.enter_context(self.nc.psum_tensor(name, list(shape), dtype))

    def _cur_sem(self, e):
        idx = self.cnt[e] // self.SEM_WRAP
        while len(self.esems[e]) <= idx:
            self.esems[e].append(self.new_sem(f"s_{e}_{len(self.esems[e])}"))
        return self.esems[e][idx]

    def _wait(self, e, tok):
        if tok is None:
            return
        key, val = tok
        if self.seen[e].get(key, 0) >= val:
            return
        self.engs[e].wait_ge(self.semobj[key], val)
        self.seen[e][key] = val

    def _deps(self, e, reads, writes, pe_acc=False):
        for b in reads:
            if b.w is not None:
                self._wait(e, b.w)
        for b in writes:
            if b.w is not None and not (pe_acc and e == "pe"):
                self._wait(e, b.w)
            for e2, tok in b.r.items():
                if e2 == e and e == "pe":
                    continue
                self._wait(e, tok)

    def _mark(self, e, tok, reads, writes):
        for b in reads:
            b.r[e] = tok
        for b in writes:
            b.w = tok
            b.r = {}

    def op(self, e, reads, writes, fn, pe_acc=False):
        self._deps(e, reads, writes, pe_acc)
        sem = self._cur_sem(e)
        ins = fn(self.engs[e])
        ins.then_inc(self.semobj[sem], 1)
        self.cnt[e] += 1
        self.ninstr += 1
        val = self.cnt[e] - (self.cnt[e] - 1) // self.SEM_WRAP * self.SEM_WRAP
        tok = (sem, val)
        self._mark(e, tok, reads, writes)
        return tok

    def dma(self, q, out, in_, reads, writes):
        if q not in self.dma_sems:
            self.dma_sems[q] = [self.new_sem(f"d_{q}_{i}") for i in range(self.ndma_sems[q])]
            for s in self.dma_sems[q]:
                self.dma_cnt[s] = 0
        sems = self.dma_sems[q]
        s = sems[self.dma_rr[q] % len(sems)]
        self.dma_rr[q] += 1
        if self.dma_cnt[s] > 0:
            self._wait(q, (s, 16 * self.dma_cnt[s]))
        self._deps(q, reads, writes)
        self.engs[q].dma_start(out=out, in_=in_).then_inc(self.semobj[s], 16)
        self.dma_cnt[s] += 1
        self.ninstr += 1
        tok = (s, 16 * self.dma_cnt[s])
        self._mark(s, tok, reads, writes)
        return tok

    def finish(self, bufs):
        for b in bufs:
            if b.w is not None:
                self._wait("sp", b.w)
        for q, sems in self.dma_sems.items():
            for s in sems:
                if self.dma_cnt[s] > 0:
                    self._wait("sp", (s, 16 * self.dma_cnt[s]))


    def barrier(self):
        toks = []
        for e in self.engs:
            if self.cnt[e] > 0:
                sem = self.esems[e][(self.cnt[e] - 1) // self.SEM_WRAP]
                val = self.cnt[e] - (self.cnt[e] - 1) // self.SEM_WRAP * self.SEM_WRAP
                toks.append((sem, val))
        for q, sems in self.dma_sems.items():
            for s in sems:
                if self.dma_cnt[s] > 0:
                    toks.append((s, 16 * self.dma_cnt[s]))
        for e in self.engs:
            for t in toks:
                self._wait(e, t)

    @contextlib.contextmanager
    def scope(self):
        sc = _Scope(self)
        try:
            yield sc
        finally:
            self.barrier()
            sc.es.close()

    def _pe_rows(self, ap):
        base = ap.base_partition()
        n = ap.shape[0]
        grp = set(range(base // 32, (base + n - 1) // 32 + 1))
        last = getattr(self, "_pe_last_grp", None)
        if last is not None and not (grp & last) and self.cnt["pe"] > 0:
            for sem_i in range(max(0, (self.cnt["pe"] - 1) // self.SEM_WRAP - 1), (self.cnt["pe"] - 1) // self.SEM_WRAP + 1):
                sem = self.esems["pe"][sem_i]
                if sem_i == (self.cnt["pe"] - 1) // self.SEM_WRAP:
                    val = self.cnt["pe"] - sem_i * self.SEM_WRAP
                else:
                    val = self.SEM_WRAP
                self._wait("pe", (sem, val))
        self._pe_last_grp = grp

    def mm(self, out, lhsT, rhs, R, W, start=True, stop=True):
        self._pe_rows(lhsT)
        return self.op("pe", R, W, lambda e: e.matmul(out, lhsT, rhs, start=start, stop=stop), pe_acc=True)

    def tr(self, out, in_, ident, R, W):
        self._pe_rows(in_)
        return self.op("pe", R, W, lambda e: e.transpose(out, in_, ident), pe_acc=True)

    def tt(self, out, in0, in1, op, R, W, e="dve"):
        return self.op(e, R, W, lambda g: g.tensor_tensor(out=out, in0=in0, in1=in1, op=op))

    def ts(self, out, in0, s1, op0, R, W, s2=None, op1=None, e="dve"):
        if op1 is None:
            return self.op(e, R, W, lambda g: g.tensor_scalar(out=out, in0=in0, scalar1=s1, scalar2=None, op0=op0))
        return self.op(e, R, W, lambda g: g.tensor_scalar(out=out, in0=in0, scalar1=s1, scalar2=s2, op0=op0, op1=op1))

    def stt(self, out, in0, scalar, in1, op0, op1, R, W, e="dve"):
        return self.op(e, R, W, lambda g: g.scalar_tensor_tensor(out=out, in0=in0, scalar=scalar, in1=in1, op0=op0, op1=op1))

    def act(self, out, in_, func, R, W, **kw):
        return self.op("act", R, W, lambda g: g.activation(out=out, in_=in_, func=func, **kw))

    def cp(self, out, in_, R, W, e="dve"):
        return self.op(e, R, W, lambda g: g.tensor_copy(out=out, in_=in_))

    def red(self, out, in_, R, W, op=None):
        return self.op("dve", R, W, lambda g: g.tensor_reduce(out=out, in_=in_, axis=AX.X, op=(op or ALU.add)))

    def recip(self, out, in_, R, W):
        return self.op("dve", R, W, lambda g: g.reciprocal(out=out, in_=in_))

    def ms(self, ap, val, W, e="dve"):
        return self.op(e, [], W, lambda g: g.memset(ap, val))


class _Scope:
    def __init__(self, k):
        self.k = k
        self.es = contextlib.ExitStack()

    def sb(self, name, shape, dtype):
        return self.es.enter_context(self.k.nc.sbuf_tensor(name, list(shape), dtype))


def build_program(nlayers=DEPTH, groups=(0, 1)):
    nc = bass.Bass("TRN2", target_bir_lowering=False)
    k = KB(nc)
    uid = [0]

    def un(name):
        uid[0] += 1
        return f"{name}_{uid[0]}"

    def din(name, shape):
        return nc.dram_tensor(name, list(shape), F32, kind="ExternalInput").ap()

    def dout(name, shape):
        return nc.dram_tensor(name, list(shape), F32, kind="ExternalOutput").ap()

    xg = din("xg", [2, GT, D])
    condT = din("condT", [128, NCH, 2])
    modw = din("modw", [DEPTH * 12, 128, NCH, WB])
    modbT = din("modbT", [128, DEPTH, 24])
    normgT = din("normgT", [128, DEPTH, NCH])
    finalgT = din("finalgT", [128, NCH])
    ident_d = din("ident", [128, 128])
    evw = din("evw", [2, 14, 128, NCH, WB])
    evo = din("evo", [2, 4, 128, NCH, WB])
    odw = din("odw", [2, 13, 128, NCH, WB])
    odo = din("odo", [2, 4, 128, NCH, WB])
    tri64_d = din("tri64", [64, 2, 2, 64])
    tri128_d = din("tri128", [128, 2, 128])
    mask64_d = din("mask64", [64, 2, 128])
    maskN_d = din("maskN", [64, 2, 64])
    mask128_d = din("mask128", [128, 2, 128])
    bd_d = din("bdones", [128, 128])
    cos_d = din("ropecos", [128, 8, 64])
    sin_d = din("ropesin", [128, 8, 64])
    w2aug_d = din("w2aug", [2, 65, 2, 512])
    a2aug_d = din("a2aug", [2, 65, 2, 512])
    gw2aug_d = din("gw2aug", [2, 17, 2, 512])
    ln256_d = din("ln256", [2, 128, 256])
    qkg_d = din("qkg", [2, 128, 2, 64])
    evs_d = din("evs", [2, 128, 34])
    cak = din("cak", [2, 512, 128])
    cav = din("cav", [2, 512, 128])
    srs_d = [din("srf", [2, 8, 64, 64]), din("srb", [2, 8, 64, 64])]
    sgs_d = [din("sgf", [2, 4, 128, 256]), din("sgb", [2, 4, 128, 256])]
    yg = dout("yg", [2, GT, D])
    nk_o = dout("nk", [4, 2, 256, 128])
    nv_o = dout("nv", [4, 2, 256, 128])
    nrs_o = [dout("nrf", [4, 2, 8, 64, 64]), dout("nrb", [4, 2, 8, 64, 64])]
    ngs_o = [dout("ngf", [4, 2, 4, 128, 256]), dout("ngb", [4, 2, 4, 128, 256])]

    with k.es:
        bC = Buf("const")

        def cload(name, src, shape, dtype=F32):
            t = k.sb(name, shape, dtype)
            k.dma("sp", t[:], src, [], [bC])
            return t

        ident = cload("ident_sb", ident_d[:, :], [128, 128])
        tri64 = cload("tri64_sb", tri64_d[:, :, :, :], [64, 2, 2, 64])
        tri128 = cload("tri128_sb", tri128_d[:, :, :], [128, 2, 128])
        mask64 = cload("mask64_sb", mask64_d[:, :, :], [64, 2, 128])
        maskN = cload("maskN_sb", maskN_d[:, :, :], [64, 2, 64])
        mask128 = cload("mask128_sb", mask128_d[:, :, :], [128, 2, 128])
        bdones = cload("bd_sb", bd_d[:, :], [128, 128])
        rope_tabs = {}
        cond_sb = cload("cond_sb", condT[:, :, :], [128, NCH, 2])
        modb_sb = cload("modb_sb", modbT[:, :, :], [128, DEPTH, 24])
        normg_sb = cload("normg_sb", normgT[:, :, :], [128, DEPTH, NCH])
        finalg_sb = cload("finalg_sb", finalgT[:, :], [128, NCH])
        ones_f = k.sb("ones_f", [128, 128], F32)
        k.ms(ones_f[:], 1.0 / D, [bC])
        ones_bf = k.sb("ones_bf", [128, 64], BF16)
        k.ms(ones_bf[:], 1.0, [bC])
        bP = Buf("params")
        w2aug = k.sb("w2aug_sb", [65, 2, 512], F32)
        a2aug = k.sb("a2aug_sb", [65, 2, 512], F32)
        gw2aug = k.sb("gw2aug_sb", [17, 2, 512], F32)
        ln256 = k.sb("ln256_sb", [128, 256], F32)
        qkg = k.sb("qkg_sb", [128, 2, 64], F32)
        evs = k.sb("evs_sb", [128, 34], F32)
        evx = k.sb("evx_sb", [128, 32], F32)

        PT = [k.ps(f"P{i}", [128, 1024], F32) for i in range(4)]
        bB = [Buf(f"bank{i}") for i in range(8)]
        rr = [0]

        def bank(i):
            return PT[i // 2][:, (i % 2) * 512:(i % 2) * 512 + 512]

        def nb(lo=0, hi=8):
            i = lo + rr[0] % (hi - lo)
            rr[0] += 1
            return i

        def nb2():
            i = (rr[0] % 8 + 1) // 2 * 2 % 8
            rr[0] += (i - rr[0] % 8) % 8 + 2
            return i

        scond = k.sb("scond", [128, NCH, 2], F32)
        k.act(scond[:], cond_sb[:], AF.Silu, [bC], [bC])
        modT = k.sb("modT", [128, DEPTH, 24, 2], F32)
        bM = Buf("mod")
        xT = k.sb("xT", [128, NCH, GT], F32)
        bX = Buf("xT")
        oT = k.sb("oT", [128, NCH, GT], BF16)
        bO = Buf("oT")
        bR = Buf("rstd")
        hcol = k.sb("hcol", [128, 3, NCH], F32)
        bH = Buf("hcol")

        with k.scope() as sc:
            wst = [sc.sb(un("mwst"), [128, NCH, WB], F32) for i in range(4)]
            bws = [Buf("mwst%d" % i) for i in range(4)]
            mrow = [sc.sb(un("mrow"), [2, WB], F32) for i in range(2)]
            bmrow = [Buf("mrow0"), Buf("mrow1")]
            wi = 0
            for L in range(nlayers):
                for blk in range(12):
                    w = wst[wi % 4]; bw = bws[wi % 4]
                    k.dma("sp" if wi % 2 == 0 else "pool", w[:], modw[L * 12 + blk], [], [bw])
                    bi = nb(); pm = bank(bi); bp = bB[bi]
                    for kc in range(NCH):
                        k.mm(pm[0:2, 0:WB], scond[:, kc, :], w[:, kc, :], [bw, bC], [bp], start=(kc == 0), stop=(kc == NCH - 1))
                    rt = mrow[wi % 2]; brt = bmrow[wi % 2]
                    k.act(rt[:, :], pm[0:2, 0:WB], AF.Copy, [bp], [brt])
                    bi2 = nb(); pm2 = bank(bi2); bp2 = bB[bi2]
                    for sub in range(2):
                        k.tr(pm2[:, sub * 2:sub * 2 + 2], rt[0:2, sub * 128:(sub + 1) * 128], ident[0:2, 0:2], [brt, bC], [bp2])
                    for sub in range(2):
                        ch = blk * 2 + sub
                        k.ts(modT[:, L, ch, :], pm2[:, sub * 2:sub * 2 + 2], modb_sb[:, L, ch:ch + 1], ALU.add, [bp2, bC], [bM])
                    wi += 1

        def compute_rstd(sc):
            rstd = sc.sb(un("rstd"), [128, GT], F32)
            sq = [sc.sb(un("sq"), [128, 512], F32) for i in range(2)]
            bsq = [Buf("sq0"), Buf("sq1")]
            for tt in range(GT // 512):
                bi = nb(); pn = bank(bi); bp = bB[bi]
                for c in range(NCH):
                    s_ = sq[c % 2]; bs_ = bsq[c % 2]
                    k.act(s_[:], xT[:, c, tt * 512:(tt + 1) * 512], AF.Square, [bX], [bs_])
                    k.mm(pn, ones_f[:], s_[:], [bs_, bC], [bp], start=(c == 0), stop=(c == NCH - 1))
                sl = rstd[:, tt * 512:(tt + 1) * 512]
                k.ts(sl, pn, EPS, ALU.add, [bp], [bR])
                k.act(sl, sl, AF.Sqrt, [bR], [bR])
                k.recip(sl, sl, [bR], [bR])
            return rstd

        def wblock_factory(sc):
            wst2 = [sc.sb(un("wst"), [128, NCH, WB], F32) for i in range(2)]
            bws2 = [Buf("wst0"), Buf("wst1")]
            wbf = [sc.sb(un("wbf"), [128, NCH, WB], BF16) for i in range(2)]
            bwb = [Buf("wbf0"), Buf("wbf1")]
            cnt = [0]

            def wblock(src):
                i = cnt[0] % 2
                cnt[0] += 1
                wst = wst2[i]; bws = bws2[i]
                k.dma("sp", wst[:], src, [], [bws])
                k.cp(wbf[i][:], wst[:], [bws], [bwb[i]], e="pool")
                return wbf[i], bwb[i]
            return wblock

        def proj_F(wblock, wsrc, blocks, hT, bHT, dest, msub=128, nsub=None):
            nsub = nsub or WB // msub
            for bi_, blk in enumerate(blocks):
                w, bw = wblock(wsrc[blk])
                for sub in range(nsub):
                    for tt in range(GT // 512):
                        bi = nb(); pb = bank(bi); bp = bB[bi]
                        for kc in range(NCH):
                            k.mm(pb[0:msub, :], w[:, kc, sub * msub:(sub + 1) * msub], hT[:, kc, tt * 512:(tt + 1) * 512],
                                 [bw, bHT], [bp], start=(kc == 0), stop=(kc == NCH - 1))
                        dest(bi_ * nsub + sub, tt, pb, bp)

        def proj_T(wblock, wsrc, blocks, hT, bHT, dest):
            for bi_, blk in enumerate(blocks):
                w, bw = wblock(wsrc[blk])
                for t8 in range(GT // 128):
                    bi = nb(); pb = bank(bi); bp = bB[bi]
                    for kc in range(NCH):
                        k.mm(pb[:, 0:WB], hT[:, kc, t8 * 128:(t8 + 1) * 128], w[:, kc, :],
                             [bw, bHT], [bp], start=(kc == 0), stop=(kc == NCH - 1))
                    dest(bi_, t8, pb, bp)

        def make_h(sc, L, g):
            hT = sc.sb(un("hT"), [128, NCH, GT], BF16)
            bHT = Buf("hT")
            k.stt(hcol[:, 0, :], modT[:, L, 8:16, g], 1.0, normg_sb[:, L, :], ALU.add, ALU.mult, [bM, bC], [bH])
            k.cp(hcol[:, 1, :], modT[:, L, 0:8, g], [bM], [bH])
            k.cp(hcol[:, 2, :], modT[:, L, 16:24, g], [bM], [bH])
            rstd = compute_rstd(sc)
            tmp = [sc.sb(un("htmp"), [128, 512], F32) for i in range(2)]
            btmp = [Buf("htmp0"), Buf("htmp1")]
            i = 0
            for tt in range(GT // 512):
                for c in range(NCH):
                    t_ = tmp[i % 2]; bt_ = btmp[i % 2]; i += 1
                    k.tt(t_[:], xT[:, c, tt * 512:(tt + 1) * 512], rstd[:, tt * 512:(tt + 1) * 512], ALU.mult, [bX, bR], [bt_])
                    k.ts(hT[:, c, tt * 512:(tt + 1) * 512], t_[:], hcol[:, 0, c:c + 1], ALU.mult, [bt_, bH], [bHT],
                         s2=hcol[:, 1, c:c + 1], op1=ALU.add)
            return hT, bHT

        def out_proj(wsrc):
            with k.scope() as sc:
                wblock = wblock_factory(sc)

                def dest(cc, tt, pb, bp):
                    sl = xT[:, cc, tt * 512:(tt + 1) * 512]
                    k.stt(sl, pb, hcol[:, 2, cc:cc + 1], sl, ALU.mult, ALU.add, [bp, bH, bX], [bX])
                proj_F(wblock, wsrc, range(4), oT, bO, dest)

        def headnorm(sc_t, pb_view, nh, gidx, out3, R, W, bT=None):
            bT = bT or bT0
            sqt, ssq = sc_t
            k.act(sqt[:, 0:nh * 64], pb_view.rearrange("p h d -> p (h d)"), AF.Square, R, [bT])
            k.red(ssq[:, 0:nh], sqt[:, 0:nh * 64].rearrange("p (h d) -> p h d", h=nh), [bT], [bT])
            k.ts(ssq[:, 0:nh], ssq[:, 0:nh], 1.0 / 64, ALU.mult, [bT], [bT], s2=EPS, op1=ALU.add)
            k.act(ssq[:, 0:nh], ssq[:, 0:nh], AF.Sqrt, [bT], [bT])
            k.recip(ssq[:, 0:nh], ssq[:, 0:nh], [bT], [bT])
            k.tt(out3, pb_view, ssq[:, 0:nh].unsqueeze(2).to_broadcast([128, nh, 64]), ALU.mult, R + [bT], W)
            k.tt(out3, out3, qkg[:, gidx, :].unsqueeze(1).to_broadcast([128, nh, 64]), ALU.mult, W + [bP], W)

        bT0 = Buf("tmpT")

        def rope(x3, nh, t8, t1, t2, R, bT=None):
            bT = bT or bT0
            cosb = rope_tabs["cos"][:, t8, :].unsqueeze(1).to_broadcast([128, nh, 64])
            k.tt(t1[:, 0:nh, :], x3, cosb, ALU.mult, R + [bC], [bT])
            x5 = x3.rearrange("p h (a q f) -> p h a q f", a=2, q=2)
            t5 = t2[:, 0:nh, :].rearrange("p h (a q f) -> p h a q f", a=2, q=2)
            s4 = rope_tabs["sin"][:, t8, :].rearrange("p (a q f) -> p a q f", a=2, q=2)
            for q_ in range(2):
                k.tt(t5[:, :, :, q_, :], x5[:, :, :, 1 - q_, :],
                     s4[:, :, q_, :].unsqueeze(1).to_broadcast([128, nh, 2, 16]), ALU.mult, R + [bC], [bT])
            k.tt(x3, t1[:, 0:nh, :], t2[:, 0:nh, :], ALU.add, [bT], R)

        def even_layer(g, j, L):
            nseq, T = (4, 256) if g == 0 else (1, 1024)
            TP = T + 2
            koff = 0 if g == 0 else 512
            SK = GT + koff
            k.dma("sp", w2aug[:], w2aug_d[j], [], [bP])
            k.dma("sp", a2aug[:], a2aug_d[j], [], [bP])
            k.dma("sp", qkg[:], qkg_d[j], [], [bP])
            k.dma("sp", evs[:], evs_d[j], [], [bP])
            k.ts(evx[:, 0:14], evs[:, 0:14], 0.5, ALU.mult, [bP], [bP])
            k.ts(evx[:, 14:28], evs[:, 0:14], -1.0, ALU.mult, [bP], [bP], s2=1.0, op1=ALU.add)
            k.ts(evx[:, 28:32], evs[:, 18:22], -1.0, ALU.mult, [bP], [bP], s2=1.0, op1=ALU.add)
            with k.scope() as scL:
                gbT = scL.sb(un("gbT"), [128, 4, GT], BF16); bGB = Buf("gbT")
                zraw = scL.sb(un("zraw"), [128, 14, nseq * TP], BF16); bZ = Buf("zraw")
                k.ms(zraw[:], 0.0, [bZ])
                zr4 = zraw[:].rearrange("p c (s t) -> p c s t", s=nseq)
                with k.scope() as scA:
                    gaT = scA.sb(un("gaT"), [128, 4, GT], BF16); bGA = Buf("gaT")
                    qT = scA.sb(un("qT"), [64, 8, GT], BF16); bQ = Buf("qT")
                    kT = scA.sb(un("kT"), [64, 2, SK], BF16); bK = Buf("kT")
                    vtok = scA.sb(un("vtok"), [128, SK // 128, 128], BF16); bV = Buf("vtok")
                    if g == 1:
                      with k.scope() as sc:
                          ck = sc.sb(un("ck"), [128, 4, 128], F32); bCK = Buf("ck")
                          k.dma("sp", ck[:], cak[j].rearrange("(i p) f -> p i f", p=128), [], [bCK])
                          for i in range(4):
                              b2 = nb(); p2 = bank(b2)
                              for hh in range(2):
                                  k.tr(p2[0:64, hh * 128:(hh + 1) * 128], ck[:, i, hh * 64:(hh + 1) * 64], ident[:], [bCK, bC], [bB[b2]])
                              k.cp(kT[:, :, i * 128:(i + 1) * 128],
                                   p2[0:64, 0:256].rearrange("p (h t) -> p h t", h=2), [bB[b2]], [bK])
                          cv = sc.sb(un("cv"), [128, 4, 128], F32); bCV = Buf("cv")
                          k.dma("sp", cv[:], cav[j].rearrange("(i p) f -> p i f", p=128), [], [bCV])
                          k.cp(vtok[:, 0:4, :], cv[:], [bCV], [bV])

                    with k.scope() as sc:
                        hT, bHT = make_h(sc, L, g)
                        wblock = wblock_factory(sc)
                        if g == 1:
                            rope_tabs["cos"] = sc.sb(un("cos_sb"), [128, 8, 64], F32)
                            rope_tabs["sin"] = sc.sb(un("sin_sb"), [128, 8, 64], F32)
                            k.dma("sp", rope_tabs["cos"][:], cos_d[:, :, :], [], [bC])
                            k.dma("sp", rope_tabs["sin"][:], sin_d[:, :, :], [], [bC])
                        print("S1 sbuf remaining", nc.sbuf_bytes_remaining)
                        sqtL = [sc.sb(un("sqt"), [128, 256], F32) for _ in range(2)]
                        ssqL = [sc.sb(un("ssq"), [128, 4], F32) for _ in range(2)]
                        qnL = [sc.sb(un("qn"), [128, 4, 64], F32) for _ in range(2)]; bQNL = [Buf("qn0"), Buf("qn1")]
                        r1L = [sc.sb(un("r1"), [128, 4, 64], F32) for _ in range(2)]
                        r2L = [sc.sb(un("r2"), [128, 4, 64], F32) for _ in range(2)]
                        bTL = [Buf("tmpT0"), Buf("tmpT1")]
                        kvo = [sc.sb(un("kvo"), [128, 256], F32) for i in range(2)]
                        bKVO = [Buf("kvo0"), Buf("kvo1")]

                        def dest_q(bi_, t8, pb, bp):
                            ix = t8 % 2
                            sqt, ssq, qn, r1, r2, bQN, bTx = sqtL[ix], ssqL[ix], qnL[ix], r1L[ix], r2L[ix], bQNL[ix], bTL[ix]
                            headnorm((sqt, ssq), pb[:, 0:256].rearrange("p (h d) -> p h d", h=4), 4, 0, qn[:], [bp], [bQN], bT=bTx)
                            if g == 1:
                                rope(qn[:], 4, t8, r1, r2, [bQN], bT=bTx)
                            b2 = nb(); p2 = bank(b2)
                            for hh in range(4):
                                k.tr(p2[0:64, hh * 128:(hh + 1) * 128], qn[:, hh, :], ident[:], [bQN, bC], [bB[b2]])
                            k.cp(qT[:, bi_ * 4:bi_ * 4 + 4, t8 * 128:(t8 + 1) * 128],
                                 p2[0:64, :].rearrange("p (h t) -> p h t", h=4), [bB[b2]], [bQ])
                        proj_T(wblock, evw[j], [0, 1], hT, bHT, dest_q)

                        def dest_kv(bi_, t8, pb, bp):
                            ko = kvo[t8 % 2]; bko = bKVO[t8 % 2]
                            kn3 = ko[:, 0:128].rearrange("p (h d) -> p h d", h=2)
                            ix = t8 % 2
                            sqt, ssq, r1, r2, bTx = sqtL[ix], ssqL[ix], r1L[ix], r2L[ix], bTL[ix]
                            headnorm((sqt, ssq), pb[:, 0:128].rearrange("p (h d) -> p h d", h=2), 2, 1, kn3, [bp], [bko], bT=bTx)
                            k.cp(ko[:, 128:256], pb[:, 128:256], [bp], [bko])
                            k.cp(vtok[:, koff // 128 + t8, :], pb[:, 128:256], [bp], [bV])
                            if g == 0:
                                b_ = t8 // 2; t0 = (t8 % 2) * 128
                                k.dma("pool", nk_o[b_, j, t0:t0 + 128, :], ko[:, 0:128], [bko], [])
                                k.dma("pool", nv_o[b_, j, t0:t0 + 128, :], ko[:, 128:256], [bko], [])
                            else:
                                rope(kn3, 2, t8, r1, r2, [bko], bT=bTx)
                            b2 = nb(); p2 = bank(b2)
                            for hh in range(2):
                                k.tr(p2[0:64, hh * 128:(hh + 1) * 128], kn3[:, hh, :], ident[:], [bko, bC], [bB[b2]])
                            k.cp(kT[:, :, koff + t8 * 128:koff + (t8 + 1) * 128],
                                 p2[0:64, 0:256].rearrange("p (h t) -> p h t", h=2), [bB[b2]], [bK])
                        proj_T(wblock, evw[j], [2], hT, bHT, dest_kv)

                        def dest_ga(cc, tt, pb, bp):
                            k.act(gaT[:, cc, tt * 512:(tt + 1) * 512], pb, AF.Silu, [bp], [bGA])
                        proj_F(wblock, evw[j], [3, 4], hT, bHT, dest_ga)

                        def dest_zb(cc, tt, pb, bp):
                            if g == 0:
                                k.cp(zr4[:, cc, 2 * tt:2 * tt + 2, 1:T + 1], pb.rearrange("p (s t) -> p s t", s=2), [bp], [bZ])
                            else:
                                k.cp(zr4[:, cc, 0, 1 + tt * 512:1 + (tt + 1) * 512], pb, [bp], [bZ])
                        proj_F(wblock, evw[j], range(5, 12), hT, bHT, dest_zb)

                        def dest_gb(cc, tt, pb, bp):
                            k.act(gbT[:, cc, tt * 512:(tt + 1) * 512], pb, AF.Silu, [bp], [bGB])
                        proj_F(wblock, evw[j], [12, 13], hT, bHT, dest_gb)

                    with (k.scope() if STOP >= 2 else contextlib.nullcontext()) as sc:
                      if STOP >= 2:
                            pexp = [sc.sb(un("pexp"), [128, 512], BF16) for i in range(2)]
                            bPE = [Buf("pexp0"), Buf("pexp1")]
                            rec = sc.sb(un("rec"), [64, 512], F32); bRec = Buf("rec")
                            oa = sc.sb(un("oa"), [128, 512], F32); bOA = Buf("oa")
                            QB = min(T, 512)
                            po = bank(6); psm = bank(7)
                            ie = 0
                            for s in range(nseq):
                                kbase = s * T if g == 0 else 0
                                nsc = (T + koff) // 128
                                for h in range(8):
                                    kv = h // 4
                                    hb = (h % 2) * 64
                                    for qb in range(T // QB):
                                        q0 = s * T + qb * QB
                                        for sc_ in range(nsc):
                                            kpos = kbase + sc_ * 128
                                            bi = nb(0, 6); pb = bank(bi); bp = bB[bi]
                                            k.mm(pb[:, 0:QB], kT[:, kv, kpos:kpos + 128], qT[:, h, q0:q0 + QB], [bK, bQ], [bp])
                                            pe_ = pexp[ie % 2]; bpe = bPE[ie % 2]; ie += 1
                                            k.act(pe_[:, 0:QB], pb[:, 0:QB], AF.Exp, [bp], [bpe], scale=0.125)
                                            k.mm(po[0:64, 0:QB], vtok[:, kpos // 128, kv * 64:(kv + 1) * 64], pe_[:, 0:QB],
                                                 [bV, bpe], [bB[6]], start=(sc_ == 0), stop=(sc_ == nsc - 1))
                                            k.mm(psm[0:64, 0:QB], ones_bf[:, :], pe_[:, 0:QB],
                                                 [bC, bpe], [bB[7]], start=(sc_ == 0), stop=(sc_ == nsc - 1))
                                        k.recip(rec[:, 0:QB], psm[0:64, 0:QB], [bB[7]], [bRec])
                                        k.tt(oa[hb:hb + 64, 0:QB], po[0:64, 0:QB], rec[:, 0:QB], ALU.mult, [bB[6], bRec], [bOA])
                                        k.tt(oT[hb:hb + 64, h // 2, q0:q0 + QB], oa[hb:hb + 64, 0:QB],
                                             gaT[hb:hb + 64, h // 2, q0:q0 + QB], ALU.mult, [bOA, bGA], [bO])

                with k.scope() as sc:
                    C = 64
                    if STOP < 3:
                        raise_skip = True
                    else:
                        raise_skip = False
                    nchunk = T // C
                    f = lambda name, shape, dt=F32: sc.sb(un(name), shape, dt)
                    yf = f("yf", [128, nseq * nchunk // 2, 512], BF16); bYF = Buf("yf")
                    ST = [f("ST", [128, 4, 64]) for d in range(2)]; bST = [Buf("ST0"), Buf("ST1")]
                    zsP = [f("zs", [128, 14, C]) for p_ in range(2)]
                    zt1 = f("zt1", [128, 14, C])
                    twa = [f("tw", [65, C]) for d in range(2)]
                    ala = [f("al", [65, C]) for d in range(2)]
                    bW = Buf("rwtmp")
                    Ls = f("Ls", [64, 512])
                    gam = f("gam", [128, 4, C]); gamp = f("gamp", [128, 4, C]); ginv = f("ginv", [128, 4, C])
                    av = [f("av", [128, 4, C]) for d in range(2)]
                    kkr = f("kkr", [128, 4, C]); ksq = f("ksq", [128, 4, C]); kk = f("kk", [128, 4, C])
                    kdP = [[f("kd", [128, 4, C]) for d in range(2)] for p_ in range(2)]
                    tmp4 = f("tmp4", [128, 4, C]); tmp5 = f("tmp5", [128, 4, C])
                    Bn = {n_: Buf(n_) for n_ in ["zs", "zt1", "ala0", "ala1", "twa0", "twa1", "av0", "av1", "kkr", "ksq", "kk", "kd0", "kd1", "tmp4", "tmp5", "Ls", "gam", "gamp", "ginv", "AR", "KBt", "Gm1", "Gm2", "Nm0", "Nm1", "Am0", "Am1", "Vt", "Us0", "Us1", "KBT", "ysum", "ysq", "yst", "bon", "obt", "tmp6"] + [x + str(p_) for p_ in range(2) for x in ["zs", "AR", "KBt", "Vt", "Gm1", "Gm2", "Nmi", "gl", "kd0_", "kd1_"]]}
                    ARP = [f("AR", [128, 4, 2, C]) for p_ in range(2)]; KBtP = [f("KBt", [128, 4, 2, C]) for p_ in range(2)]
                    Gm1P = [f("Gm1", [64, 8, 128]) for p_ in range(2)]; Gm2P = [f("Gm2", [64, 8, 128]) for p_ in range(2)]; Nm = [f("Nm", [64, 8, 64]) for i in range(2)]
                    NmiP = [f("Nmi", [64, 8, 64]) for p_ in range(2)]; glP = [f("gl", [128, 4, 1]) for p_ in range(2)]; tmp6 = f("tmp6", [128, 4, C])
                    Am = [f("Am", [64, 8, 64]) for i in range(2)]
                    VtP = [f("Vt", [64, 512]) for p_ in range(2)]; Us = [f("Us", [64, 512]) for i in range(2)]
                    KBT = f("KBT", [64, 4, 2, 128])
                    ysum = f("ysum", [64, 8, 64]); ysq = f("ysq", [64, 8, 64]); yst = f("yst", [64, 16])
                    bon = f("bon", [128, 4, C]); obt = f("obt", [128, 4, C])
                    sto = f("sto", [64, 4, 128]); bSTO = Buf("sto")
                    sld = f("sld", [64, 8, 64]); bSLD = Buf("sld")
                    for d in range(2):
                        k.ms(twa[d][:], 1.0, [Bn["twa%d" % d]])
                        k.ms(ala[d][:], 1.0, [Bn["ala%d" % d]])

                    units = []
                    for s in range(min(nseq, KNS) if STOP >= 3 else 0):
                        for d in range(KND):
                            order = list(range(nchunk) if d == 0 else range(nchunk - 1, -1, -1))[:KNC]
                            for ci_, c in enumerate(order):
                                units.append((s, d, c, ci_ == 0, ci_ == len(order) - 1))
                    HO = [0, 2, 4, 6, 1, 3, 5, 7]
                    HOr = [1, 3, 5, 7, 0, 2, 4, 6]
                    rrA = [0]; rrB = [0]

                    def nbA():
                        rrA[0] += 1
                        return rrA[0] % 5

                    def nbB():
                        rrB[0] += 1
                        return 5 + rrB[0] % 3

                    def stageA(u, p):
                        s, d, c, first, last = u
                        tcol = s * TP + c * C
                        gt0 = s * T + c * C
                        yield
                        k.tt(zt1[:], zraw[:, :, tcol:tcol + C], zraw[:, :, tcol + 2:tcol + 2 + C], ALU.add, [bZ], [Bn["zt1"]], e="pool")
                        k.tt(zt1[:], zt1[:], evx[:, 0:14].unsqueeze(2).to_broadcast([128, 14, C]), ALU.mult, [Bn["zt1"], bP], [Bn["zt1"]], e="pool")
                        k.tt(zsP[p][:], zraw[:, :, tcol + 1:tcol + 1 + C], evx[:, 14:28].unsqueeze(2).to_broadcast([128, 14, C]),
                             ALU.mult, [bZ, bP], [Bn["zs%d" % p]])
                        k.tt(zsP[p][:], zsP[p][:], zt1[:], ALU.add, [Bn["zs%d" % p], Bn["zt1"]], [Bn["zs%d" % p]])
                        r_ = zsP[p][:, 0:4, :]; kraw = zsP[p][:, 4:8, :]; vv = zsP[p][:, 8:12, :]
                        dirs = [d] if d == 0 else [0, 1]
                        yield
                        for dd in dirs:
                            bal = Bn["ala%d" % dd]; bav = Bn["av%d" % dd]; bkd = Bn["kd%d_%d" % (dd, p)]
                            k.cp(ala[dd][0:64, :], zsP[p][dd * 64:(dd + 1) * 64, 13, :], [Bn["zs%d" % p]], [bal])
                            b2 = nbA(); p2 = bank(b2)
                            for cp in range(4):
                                k.mm(p2[:, cp * C:(cp + 1) * C], a2aug[:, dd, cp * 128:(cp + 1) * 128], ala[dd][:, :],
                                     [bal, bP], [bB[b2]])
                            k.act(av[dd][:].rearrange("p c t -> p (c t)"), p2[:, 0:4 * C], AF.Sigmoid, [bB[b2]], [bav])
                            k.tt(tmp4[:], av[dd][:], evs[:, 18:22].unsqueeze(2).to_broadcast([128, 4, C]), ALU.mult, [bav, bP], [Bn["tmp4"]], e="pool")
                            k.tt(tmp4[:], tmp4[:], evx[:, 28:32].unsqueeze(2).to_broadcast([128, 4, C]), ALU.add, [Bn["tmp4"], bP], [Bn["tmp4"]], e="pool")
                            k.tt(kdP[p][dd][:], kraw, tmp4[:], ALU.mult, [Bn["zs%d" % p], Bn["tmp4"]], [bkd], e="pool")
                        yield
                        k.tt(kkr[:], kraw, evs[:, 14:18].unsqueeze(2).to_broadcast([128, 4, C]), ALU.mult, [Bn["zs%d" % p], bP], [Bn["kkr"]])
                        k.tt(ksq[:], kkr[:], kkr[:], ALU.mult, [Bn["kkr"]], [Bn["ksq"]])
                        b2 = nbA(); p2 = bank(b2)
                        k.mm(p2[:, 0:4 * C], bdones[:, :], ksq[:].rearrange("p c t -> p (c t)"), [Bn["ksq"], bC], [bB[b2]])
                        k.ts(ksq[:].rearrange("p c t -> p (c t)"), p2[:, 0:4 * C], 1e-12, ALU.add, [bB[b2]], [Bn["ksq"]])
                        k.act(ksq[:], ksq[:], AF.Sqrt, [Bn["ksq"]], [Bn["ksq"]])
                        k.recip(ksq[:], ksq[:], [Bn["ksq"]], [Bn["ksq"]])
                        k.tt(kk[:], kkr[:], ksq[:], ALU.mult, [Bn["kkr"], Bn["ksq"]], [Bn["kk"]])
                        yield
                        btw = Bn["twa%d" % d]
                        k.act(twa[d][0:64, :], zsP[p][d * 64:(d + 1) * 64, 12, :], AF.Tanh, [Bn["zs%d" % p]], [btw])
                        b2 = nbA(); p2 = bank(b2)
                        k.mm(p2[0:64, :], twa[d][:, :], w2aug[:, d, :], [btw, bP], [bB[b2]])
                        k.act(Ls[:], p2[0:64, :], AF.Sigmoid, [bB[b2]], [Bn["Ls"]])
                        b2 = nbA(); p2 = bank(b2)
                        for cp in range(4):
                            k.mm(p2[:, cp * 128:(cp + 1) * 128], Ls[:, cp * 128:(cp + 1) * 128],
                                 tri64[:, d, :, :].rearrange("p a t -> p (a t)"), [Bn["Ls"], bC], [bB[b2]])
                        p4 = p2.rearrange("p (c a t) -> p c a t", c=4, a=2)
                        k.act(gamp[:], p4[:, :, 1, :], AF.Exp, [bB[b2]], [Bn["gamp"]])
                        k.act(ginv[:], p4[:, :, 0, :], AF.Exp, [bB[b2]], [Bn["ginv"]], scale=-1.0)
                        k.act(gam[:], p4[:, :, 0, :], AF.Exp, [bB[b2]], [Bn["gam"]])
                        k.cp(glP[p][:], (gam[:, :, C - 1:C] if d == 0 else gam[:, :, 0:1]), [Bn["gam"]], [Bn["gl%d" % p]])
                        yield
                        bav = Bn["av%d" % d]; bkd = Bn["kd%d_%d" % (d, p)]
                        k.stt(ARP[p][:, :, 0, :], kk[:], -1.0, gamp[:], ALU.mult, ALU.mult, [Bn["kk"], Bn["gamp"]], [Bn["AR%d" % p]])
                        k.tt(KBtP[p][:, :, 0, :], kdP[p][d][:], ginv[:], ALU.mult, [bkd, Bn["ginv"]], [Bn["KBt%d" % p]])
                        k.tt(tmp5[:], kk[:], av[d][:], ALU.mult, [Bn["kk"], bav], [Bn["tmp5"]])
                        k.tt(KBtP[p][:, :, 1, :], tmp5[:], ginv[:], ALU.mult, [Bn["tmp5"], Bn["ginv"]], [Bn["KBt%d" % p]])
                        k.tt(ARP[p][:, :, 1, :], r_, gam[:], ALU.mult, [Bn["zs%d" % p], Bn["gam"]], [Bn["AR%d" % p]])
                        yield
                        b2 = nbA(); p2 = bank(b2)
                        for cp in range(4):
                            k.tr(p2[0:64, cp * 128:(cp + 1) * 128], zsP[p][:, 8 + cp, :], ident[:], [Bn["zs%d" % p], bC], [bB[b2]])
                        k.act(VtP[p][:], p2[0:64, :], AF.Copy, [bB[b2]], [Bn["Vt%d" % p]])
                        yield
                        g1, g2, gn = 0, 2, 4
                        for h in HO:
                            cp = h // 2; hb = (h % 2) * 64
                            arh = ARP[p][hb:hb + 64, cp, :, :].rearrange("p a t -> p (a t)")
                            k.mm(bank(g1 + h // 4)[0:64, (h % 4) * 128:(h % 4 + 1) * 128], KBtP[p][hb:hb + 64, cp, 0, :], arh,
                                 [Bn["KBt%d" % p], Bn["AR%d" % p]], [bB[g1 + h // 4]])
                            k.mm(bank(g2 + h // 4)[0:64, (h % 4) * 128:(h % 4 + 1) * 128], KBtP[p][hb:hb + 64, cp, 1, :], arh,
                                 [Bn["KBt%d" % p], Bn["AR%d" % p]], [bB[g2 + h // 4]])
                            k.mm(bank(gn)[0:64, h * 64:(h + 1) * 64], ARP[p][hb:hb + 64, cp, 0, :], KBtP[p][hb:hb + 64, cp, 1, :],
                                 [Bn["KBt%d" % p], Bn["AR%d" % p]], [bB[gn]])
                        yield
                        m64b = mask64[:, d, :].unsqueeze(1).to_broadcast([64, 4, 128])
                        for hf in range(2):
                            k.tt(Gm1P[p][:, hf * 4:hf * 4 + 4, :], bank(g1 + hf)[0:64, :].rearrange("p (h t) -> p h t", h=4), m64b,
                                 ALU.mult, [bB[g1 + hf], bC], [Bn["Gm1%d" % p]])
                        for hf in range(2):
                            k.tt(Gm2P[p][:, hf * 4:hf * 4 + 4, :], bank(g2 + hf)[0:64, :].rearrange("p (h t) -> p h t", h=4), m64b,
                                 ALU.mult, [bB[g2 + hf], bC], [Bn["Gm2%d" % p]])
                        k.tt(NmiP[p][:], bank(gn)[0:64, :].rearrange("p (h t) -> p h t", h=8),
                             maskN[:, d, :].unsqueeze(1).to_broadcast([64, 8, 64]), ALU.mult, [bB[gn], bC], [Bn["Nmi%d" % p]])

                        yield

                    def stageB(u, p, pull):
                        s, d, c, first, last = u
                        tcol = s * TP + c * C
                        gt0 = s * T + c * C
                        r_ = zsP[p][:, 0:4, :]; vv = zsP[p][:, 8:12, :]
                        if first:
                            if g == 0:
                                k.ms(ST[d][:], 0.0, [bST[d]])
                            else:
                                k.dma("sp", sld[:], srs_d[d][j].rearrange("h v k -> v h k"), [], [bSLD])
                                b2 = nbB(); p2 = bank(b2)
                                for cp in range(4):
                                    k.tr(p2[:, cp * 64:(cp + 1) * 64], sld[:, 2 * cp:2 * cp + 2, :].rearrange("v h k -> v (h k)"),
                                         ident[0:64, 0:64], [bSLD, bC], [bB[b2]])
                                k.cp(ST[d][:], p2[:, 0:256].rearrange("p (c v) -> p c v", c=4), [bB[b2]], [bST[d]])

                        b2 = nbB(); p2 = bank(b2)
                        for h in HOr:
                            cp = h // 2; hb = (h % 2) * 64
                            k.mm(p2[0:64, h * 64:(h + 1) * 64], ARP[p][hb:hb + 64, cp, 0, :], ST[d][hb:hb + 64, cp, :],
                                 [Bn["AR%d" % p], bST[d]], [bB[b2]])
                        b2b = nbB(); p2b = bank(b2b)
                        for h in range(8):
                            k.mm(p2b[0:64, h * 64:(h + 1) * 64], Gm1P[p][:, h, 0:64], VtP[p][:, h * 64:(h + 1) * 64],
                                 [Bn["Gm1%d" % p], Bn["Vt%d" % p]], [bB[b2b]])
                        k.act(Us[0][:], p2[0:64, :], AF.Copy, [bB[b2]], [Bn["Us0"]])
                        k.tt(Us[0][:], Us[0][:], p2b[0:64, :], ALU.add, [Bn["Us0"], bB[b2b]], [Bn["Us0"]])
                        ui = 0; ai = 0
                        for jn in range(6):
                            bA = Bn["Am%d" % ai] if jn else Bn["Gm2%d" % p]; bN = Bn["Nm%d" % ai] if jn else Bn["Nmi%d" % p]
                            bA2 = Bn["Am%d" % (1 - ai)]; bN2 = Bn["Nm%d" % (1 - ai)]
                            Acur = (lambda h, ai=ai: Am[ai][:, h, :]) if jn else (lambda h: Gm2P[p][:, h, 0:64])
                            Ncur = (lambda h, ai=ai: Nm[ai][:, h, :]) if jn else (lambda h: NmiP[p][:, h, :])
                            pull()
                            bU = Bn["Us%d" % ui]; bU2 = Bn["Us%d" % (1 - ui)]
                            b2 = nbB(); p2 = bank(b2)
                            for h in range(8):
                                k.mm(p2[0:64, h * 64:(h + 1) * 64], Acur(h), Us[ui][:, h * 64:(h + 1) * 64], [bA, bU], [bB[b2]])
                            if jn < 5:
                                b3 = nbB(); p3 = bank(b3)
                                for h in range(8):
                                    k.mm(p3[0:64, h * 64:(h + 1) * 64], Ncur(h), Acur(h), [bN, bA], [bB[b3]])
                                if jn < 4:
                                    b4 = nbB(); p4_ = bank(b4)
                                    for h in range(8):
                                        k.mm(p4_[0:64, h * 64:(h + 1) * 64], Acur(h), Ncur(h), [bA, bN], [bB[b4]])
                            k.tt(Us[1 - ui][:], Us[ui][:], p2[0:64, :], ALU.add, [bU, bB[b2]], [bU2])
                            ui = 1 - ui
                            pull()
                            if jn < 5:
                                k.act(Am[1 - ai][:].rearrange("p h t -> p (h t)"), p3[0:64, :], AF.Copy, [bB[b3]], [bA2])
                                if jn < 4:
                                    k.act(Nm[1 - ai][:].rearrange("p h t -> p (h t)"), p4_[0:64, :], AF.Copy, [bB[b4]], [bN2])
                                ai = 1 - ai
                        U = Us[ui]; bU = Bn["Us%d" % ui]
                        by = nbB(); py = bank(by)
                        by0 = nbB(); py0 = bank(by0)
                        for h in range(8):
                            o_ = py[0:64, h * 64:(h + 1) * 64]
                            k.mm(o_, Gm2P[p][:, h, 64:128], U[:, h * 64:(h + 1) * 64], [Bn["Gm2%d" % p], bU], [bB[by]], start=True, stop=False)
                            k.mm(o_, Gm1P[p][:, h, 64:128], VtP[p][:, h * 64:(h + 1) * 64], [Bn["Gm1%d" % p], Bn["Vt%d" % p]], [bB[by]], start=False, stop=True)
                        for h in HO:
                            cp = h // 2; hb = (h % 2) * 64
                            k.mm(py0[0:64, h * 64:(h + 1) * 64], ARP[p][hb:hb + 64, cp, 1, :], ST[d][hb:hb + 64, cp, :], [Bn["AR%d" % p], bST[d]], [bB[by0]])
                        ci = s * nchunk + c
                        ys2 = ysum[:].rearrange("p h v -> p (h v)")
                        yfs = yf[(ci % 2) * 64:(ci % 2) * 64 + 64, ci // 2, :]
                        bys = Bn["ysum"]
                        if d == 0:
                            k.act(ys2, py0[0:64, :], AF.Copy, [bB[by0]], [bys])
                            k.tt(ys2, ys2, py[0:64, :], ALU.add, [bys, bB[by]], [bys])
                            k.cp(yfs, ys2, [bys], [bYF])
                        else:
                            k.tt(ys2, py[0:64, :], yfs, ALU.add, [bB[by], bYF], [bys])
                            k.tt(ys2, ys2, py0[0:64, :], ALU.add, [bys, bB[by0]], [bys])
                        pull()
                        bt1 = 6
                        for cp in range(4):
                            for a_ in range(2):
                                idx = cp * 2 + a_
                                k.tr(bank(bt1 + idx // 4)[0:64, (idx % 4) * 128:(idx % 4 + 1) * 128], KBtP[p][:, cp, a_, :], ident[:],
                                     [Bn["KBt%d" % p], bC], [bB[bt1 + idx // 4]])
                        k.act(KBT[:, 0:2, :, :].rearrange("p c a k -> p (c a k)"), bank(bt1)[0:64, :], AF.Copy, [bB[bt1]], [Bn["KBT"]])
                        k.cp(KBT[:, 2:4, :, :].rearrange("p c a k -> p (c a k)"), bank(bt1 + 1)[0:64, :], [bB[bt1 + 1]], [Bn["KBT"]])
                        bs = 5; psu = bank(bs)
                        for cp in range(4):
                            k.mm(psu[:, cp * 128:(cp + 1) * 128], KBT[:, cp, 0, :], VtP[p][:, cp * 128:(cp + 1) * 128], [Bn["KBT"], Bn["Vt%d" % p]], [bB[bs]],
                                 start=True, stop=False)
                            k.mm(psu[:, cp * 128:(cp + 1) * 128], KBT[:, cp, 1, :], U[:, cp * 128:(cp + 1) * 128], [Bn["KBT"], bU], [bB[bs]],
                                 start=False, stop=True)
                        ps4 = psu.rearrange("p (c x) -> p c x", c=4)
                        for hh in range(2):
                            hb = hh * 64
                            k.tt(ST[d][hb:hb + 64, :, :], ST[d][hb:hb + 64, :, :], ps4[hb:hb + 64, :, hb:hb + 64], ALU.add,
                                 [bST[d], bB[bs]], [bST[d]])
                            k.tt(ST[d][hb:hb + 64, :, :], ST[d][hb:hb + 64, :, :],
                                 glP[p][hb:hb + 64, :, :].to_broadcast([64, 4, 64]), ALU.mult, [bST[d], Bn["gl%d" % p]], [bST[d]])
                        pull()
                        if d == 1:
                            k.red(yst[:, 0:8], ysum[:], [bys], [Bn["yst"]])
                            k.ts(yst[:, 0:8], yst[:, 0:8], 1.0 / 64, ALU.mult, [Bn["yst"]], [Bn["yst"]])
                            k.tt(ysum[:], ysum[:], yst[:, 0:8].unsqueeze(2).to_broadcast([64, 8, 64]), ALU.subtract, [bys, Bn["yst"]], [bys])
                            k.tt(ysq[:], ysum[:], ysum[:], ALU.mult, [bys], [Bn["ysq"]])
                            k.red(yst[:, 8:16], ysq[:], [Bn["ysq"]], [Bn["yst"]])
                            k.ts(yst[:, 8:16], yst[:, 8:16], 1.0 / 64, ALU.mult, [Bn["yst"]], [Bn["yst"]], s2=GN_EPS, op1=ALU.add)
                            k.act(yst[:, 8:16], yst[:, 8:16], AF.Sqrt, [Bn["yst"]], [Bn["yst"]])
                            k.recip(yst[:, 8:16], yst[:, 8:16], [Bn["yst"]], [Bn["yst"]])
                            k.tt(ysum[:], ysum[:], yst[:, 8:16].unsqueeze(2).to_broadcast([64, 8, 64]), ALU.mult, [bys, Bn["yst"]], [bys])
                            k.tt(tmp6[:], kdP[p][0][:], kdP[p][1][:], ALU.add, [Bn["kd0_%d" % p], Bn["kd1_%d" % p]], [Bn["tmp6"]], e="pool")
                            k.tt(tmp6[:], tmp6[:], r_, ALU.mult, [Bn["tmp6"], Bn["zs%d" % p]], [Bn["tmp6"]], e="pool")
                            k.tt(tmp6[:], tmp6[:], evs[:, 22:26].unsqueeze(2).to_broadcast([128, 4, C]), ALU.mult, [Bn["tmp6"], bP], [Bn["tmp6"]], e="pool")
                            b2 = nbB(); p2 = bank(b2)
                            k.mm(p2[:, 0:4 * C], bdones[:, :], tmp6[:].rearrange("p c t -> p (c t)"), [Bn["tmp6"], bC], [bB[b2]])
                            k.tt(bon[:], p2[:, 0:4 * C].rearrange("p (c t) -> p c t", c=4), vv, ALU.mult, [bB[b2], Bn["zs%d" % p]], [Bn["bon"]])
                            b3 = nbB(); p3 = bank(b3)
                            for cp in range(4):
                                k.tr(p3[:, cp * C:(cp + 1) * C], ysum[:, 2 * cp:2 * cp + 2, :].rearrange("p h v -> p (h v)"),
                                     ident[0:64, 0:64], [bys, bC], [bB[b3]])
                            p3v = p3[:, 0:4 * C].rearrange("p (c t) -> p c t", c=4)
                            k.tt(obt[:], p3v, evs[:, 26:30].unsqueeze(2).to_broadcast([128, 4, C]), ALU.mult, [bB[b3], bP], [Bn["obt"]])
                            k.tt(obt[:], obt[:], evs[:, 30:34].unsqueeze(2).to_broadcast([128, 4, C]), ALU.add, [Bn["obt"], bP], [Bn["obt"]])
                            k.tt(obt[:], obt[:], bon[:], ALU.add, [Bn["obt"], Bn["bon"]], [Bn["obt"]])
                            k.tt(oT[:, 4:8, gt0:gt0 + C], obt[:], gbT[:, :, gt0:gt0 + C], ALU.mult, [Bn["obt"], bGB], [bO])

                        if last:
                            if g == 0:
                                b2 = nbB(); p2 = bank(b2)
                                for cp in range(4):
                                    k.tr(p2[0:64, cp * 128:(cp + 1) * 128], ST[d][:, cp, :], ident[:], [bST[d], bC], [bB[b2]])
                                k.cp(sto[:].rearrange("p c x -> p (c x)"), p2[0:64, :], [bB[b2]], [bSTO])
                                k.dma("pool", nrs_o[d][s, j].rearrange("h v k -> v h k"),
                                      sto[:].rearrange("p c (h k) -> p (c h) k", h=2), [bSTO], [])

                    gens = {}
                    if units:
                        for _ in stageA(units[0], 0):
                            pass
                    for ui_, u in enumerate(units):
                        nxt = stageA(units[ui_ + 1], (ui_ + 1) % 2) if ui_ + 1 < len(units) else None

                        def pull(nxt=nxt, n=2):
                            if nxt is None:
                                return
                            for _ in range(n):
                                try:
                                    next(nxt)
                                except StopIteration:
                                    return
                        stageB(u, ui_ % 2, pull)
                        if nxt is not None:
                            for _ in nxt:
                                pass
            if STOP >= 4:
                out_proj(evo[j])

        def odd_layer(g, j, L):
            nseq, T = (4, 256) if g == 0 else (1, 1024)
            C = 128
            nchunk = T // C
            k.dma("sp", gw2aug[:], gw2aug_d[j], [], [bP])
            k.dma("sp", ln256[:], ln256_d[j], [], [bP])
            with k.scope() as scL:
                qT = scL.sb(un("gqT"), [128, 4, GT], BF16); bQ = Buf("gqT")
                kT = scL.sb(un("gkT"), [128, 4, GT], BF16); bK = Buf("gkT")
                vt = scL.sb(un("gvt"), [128, 8, 1024], BF16); bV = Buf("gvt")
                gs = scL.sb(un("ggs"), [128, 8, GT], BF16); bG = Buf("ggs")
                glT = scL.sb(un("glT"), [17, 2, GT], F32); bGL = Buf("glT")
                of = scL.sb(un("gof"), [128, 8, 1024], BF16); bOF = Buf("gof")
                k.ms(glT[:], 1.0, [bGL])
                with k.scope() as sc:
                    hT, bHT = make_h(sc, L, g)
                    wblock = wblock_factory(sc)

                    def dest_q(cc, tt, pb, bp):
                        k.ts(qT[:, cc, tt * 512:(tt + 1) * 512], pb, float(128 ** -0.5), ALU.mult, [bp], [bQ])
                    proj_F(wblock, odw[j], [0, 1], hT, bHT, dest_q)

                    def dest_k(cc, tt, pb, bp):
                        k.cp(kT[:, cc, tt * 512:(tt + 1) * 512], pb, [bp], [bK])
                    proj_F(wblock, odw[j], [2, 3], hT, bHT, dest_k)

                    def dest_v(bi_, t8, pb, bp):
                        k.cp(vt[:, t8, bi_ * 256:(bi_ + 1) * 256], pb[:, 0:256], [bp], [bV])
                    proj_T(wblock, odw[j], [4, 5, 6, 7], hT, bHT, dest_v)

                    def dest_g(cc, tt, pb, bp):
                        k.act(gs[:, cc, tt * 512:(tt + 1) * 512], pb, AF.Silu, [bp], [bG])
                    proj_F(wblock, odw[j], [8, 9, 10, 11], hT, bHT, dest_g)

                    def dest_gl(cc, tt, pb, bp):
                        if cc < 2:
                            k.cp(glT[0:16, cc, tt * 512:(tt + 1) * 512], pb[0:16, :], [bp], [bGL])
                    proj_F(wblock, odw[j], [12], hT, bHT, dest_gl, msub=16, nsub=2)

                with k.scope() as sc:
                    f = lambda name, shape, dt=F32: sc.sb(un(name), shape, dt)
                    S = [f("gS", [128, 4, 256]) for d in range(2)]; bS = [Buf("gS0"), Buf("gS1")]
                    bW = Buf("glatmp")
                    Lg = f("Lg", [128, 512])
                    gam = f("ggam", [128, 4, C]); ginv = f("gginv", [128, 4, C])
                    qs = f("gqs", [128, 4, C]); ks = f("gks", [128, 4, C])
                    Am = f("gAm", [128, 4, C], BF16); kTt = f("gkTt", [128, 4, C], BF16)
                    osum = f("gosum", [128, 4, 256]); osq = f("gosq", [128, 4, 256]); ost = f("gost", [128, 4])
                    print("GLA sbuf remaining", nc.sbuf_bytes_remaining)
                    for s in range(nseq):
                        for d in range(2):
                            if g == 0:
                                k.ms(S[d][:], 0.0, [bS[d]])
                            else:
                                k.dma("sp", S[d][:], sgs_d[d][j].rearrange("h k v -> k h v"), [], [bS[d]])
                            order = range(nchunk) if d == 0 else range(nchunk - 1, -1, -1)
                            for c in order:
                                gt0 = s * T + c * C
                                t8 = gt0 // 128
                                b2 = nb(); p2 = bank(b2)
                                k.mm(p2, glT[:, d, gt0:gt0 + C], gw2aug[:, d, :], [bGL, bP], [bB[b2]])
                                k.act(Lg[:], p2, AF.Sigmoid, [bB[b2]], [bW])
                                k.act(Lg[:], Lg[:], AF.Ln, [bW], [bW])
                                b2 = nb(); p2 = bank(b2)
                                for h in range(4):
                                    k.mm(p2[:, h * C:(h + 1) * C], Lg[:, h * 128:(h + 1) * 128], tri128[:, d, :], [bW, bC], [bB[b2]])
                                k.act(gam[:].rearrange("p h t -> p (h t)"), p2, AF.Exp, [bB[b2]], [bW])
                                k.act(ginv[:].rearrange("p h t -> p (h t)"), p2, AF.Exp, [bB[b2]], [bW], scale=-1.0)
                                glast = gam[:, :, C - 1:C] if d == 0 else gam[:, :, 0:1]
                                k.tt(qs[:], qT[:, :, gt0:gt0 + C], gam[:], ALU.mult, [bQ, bW], [bW])
                                k.tt(ks[:], kT[:, :, gt0:gt0 + C], ginv[:], ALU.mult, [bK, bW], [bW])
                                b2 = nb(); p2 = bank(b2)
                                for h in range(4):
                                    k.mm(p2[:, h * C:(h + 1) * C], ks[:, h, :], qs[:, h, :], [bW], [bB[b2]])
                                k.tt(Am[:], p2.rearrange("p (h t) -> p h t", h=4), mask128[:, d, :].unsqueeze(1).to_broadcast([128, 4, C]),
                                     ALU.mult, [bB[b2], bC], [bW])
                                b2 = nb(); p2 = bank(b2)
                                for h in range(4):
                                    k.tr(p2[:, h * C:(h + 1) * C], ks[:, h, :], ident[:], [bW, bC], [bB[b2]])
                                k.cp(kTt[:].rearrange("p h t -> p (h t)"), p2, [bB[b2]], [bW])
                                by = nb2()
                                for h in range(4):
                                    o_ = bank(by + h // 2)[:, (h % 2) * 256:(h % 2 + 1) * 256]
                                    k.mm(o_, qs[:, h, :], S[d][:, h, :], [bW, bS[d]], [bB[by + h // 2]], start=True, stop=False)
                                    k.mm(o_, Am[:, h, :], vt[:, t8, h * 256:(h + 1) * 256], [bW, bV], [bB[by + h // 2]], start=False, stop=True)
                                bs = nb2()
                                for h in range(4):
                                    k.mm(bank(bs + h // 2)[:, (h % 2) * 256:(h % 2 + 1) * 256], kTt[:, h, :], vt[:, t8, h * 256:(h + 1) * 256],
                                         [bW, bV], [bB[bs + h // 2]])
                                for hf in range(2):
                                    sl = S[d][:, 2 * hf:2 * hf + 2, :]
                                    k.tt(sl, sl, bank(bs + hf).rearrange("p (h v) -> p h v", h=2), ALU.add, [bS[d], bB[bs + hf]], [bS[d]])
                                k.tt(S[d][:], S[d][:], glast.to_broadcast([128, 4, 256]), ALU.mult, [bS[d], bW], [bS[d]])
                                if d == 0:
                                    for hf in range(2):
                                        k.cp(of[:, t8, hf * 512:(hf + 1) * 512], bank(by + hf), [bB[by + hf]], [bOF])
                                else:
                                    for hf in range(2):
                                        k.tt(osum[:, 2 * hf:2 * hf + 2, :].rearrange("p h v -> p (h v)"), bank(by + hf),
                                             of[:, t8, hf * 512:(hf + 1) * 512], ALU.add, [bB[by + hf], bOF], [bW])
                                    k.act(osq[:], osum[:], AF.Square, [bW], [bW])
                                    k.red(ost[:], osq[:], [bW], [bW])
                                    k.ts(ost[:], ost[:], 1.0 / 256, ALU.mult, [bW], [bW], s2=EPS, op1=ALU.add)
                                    k.act(ost[:], ost[:], AF.Sqrt, [bW], [bW])
                                    k.recip(ost[:], ost[:], [bW], [bW])
                                    k.tt(osum[:], osum[:], ost[:].unsqueeze(2).to_broadcast([128, 4, 256]), ALU.mult, [bW], [bW])
                                    k.tt(osum[:], osum[:], ln256[:, :].unsqueeze(1).to_broadcast([128, 4, 256]), ALU.mult, [bW, bP], [bW])
                                    bt = nb2()
                                    o2 = osum[:].rearrange("p h v -> p (h v)")
                                    for cc in range(8):
                                        k.tr(bank(bt + cc // 4)[:, (cc % 4) * 128:(cc % 4 + 1) * 128], o2[:, cc * 128:(cc + 1) * 128], ident[:],
                                             [bW, bC], [bB[bt + cc // 4]])
                                    for hf in range(2):
                                        k.tt(oT[:, 4 * hf:4 * hf + 4, gt0:gt0 + C], bank(bt + hf).rearrange("p (c t) -> p c t", c=4),
                                             gs[:, 4 * hf:4 * hf + 4, gt0:gt0 + C], ALU.mult, [bB[bt + hf], bG], [bO])
                            if g == 0:
                                k.dma("pool", ngs_o[d][s, j].rearrange("h k v -> k h v"), S[d][:], [bS[d]], [])
            out_proj(odo[j])

        for g in groups:
            with k.scope() as sc:
                xin = [sc.sb(un("xin"), [128, D], F32) for i in range(2)]
                bxin = [Buf("xin0"), Buf("xin1")]
                for t8 in range(GT // 128):
                    xi = xin[t8 % 2]; bxi = bxin[t8 % 2]
                    k.dma("sp", xi[:], xg[g, t8 * 128:(t8 + 1) * 128, :], [], [bxi])
                    for half in range(2):
                        b2 = nb(); p2 = bank(b2)
                        for jj in range(4):
                            cch = half * 4 + jj
                            k.tr(p2[:, jj * 128:(jj + 1) * 128], xi[:, cch * 128:(cch + 1) * 128], ident[:], [bxi, bC], [bB[b2]])
                        k.cp(xT[:, half * 4:(half + 1) * 4, t8 * 128:(t8 + 1) * 128], p2.rearrange("p (j t) -> p j t", j=4), [bB[b2]], [bX])
            for L in range(nlayers):
                if L % 2 == 0:
                    even_layer(g, L // 2, L)
                else:
                    odd_layer(g, L // 2, L)
            with k.scope() as sc:
                rstd = compute_rstd(sc)
                yout = [sc.sb(un("yout"), [128, D], F32) for i in range(2)]
                byo = [Buf("yout0"), Buf("yout1")]
                tmpn = [sc.sb(un("tmpn"), [128, 512], F32) for i in range(2)]
                btn = [Buf("tmpn0"), Buf("tmpn1")]
                it = 0
                for t8 in range(GT // 128):
                    yo = yout[t8 % 2]; by_ = byo[t8 % 2]
                    for half in range(2):
                        b2 = nb(); p2 = bank(b2)
                        tn = tmpn[it % 2]; bt_ = btn[it % 2]; it += 1
                        for jj in range(4):
                            cch = half * 4 + jj
                            k.stt(tn[:, jj * 128:(jj + 1) * 128], xT[:, cch, t8 * 128:(t8 + 1) * 128], finalg_sb[:, cch:cch + 1],
                                  rstd[:, t8 * 128:(t8 + 1) * 128], ALU.mult, ALU.mult, [bX, bR, bC], [bt_])
                            k.tr(p2[:, jj * 128:(jj + 1) * 128], tn[:, jj * 128:(jj + 1) * 128], ident[:], [bt_, bC], [bB[b2]])
                        k.act(yo[:, half * 512:(half + 1) * 512], p2, AF.Copy, [bB[b2]], [by_])
                    k.dma("pool", yg[g, t8 * 128:(t8 + 1) * 128, :], yo[:], [by_], [])
        k.barrier()
    print("instructions:", k.ninstr, {e: c for e, c in k.cnt.items()})
    return nc


_CACHE = {}


def _consts():
    f32 = np.float32
    c = {}
    c["ident"] = np.eye(128, dtype=f32)
    s = np.arange(64)[:, None]; t = np.arange(64)[None, :]
    tri = np.zeros((64, 2, 2, 64), f32)
    tri[:, 0, 0] = (s <= t); tri[:, 0, 1] = (s < t); tri[:, 1, 0] = (s >= t); tri[:, 1, 1] = (s > t)
    c["tri64"] = tri * f32(-RWKV_DECAY_SCALE)
    s1 = np.arange(128)[:, None]; t1 = np.arange(128)[None, :]
    tri128 = np.zeros((128, 2, 128), f32)
    tri128[:, 0] = (s1 <= t1); tri128[:, 1] = (s1 >= t1)
    c["tri128"] = tri128 / f32(16.0)
    c["mask128"] = tri128.copy()
    m64 = np.zeros((64, 2, 128), f32)
    m64[:, 0, 0:64] = (s < t); m64[:, 0, 64:128] = (s <= t); m64[:, 1, 0:64] = (s > t); m64[:, 1, 64:128] = (s >= t)
    c["mask64"] = m64
    mN = np.zeros((64, 2, 64), f32)
    mN[:, 0] = (t < s); mN[:, 1] = (t > s)
    c["maskN"] = mN
    bd = np.zeros((128, 128), f32); bd[0:64, 0:64] = 1; bd[64:128, 64:128] = 1
    c["bdones"] = bd
    T = 1024
    row = np.repeat(np.arange(T // 64), 64).astype(f32); col = np.tile(np.arange(64), T // 64).astype(f32)
    inv = (f32(10000.0) ** (-np.arange(16, dtype=f32) / f32(16))).astype(f32)
    ang = np.stack([row[:, None] * inv, col[:, None] * inv], axis=1).astype(f32)
    cos = np.cos(ang).astype(f32); sin = np.sin(ang).astype(f32)
    cos64 = np.stack([cos, cos], axis=2).reshape(T, 64)
    sin64 = np.stack([-sin, sin], axis=2).reshape(T, 64)
    c["ropecos"] = np.ascontiguousarray(cos64.reshape(8, 128, 64).transpose(1, 0, 2))
    c["ropesin"] = np.ascontiguousarray(sin64.reshape(8, 128, 64).transpose(1, 0, 2))
    return c


def kernel(**inp):
    f32 = np.float32
    n = 8
    A = lambda name: np.asarray(inp[name], f32)
    x_prompt = A("x_prompt"); x_sample = A("x_sample"); c = A("c"); c_ctx = A("c_ctx")

    def pc(v, nchunk):
        v = np.asarray(v, f32)
        lead = v.shape[:-1]
        v = v.reshape(lead + (nchunk, 128))
        return np.ascontiguousarray(np.moveaxis(v, -1, 0))

    def wblk(w, nblk):
        Ln, rows, cols = w.shape
        if cols < nblk * WB:
            w = np.concatenate([w, np.zeros((Ln, rows, nblk * WB - cols), f32)], axis=2)
        return np.ascontiguousarray(w.reshape(Ln, NCH, 128, nblk, WB).transpose(0, 3, 2, 1, 4))

    shared = dict(_consts())
    shared["modw"] = wblk(A("mod_w"), 12).reshape(DEPTH * 12, 128, NCH, WB)
    shared["modbT"] = pc(A("mod_b"), 24)
    shared["normgT"] = pc(A("norm_g"), NCH)
    shared["finalgT"] = pc(A("final_g"), NCH)
    shared["evw"] = wblk(A("ev_w_in"), 14)
    shared["evo"] = wblk(A("ev_w_out"), 4)
    shared["odw"] = wblk(A("od_w_in"), 13)
    shared["odo"] = wblk(A("od_w_out"), 4)
    shared["w2aug"] = np.ascontiguousarray(np.concatenate([A("rw_w2"), A("rw_w0")[:, :, None, :]], axis=2).transpose(0, 2, 1, 3))
    shared["a2aug"] = np.ascontiguousarray(np.concatenate([A("rw_a2"), A("rw_a0")[:, :, None, :]], axis=2).transpose(0, 2, 1, 3))
    shared["gw2aug"] = np.ascontiguousarray(np.concatenate([A("gla_w2"), A("gla_b")[:, :, None, :]], axis=2).transpose(0, 2, 1, 3))
    shared["ln256"] = np.ascontiguousarray(np.broadcast_to(A("gla_ln_g")[:, None, :], (2, 128, 256)))
    qk = np.stack([A("ev_qn_g"), A("ev_kn_g")], axis=1)
    shared["qkg"] = np.ascontiguousarray(np.broadcast_to(qk[:, None], (2, 128, 2, 64)))
    evs = np.concatenate([pc(A("ev_shift_mu"), 14), pc(A("rw_kk"), 4), pc(A("rw_ka"), 4), pc(A("rw_rk").reshape(2, 512), 4),
                          pc(A("rw_ln_g"), 4), pc(A("rw_ln_b"), 4)], axis=2)
    shared["evs"] = np.ascontiguousarray(evs.transpose(1, 0, 2))
    in_maps = []
    for core in range(n):
        sb = core % 4
        m = dict(shared)
        m["xg"] = np.ascontiguousarray(np.stack([x_prompt[4 * core:4 * core + 4].reshape(GT, D), x_sample[sb]], axis=0))
        m["condT"] = np.ascontiguousarray(np.stack([pc(c_ctx, NCH), pc(c[sb], NCH)], axis=-1))
        m["cak"] = np.ascontiguousarray(A("cache_attn_k")[sb].reshape(2, 512, 128))
        m["cav"] = np.ascontiguousarray(A("cache_attn_v")[sb].reshape(2, 512, 128))
        m["srf"] = np.ascontiguousarray(A("state_rwkv_fwd")[sb]); m["srb"] = np.ascontiguousarray(A("state_rwkv_bwd")[sb])
        m["sgf"] = np.ascontiguousarray(A("state_gla_fwd")[sb]); m["sgb"] = np.ascontiguousarray(A("state_gla_bwd")[sb])
        in_maps.append(m)
    if "nc" not in _CACHE:
        _CACHE["nc"] = build_program()
    res = run_bass_kernel_spmd(_CACHE["nc"], in_maps, core_ids=list(range(n)))
    R = res.results
    cat = lambda name: np.concatenate([R[i][name] for i in range(n)], axis=0)
    y_prompt = np.concatenate([R[i]["yg"][0].reshape(4, 256, D) for i in range(n)], axis=0)
    y_sample = np.stack([R[i]["yg"][1] for i in range(4)], axis=0)
    new_k = cat("nk").reshape(32, 2, 256, 2, 64)
    new_v = cat("nv").reshape(32, 2, 256, 2, 64)
    return (y_prompt, y_sample, new_k, new_v, cat("nrf"), cat("nrb"), cat("ngf"), cat("ngb"))
```

```python
import contextlib
import os
STOP = int(os.environ.get('KSTOP', '9'))
SUB = float(os.environ.get('KSUB', '9'))
KNS = int(os.environ.get('KNS', '99'))
KNC = int(os.environ.get('KNC', '99'))
KND = int(os.environ.get('KND', '2'))
import numpy as np
import concourse.bass as bass
import concourse.mybir as mybir
from concourse.bass_utils import run_bass_kernel_spmd

F32 = mybir.dt.float32
BF16 = mybir.dt.bfloat16
AF = mybir.ActivationFunctionType
ALU = mybir.AluOpType
AX = mybir.AxisListType

D = 1024
NCH = 8
DEPTH = 4
GT = 1024
EV_COLS = 3584
OD_COLS = 3104
EPS = 1e-6
GN_EPS = 64e-5
RWKV_DECAY_SCALE = 0.606531
WB = 256


class Buf:
    __slots__ = ("name", "w", "r")

    def __init__(self, name):
        self.name = name
        self.w = None
        self.r = {}


class KB:
    SEM_WRAP = 20000

    def __init__(self, nc):
        self.nc = nc
        self.es = contextlib.ExitStack()
        self.engs = {"pe": nc.tensor, "act": nc.scalar, "dve": nc.vector, "pool": nc.gpsimd, "sp": nc.sync}
        self.cnt = {e: 0 for e in self.engs}
        self.esems = {e: [] for e in self.engs}
        self.seen = {e: {} for e in self.engs}
        self.semobj = {}
        self.ndma_sems = {"sp": 12, "pool": 6, "act": 4}
        self.dma_sems = {}
        self.dma_rr = {q: 0 for q in self.ndma_sems}
        self.dma_cnt = {}
        self.nsem = 0
        self.ninstr = 0

    def new_sem(self, name):
        s = self.es.enter_context(self.nc.semaphore(name))
        self.semobj[name] = s
        return name

    def sb(self, name, shape, dtype):
        return self.es.enter_context(self.nc.sbuf_tensor(name, list(shape), dtype))

    def ps(self, name, shape, dtype=F32):
        return self.es.enter_context(self.nc.psum_tensor(name, list(shape), dtype))

    def _cur_sem(self, e):
        idx = self.cnt[e] // self.SEM_WRAP
        while len(self.esems[e]) <= idx:
            self.esems[e].append(self.new_sem(f"s_{e}_{len(self.esems[e])}"))
        return self.esems[e][idx]

    def _wait(self, e, tok):
        if tok is None:
            return
        key, val = tok
        if self.seen[e].get(key, 0) >= val:
            return
        self.engs[e].wait_ge(self.semobj[key], val)
        self.seen[e][key] = val

    def _deps(self, e, reads, writes, pe_acc=False):
        for b in reads:
            if b.w is not None:
                self._wait(e, b.w)
        for b in writes:
            if b.w is not None and not (pe_acc and e == "pe"):
                self._wait(e, b.w)
            for e2, tok in b.r.items():
                if e2 == e and e == "pe":
                    continue
                self._wait(e, tok)

    def _mark(self, e, tok, reads, writes):
        for b in reads:
            b.r[e] = tok
        for b in writes:
            b.w = tok
            b.r = {}

    def op(self, e, reads, writes, fn, pe_acc=False):
        self._deps(e, reads, writes, pe_acc)
        sem = self._cur_sem(e)
        ins = fn(self.engs[e])
        ins.then_inc(self.semobj[sem], 1)
        self.cnt[e] += 1
        self.ninstr += 1
        val = self.cnt[e] - (self.cnt[e] - 1) // self.SEM_WRAP * self.SEM_WRAP
        tok = (sem, val)
        self._mark(e, tok, reads, writes)
        return tok

    def dma(self, q, out, in_, reads, writes):
        if q not in self.dma_sems:
            self.dma_sems[q] = [self.new_sem(f"d_{q}_{i}") for i in range(self.ndma_sems[q])]
            for s in self.dma_sems[q]:
                self.dma_cnt[s] = 0
        sems = self.dma_sems[q]
        s = sems[self.dma_rr[q] % len(sems)]
        self.dma_rr[q] += 1
        if self.dma_cnt[s] > 0:
            self._wait(q, (s, 16 * self.dma_cnt[s]))
        self._deps(q, reads, writes)
        self.engs[q].dma_start(out=out, in_=in_).then_inc(self.semobj[s], 16)
        self.dma_cnt[s] += 1
        self.ninstr += 1
        tok = (s, 16 * self.dma_cnt[s])
        self._mark(s, tok, reads, writes)
        return tok

    def finish(self, bufs):
        for b in bufs:
            if b.w is not None:
                self._wait("sp", b.w)
        for q, sems in self.dma_sems.items():
            for s in sems:
                if self.dma_cnt[s] > 0:
                    self._wait("sp", (s, 16 * self.dma_cnt[s]))


    def barrier(self):
        toks = []
        for e in self.engs:
            if self.cnt[e] > 0:
                sem = self.esems[e][(self.cnt[e] - 1) // self.SEM_WRAP]
                val = self.cnt[e] - (self.cnt[e] - 1) // self.SEM_WRAP * self.SEM_WRAP
                toks.append((sem, val))
        for q, sems in self.dma_sems.items():
            for s in sems:
                if self.dma_cnt[s] > 0:
                    toks.append((s, 16 * self.dma_cnt[s]))
        for e in self.engs:
            for t in toks:
                self._wait(e, t)

    @contextlib.contextmanager
    def scope(self):
        sc = _Scope(self)
        try:
            yield sc
        finally:
            self.barrier()
            sc.es.close()

    def _pe_rows(self, ap):
        base = ap.base_partition()
        n = ap.shape[0]
        grp = set(range(base // 32, (base + n - 1) // 32 + 1))
        last = getattr(self, "_pe_last_grp", None)
        if last is not None and not (grp & last) and self.cnt["pe"] > 0:
            for sem_i in range(max(0, (self.cnt["pe"] - 1) // self.SEM_WRAP - 1), (self.cnt["pe"] - 1) // self.SEM_WRAP + 1):
                sem = self.esems["pe"][sem_i]
                if sem_i == (self.cnt["pe"] - 1) // self.SEM_WRAP:
                    val = self.cnt["pe"] - sem_i * self.SEM_WRAP
                else:
                    val = self.SEM_WRAP
                self._wait("pe", (sem, val))
        self._pe_last_grp = grp

    def mm(self, out, lhsT, rhs, R, W, start=True, stop=True):
        self._pe_rows(lhsT)
        return self.op("pe", R, W, lambda e: e.matmul(out, lhsT, rhs, start=start, stop=stop), pe_acc=True)

    def tr(self, out, in_, ident, R, W):
        self._pe_rows(in_)
        return self.op("pe", R, W, lambda e: e.transpose(out, in_, ident), pe_acc=True)

    def tt(self, out, in0, in1, op, R, W, e="dve"):
        return self.op(e, R, W, lambda g: g.tensor_tensor(out=out, in0=in0, in1=in1, op=op))

    def ts(self, out, in0, s1, op0, R, W, s2=None, op1=None, e="dve"):
        if op1 is None:
            return self.op(e, R, W, lambda g: g.tensor_scalar(out=out, in0=in0, scalar1=s1, scalar2=None, op0=op0))
        return self.op(e, R, W, lambda g: g.tensor_scalar(out=out, in0=in0, scalar1=s1, scalar2=s2, op0=op0, op1=op1))

    def stt(self, out, in0, scalar, in1, op0, op1, R, W, e="dve"):
        return self.op(e, R, W, lambda g: g.scalar_tensor_tensor(out=out, in0=in0, scalar=scalar, in1=in1, op0=op0, op1=op1))

    def act(self, out, in_, func, R, W, **kw):
        return self.op("act", R, W, lambda g: g.activation(out=out, in_=in_, func=func, **kw))

    def cp(self, out, in_, R, W, e="dve"):
        return self.op(e, R, W, lambda g: g.tensor_copy(out=out, in_=in_))

    def red(self, out, in_, R, W, op=None):
        return self.op("dve", R, W, lambda g: g.tensor_reduce(out=out, in_=in_, axis=AX.X, op=(op or ALU.add)))

    def recip(self, out, in_, R, W):
        return self.op("dve", R, W, lambda g: g.reciprocal(out=out, in_=in_))

    def ms(self, ap, val, W, e="dve"):
        return self.op(e, [], W, lambda g: g.memset(ap, val))


class _Scope:
    def __init__(self, k):
        self.k = k
        self.es = contextlib.ExitStack()

    def sb(self, name, shape, dtype):
        return self.es.enter_context(self.k.nc.sbuf_tensor(name, list(shape), dtype))


def build_program(nlayers=DEPTH, groups=(0, 1)):
    nc = bass.Bass("TRN2", target_bir_lowering=False)
    k = KB(nc)
    uid = [0]

    def un(name):
        uid[0] += 1
        return f"{name}_{uid[0]}"

    def din(name, shape):
        return nc.dram_tensor(name, list(shape), F32, kind="ExternalInput").ap()

    def dout(name, shape):
        return nc.dram_tensor(name, list(shape), F32, kind="ExternalOutput").ap()

    xg = din("xg", [2, GT, D])
    condT = din("condT", [128, NCH, 2])
    modw = din("modw", [DEPTH * 12, 128, NCH, WB])
    modbT = din("modbT", [128, DEPTH, 24])
    normgT = din("normgT", [128, DEPTH, NCH])
    finalgT = din("finalgT", [128, NCH])
    ident_d = din("ident", [128, 128])
    evw = din("evw", [2, 14, 128, NCH, WB])
    evo = din("evo", [2, 4, 128, NCH, WB])
    odw = din("odw", [2, 13, 128, NCH, WB])
    odo = din("odo", [2, 4, 128, NCH, WB])
    tri64_d = din("tri64", [64, 2, 2, 64])
    tri128_d = din("tri128", [128, 2, 128])
    mask64_d = din("mask64", [64, 2, 128])
    maskN_d = din("maskN", [64, 2, 64])
    mask128_d = din("mask128", [128, 2, 128])
    bd_d = din("bdones", [128, 128])
    cos_d = din("ropecos", [128, 8, 64])
    sin_d = din("ropesin", [128, 8, 64])
    w2aug_d = din("w2aug", [2, 65, 2, 512])
    a2aug_d = din("a2aug", [2, 65, 2, 512])
    gw2aug_d = din("gw2aug", [2, 17, 2, 512])
    ln256_d = din("ln256", [2, 128, 256])
    qkg_d = din("qkg", [2, 128, 2, 64])
    evs_d = din("evs", [2, 128, 34])
    cak = din("cak", [2, 512, 128])
    cav = din("cav", [2, 512, 128])
    srs_d = [din("srf", [2, 8, 64, 64]), din("srb", [2, 8, 64, 64])]
    sgs_d = [din("sgf", [2, 4, 128, 256]), din("sgb", [2, 4, 128, 256])]
    yg = dout("yg", [2, GT, D])
    nk_o = dout("nk", [4, 2, 256, 128])
    nv_o = dout("nv", [4, 2, 256, 128])
    nrs_o = [dout("nrf", [4, 2, 8, 64, 64]), dout("nrb", [4, 2, 8, 64, 64])]
    ngs_o = [dout("ngf", [4, 2, 4, 128, 256]), dout("ngb", [4, 2, 4, 128, 256])]

    with k.es:
        bC = Buf("const")

        def cload(name, src, shape, dtype=F32):
            t = k.sb(name, shape, dtype)
            k.dma("sp", t[:], src, [], [bC])
            return t

        ident = cload("ident_sb", ident_d[:, :], [128, 128])
        tri64 = cload("tri64_sb", tri64_d[:, :, :, :], [64, 2, 2, 64])
        tri128 = cload("tri128_sb", tri128_d[:, :, :], [128, 2, 128])
        mask64 = cload("mask64_sb", mask64_d[:, :, :], [64, 2, 128])
        maskN = cload("maskN_sb", maskN_d[:, :, :], [64, 2, 64])
        mask128 = cload("mask128_sb", mask128_d[:, :, :], [128, 2, 128])
        bdones = cload("bd_sb", bd_d[:, :], [128, 128])
        rope_tabs = {}
        cond_sb = cload("cond_sb", condT[:, :, :], [128, NCH, 2])
        modb_sb = cload("modb_sb", modbT[:, :, :], [128, DEPTH, 24])
        normg_sb = cload("normg_sb", normgT[:, :, :], [128, DEPTH, NCH])
        finalg_sb = cload("finalg_sb", finalgT[:, :], [128, NCH])
        ones_f = k.sb("ones_f", [128, 128], F32)
        k.ms(ones_f[:], 1.0 / D, [bC])
        ones_bf = k.sb("ones_bf", [128, 64], BF16)
        k.ms(ones_bf[:], 1.0, [bC])
        bP = Buf("params")
        w2aug = k.sb("w2aug_sb", [65, 2, 512], F32)
        a2aug = k.sb("a2aug_sb", [65, 2, 512], F32)
        gw2aug = k.sb("gw2aug_sb", [17, 2, 512], F32)
        ln256 = k.sb("ln256_sb", [128, 256], F32)
        qkg = k.sb("qkg_sb", [128, 2, 64], F32)
        evs = k.sb("evs_sb", [128, 34], F32)
        evx = k.sb("evx_sb", [128, 32], F32)

        PT = [k.ps(f"P{i}", [128, 1024], F32) for i in range(4)]
        bB = [Buf(f"bank{i}") for i in range(8)]
        rr = [0]

        def bank(i):
            return PT[i // 2][:, (i % 2) * 512:(i % 2) * 512 + 512]

        def nb(lo=0, hi=8):
            i = lo + rr[0] % (hi - lo)
            rr[0] += 1
            return i

        def nb2():
            i = (rr[0] % 8 + 1) // 2 * 2 % 8
            rr[0] += (i - rr[0] % 8) % 8 + 2
            return i

        scond = k.sb("scond", [128, NCH, 2], F32)
        k.act(scond[:], cond_sb[:], AF.Silu, [bC], [bC])
        modT = k.sb("modT", [128, DEPTH, 24, 2], F32)
        bM = Buf("mod")
        xT = k.sb("xT", [128, NCH, GT], F32)
        bX = Buf("xT")
        oT = k.sb("oT", [128, NCH, GT], BF16)
        bO = Buf("oT")
        bR = Buf("rstd")
        hcol = k.sb("hcol", [128, 3, NCH], F32)
        bH = Buf("hcol")

        with k.scope() as sc:
            wst = [sc.sb(un("mwst"), [128, NCH, WB], F32) for i in range(4)]
            bws = [Buf("mwst%d" % i) for i in range(4)]
            mrow = [sc.sb(un("mrow"), [2, WB], F32) for i in range(2)]
            bmrow = [Buf("mrow0"), Buf("mrow1")]
            wi = 0
            for L in range(nlayers):
                for blk in range(12):
                    w = wst[wi % 4]; bw = bws[wi % 4]
                    k.dma("sp" if wi % 2 == 0 else "pool", w[:], modw[L * 12 + blk], [], [bw])
                    bi = nb(); pm = bank(bi); bp = bB[bi]
                    for kc in range(NCH):
                        k.mm(pm[0:2, 0:WB], scond[:, kc, :], w[:, kc, :], [bw, bC], [bp], start=(kc == 0), stop=(kc == NCH - 1))
                    rt = mrow[wi % 2]; brt = bmrow[wi % 2]
                    k.act(rt[:, :], pm[0:2, 0:WB], AF.Copy, [bp], [brt])
                    bi2 = nb(); pm2 = bank(bi2); bp2 = bB[bi2]
                    for sub in range(2):
                        k.tr(pm2[:, sub * 2:sub * 2 + 2], rt[0:2, sub * 128:(sub + 1) * 128], ident[0:2, 0:2], [brt, bC], [bp2])
                    for sub in range(2):
                        ch = blk * 2 + sub
                        k.ts(modT[:, L, ch, :], pm2[:, sub * 2:sub * 2 + 2], modb_sb[:, L, ch:ch + 1], ALU.add, [bp2, bC], [bM])
                    wi += 1

        def compute_rstd(sc):
            rstd = sc.sb(un("rstd"), [128, GT], F32)
            sq = [sc.sb(un("sq"), [128, 512], F32) for i in range(2)]
            bsq = [Buf("sq0"), Buf("sq1")]
            for tt in range(GT // 512):
                bi = nb(); pn = bank(bi); bp = bB[bi]
                for c in range(NCH):
                    s_ = sq[c % 2]; bs_ = bsq[c % 2]
                    k.act(s_[:], xT[:, c, tt * 512:(tt + 1) * 512], AF.Square, [bX], [bs_])
                    k.mm(pn, ones_f[:], s_[:], [bs_, bC], [bp], start=(c == 0), stop=(c == NCH - 1))
                sl = rstd[:, tt * 512:(tt + 1) * 512]
                k.ts(sl, pn, EPS, ALU.add, [bp], [bR])
                k.act(sl, sl, AF.Sqrt, [bR], [bR])
                k.recip(sl, sl, [bR], [bR])
            return rstd

        def wblock_factory(sc):
            wst2 = [sc.sb(un("wst"), [128, NCH, WB], F32) for i in range(2)]
            bws2 = [Buf("wst0"), Buf("wst1")]
            wbf = [sc.sb(un("wbf"), [128, NCH, WB], BF16) for i in range(2)]
            bwb = [Buf("wbf0"), Buf("wbf1")]
            cnt = [0]

            def wblock(src):
                i = cnt[0] % 2
                cnt[0] += 1
                wst = wst2[i]; bws = bws2[i]
                k.dma("sp", wst[:], src, [], [bws])
                k.cp(wbf[i][:], wst[:], [bws], [bwb[i]], e="pool")
                return wbf[i], bwb[i]
            return wblock

        def proj_F(wblock, wsrc, blocks, hT, bHT, dest, msub=128, nsub=None):
            nsub = nsub or WB // msub
            for bi_, blk in enumerate(blocks):
                w, bw = wblock(wsrc[blk])
                for sub in range(nsub):
                    for tt in range(GT // 512):
                        bi = nb(); pb = bank(bi); bp = bB[bi]
                        for kc in range(NCH):
                            k.mm(pb[0:msub, :], w[:, kc, sub * msub:(sub + 1) * msub], hT[:, kc, tt * 512:(tt + 1) * 512],
                                 [bw, bHT], [bp], start=(kc == 0), stop=(kc == NCH - 1))
                        dest(bi_ * nsub + sub, tt, pb, bp)

        def proj_T(wblock, wsrc, blocks, hT, bHT, dest):
            for bi_, blk in enumerate(blocks):
                w, bw = wblock(wsrc[blk])
                for t8 in range(GT // 128):
                    bi = nb(); pb = bank(bi); bp = bB[bi]
                    for kc in range(NCH):
                        k.mm(pb[:, 0:WB], hT[:, kc, t8 * 128:(t8 + 1) * 128], w[:, kc, :],
                             [bw, bHT], [bp], start=(kc == 0), stop=(kc == NCH - 1))
                    dest(bi_, t8, pb, bp)

        def make_h(sc, L, g):
            hT = sc.sb(un("hT"), [128, NCH, GT], BF16)
            bHT = Buf("hT")
            k.stt(hcol[:, 0, :], modT[:, L, 8:16, g], 1.0, normg_sb[:, L, :], ALU.add, ALU.mult, [bM, bC], [bH])
            k.cp(hcol[:, 1, :], modT[:, L, 0:8, g], [bM], [bH])
            k.cp(hcol[:, 2, :], modT[:, L, 16:24, g], [bM], [bH])
            rstd = compute_rstd(sc)
            tmp = [sc.sb(un("htmp"), [128, 512], F32) for i in range(2)]
            btmp = [Buf("htmp0"), Buf("htmp1")]
            i = 0
            for tt in range(GT // 512):
                for c in range(NCH):
                    t_ = tmp[i % 2]; bt_ = btmp[i % 2]; i += 1
                    k.tt(t_[:], xT[:, c, tt * 512:(tt + 1) * 512], rstd[:, tt * 512:(tt + 1) * 512], ALU.mult, [bX, bR], [bt_])
                    k.ts(hT[:, c, tt * 512:(tt + 1) * 512], t_[:], hcol[:, 0, c:c + 1], ALU.mult, [bt_, bH], [bHT],
                         s2=hcol[:, 1, c:c + 1], op1=ALU.add)
            return hT, bHT

        def out_proj(wsrc):
            with k.scope() as sc:
                wblock = wblock_factory(sc)

                def dest(cc, tt, pb, bp):
                    sl = xT[:, cc, tt * 512:(tt + 1) * 512]
                    k.stt(sl, pb, hcol[:, 2, cc:cc + 1], sl, ALU.mult, ALU.add, [bp, bH, bX], [bX])
                proj_F(wblock, wsrc, range(4), oT, bO, dest)

        def headnorm(sc_t, pb_view, nh, gidx, out3, R, W, bT=None):
            bT = bT or bT0
            sqt, ssq = sc_t
            k.act(sqt[:, 0:nh * 64], pb_view.rearrange("p h d -> p (h d)"), AF.Square, R, [bT])
            k.red(ssq[:, 0:nh], sqt[:, 0:nh * 64].rearrange("p (h d) -> p h d", h=nh), [bT], [bT])
            k.ts(ssq[:, 0:nh], ssq[:, 0:nh], 1.0 / 64, ALU.mult, [bT], [bT], s2=EPS, op1=ALU.add)
            k.act(ssq[:, 0:nh], ssq[:, 0:nh], AF.Sqrt, [bT], [bT])
            k.recip(ssq[:, 0:nh], ssq[:, 0:nh], [bT], [bT])
            k.tt(out3, pb_view, ssq[:, 0:nh].unsqueeze(2).to_broadcast([128, nh, 64]), ALU.mult, R + [bT], W)
            k.tt(out3, out3, qkg[:, gidx, :].unsqueeze(1).to_broadcast([128, nh, 64]), ALU.mult, W + [bP], W)

        bT0 = Buf("tmpT")

        def rope(x3, nh, t8, t1, t2, R, bT=None):
            bT = bT or bT0
            cosb = rope_tabs["cos"][:, t8, :].unsqueeze(1).to_broadcast([128, nh, 64])
            k.tt(t1[:, 0:nh, :], x3, cosb, ALU.mult, R + [bC], [bT])
            x5 = x3.rearrange("p h (a q f) -> p h a q f", a=2, q=2)
            t5 = t2[:, 0:nh, :].rearrange("p h (a q f) -> p h a q f", a=2, q=2)
            s4 = rope_tabs["sin"][:, t8, :].rearrange("p (a q f) -> p a q f", a=2, q=2)
            for q_ in range(2):
                k.tt(t5[:, :, :, q_, :], x5[:, :, :, 1 - q_, :],
                     s4[:, :, q_, :].unsqueeze(1).to_broadcast([128, nh, 2, 16]), ALU.mult, R + [bC], [bT])
            k.tt(x3, t1[:, 0:nh, :], t2[:, 0:nh, :], ALU.add, [bT], R)

        def even_layer(g, j, L):
            nseq, T = (4, 256) if g == 0 else (1, 1024)
            TP = T + 2
            koff = 0 if g == 0 else 512
            SK = GT + koff
            k.dma("sp", w2aug[:], w2aug_d[j], [], [bP])
            k.dma("sp", a2aug[:], a2aug_d[j], [], [bP])
            k.dma("sp", qkg[:], qkg_d[j], [], [bP])
            k.dma("sp", evs[:], evs_d[j], [], [bP])
            k.ts(evx[:, 0:14], evs[:, 0:14], 0.5, ALU.mult, [bP], [bP])
            k.ts(evx[:, 14:28], evs[:, 0:14], -1.0, ALU.mult, [bP], [bP], s2=1.0, op1=ALU.add)
            k.ts(evx[:, 28:32], evs[:, 18:22], -1.0, ALU.mult, [bP], [bP], s2=1.0, op1=ALU.add)
            with k.scope() as scL:
                gbT = scL.sb(un("gbT"), [128, 4, GT], BF16); bGB = Buf("gbT")
                zraw = scL.sb(un("zraw"), [128, 14, nseq * TP], BF16); bZ = Buf("zraw")
                k.ms(zraw[:], 0.0, [bZ])
                zr4 = zraw[:].rearrange("p c (s t) -> p c s t", s=nseq)
                with k.scope() as scA:
                    gaT = scA.sb(un("gaT"), [128, 4, GT], BF16); bGA = Buf("gaT")
                    qT = scA.sb(un("qT"), [64, 8, GT], BF16); bQ = Buf("qT")
                    kT = scA.sb(un("kT"), [64, 2, SK], BF16); bK = Buf("kT")
                    vtok = scA.sb(un("vtok"), [128, SK // 128, 128], BF16); bV = Buf("vtok")
                    if g == 1:
                      with k.scope() as sc:
                          ck = sc.sb(un("ck"), [128, 4, 128], F32); bCK = Buf("ck")
                          k.dma("sp", ck[:], cak[j].rearrange("(i p) f -> p i f", p=128), [], [bCK])
                          for i in range(4):
                              b2 = nb(); p2 = bank(b2)
                              for hh in range(2):
                                  k.tr(p2[0:64, hh * 128:(hh + 1) * 128], ck[:, i, hh * 64:(hh + 1) * 64], ident[:], [bCK, bC], [bB[b2]])
                              k.cp(kT[:, :, i * 128:(i + 1) * 128],
                                   p2[0:64, 0:256].rearrange("p (h t) -> p h t", h=2), [bB[b2]], [bK])
                          cv = sc.sb(un("cv"), [128, 4, 128], F32); bCV = Buf("cv")
                          k.dma("sp", cv[:], cav[j].rearrange("(i p) f -> p i f", p=128), [], [bCV])
                          k.cp(vtok[:, 0:4, :], cv[:], [bCV], [bV])

                    with k.scope() as sc:
                        hT, bHT = make_h(sc, L, g)
                        wblock = wblock_factory(sc)
                        if g == 1:
                            rope_tabs["cos"] = sc.sb(un("cos_sb"), [128, 8, 64], F32)
                            rope_tabs["sin"] = sc.sb(un("sin_sb"), [128, 8, 64], F32)
                            k.dma("sp", rope_tabs["cos"][:], cos_d[:, :, :], [], [bC])
                            k.dma("sp", rope_tabs["sin"][:], sin_d[:, :, :], [], [bC])
                        print("S1 sbuf remaining", nc.sbuf_bytes_remaining)
                        sqtL = [sc.sb(un("sqt"), [128, 256], F32) for _ in range(2)]
                        ssqL = [sc.sb(un("ssq"), [128, 4], F32) for _ in range(2)]
                        qnL = [sc.sb(un("qn"), [128, 4, 64], F32) for _ in range(2)]; bQNL = [Buf("qn0"), Buf("qn1")]
                        r1L = [sc.sb(un("r1"), [128, 4, 64], F32) for _ in range(2)]
                        r2L = [sc.sb(un("r2"), [128, 4, 64], F32) for _ in range(2)]
                        bTL = [Buf("tmpT0"), Buf("tmpT1")]
                        kvo = [sc.sb(un("kvo"), [128, 256], F32) for i in range(2)]
                        bKVO = [Buf("kvo0"), Buf("kvo1")]

                        def dest_q(bi_, t8, pb, bp):
                            ix = t8 % 2
                            sqt, ssq, qn, r1, r2, bQN, bTx = sqtL[ix], ssqL[ix], qnL[ix], r1L[ix], r2L[ix], bQNL[ix], bTL[ix]
                            headnorm((sqt, ssq), pb[:, 0:256].rearrange("p (h d) -> p h d", h=4), 4, 0, qn[:], [bp], [bQN], bT=bTx)
                            if g == 1:
                                rope(qn[:], 4, t8, r1, r2, [bQN], bT=bTx)
                            b2 = nb(); p2 = bank(b2)
                            for hh in range(4):
                                k.tr(p2[0:64, hh * 128:(hh + 1) * 128], qn[:, hh, :], ident[:], [bQN, bC], [bB[b2]])
                            k.cp(qT[:, bi_ * 4:bi_ * 4 + 4, t8 * 128:(t8 + 1) * 128],
                                 p2[0:64, :].rearrange("p (h t) -> p h t", h=4), [bB[b2]], [bQ])
                        proj_T(wblock, evw[j], [0, 1], hT, bHT, dest_q)

                        def dest_kv(bi_, t8, pb, bp):
                            ko = kvo[t8 % 2]; bko = bKVO[t8 % 2]
                            kn3 = ko[:, 0:128].rearrange("p (h d) -> p h d", h=2)
                            ix = t8 % 2
                            sqt, ssq, r1, r2, bTx = sqtL[ix], ssqL[ix], r1L[ix], r2L[ix], bTL[ix]
                            headnorm((sqt, ssq), pb[:, 0:128].rearrange("p (h d) -> p h d", h=2), 2, 1, kn3, [bp], [bko], bT=bTx)
                            k.cp(ko[:, 128:256], pb[:, 128:256], [bp], [bko])
                            k.cp(vtok[:, koff // 128 + t8, :], pb[:, 128:256], [bp], [bV])
                            if g == 0:
                                b_ = t8 // 2; t0 = (t8 % 2) * 128
                                k.dma("pool", nk_o[b_, j, t0:t0 + 128, :], ko[:, 0:128], [bko], [])
                                k.dma("pool", nv_o[b_, j, t0:t0 + 128, :], ko[:, 128:256], [bko], [])
                            else:
                                rope(kn3, 2, t8, r1, r2, [bko], bT=bTx)
                            b2 = nb(); p2 = bank(b2)
                            for hh in range(2):
                                k.tr(p2[0:64, hh * 128:(hh + 1) * 128], kn3[:, hh, :], ident[:], [bko, bC], [bB[b2]])
                            k.cp(kT[:, :, koff + t8 * 128:koff + (t8 + 1) * 128],
                                 p2[0:64, 0:256].rearrange("p (h t) -> p h t", h=2), [bB[b2]], [bK])
                        proj_T(wblock, evw[j], [2], hT, bHT, dest_kv)

                        def dest_ga(cc, tt, pb, bp):
                            k.act(gaT[:, cc, tt * 512:(tt + 1) * 512], pb, AF.Silu, [bp], [bGA])
                        proj_F(wblock, evw[j], [3, 4], hT, bHT, dest_ga)

                        def dest_zb(cc, tt, pb, bp):
                            if g == 0:
                                k.cp(zr4[:, cc, 2 * tt:2 * tt + 2, 1:T + 1], pb.rearrange("p (s t) -> p s t", s=2), [bp], [bZ])
                            else:
                                k.cp(zr4[:, cc, 0, 1 + tt * 512:1 + (tt + 1) * 512], pb, [bp], [bZ])
                        proj_F(wblock, evw[j], range(5, 12), hT, bHT, dest_zb)

                        def dest_gb(cc, tt, pb, bp):
                            k.act(gbT[:, cc, tt * 512:(tt + 1) * 512], pb, AF.Silu, [bp], [bGB])
                        proj_F(wblock, evw[j], [12, 13], hT, bHT, dest_gb)

                    with (k.scope() if STOP >= 2 else contextlib.nullcontext()) as sc:
                      if STOP >= 2:
                            pexp = [sc.sb(un("pexp"), [128, 512], BF16) for i in range(2)]
                            bPE = [Buf("pexp0"), Buf("pexp1")]
                            rec = sc.sb(un("rec"), [64, 512], F32); bRec = Buf("rec")
                            oa = sc.sb(un("oa"), [128, 512], F32); bOA = Buf("oa")
                            QB = min(T, 512)
                            po = bank(6); psm = bank(7)
                            ie = 0
                            for s in range(nseq):
                                kbase = s * T if g == 0 else 0
                                nsc = (T + koff) // 128
                                for h in range(8):
                                    kv = h // 4
                                    hb = (h % 2) * 64
                                    for qb in range(T // QB):
                                        q0 = s * T + qb * QB
                                        for sc_ in range(nsc):
                                            kpos = kbase + sc_ * 128
                                            bi = nb(0, 6); pb = bank(bi); bp = bB[bi]
                                            k.mm(pb[:, 0:QB], kT[:, kv, kpos:kpos + 128], qT[:, h, q0:q0 + QB], [bK, bQ], [bp])
                                            pe_ = pexp[ie % 2]; bpe = bPE[ie % 2]; ie += 1
                                            k.act(pe_[:, 0:QB], pb[:, 0:QB], AF.Exp, [bp], [bpe], scale=0.125)
                                            k.mm(po[0:64, 0:QB], vtok[:, kpos // 128, kv * 64:(kv + 1) * 64], pe_[:, 0:QB],
                                                 [bV, bpe], [bB[6]], start=(sc_ == 0), stop=(sc_ == nsc - 1))
                                            k.mm(psm[0:64, 0:QB], ones_bf[:, :], pe_[:, 0:QB],
                                                 [bC, bpe], [bB[7]], start=(sc_ == 0), stop=(sc_ == nsc - 1))
                                        k.recip(rec[:, 0:QB], psm[0:64, 0:QB], [bB[7]], [bRec])
                                        k.tt(oa[hb:hb + 64, 0:QB], po[0:64, 0:QB], rec[:, 0:QB], ALU.mult, [bB[6], bRec], [bOA])
                                        k.tt(oT[hb:hb + 64, h // 2, q0:q0 + QB], oa[hb:hb + 64, 0:QB],
                                             gaT[hb:hb + 64, h // 2, q0:q0 + QB], ALU.mult, [bOA, bGA], [bO])

                with k.scope() as sc:
                    C = 64
                    if STOP < 3:
                        raise_skip = True
                    else:
                        raise_skip = False
                    nchunk = T // C
                    f = lambda name, shape, dt=F32: sc.sb(un(name), shape, dt)
                    yf = f("yf", [128, nseq * nchunk // 2, 512], BF16); bYF = Buf("yf")
                    ST = [f("ST", [128, 4, 64]) for d in range(2)]; bST = [Buf("ST0"), Buf("ST1")]
                    zsP = [f("zs", [128, 14, C]) for p_ in range(2)]
                    zt1 = f("zt1", [128, 14, C])
                    twa = [f("tw", [65, C]) for d in range(2)]
                    ala = [f("al", [65, C]) for d in range(2)]
                    bW = Buf("rwtmp")
                    Ls = f("Ls", [64, 512])
                    gam = f("gam", [128, 4, C]); gamp = f("gamp", [128, 4, C]); ginv = f("ginv", [128, 4, C])
                    av = [f("av", [128, 4, C]) for d in range(2)]
                    kkr = f("kkr", [128, 4, C]); ksq = f("ksq", [128, 4, C]); kk = f("kk", [128, 4, C])
                    kdP = [[f("kd", [128, 4, C]) for d in range(2)] for p_ in range(2)]
                    tmp4 = f("tmp4", [128, 4, C]); tmp5 = f("tmp5", [128, 4, C])
                    Bn = {n_: Buf(n_) for n_ in ["zs", "zt1", "ala0", "ala1", "twa0", "twa1", "av0", "av1", "kkr", "ksq", "kk", "kd0", "kd1", "tmp4", "tmp5", "Ls", "gam", "gamp", "ginv", "AR", "KBt", "Gm1", "Gm2", "Nm0", "Nm1", "Am0", "Am1", "Vt", "Us0", "Us1", "KBT", "ysum", "ysq", "yst", "bon", "obt", "tmp6"] + [x + str(p_) for p_ in range(2) for x in ["zs", "AR", "KBt", "Vt", "Gm1", "Gm2", "Nmi", "gl", "kd0_", "kd1_"]]}
                    ARP = [f("AR", [128, 4, 2, C]) for p_ in range(2)]; KBtP = [f("KBt", [128, 4, 2, C]) for p_ in range(2)]
                    Gm1P = [f("Gm1", [64, 8, 128]) for p_ in range(2)]; Gm2P = [f("Gm2", [64, 8, 128]) for p_ in range(2)]; Nm = [f("Nm", [64, 8, 64]) for i in range(2)]
                    NmiP = [f("Nmi", [64, 8, 64]) for p_ in range(2)]; glP = [f("gl", [128, 4, 1]) for p_ in range(2)]; tmp6 = f("tmp6", [128, 4, C])
                    Am = [f("Am", [64, 8, 64]) for i in range(2)]
                    VtP = [f("Vt", [64, 512]) for p_ in range(2)]; Us = [f("Us", [64, 512]) for i in range(2)]
                    KBT = f("KBT", [64, 4, 2, 128])
                    ysum = f("ysum", [64, 8, 64]); ysq = f("ysq", [64, 8, 64]); yst = f("yst", [64, 16])
                    bon = f("bon", [128, 4, C]); obt = f("obt", [128, 4, C])
                    sto = f("sto", [64, 4, 128]); bSTO = Buf("sto")
                    sld = f("sld", [64, 8, 64]); bSLD = Buf("sld")
                    for d in range(2):
                        k.ms(twa[d][:], 1.0, [Bn["twa%d" % d]])
                        k.ms(ala[d][:], 1.0, [Bn["ala%d" % d]])

                    units = []
                    for s in range(min(nseq, KNS) if STOP >= 3 else 0):
                        for d in range(KND):
                            order = list(range(nchunk) if d == 0 else range(nchunk - 1, -1, -1))[:KNC]
                            for ci_, c in enumerate(order):
                                units.append((s, d, c, ci_ == 0, ci_ == len(order) - 1))
                    HO = [0, 2, 4, 6, 1, 3, 5, 7]
                    HOr = [1, 3, 5, 7, 0, 2, 4, 6]
                    rrA = [0]; rrB = [0]

                    def nbA():
                        rrA[0] += 1
                        return rrA[0] % 5

                    def nbB():
                        rrB[0] += 1
                        return 5 + rrB[0] % 3

                    def stageA(u, p):
                        s, d, c, first, last = u
                        tcol = s * TP + c * C
                        gt0 = s * T + c * C
                        yield
                        k.tt(zt1[:], zraw[:, :, tcol:tcol + C], zraw[:, :, tcol + 2:tcol + 2 + C], ALU.add, [bZ], [Bn["zt1"]], e="pool")
                        k.tt(zt1[:], zt1[:], evx[:, 0:14].unsqueeze(2).to_broadcast([128, 14, C]), ALU.mult, [Bn["zt1"], bP], [Bn["zt1"]], e="pool")
                        k.tt(zsP[p][:], zraw[:, :, tcol + 1:tcol + 1 + C], evx[:, 14:28].unsqueeze(2).to_broadcast([128, 14, C]),
                             ALU.mult, [bZ, bP], [Bn["zs%d" % p]])
                        k.tt(zsP[p][:], zsP[p][:], zt1[:], ALU.add, [Bn["zs%d" % p], Bn["zt1"]], [Bn["zs%d" % p]])
                        r_ = zsP[p][:, 0:4, :]; kraw = zsP[p][:, 4:8, :]; vv = zsP[p][:, 8:12, :]
                        dirs = [d] if d == 0 else [0, 1]
                        yield
                        for dd in dirs:
                            bal = Bn["ala%d" % dd]; bav = Bn["av%d" % dd]; bkd = Bn["kd%d_%d" % (dd, p)]
                            k.cp(ala[dd][0:64, :], zsP[p][dd * 64:(dd + 1) * 64, 13, :], [Bn["zs%d" % p]], [bal])
                            b2 = nbA(); p2 = bank(b2)
                            for cp in range(4):
                                k.mm(p2[:, cp * C:(cp + 1) * C], a2aug[:, dd, cp * 128:(cp + 1) * 128], ala[dd][:, :],
                                     [bal, bP], [bB[b2]])
                            k.act(av[dd][:].rearrange("p c t -> p (c t)"), p2[:, 0:4 * C], AF.Sigmoid, [bB[b2]], [bav])
                            k.tt(tmp4[:], av[dd][:], evs[:, 18:22].unsqueeze(2).to_broadcast([128, 4, C]), ALU.mult, [bav, bP], [Bn["tmp4"]], e="pool")
                            k.tt(tmp4[:], tmp4[:], evx[:, 28:32].unsqueeze(2).to_broadcast([128, 4, C]), ALU.add, [Bn["tmp4"], bP], [Bn["tmp4"]], e="pool")
                            k.tt(kdP[p][dd][:], kraw, tmp4[:], ALU.mult, [Bn["zs%d" % p], Bn["tmp4"]], [bkd], e="pool")
                        yield
                        k.tt(kkr[:], kraw, evs[:, 14:18].unsqueeze(2).to_broadcast([128, 4, C]), ALU.mult, [Bn["zs%d" % p], bP], [Bn["kkr"]])
                        k.tt(ksq[:], kkr[:], kkr[:], ALU.mult, [Bn["kkr"]], [Bn["ksq"]])
                        b2 = nbA(); p2 = bank(b2)
                        k.mm(p2[:, 0:4 * C], bdones[:, :], ksq[:].rearrange("p c t -> p (c t)"), [Bn["ksq"], bC], [bB[b2]])
                        k.ts(ksq[:].rearrange("p c t -> p (c t)"), p2[:, 0:4 * C], 1e-12, ALU.add, [bB[b2]], [Bn["ksq"]])
                        k.act(ksq[:], ksq[:], AF.Sqrt, [Bn["ksq"]], [Bn["ksq"]])
                        k.recip(ksq[:], ksq[:], [Bn["ksq"]], [Bn["ksq"]])
                        k.tt(kk[:], kkr[:], ksq[:], ALU.mult, [Bn["kkr"], Bn["ksq"]], [Bn["kk"]])
                        yield
                        btw = Bn["twa%d" % d]
                        k.act(twa[d][0:64, :], zsP[p][d * 64:(d + 1) * 64, 12, :], AF.Tanh, [Bn["zs%d" % p]], [btw])
                        b2 = nbA(); p2 = bank(b2)
                        k.mm(p2[0:64, :], twa[d][:, :], w2aug[:, d, :], [btw, bP], [bB[b2]])
                        k.act(Ls[:], p2[0:64, :], AF.Sigmoid, [bB[b2]], [Bn["Ls"]])
                        b2 = nbA(); p2 = bank(b2)
                        for cp in range(4):
                            k.mm(p2[:, cp * 128:(cp + 1) * 128], Ls[:, cp * 128:(cp + 1) * 128],
                                 tri64[:, d, :, :].rearrange("p a t -> p (a t)"), [Bn["Ls"], bC], [bB[b2]])
                        p4 = p2.rearrange("p (c a t) -> p c a t", c=4, a=2)
                        k.act(gamp[:], p4[:, :, 1, :], AF.Exp, [bB[b2]], [Bn["gamp"]])
                        k.act(ginv[:], p4[:, :, 0, :], AF.Exp, [bB[b2]], [Bn["ginv"]], scale=-1.0)
                        k.act(gam[:], p4[:, :, 0, :], AF.Exp, [bB[b2]], [Bn["gam"]])
                        k.cp(glP[p][:], (gam[:, :, C - 1:C] if d == 0 else gam[:, :, 0:1]), [Bn["gam"]], [Bn["gl%d" % p]])
                        yield
                        bav = Bn["av%d" % d]; bkd = Bn["kd%d_%d" % (d, p)]
                        k.stt(ARP[p][:, :, 0, :], kk[:], -1.0, gamp[:], ALU.mult, ALU.mult, [Bn["kk"], Bn["gamp"]], [Bn["AR%d" % p]])
                        k.tt(KBtP[p][:, :, 0, :], kdP[p][d][:], ginv[:], ALU.mult, [bkd, Bn["ginv"]], [Bn["KBt%d" % p]])
                        k.tt(tmp5[:], kk[:], av[d][:], ALU.mult, [Bn["kk"], bav], [Bn["tmp5"]])
                        k.tt(KBtP[p][:, :, 1, :], tmp5[:], ginv[:], ALU.mult, [Bn["tmp5"], Bn["ginv"]], [Bn["KBt%d" % p]])
                        k.tt(ARP[p][:, :, 1, :], r_, gam[:], ALU.mult, [Bn["zs%d" % p], Bn["gam"]], [Bn["AR%d" % p]])
                        yield
                        b2 = nbA(); p2 = bank(b2)
                        for cp in range(4):
                            k.tr(p2[0:64, cp * 128:(cp + 1) * 128], zsP[p][:, 8 + cp, :], ident[:], [Bn["zs%d" % p], bC], [bB[b2]])
                        k.act(VtP[p][:], p2[0:64, :], AF.Copy, [bB[b2]], [Bn["Vt%d" % p]])
                        yield
                        g1, g2, gn = 0, 2, 4
                        for h in HO:
                            cp = h // 2; hb = (h % 2) * 64
                            arh = ARP[p][hb:hb + 64, cp, :, :].rearrange("p a t -> p (a t)")
                            k.mm(bank(g1 + h // 4)[0:64, (h % 4) * 128:(h % 4 + 1) * 128], KBtP[p][hb:hb + 64, cp, 0, :], arh,
                                 [Bn["KBt%d" % p], Bn["AR%d" % p]], [bB[g1 + h // 4]])
                            k.mm(bank(g2 + h // 4)[0:64, (h % 4) * 128:(h % 4 + 1) * 128], KBtP[p][hb:hb + 64, cp, 1, :], arh,
                                 [Bn["KBt%d" % p], Bn["AR%d" % p]], [bB[g2 + h // 4]])
                            k.mm(bank(gn)[0:64, h * 64:(h + 1) * 64], ARP[p][hb:hb + 64, cp, 0, :], KBtP[p][hb:hb + 64, cp, 1, :],
                                 [Bn["KBt%d" % p], Bn["AR%d" % p]], [bB[gn]])
                        yield
                        m64b = mask64[:, d, :].unsqueeze(1).to_broadcast([64, 4, 128])
                        for hf in range(2):
                            k.tt(Gm1P[p][:, hf * 4:hf * 4 + 4, :], bank(g1 + hf)[0:64, :].rearrange("p (h t) -> p h t", h=4), m64b,
                                 ALU.mult, [bB[g1 + hf], bC], [Bn["Gm1%d" % p]])
                        for hf in range(2):
                            k.tt(Gm2P[p][:, hf * 4:hf * 4 + 4, :], bank(g2 + hf)[0:64, :].rearrange("p (h t) -> p h t", h=4), m64b,
                                 ALU.mult, [bB[g2 + hf], bC], [Bn["Gm2%d" % p]])
                        k.tt(NmiP[p][:], bank(gn)[0:64, :].rearrange("p (h t) -> p h t", h=8),
                             maskN[:, d, :].unsqueeze(1).to_broadcast([64, 8, 64]), ALU.mult, [bB[gn], bC], [Bn["Nmi%d" % p]])

                        yield

                    def stageB(u, p, pull):
                        s, d, c, first, last = u
                        tcol = s * TP + c * C
                        gt0 = s * T + c * C
                        r_ = zsP[p][:, 0:4, :]; vv = zsP[p][:, 8:12, :]
                        if first:
                            if g == 0:
                                k.ms(ST[d][:], 0.0, [bST[d]])
                            else:
                                k.dma("sp", sld[:], srs_d[d][j].rearrange("h v k -> v h k"), [], [bSLD])
                                b2 = nbB(); p2 = bank(b2)
                                for cp in range(4):
                                    k.tr(p2[:, cp * 64:(cp + 1) * 64], sld[:, 2 * cp:2 * cp + 2, :].rearrange("v h k -> v (h k)"),
                                         ident[0:64, 0:64], [bSLD, bC], [bB[b2]])
                                k.cp(ST[d][:], p2[:, 0:256].rearrange("p (c v) -> p c v", c=4), [bB[b2]], [bST[d]])

                        b2 = nbB(); p2 = bank(b2)
                        for h in HOr:
                            cp = h // 2; hb = (h % 2) * 64
                            k.mm(p2[0:64, h * 64:(h + 1) * 64], ARP[p][hb:hb + 64, cp, 0, :], ST[d][hb:hb + 64, cp, :],
                                 [Bn["AR%d" % p], bST[d]], [bB[b2]])
                        b2b = nbB(); p2b = bank(b2b)
                        for h in range(8):
                            k.mm(p2b[0:64, h * 64:(h + 1) * 64], Gm1P[p][:, h, 0:64], VtP[p][:, h * 64:(h + 1) * 64],
                                 [Bn["Gm1%d" % p], Bn["Vt%d" % p]], [bB[b2b]])
                        k.act(Us[0][:], p2[0:64, :], AF.Copy, [bB[b2]], [Bn["Us0"]])
                        k.tt(Us[0][:], Us[0][:], p2b[0:64, :], ALU.add, [Bn["Us0"], bB[b2b]], [Bn["Us0"]])
                        ui = 0; ai = 0
                        for jn in range(6):
                            bA = Bn["Am%d" % ai] if jn else Bn["Gm2%d" % p]; bN = Bn["Nm%d" % ai] if jn else Bn["Nmi%d" % p]
                            bA2 = Bn["Am%d" % (1 - ai)]; bN2 = Bn["Nm%d" % (1 - ai)]
                            Acur = (lambda h, ai=ai: Am[ai][:, h, :]) if jn else (lambda h: Gm2P[p][:, h, 0:64])
                            Ncur = (lambda h, ai=ai: Nm[ai][:, h, :]) if jn else (lambda h: NmiP[p][:, h, :])
                            pull()
                            bU = Bn["Us%d" % ui]; bU2 = Bn["Us%d" % (1 - ui)]
                            b2 = nbB(); p2 = bank(b2)
                            for h in range(8):
                                k.mm(p2[0:64, h * 64:(h + 1) * 64], Acur(h), Us[ui][:, h * 64:(h + 1) * 64], [bA, bU], [bB[b2]])
                            if jn < 5:
                                b3 = nbB(); p3 = bank(b3)
                                for h in range(8):
                                    k.mm(p3[0:64, h * 64:(h + 1) * 64], Ncur(h), Acur(h), [bN, bA], [bB[b3]])
                                if jn < 4:
                                    b4 = nbB(); p4_ = bank(b4)
                                    for h in range(8):
                                        k.mm(p4_[0:64, h * 64:(h + 1) * 64], Acur(h), Ncur(h), [bA, bN], [bB[b4]])
                            k.tt(Us[1 - ui][:], Us[ui][:], p2[0:64, :], ALU.add, [bU, bB[b2]], [bU2])
                            ui = 1 - ui
                            pull()
                            if jn < 5:
                                k.act(Am[1 - ai][:].rearrange("p h t -> p (h t)"), p3[0:64, :], AF.Copy, [bB[b3]], [bA2])
                                if jn < 4:
                                    k.act(Nm[1 - ai][:].rearrange("p h t -> p (h t)"), p4_[0:64, :], AF.Copy, [bB[b4]], [bN2])
                                ai = 1 - ai
                        U = Us[ui]; bU = Bn["Us%d" % ui]
                        by = nbB(); py = bank(by)
                        by0 = nbB(); py0 = bank(by0)
                        for h in range(8):
                            o_ = py[0:64, h * 64:(h + 1) * 64]
                            k.mm(o_, Gm2P[p][:, h, 64:128], U[:, h * 64:(h + 1) * 64], [Bn["Gm2%d" % p], bU], [bB[by]], start=True, stop=False)
                            k.mm(o_, Gm1P[p][:, h, 64:128], VtP[p][:, h * 64:(h + 1) * 64], [Bn["Gm1%d" % p], Bn["Vt%d" % p]], [bB[by]], start=False, stop=True)
                        for h in HO:
                            cp = h // 2; hb = (h % 2) * 64
                            k.mm(py0[0:64, h * 64:(h + 1) * 64], ARP[p][hb:hb + 64, cp, 1, :], ST[d][hb:hb + 64, cp, :], [Bn["AR%d" % p], bST[d]], [bB[by0]])
                        ci = s * nchunk + c
                        ys2 = ysum[:].rearrange("p h v -> p (h v)")
                        yfs = yf[(ci % 2) * 64:(ci % 2) * 64 + 64, ci // 2, :]
                        bys = Bn["ysum"]
                        if d == 0:
                            k.act(ys2, py0[0:64, :], AF.Copy, [bB[by0]], [bys])
                            k.tt(ys2, ys2, py[0:64, :], ALU.add, [bys, bB[by]], [bys])
                            k.cp(yfs, ys2, [bys], [bYF])
                        else:
                            k.tt(ys2, py[0:64, :], yfs, ALU.add, [bB[by], bYF], [bys])
                            k.tt(ys2, ys2, py0[0:64, :], ALU.add, [bys, bB[by0]], [bys])
                        pull()
                        bt1 = 6
                        for cp in range(4):
                            for a_ in range(2):
                                idx = cp * 2 + a_
                                k.tr(bank(bt1 + idx // 4)[0:64, (idx % 4) * 128:(idx % 4 + 1) * 128], KBtP[p][:, cp, a_, :], ident[:],
                                     [Bn["KBt%d" % p], bC], [bB[bt1 + idx // 4]])
                        k.act(KBT[:, 0:2, :, :].rearrange("p c a k -> p (c a k)"), bank(bt1)[0:64, :], AF.Copy, [bB[bt1]], [Bn["KBT"]])
                        k.cp(KBT[:, 2:4, :, :].rearrange("p c a k -> p (c a k)"), bank(bt1 + 1)[0:64, :], [bB[bt1 + 1]], [Bn["KBT"]])
                        bs = 5; psu = bank(bs)
                        for cp in range(4):
                            k.mm(psu[:, cp * 128:(cp + 1) * 128], KBT[:, cp, 0, :], VtP[p][:, cp * 128:(cp + 1) * 128], [Bn["KBT"], Bn["Vt%d" % p]], [bB[bs]],
                                 start=True, stop=False)
                            k.mm(psu[:, cp * 128:(cp + 1) * 128], KBT[:, cp, 1, :], U[:, cp * 128:(cp + 1) * 128], [Bn["KBT"], bU], [bB[bs]],
                                 start=False, stop=True)
                        ps4 = psu.rearrange("p (c x) -> p c x", c=4)
                        for hh in range(2):
                            hb = hh * 64
                            k.tt(ST[d][hb:hb + 64, :, :], ST[d][hb:hb + 64, :, :], ps4[hb:hb + 64, :, hb:hb + 64], ALU.add,
                                 [bST[d], bB[bs]], [bST[d]])
                            k.tt(ST[d][hb:hb + 64, :, :], ST[d][hb:hb + 64, :, :],
                                 glP[p][hb:hb + 64, :, :].to_broadcast([64, 4, 64]), ALU.mult, [bST[d], Bn["gl%d" % p]], [bST[d]])
                        pull()
                        if d == 1:
                            k.red(yst[:, 0:8], ysum[:], [bys], [Bn["yst"]])
                            k.ts(yst[:, 0:8], yst[:, 0:8], 1.0 / 64, ALU.mult, [Bn["yst"]], [Bn["yst"]])
                            k.tt(ysum[:], ysum[:], yst[:, 0:8].unsqueeze(2).to_broadcast([64, 8, 64]), ALU.subtract, [bys, Bn["yst"]], [bys])
                            k.tt(ysq[:], ysum[:], ysum[:], ALU.mult, [bys], [Bn["ysq"]])
                            k.red(yst[:, 8:16], ysq[:], [Bn["ysq"]], [Bn["yst"]])
                            k.ts(yst[:, 8:16], yst[:, 8:16], 1.0 / 64, ALU.mult, [Bn["yst"]], [Bn["yst"]], s2=GN_EPS, op1=ALU.add)
                            k.act(yst[:, 8:16], yst[:, 8:16], AF.Sqrt, [Bn["yst"]], [Bn["yst"]])
                            k.recip(yst[:, 8:16], yst[:, 8:16], [Bn["yst"]], [Bn["yst"]])
                            k.tt(ysum[:], ysum[:], yst[:, 8:16].unsqueeze(2).to_broadcast([64, 8, 64]), ALU.mult, [bys, Bn["yst"]], [bys])
                            k.tt(tmp6[:], kdP[p][0][:], kdP[p][1][:], ALU.add, [Bn["kd0_%d" % p], Bn["kd1_%d" % p]], [Bn["tmp6"]], e="pool")
                            k.tt(tmp6[:], tmp6[:], r_, ALU.mult, [Bn["tmp6"], Bn["zs%d" % p]], [Bn["tmp6"]], e="pool")
                            k.tt(tmp6[:], tmp6[:], evs[:, 22:26].unsqueeze(2).to_broadcast([128, 4, C]), ALU.mult, [Bn["tmp6"], bP], [Bn["tmp6"]], e="pool")
                            b2 = nbB(); p2 = bank(b2)
                            k.mm(p2[:, 0:4 * C], bdones[:, :], tmp6[:].rearrange("p c t -> p (c t)"), [Bn["tmp6"], bC], [bB[b2]])
                            k.tt(bon[:], p2[:, 0:4 * C].rearrange("p (c t) -> p c t", c=4), vv, ALU.mult, [bB[b2], Bn["zs%d" % p]], [Bn["bon"]])
                            b3 = nbB(); p3 = bank(b3)
                            for cp in range(4):
                                k.tr(p3[:, cp * C:(cp + 1) * C], ysum[:, 2 * cp:2 * cp + 2, :].rearrange("p h v -> p (h v)"),
                                     ident[0:64, 0:64], [bys, bC], [bB[b3]])
                            p3v = p3[:, 0:4 * C].rearrange("p (c t) -> p c t", c=4)
                            k.tt(obt[:], p3v, evs[:, 26:30].unsqueeze(2).to_broadcast([128, 4, C]), ALU.mult, [bB[b3], bP], [Bn["obt"]])
                            k.tt(obt[:], obt[:], evs[:, 30:34].unsqueeze(2).to_broadcast([128, 4, C]), ALU.add, [Bn["obt"], bP], [Bn["obt"]])
                            k.tt(obt[:], obt[:], bon[:], ALU.add, [Bn["obt"], Bn["bon"]], [Bn["obt"]])
                            k.tt(oT[:, 4:8, gt0:gt0 + C], obt[:], gbT[:, :, gt0:gt0 + C], ALU.mult, [Bn["obt"], bGB], [bO])

                        if last:
                            if g == 0:
                                b2 = nbB(); p2 = bank(b2)
                                for cp in range(4):
                                    k.tr(p2[0:64, cp * 128:(cp + 1) * 128], ST[d][:, cp, :], ident[:], [bST[d], bC], [bB[b2]])
                                k.cp(sto[:].rearrange("p c x -> p (c x)"), p2[0:64, :], [bB[b2]], [bSTO])
                                k.dma("pool", nrs_o[d][s, j].rearrange("h v k -> v h k"),
                                      sto[:].rearrange("p c (h k) -> p (c h) k", h=2), [bSTO], [])

                    gens = {}
                    if units:
                        for _ in stageA(units[0], 0):
                            pass
                    for ui_, u in enumerate(units):
                        nxt = stageA(units[ui_ + 1], (ui_ + 1) % 2) if ui_ + 1 < len(units) else None

                        def pull(nxt=nxt, n=2):
                            if nxt is None:
                                return
                            for _ in range(n):
                                try:
                                    next(nxt)
                                except StopIteration:
                                    return
                        stageB(u, ui_ % 2, pull)
                        if nxt is not None:
                            for _ in nxt:
                                pass
            if STOP >= 4:
                out_proj(evo[j])

        def odd_layer(g, j, L):
            nseq, T = (4, 256) if g == 0 else (1, 1024)
            C = 128
            nchunk = T // C
            k.dma("sp", gw2aug[:], gw2aug_d[j], [], [bP])
            k.dma("sp", ln256[:], ln256_d[j], [], [bP])
            with k.scope() as scL:
                qT = scL.sb(un("gqT"), [128, 4, GT], BF16); bQ = Buf("gqT")
                kT = scL.sb(un("gkT"), [128, 4, GT], BF16); bK = Buf("gkT")
                vt = scL.sb(un("gvt"), [128, 8, 1024], BF16); bV = Buf("gvt")
                gs = scL.sb(un("ggs"), [128, 8, GT], BF16); bG = Buf("ggs")
                glT = scL.sb(un("glT"), [17, 2, GT], F32); bGL = Buf("glT")
                of = scL.sb(un("gof"), [128, 8, 1024], BF16); bOF = Buf("gof")
                k.ms(glT[:], 1.0, [bGL])
                with k.scope() as sc:
                    hT, bHT = make_h(sc, L, g)
                    wblock = wblock_factory(sc)

                    def dest_q(cc, tt, pb, bp):
                        k.ts(qT[:, cc, tt * 512:(tt + 1) * 512], pb, float(128 ** -0.5), ALU.mult, [bp], [bQ])
                    proj_F(wblock, odw[j], [0, 1], hT, bHT, dest_q)

                    def dest_k(cc, tt, pb, bp):
                        k.cp(kT[:, cc, tt * 512:(tt + 1) * 512], pb, [bp], [bK])
                    proj_F(wblock, odw[j], [2, 3], hT, bHT, dest_k)

                    def dest_v(bi_, t8, pb, bp):
                        k.cp(vt[:, t8, bi_ * 256:(bi_ + 1) * 256], pb[:, 0:256], [bp], [bV])
                    proj_T(wblock, odw[j], [4, 5, 6, 7], hT, bHT, dest_v)

                    def dest_g(cc, tt, pb, bp):
                        k.act(gs[:, cc, tt * 512:(tt + 1) * 512], pb, AF.Silu, [bp], [bG])
                    proj_F(wblock, odw[j], [8, 9, 10, 11], hT, bHT, dest_g)

                    def dest_gl(cc, tt, pb, bp):
                        if cc < 2:
                            k.cp(glT[0:16, cc, tt * 512:(tt + 1) * 512], pb[0:16, :], [bp], [bGL])
                    proj_F(wblock, odw[j], [12], hT, bHT, dest_gl, msub=16, nsub=2)

                with k.scope() as sc:
                    f = lambda name, shape, dt=F32: sc.sb(un(name), shape, dt)
                    S = [f("gS", [128, 4, 256]) for d in range(2)]; bS = [Buf("gS0"), Buf("gS1")]
                    bW = Buf("glatmp")
                    G = {n_: Buf("g_" + n_) for n_ in ["Lg", "gam", "ginv", "qs", "ks", "Am", "kTt", "osum", "osq", "ost"]}
                    Lg = f("Lg", [128, 512])
                    gam = f("ggam", [128, 4, C]); ginv = f("gginv", [128, 4, C])
                    qs = f("gqs", [128, 4, C]); ks = f("gks", [128, 4, C])
                    Am = f("gAm", [128, 4, C], BF16); kTt = f("gkTt", [128, 4, C], BF16)
                    osum = f("gosum", [128, 4, 256]); osq = f("gosq", [128, 4, 256]); ost = f("gost", [128, 4])
                    print("GLA sbuf remaining", nc.sbuf_bytes_remaining)
                    for s in range(nseq):
                        for d in range(2):
                            if g == 0:
                                k.ms(S[d][:], 0.0, [bS[d]])
                            else:
                                k.dma("sp", S[d][:], sgs_d[d][j].rearrange("h k v -> k h v"), [], [bS[d]])
                            order = range(nchunk) if d == 0 else range(nchunk - 1, -1, -1)
                            for c in order:
                                gt0 = s * T + c * C
                                t8 = gt0 // 128
                                b2 = nb(); p2 = bank(b2)
                                k.mm(p2, glT[:, d, gt0:gt0 + C], gw2aug[:, d, :], [bGL, bP], [bB[b2]])
                                k.act(Lg[:], p2, AF.Sigmoid, [bB[b2]], [G["Lg"]])
                                k.act(Lg[:], Lg[:], AF.Ln, [G["Lg"]], [G["Lg"]])
                                b2 = nb(); p2 = bank(b2)
                                for h in range(4):
                                    k.mm(p2[:, h * C:(h + 1) * C], Lg[:, h * 128:(h + 1) * 128], tri128[:, d, :], [G["Lg"], bC], [bB[b2]])
                                k.act(gam[:].rearrange("p h t -> p (h t)"), p2, AF.Exp, [bB[b2]], [G["gam"]])
                                k.act(ginv[:].rearrange("p h t -> p (h t)"), p2, AF.Exp, [bB[b2]], [G["ginv"]], scale=-1.0)
                                glast = gam[:, :, C - 1:C] if d == 0 else gam[:, :, 0:1]
                                k.tt(qs[:], qT[:, :, gt0:gt0 + C], gam[:], ALU.mult, [bQ, G["gam"]], [G["qs"]])
                                k.tt(ks[:], kT[:, :, gt0:gt0 + C], ginv[:], ALU.mult, [bK, G["ginv"]], [G["ks"]])
                                b2 = nb(); p2 = bank(b2)
                                for h in range(4):
                                    k.mm(p2[:, h * C:(h + 1) * C], ks[:, h, :], qs[:, h, :], [G["ks"], G["qs"]], [bB[b2]])
                                k.tt(Am[:], p2.rearrange("p (h t) -> p h t", h=4), mask128[:, d, :].unsqueeze(1).to_broadcast([128, 4, C]),
                                     ALU.mult, [bB[b2], bC], [G["Am"]])
                                b2 = nb(); p2 = bank(b2)
                                for h in range(4):
                                    k.tr(p2[:, h * C:(h + 1) * C], ks[:, h, :], ident[:], [G["ks"], bC], [bB[b2]])
                                k.cp(kTt[:].rearrange("p h t -> p (h t)"), p2, [bB[b2]], [G["kTt"]])
                                by = nb2()
                                for h in range(4):
                                    o_ = bank(by + h // 2)[:, (h % 2) * 256:(h % 2 + 1) * 256]
                                    k.mm(o_, qs[:, h, :], S[d][:, h, :], [G["qs"], bS[d]], [bB[by + h // 2]], start=True, stop=False)
                                    k.mm(o_, Am[:, h, :], vt[:, t8, h * 256:(h + 1) * 256], [G["Am"], bV], [bB[by + h // 2]], start=False, stop=True)
                                bs = nb2()
                                for h in range(4):
                                    k.mm(bank(bs + h // 2)[:, (h % 2) * 256:(h % 2 + 1) * 256], kTt[:, h, :], vt[:, t8, h * 256:(h + 1) * 256],
                                         [G["kTt"], bV], [bB[bs + h // 2]])
                                for hf in range(2):
                                    sl = S[d][:, 2 * hf:2 * hf + 2, :]
                                    k.tt(sl, sl, bank(bs + hf).rearrange("p (h v) -> p h v", h=2), ALU.add, [bS[d], bB[bs + hf]], [bS[d]])
                                k.tt(S[d][:], S[d][:], glast.to_broadcast([128, 4, 256]), ALU.mult, [bS[d], G["gam"]], [bS[d]])
                                if d == 0:
                                    for hf in range(2):
                                        k.cp(of[:, t8, hf * 512:(hf + 1) * 512], bank(by + hf), [bB[by + hf]], [bOF])
                                else:
                                    for hf in range(2):
                                        k.tt(osum[:, 2 * hf:2 * hf + 2, :].rearrange("p h v -> p (h v)"), bank(by + hf),
                                             of[:, t8, hf * 512:(hf + 1) * 512], ALU.add, [bB[by + hf], bOF], [G["osum"]])
                                    k.act(osq[:], osum[:], AF.Square, [G["osum"]], [G["osq"]])
                                    k.red(ost[:], osq[:], [G["osq"]], [G["ost"]])
                                    k.ts(ost[:], ost[:], 1.0 / 256, ALU.mult, [G["ost"]], [G["ost"]], s2=EPS, op1=ALU.add)
                                    k.act(ost[:], ost[:], AF.Sqrt, [G["ost"]], [G["ost"]])
                                    k.recip(ost[:], ost[:], [G["ost"]], [G["ost"]])
                                    k.tt(osum[:], osum[:], ost[:].unsqueeze(2).to_broadcast([128, 4, 256]), ALU.mult, [G["osum"], G["ost"]], [G["osum"]])
                                    k.tt(osum[:], osum[:], ln256[:, :].unsqueeze(1).to_broadcast([128, 4, 256]), ALU.mult, [G["osum"], bP], [G["osum"]])
                                    bt = nb2()
                                    o2 = osum[:].rearrange("p h v -> p (h v)")
                                    for cc in range(8):
                                        k.tr(bank(bt + cc // 4)[:, (cc % 4) * 128:(cc % 4 + 1) * 128], o2[:, cc * 128:(cc + 1) * 128], ident[:],
                                             [G["osum"], bC], [bB[bt + cc // 4]])
                                    for hf in range(2):
                                        k.tt(oT[:, 4 * hf:4 * hf + 4, gt0:gt0 + C], bank(bt + hf).rearrange("p (c t) -> p c t", c=4),
                                             gs[:, 4 * hf:4 * hf + 4, gt0:gt0 + C], ALU.mult, [bB[bt + hf], bG], [bO])
                            if g == 0:
                                k.dma("pool", ngs_o[d][s, j].rearrange("h k v -> k h v"), S[d][:], [bS[d]], [])
            out_proj(odo[j])

        for g in groups:
            with k.scope() as sc:
                xin = [sc.sb(un("xin"), [128, D], F32) for i in range(2)]
                bxin = [Buf("xin0"), Buf("xin1")]
                for t8 in range(GT // 128):
                    xi = xin[t8 % 2]; bxi = bxin[t8 % 2]
                    k.dma("sp", xi[:], xg[g, t8 * 128:(t8 + 1) * 128, :], [], [bxi])
                    for half in range(2):
                        b2 = nb(); p2 = bank(b2)
                        for jj in range(4):
                            cch = half * 4 + jj
                            k.tr(p2[:, jj * 128:(jj + 1) * 128], xi[:, cch * 128:(cch + 1) * 128], ident[:], [bxi, bC], [bB[b2]])
                        k.cp(xT[:, half * 4:(half + 1) * 4, t8 * 128:(t8 + 1) * 128], p2.rearrange("p (j t) -> p j t", j=4), [bB[b2]], [bX])
            for L in range(nlayers):
                if L % 2 == 0:
                    even_layer(g, L // 2, L)
                else:
                    odd_layer(g, L // 2, L)
            with k.scope() as sc:
                rstd = compute_rstd(sc)
                yout = [sc.sb(un("yout"), [128, D], F32) for i in range(2)]
                byo = [Buf("yout0"), Buf("yout1")]
                tmpn = [sc.sb(un("tmpn"), [128, 512], F32) for i in range(2)]
                btn = [Buf("tmpn0"), Buf("tmpn1")]
                it = 0
                for t8 in range(GT // 128):
                    yo = yout[t8 % 2]; by_ = byo[t8 % 2]
                    for half in range(2):
                        b2 = nb(); p2 = bank(b2)
                        tn = tmpn[it % 2]; bt_ = btn[it % 2]; it += 1
                        for jj in range(4):
                            cch = half * 4 + jj
                            k.stt(tn[:, jj * 128:(jj + 1) * 128], xT[:, cch, t8 * 128:(t8 + 1) * 128], finalg_sb[:, cch:cch + 1],
                                  rstd[:, t8 * 128:(t8 + 1) * 128], ALU.mult, ALU.mult, [bX, bR, bC], [bt_])
                            k.tr(p2[:, jj * 128:(jj + 1) * 128], tn[:, jj * 128:(jj + 1) * 128], ident[:], [bt_, bC], [bB[b2]])
                        k.act(yo[:, half * 512:(half + 1) * 512], p2, AF.Copy, [bB[b2]], [by_])
                    k.dma("pool", yg[g, t8 * 128:(t8 + 1) * 128, :], yo[:], [by_], [])
        k.barrier()
    print("instructions:", k.ninstr, {e: c for e, c in k.cnt.items()})
    return nc


_CACHE = {}


def _consts():
    f32 = np.float32
    c = {}
    c["ident"] = np.eye(128, dtype=f32)
    s = np.arange(64)[:, None]; t = np.arange(64)[None, :]
    tri = np.zeros((64, 2, 2, 64), f32)
    tri[:, 0, 0] = (s <= t); tri[:, 0, 1] = (s < t); tri[:, 1, 0] = (s >= t); tri[:, 1, 1] = (s > t)
    c["tri64"] = tri * f32(-RWKV_DECAY_SCALE)
    s1 = np.arange(128)[:, None]; t1 = np.arange(128)[None, :]
    tri128 = np.zeros((128, 2, 128), f32)
    tri128[:, 0] = (s1 <= t1); tri128[:, 1] = (s1 >= t1)
    c["tri128"] = tri128 / f32(16.0)
    c["mask128"] = tri128.copy()
    m64 = np.zeros((64, 2, 128), f32)
    m64[:, 0, 0:64] = (s < t); m64[:, 0, 64:128] = (s <= t); m64[:, 1, 0:64] = (s > t); m64[:, 1, 64:128] = (s >= t)
    c["mask64"] = m64
    mN = np.zeros((64, 2, 64), f32)
    mN[:, 0] = (t < s); mN[:, 1] = (t > s)
    c["maskN"] = mN
    bd = np.zeros((128, 128), f32); bd[0:64, 0:64] = 1; bd[64:128, 64:128] = 1
    c["bdones"] = bd
    T = 1024
    row = np.repeat(np.arange(T // 64), 64).astype(f32); col = np.tile(np.arange(64), T // 64).astype(f32)
    inv = (f32(10000.0) ** (-np.arange(16, dtype=f32) / f32(16))).astype(f32)
    ang = np.stack([row[:, None] * inv, col[:, None] * inv], axis=1).astype(f32)
    cos = np.cos(ang).astype(f32); sin = np.sin(ang).astype(f32)
    cos64 = np.stack([cos, cos], axis=2).reshape(T, 64)
    sin64 = np.stack([-sin, sin], axis=2).reshape(T, 64)
    c["ropecos"] = np.ascontiguousarray(cos64.reshape(8, 128, 64).transpose(1, 0, 2))
    c["ropesin"] = np.ascontiguousarray(sin64.reshape(8, 128, 64).transpose(1, 0, 2))
    return c


def kernel(**inp):
    f32 = np.float32
    n = 8
    A = lambda name: np.asarray(inp[name], f32)
    x_prompt = A("x_prompt"); x_sample = A("x_sample"); c = A("c"); c_ctx = A("c_ctx")

    def pc(v, nchunk):
        v = np.asarray(v, f32)
        lead = v.shape[:-1]
        v = v.reshape(lead + (nchunk, 128))
        return np.ascontiguousarray(np.moveaxis(v, -1, 0))

    def wblk(w, nblk):
        Ln, rows, cols = w.shape
        if cols < nblk * WB:
            w = np.concatenate([w, np.zeros((Ln, rows, nblk * WB - cols), f32)], axis=2)
        return np.ascontiguousarray(w.reshape(Ln, NCH, 128, nblk, WB).transpose(0, 3, 2, 1, 4))

    shared = dict(_consts())
    shared["modw"] = wblk(A("mod_w"), 12).reshape(DEPTH * 12, 128, NCH, WB)
    shared["modbT"] = pc(A("mod_b"), 24)
    shared["normgT"] = pc(A("norm_g"), NCH)
    shared["finalgT"] = pc(A("final_g"), NCH)
    shared["evw"] = wblk(A("ev_w_in"), 14)
    shared["evo"] = wblk(A("ev_w_out"), 4)
    shared["odw"] = wblk(A("od_w_in"), 13)
    shared["odo"] = wblk(A("od_w_out"), 4)
    shared["w2aug"] = np.ascontiguousarray(np.concatenate([A("rw_w2"), A("rw_w0")[:, :, None, :]], axis=2).transpose(0, 2, 1, 3))
    shared["a2aug"] = np.ascontiguousarray(np.concatenate([A("rw_a2"), A("rw_a0")[:, :, None, :]], axis=2).transpose(0, 2, 1, 3))
    shared["gw2aug"] = np.ascontiguousarray(np.concatenate([A("gla_w2"), A("gla_b")[:, :, None, :]], axis=2).transpose(0, 2, 1, 3))
    shared["ln256"] = np.ascontiguousarray(np.broadcast_to(A("gla_ln_g")[:, None, :], (2, 128, 256)))
    qk = np.stack([A("ev_qn_g"), A("ev_kn_g")], axis=1)
    shared["qkg"] = np.ascontiguousarray(np.broadcast_to(qk[:, None], (2, 128, 2, 64)))
    evs = np.concatenate([pc(A("ev_shift_mu"), 14), pc(A("rw_kk"), 4), pc(A("rw_ka"), 4), pc(A("rw_rk").reshape(2, 512), 4),
                          pc(A("rw_ln_g"), 4), pc(A("rw_ln_b"), 4)], axis=2)
    shared["evs"] = np.ascontiguousarray(evs.transpose(1, 0, 2))
    in_maps = []
    for core in range(n):
        sb = core % 4
        m = dict(shared)
        m["xg"] = np.ascontiguousarray(np.stack([x_prompt[4 * core:4 * core + 4].reshape(GT, D), x_sample[sb]], axis=0))
        m["condT"] = np.ascontiguousarray(np.stack([pc(c_ctx, NCH), pc(c[sb], NCH)], axis=-1))
        m["cak"] = np.ascontiguousarray(A("cache_attn_k")[sb].reshape(2, 512, 128))
        m["cav"] = np.ascontiguousarray(A("cache_attn_v")[sb].reshape(2, 512, 128))
        m["srf"] = np.ascontiguousarray(A("state_rwkv_fwd")[sb]); m["srb"] = np.ascontiguousarray(A("state_rwkv_bwd")[sb])
        m["sgf"] = np.ascontiguousarray(A("state_gla_fwd")[sb]); m["sgb"] = np.ascontiguousarray(A("state_gla_bwd")[sb])
        in_maps.append(m)
    if "nc" not in _CACHE:
        _CACHE["nc"] = build_program()
    res = run_bass_kernel_spmd(_CACHE["nc"], in_maps, core_ids=list(range(n)))
    R = res.results
    cat = lambda name: np.concatenate([R[i][name] for i in range(n)], axis=0)
    y_prompt = np.concatenate([R[i]["yg"][0].reshape(4, 256, D) for i in range(n)], axis=0)
    y_sample = np.stack([R[i]["yg"][1] for i in range(4)], axis=0)
    new_k = cat("nk").reshape(32, 2, 256, 2, 64)
    new_v = cat("nv").reshape(32, 2, 256, 2, 64)
    return (y_prompt, y_sample, new_k, new_v, cat("nrf"), cat("nrb"), cat("ngf"), cat("ngb"))
```

```python
import contextlib
import os
STOP = int(os.environ.get('KSTOP', '9'))
SUB = float(os.environ.get('KSUB', '9'))
KNS = int(os.environ.get('KNS', '99'))
KNC = int(os.environ.get('KNC', '99'))
KND = int(os.environ.get('KND', '2'))
import numpy as np
import concourse.bass as bass
import concourse.mybir as mybir
from concourse.bass_utils import run_bass_kernel_spmd

F32 = mybir.dt.float32
BF16 = mybir.dt.bfloat16
AF = mybir.ActivationFunctionType
ALU = mybir.AluOpType
AX = mybir.AxisListType

D = 1024
NCH = 8
DEPTH = 4
GT = 1024
EV_COLS = 3584
OD_COLS = 3104
EPS = 1e-6
GN_EPS = 64e-5
RWKV_DECAY_SCALE = 0.606531
WB = 256


class Buf:
    __slots__ = ("name", "w", "r")

    def __init__(self, name):
        self.name = name
        self.w = None
        self.r = {}


class KB:
    SEM_WRAP = 20000

    def __init__(self, nc):
        self.nc = nc
        self.es = contextlib.ExitStack()
        self.engs = {"pe": nc.tensor, "act": nc.scalar, "dve": nc.vector, "pool": nc.gpsimd, "sp": nc.sync}
        self.cnt = {e: 0 for e in self.engs}
        self.esems = {e: [] for e in self.engs}
        self.seen = {e: {} for e in self.engs}
        self.semobj = {}
        self.ndma_sems = {"sp": 12, "pool": 6, "act": 4}
        self.dma_sems = {}
        self.dma_rr = {q: 0 for q in self.ndma_sems}
        self.dma_cnt = {}
        self.nsem = 0
        self.ninstr = 0

    def new_sem(self, name):
        s = self.es.enter_context(self.nc.semaphore(name))
        self.semobj[name] = s
        return name

    def sb(self, name, shape, dtype):
        return self.es.enter_context(self.nc.sbuf_tensor(name, list(shape), dtype))

    def ps(self, name, shape, dtype=F32):
        return self.es.enter_context(self.nc.psum_tensor(name, list(shape), dtype))

    def _cur_sem(self, e):
        idx = self.cnt[e] // self.SEM_WRAP
        while len(self.esems[e]) <= idx:
            self.esems[e].append(self.new_sem(f"s_{e}_{len(self.esems[e])}"))
        return self.esems[e][idx]

    def _wait(self, e, tok):
        if tok is None:
            return
        key, val = tok
        if self.seen[e].get(key, 0) >= val:
            return
        self.engs[e].wait_ge(self.semobj[key], val)
        self.seen[e][key] = val

    def _deps(self, e, reads, writes, pe_acc=False):
        for b in reads:
            if b.w is not None:
                self._wait(e, b.w)
        for b in writes:
            if b.w is not None and not (pe_acc and e == "pe"):
                self._wait(e, b.w)
            for e2, tok in b.r.items():
                if e2 == e and e == "pe":
                    continue
                self._wait(e, tok)

    def _mark(self, e, tok, reads, writes):
        for b in reads:
            b.r[e] = tok
        for b in writes:
            b.w = tok
            b.r = {}

    def op(self, e, reads, writes, fn, pe_acc=False):
        self._deps(e, reads, writes, pe_acc)
        sem = self._cur_sem(e)
        ins = fn(self.engs[e])
        ins.then_inc(self.semobj[sem], 1)
        self.cnt[e] += 1
        self.ninstr += 1
        val = self.cnt[e] - (self.cnt[e] - 1) // self.SEM_WRAP * self.SEM_WRAP
        tok = (sem, val)
        self._mark(e, tok, reads, writes)
        return tok

    def dma(self, q, out, in_, reads, writes):
        if q not in self.dma_sems:
            self.dma_sems[q] = [self.new_sem(f"d_{q}_{i}") for i in range(self.ndma_sems[q])]
            for s in self.dma_sems[q]:
                self.dma_cnt[s] = 0
        sems = self.dma_sems[q]
        s = sems[self.dma_rr[q] % len(sems)]
        self.dma_rr[q] += 1
        if self.dma_cnt[s] > 0:
            self._wait(q, (s, 16 * self.dma_cnt[s]))
        self._deps(q, reads, writes)
        self.engs[q].dma_start(out=out, in_=in_).then_inc(self.semobj[s], 16)
        self.dma_cnt[s] += 1
        self.ninstr += 1
        tok = (s, 16 * self.dma_cnt[s])
        self._mark(s, tok, reads, writes)
        return tok

    def finish(self, bufs):
        for b in bufs:
            if b.w is not None:
                self._wait("sp", b.w)
        for q, sems in self.dma_sems.items():
            for s in sems:
                if self.dma_cnt[s] > 0:
                    self._wait("sp", (s, 16 * self.dma_cnt[s]))


    def barrier(self):
        toks = []
        for e in self.engs:
            if self.cnt[e] > 0:
                sem = self.esems[e][(self.cnt[e] - 1) // self.SEM_WRAP]
                val = self.cnt[e] - (self.cnt[e] - 1) // self.SEM_WRAP * self.SEM_WRAP
                toks.append((sem, val))
        for q, sems in self.dma_sems.items():
            for s in sems:
                if self.dma_cnt[s] > 0:
                    toks.append((s, 16 * self.dma_cnt[s]))
        for e in self.engs:
            for t in toks:
                self._wait(e, t)

    @contextlib.contextmanager
    def scope(self):
        sc = _Scope(self)
        try:
            yield sc
        finally:
            self.barrier()
            sc.es.close()

    def _pe_rows(self, ap):
        base = ap.base_partition()
        n = ap.shape[0]
        grp = set(range(base // 32, (base + n - 1) // 32 + 1))
        last = getattr(self, "_pe_last_grp", None)
        if last is not None and not (grp & last) and self.cnt["pe"] > 0:
            for sem_i in range(max(0, (self.cnt["pe"] - 1) // self.SEM_WRAP - 1), (self.cnt["pe"] - 1) // self.SEM_WRAP + 1):
                sem = self.esems["pe"][sem_i]
                if sem_i == (self.cnt["pe"] - 1) // self.SEM_WRAP:
                    val = self.cnt["pe"] - sem_i * self.SEM_WRAP
                else:
                    val = self.SEM_WRAP
                self._wait("pe", (sem, val))
        self._pe_last_grp = grp

    def mm(self, out, lhsT, rhs, R, W, start=True, stop=True):
        self._pe_rows(lhsT)
        return self.op("pe", R, W, lambda e: e.matmul(out, lhsT, rhs, start=start, stop=stop), pe_acc=True)

    def tr(self, out, in_, ident, R, W):
        self._pe_rows(in_)
        return self.op("pe", R, W, lambda e: e.transpose(out, in_, ident), pe_acc=True)

    def tt(self, out, in0, in1, op, R, W, e="dve"):
        return self.op(e, R, W, lambda g: g.tensor_tensor(out=out, in0=in0, in1=in1, op=op))

    def ts(self, out, in0, s1, op0, R, W, s2=None, op1=None, e="dve"):
        if op1 is None:
            return self.op(e, R, W, lambda g: g.tensor_scalar(out=out, in0=in0, scalar1=s1, scalar2=None, op0=op0))
        return self.op(e, R, W, lambda g: g.tensor_scalar(out=out, in0=in0, scalar1=s1, scalar2=s2, op0=op0, op1=op1))

    def stt(self, out, in0, scalar, in1, op0, op1, R, W, e="dve"):
        return self.op(e, R, W, lambda g: g.scalar_tensor_tensor(out=out, in0=in0, scalar=scalar, in1=in1, op0=op0, op1=op1))

    def act(self, out, in_, func, R, W, **kw):
        return self.op("act", R, W, lambda g: g.activation(out=out, in_=in_, func=func, **kw))

    def cp(self, out, in_, R, W, e="dve"):
        return self.op(e, R, W, lambda g: g.tensor_copy(out=out, in_=in_))

    def red(self, out, in_, R, W, op=None):
        return self.op("dve", R, W, lambda g: g.tensor_reduce(out=out, in_=in_, axis=AX.X, op=(op or ALU.add)))

    def recip(self, out, in_, R, W):
        return self.op("dve", R, W, lambda g: g.reciprocal(out=out, in_=in_))

    def ms(self, ap, val, W, e="dve"):
        return self.op(e, [], W, lambda g: g.memset(ap, val))


class _Scope:
    def __init__(self, k):
        self.k = k
        self.es = contextlib.ExitStack()

    def sb(self, name, shape, dtype):
        return self.es.enter_context(self.k.nc.sbuf_tensor(name, list(shape), dtype))


def build_program(nlayers=DEPTH, groups=(0, 1)):
    nc = bass.Bass("TRN2", target_bir_lowering=False)
    k = KB(nc)
    uid = [0]

    def un(name):
        uid[0] += 1
        return f"{name}_{uid[0]}"

    def din(name, shape):
        return nc.dram_tensor(name, list(shape), F32, kind="ExternalInput").ap()

    def dout(name, shape):
        return nc.dram_tensor(name, list(shape), F32, kind="ExternalOutput").ap()

    xg = din("xg", [2, GT, D])
    condT = din("condT", [128, NCH, 2])
    modw = din("modw", [DEPTH * 12, 128, NCH, WB])
    modbT = din("modbT", [128, DEPTH, 24])
    normgT = din("normgT", [128, DEPTH, NCH])
    finalgT = din("finalgT", [128, NCH])
    ident_d = din("ident", [128, 128])
    evw = din("evw", [2, 14, 128, NCH, WB])
    evo = din("evo", [2, 4, 128, NCH, WB])
    odw = din("odw", [2, 13, 128, NCH, WB])
    odo = din("odo", [2, 4, 128, NCH, WB])
    tri64_d = din("tri64", [64, 2, 2, 64])
    tri128_d = din("tri128", [128, 2, 128])
    mask64_d = din("mask64", [64, 2, 128])
    maskN_d = din("maskN", [64, 2, 64])
    mask128_d = din("mask128", [128, 2, 128])
    bd_d = din("bdones", [128, 128])
    cos_d = din("ropecos", [128, 8, 64])
    sin_d = din("ropesin", [128, 8, 64])
    w2aug_d = din("w2aug", [2, 65, 2, 512])
    a2aug_d = din("a2aug", [2, 65, 2, 512])
    gw2aug_d = din("gw2aug", [2, 17, 2, 512])
    ln256_d = din("ln256", [2, 128, 256])
    qkg_d = din("qkg", [2, 128, 2, 64])
    evs_d = din("evs", [2, 128, 34])
    cak = din("cak", [2, 512, 128])
    cav = din("cav", [2, 512, 128])
    srs_d = [din("srf", [2, 8, 64, 64]), din("srb", [2, 8, 64, 64])]
    sgs_d = [din("sgf", [2, 4, 128, 256]), din("sgb", [2, 4, 128, 256])]
    yg = dout("yg", [2, GT, D])
    nk_o = dout("nk", [4, 2, 256, 128])
    nv_o = dout("nv", [4, 2, 256, 128])
    nrs_o = [dout("nrf", [4, 2, 8, 64, 64]), dout("nrb", [4, 2, 8, 64, 64])]
    ngs_o = [dout("ngf", [4, 2, 4, 128, 256]), dout("ngb", [4, 2, 4, 128, 256])]

    with k.es:
        bC = Buf("const")

        def cload(name, src, shape, dtype=F32):
            t = k.sb(name, shape, dtype)
            k.dma("sp", t[:], src, [], [bC])
            return t

        ident = cload("ident_sb", ident_d[:, :], [128, 128])
        tri64 = cload("tri64_sb", tri64_d[:, :, :, :], [64, 2, 2, 64])
        tri128 = cload("tri128_sb", tri128_d[:, :, :], [128, 2, 128])
        mask64 = cload("mask64_sb", mask64_d[:, :, :], [64, 2, 128])
        maskN = cload("maskN_sb", maskN_d[:, :, :], [64, 2, 64])
        mask128 = cload("mask128_sb", mask128_d[:, :, :], [128, 2, 128])
        bdones = cload("bd_sb", bd_d[:, :], [128, 128])
        rope_tabs = {}
        cond_sb = cload("cond_sb", condT[:, :, :], [128, NCH, 2])
        modb_sb = cload("modb_sb", modbT[:, :, :], [128, DEPTH, 24])
        normg_sb = cload("normg_sb", normgT[:, :, :], [128, DEPTH, NCH])
        finalg_sb = cload("finalg_sb", finalgT[:, :], [128, NCH])
        ones_f = k.sb("ones_f", [128, 128], F32)
        k.ms(ones_f[:], 1.0 / D, [bC])
        ones_bf = k.sb("ones_bf", [128, 64], BF16)
        k.ms(ones_bf[:], 1.0, [bC])
        bP = Buf("params")
        w2aug = k.sb("w2aug_sb", [65, 2, 512], F32)
        a2aug = k.sb("a2aug_sb", [65, 2, 512], F32)
        gw2aug = k.sb("gw2aug_sb", [17, 2, 512], F32)
        ln256 = k.sb("ln256_sb", [128, 256], F32)
        qkg = k.sb("qkg_sb", [128, 2, 64], F32)
        evs = k.sb("evs_sb", [128, 34], F32)
        evx = k.sb("evx_sb", [128, 32], F32)

        PT = [k.ps(f"P{i}", [128, 1024], F32) for i in range(4)]
        bB = [Buf(f"bank{i}") for i in range(8)]
        rr = [0]

        def bank(i):
            return PT[i // 2][:, (i % 2) * 512:(i % 2) * 512 + 512]

        def nb(lo=0, hi=8):
            i = lo + rr[0] % (hi - lo)
            rr[0] += 1
            return i

        def nb2():
            i = (rr[0] % 8 + 1) // 2 * 2 % 8
            rr[0] += (i - rr[0] % 8) % 8 + 2
            return i

        scond = k.sb("scond", [128, NCH, 2], F32)
        k.act(scond[:], cond_sb[:], AF.Silu, [bC], [bC])
        modT = k.sb("modT", [128, DEPTH, 24, 2], F32)
        bM = Buf("mod")
        xT = k.sb("xT", [128, NCH, GT], F32)
        bX = Buf("xT")
        oT = k.sb("oT", [128, NCH, GT], BF16)
        bO = Buf("oT")
        bR = Buf("rstd")
        hcol = k.sb("hcol", [128, 3, NCH], F32)
        bH = Buf("hcol")

        with k.scope() as sc:
            wst = [sc.sb(un("mwst"), [128, NCH, WB], F32) for i in range(4)]
            bws = [Buf("mwst%d" % i) for i in range(4)]
            mrow = [sc.sb(un("mrow"), [2, WB], F32) for i in range(2)]
            bmrow = [Buf("mrow0"), Buf("mrow1")]
            wi = 0
            for L in range(nlayers):
                for blk in range(12):
                    w = wst[wi % 4]; bw = bws[wi % 4]
                    k.dma("sp" if wi % 2 == 0 else "pool", w[:], modw[L * 12 + blk], [], [bw])
                    bi = nb(); pm = bank(bi); bp = bB[bi]
                    for kc in range(NCH):
                        k.mm(pm[0:2, 0:WB], scond[:, kc, :], w[:, kc, :], [bw, bC], [bp], start=(kc == 0), stop=(kc == NCH - 1))
                    rt = mrow[wi % 2]; brt = bmrow[wi % 2]
                    k.act(rt[:, :], pm[0:2, 0:WB], AF.Copy, [bp], [brt])
                    bi2 = nb(); pm2 = bank(bi2); bp2 = bB[bi2]
                    for sub in range(2):
                        k.tr(pm2[:, sub * 2:sub * 2 + 2], rt[0:2, sub * 128:(sub + 1) * 128], ident[0:2, 0:2], [brt, bC], [bp2])
                    for sub in range(2):
                        ch = blk * 2 + sub
                        k.ts(modT[:, L, ch, :], pm2[:, sub * 2:sub * 2 + 2], modb_sb[:, L, ch:ch + 1], ALU.add, [bp2, bC], [bM])
                    wi += 1

        def compute_rstd(sc):
            rstd = sc.sb(un("rstd"), [128, GT], F32)
            sq = [sc.sb(un("sq"), [128, 512], F32) for i in range(2)]
            bsq = [Buf("sq0"), Buf("sq1")]
            for tt in range(GT // 512):
                bi = nb(); pn = bank(bi); bp = bB[bi]
                for c in range(NCH):
                    s_ = sq[c % 2]; bs_ = bsq[c % 2]
                    k.act(s_[:], xT[:, c, tt * 512:(tt + 1) * 512], AF.Square, [bX], [bs_])
                    k.mm(pn, ones_f[:], s_[:], [bs_, bC], [bp], start=(c == 0), stop=(c == NCH - 1))
                sl = rstd[:, tt * 512:(tt + 1) * 512]
                k.ts(sl, pn, EPS, ALU.add, [bp], [bR])
                k.act(sl, sl, AF.Sqrt, [bR], [bR])
                k.recip(sl, sl, [bR], [bR])
            return rstd

        def wblock_factory(sc):
            wst2 = [sc.sb(un("wst"), [128, NCH, WB], F32) for i in range(2)]
            bws2 = [Buf("wst0"), Buf("wst1")]
            wbf = [sc.sb(un("wbf"), [128, NCH, WB], BF16) for i in range(2)]
            bwb = [Buf("wbf0"), Buf("wbf1")]
            cnt = [0]

            def wblock(src):
                i = cnt[0] % 2
                cnt[0] += 1
                wst = wst2[i]; bws = bws2[i]
                k.dma("sp", wst[:], src, [], [bws])
                k.cp(wbf[i][:], wst[:], [bws], [bwb[i]], e="pool")
                return wbf[i], bwb[i]
            return wblock

        def proj_F(wblock, wsrc, blocks, hT, bHT, dest, msub=128, nsub=None):
            nsub = nsub or WB // msub
            for bi_, blk in enumerate(blocks):
                w, bw = wblock(wsrc[blk])
                for sub in range(nsub):
                    for tt in range(GT // 512):
                        bi = nb(); pb = bank(bi); bp = bB[bi]
                        for kc in range(NCH):
                            k.mm(pb[0:msub, :], w[:, kc, sub * msub:(sub + 1) * msub], hT[:, kc, tt * 512:(tt + 1) * 512],
                                 [bw, bHT], [bp], start=(kc == 0), stop=(kc == NCH - 1))
                        dest(bi_ * nsub + sub, tt, pb, bp)

        def proj_T(wblock, wsrc, blocks, hT, bHT, dest):
            for bi_, blk in enumerate(blocks):
                w, bw = wblock(wsrc[blk])
                for t8 in range(GT // 128):
                    bi = nb(); pb = bank(bi); bp = bB[bi]
                    for kc in range(NCH):
                        k.mm(pb[:, 0:WB], hT[:, kc, t8 * 128:(t8 + 1) * 128], w[:, kc, :],
                             [bw, bHT], [bp], start=(kc == 0), stop=(kc == NCH - 1))
                    dest(bi_, t8, pb, bp)

        def make_h(sc, L, g):
            hT = sc.sb(un("hT"), [128, NCH, GT], BF16)
            bHT = Buf("hT")
            k.stt(hcol[:, 0, :], modT[:, L, 8:16, g], 1.0, normg_sb[:, L, :], ALU.add, ALU.mult, [bM, bC], [bH])
            k.cp(hcol[:, 1, :], modT[:, L, 0:8, g], [bM], [bH])
            k.cp(hcol[:, 2, :], modT[:, L, 16:24, g], [bM], [bH])
            rstd = compute_rstd(sc)
            tmp = [sc.sb(un("htmp"), [128, 512], F32) for i in range(2)]
            btmp = [Buf("htmp0"), Buf("htmp1")]
            i = 0
            for tt in range(GT // 512):
                for c in range(NCH):
                    t_ = tmp[i % 2]; bt_ = btmp[i % 2]; i += 1
                    k.tt(t_[:], xT[:, c, tt * 512:(tt + 1) * 512], rstd[:, tt * 512:(tt + 1) * 512], ALU.mult, [bX, bR], [bt_])
                    k.ts(hT[:, c, tt * 512:(tt + 1) * 512], t_[:], hcol[:, 0, c:c + 1], ALU.mult, [bt_, bH], [bHT],
                         s2=hcol[:, 1, c:c + 1], op1=ALU.add)
            return hT, bHT

        def out_proj(wsrc):
            with k.scope() as sc:
                wblock = wblock_factory(sc)

                def dest(cc, tt, pb, bp):
                    sl = xT[:, cc, tt * 512:(tt + 1) * 512]
                    k.stt(sl, pb, hcol[:, 2, cc:cc + 1], sl, ALU.mult, ALU.add, [bp, bH, bX], [bX])
                proj_F(wblock, wsrc, range(4), oT, bO, dest)

        def headnorm(sc_t, pb_view, nh, gidx, out3, R, W, bT=None):
            bT = bT or bT0
            sqt, ssq = sc_t
            k.act(sqt[:, 0:nh * 64], pb_view.rearrange("p h d -> p (h d)"), AF.Square, R, [bT])
            k.red(ssq[:, 0:nh], sqt[:, 0:nh * 64].rearrange("p (h d) -> p h d", h=nh), [bT], [bT])
            k.ts(ssq[:, 0:nh], ssq[:, 0:nh], 1.0 / 64, ALU.mult, [bT], [bT], s2=EPS, op1=ALU.add)
            k.act(ssq[:, 0:nh], ssq[:, 0:nh], AF.Sqrt, [bT], [bT])
            k.recip(ssq[:, 0:nh], ssq[:, 0:nh], [bT], [bT])
            k.tt(out3, pb_view, ssq[:, 0:nh].unsqueeze(2).to_broadcast([128, nh, 64]), ALU.mult, R + [bT], W)
            k.tt(out3, out3, qkg[:, gidx, :].unsqueeze(1).to_broadcast([128, nh, 64]), ALU.mult, W + [bP], W)

        bT0 = Buf("tmpT")

        def rope(x3, nh, t8, t1, t2, R, bT=None):
            bT = bT or bT0
            cosb = rope_tabs["cos"][:, t8, :].unsqueeze(1).to_broadcast([128, nh, 64])
            k.tt(t1[:, 0:nh, :], x3, cosb, ALU.mult, R + [bC], [bT])
            x5 = x3.rearrange("p h (a q f) -> p h a q f", a=2, q=2)
            t5 = t2[:, 0:nh, :].rearrange("p h (a q f) -> p h a q f", a=2, q=2)
            s4 = rope_tabs["sin"][:, t8, :].rearrange("p (a q f) -> p a q f", a=2, q=2)
            for q_ in range(2):
                k.tt(t5[:, :, :, q_, :], x5[:, :, :, 1 - q_, :],
                     s4[:, :, q_, :].unsqueeze(1).to_broadcast([128, nh, 2, 16]), ALU.mult, R + [bC], [bT])
            k.tt(x3, t1[:, 0:nh, :], t2[:, 0:nh, :], ALU.add, [bT], R)

        def even_layer(g, j, L):
            nseq, T = (4, 256) if g == 0 else (1, 1024)
            TP = T + 2
            koff = 0 if g == 0 else 512
            SK = GT + koff
            k.dma("sp", w2aug[:], w2aug_d[j], [], [bP])
            k.dma("sp", a2aug[:], a2aug_d[j], [], [bP])
            k.dma("sp", qkg[:], qkg_d[j], [], [bP])
            k.dma("sp", evs[:], evs_d[j], [], [bP])
            k.ts(evx[:, 0:14], evs[:, 0:14], 0.5, ALU.mult, [bP], [bP])
            k.ts(evx[:, 14:28], evs[:, 0:14], -1.0, ALU.mult, [bP], [bP], s2=1.0, op1=ALU.add)
            k.ts(evx[:, 28:32], evs[:, 18:22], -1.0, ALU.mult, [bP], [bP], s2=1.0, op1=ALU.add)
            with k.scope() as scL:
                gbT = scL.sb(un("gbT"), [128, 4, GT], BF16); bGB = Buf("gbT")
                zraw = scL.sb(un("zraw"), [128, 14, nseq * TP], BF16); bZ = Buf("zraw")
                k.ms(zraw[:], 0.0, [bZ])
                zr4 = zraw[:].rearrange("p c (s t) -> p c s t", s=nseq)
                with k.scope() as scA:
                    gaT = scA.sb(un("gaT"), [128, 4, GT], BF16); bGA = Buf("gaT")
                    qT = scA.sb(un("qT"), [64, 8, GT], BF16); bQ = Buf("qT")
                    kT = scA.sb(un("kT"), [64, 2, SK], BF16); bK = Buf("kT")
                    vtok = scA.sb(un("vtok"), [128, SK // 128, 128], BF16); bV = Buf("vtok")
                    if g == 1:
                      with k.scope() as sc:
                          ck = sc.sb(un("ck"), [128, 4, 128], F32); bCK = Buf("ck")
                          k.dma("sp", ck[:], cak[j].rearrange("(i p) f -> p i f", p=128), [], [bCK])
                          for i in range(4):
                              b2 = nb(); p2 = bank(b2)
                              for hh in range(2):
                                  k.tr(p2[0:64, hh * 128:(hh + 1) * 128], ck[:, i, hh * 64:(hh + 1) * 64], ident[:], [bCK, bC], [bB[b2]])
                              k.cp(kT[:, :, i * 128:(i + 1) * 128],
                                   p2[0:64, 0:256].rearrange("p (h t) -> p h t", h=2), [bB[b2]], [bK])
                          cv = sc.sb(un("cv"), [128, 4, 128], F32); bCV = Buf("cv")
                          k.dma("sp", cv[:], cav[j].rearrange("(i p) f -> p i f", p=128), [], [bCV])
                          k.cp(vtok[:, 0:4, :], cv[:], [bCV], [bV])

                    with k.scope() as sc:
                        hT, bHT = make_h(sc, L, g)
                        wblock = wblock_factory(sc)
                        if g == 1:
                            rope_tabs["cos"] = sc.sb(un("cos_sb"), [128, 8, 64], F32)
                            rope_tabs["sin"] = sc.sb(un("sin_sb"), [128, 8, 64], F32)
                            k.dma("sp", rope_tabs["cos"][:], cos_d[:, :, :], [], [bC])
                            k.dma("sp", rope_tabs["sin"][:], sin_d[:, :, :], [], [bC])
                        print("S1 sbuf remaining", nc.sbuf_bytes_remaining)
                        sqtL = [sc.sb(un("sqt"), [128, 256], F32) for _ in range(2)]
                        ssqL = [sc.sb(un("ssq"), [128, 4], F32) for _ in range(2)]
                        qnL = [sc.sb(un("qn"), [128, 4, 64], F32) for _ in range(2)]; bQNL = [Buf("qn0"), Buf("qn1")]
                        r1L = [sc.sb(un("r1"), [128, 4, 64], F32) for _ in range(2)]
                        r2L = [sc.sb(un("r2"), [128, 4, 64], F32) for _ in range(2)]
                        bTL = [Buf("tmpT0"), Buf("tmpT1")]
                        kvo = [sc.sb(un("kvo"), [128, 256], F32) for i in range(2)]
                        bKVO = [Buf("kvo0"), Buf("kvo1")]

                        def dest_q(bi_, t8, pb, bp):
                            ix = t8 % 2
                            sqt, ssq, qn, r1, r2, bQN, bTx = sqtL[ix], ssqL[ix], qnL[ix], r1L[ix], r2L[ix], bQNL[ix], bTL[ix]
                            headnorm((sqt, ssq), pb[:, 0:256].rearrange("p (h d) -> p h d", h=4), 4, 0, qn[:], [bp], [bQN], bT=bTx)
                            if g == 1:
                                rope(qn[:], 4, t8, r1, r2, [bQN], bT=bTx)
                            b2 = nb(); p2 = bank(b2)
                            for hh in range(4):
                                k.tr(p2[0:64, hh * 128:(hh + 1) * 128], qn[:, hh, :], ident[:], [bQN, bC], [bB[b2]])
                            k.cp(qT[:, bi_ * 4:bi_ * 4 + 4, t8 * 128:(t8 + 1) * 128],
                                 p2[0:64, :].rearrange("p (h t) -> p h t", h=4), [bB[b2]], [bQ])
                        proj_T(wblock, evw[j], [0, 1], hT, bHT, dest_q)

                        def dest_kv(bi_, t8, pb, bp):
                            ko = kvo[t8 % 2]; bko = bKVO[t8 % 2]
                            kn3 = ko[:, 0:128].rearrange("p (h d) -> p h d", h=2)
                            ix = t8 % 2
                            sqt, ssq, r1, r2, bTx = sqtL[ix], ssqL[ix], r1L[ix], r2L[ix], bTL[ix]
                            headnorm((sqt, ssq), pb[:, 0:128].rearrange("p (h d) -> p h d", h=2), 2, 1, kn3, [bp], [bko], bT=bTx)
                            k.cp(ko[:, 128:256], pb[:, 128:256], [bp], [bko])
                            k.cp(vtok[:, koff // 128 + t8, :], pb[:, 128:256], [bp], [bV])
                            if g == 0:
                                b_ = t8 // 2; t0 = (t8 % 2) * 128
                                k.dma("pool", nk_o[b_, j, t0:t0 + 128, :], ko[:, 0:128], [bko], [])
                                k.dma("pool", nv_o[b_, j, t0:t0 + 128, :], ko[:, 128:256], [bko], [])
                            else:
                                rope(kn3, 2, t8, r1, r2, [bko], bT=bTx)
                            b2 = nb(); p2 = bank(b2)
                            for hh in range(2):
                                k.tr(p2[0:64, hh * 128:(hh + 1) * 128], kn3[:, hh, :], ident[:], [bko, bC], [bB[b2]])
                            k.cp(kT[:, :, koff + t8 * 128:koff + (t8 + 1) * 128],
                                 p2[0:64, 0:256].rearrange("p (h t) -> p h t", h=2), [bB[b2]], [bK])
                        proj_T(wblock, evw[j], [2], hT, bHT, dest_kv)

                        def dest_ga(cc, tt, pb, bp):
                            k.act(gaT[:, cc, tt * 512:(tt + 1) * 512], pb, AF.Silu, [bp], [bGA])
                        proj_F(wblock, evw[j], [3, 4], hT, bHT, dest_ga)

                        def dest_zb(cc, tt, pb, bp):
                            if g == 0:
                                k.cp(zr4[:, cc, 2 * tt:2 * tt + 2, 1:T + 1], pb.rearrange("p (s t) -> p s t", s=2), [bp], [bZ])
                            else:
                                k.cp(zr4[:, cc, 0, 1 + tt * 512:1 + (tt + 1) * 512], pb, [bp], [bZ])
                        proj_F(wblock, evw[j], range(5, 12), hT, bHT, dest_zb)

                        def dest_gb(cc, tt, pb, bp):
                            k.act(gbT[:, cc, tt * 512:(tt + 1) * 512], pb, AF.Silu, [bp], [bGB])
                        proj_F(wblock, evw[j], [12, 13], hT, bHT, dest_gb)

                    with (k.scope() if STOP >= 2 else contextlib.nullcontext()) as sc:
                      if STOP >= 2:
                            pexp = [sc.sb(un("pexp"), [128, 512], BF16) for i in range(4)]
                            bPE = [Buf("pexp%d" % i) for i in range(4)]
                            recL = [sc.sb(un("rec"), [64, 512], F32) for i in range(2)]; bRecL = [Buf("rec0"), Buf("rec1")]
                            oaL = [sc.sb(un("oa"), [128, 512], F32) for i in range(2)]; bOAL = [Buf("oa0"), Buf("oa1")]
                            QB = min(T, 512)
                            ie = 0; blk_i = 0
                            for s in range(nseq):
                                kbase = s * T if g == 0 else 0
                                nsc = (T + koff) // 128
                                for h in range(8):
                                    kv = h // 4
                                    hb = (h % 2) * 64
                                    for qb in range(T // QB):
                                        q0 = s * T + qb * QB
                                        bo_, bs_ = (6, 7) if blk_i % 2 == 0 else (4, 5)
                                        po = bank(bo_); psm = bank(bs_)
                                        rec = recL[blk_i % 2]; bRec = bRecL[blk_i % 2]; oa = oaL[blk_i % 2]; bOA = bOAL[blk_i % 2]
                                        blk_i += 1
                                        for sc_ in range(nsc):
                                            kpos = kbase + sc_ * 128
                                            bi = nb(0, 4); pb = bank(bi); bp = bB[bi]
                                            k.mm(pb[:, 0:QB], kT[:, kv, kpos:kpos + 128], qT[:, h, q0:q0 + QB], [bK, bQ], [bp])
                                            pe_ = pexp[ie % 4]; bpe = bPE[ie % 4]; ie += 1
                                            k.act(pe_[:, 0:QB], pb[:, 0:QB], AF.Exp, [bp], [bpe], scale=0.125)
                                            k.mm(po[0:64, 0:QB], vtok[:, kpos // 128, kv * 64:(kv + 1) * 64], pe_[:, 0:QB],
                                                 [bV, bpe], [bB[bo_]], start=(sc_ == 0), stop=(sc_ == nsc - 1))
                                            k.mm(psm[0:64, 0:QB], ones_bf[:, :], pe_[:, 0:QB],
                                                 [bC, bpe], [bB[bs_]], start=(sc_ == 0), stop=(sc_ == nsc - 1))
                                        k.recip(rec[:, 0:QB], psm[0:64, 0:QB], [bB[bs_]], [bRec])
                                        k.tt(oa[hb:hb + 64, 0:QB], po[0:64, 0:QB], rec[:, 0:QB], ALU.mult, [bB[bo_], bRec], [bOA])
                                        k.tt(oT[hb:hb + 64, h // 2, q0:q0 + QB], oa[hb:hb + 64, 0:QB],
                                             gaT[hb:hb + 64, h // 2, q0:q0 + QB], ALU.mult, [bOA, bGA], [bO])

                with k.scope() as sc:
                    C = 64
                    if STOP < 3:
                        raise_skip = True
                    else:
                        raise_skip = False
                    nchunk = T // C
                    f = lambda name, shape, dt=F32: sc.sb(un(name), shape, dt)
                    yf = f("yf", [128, nseq * nchunk // 2, 512], BF16); bYF = Buf("yf")
                    ST = [f("ST", [128, 4, 64]) for d in range(2)]; bST = [Buf("ST0"), Buf("ST1")]
                    zsP = [f("zs", [128, 14, C]) for p_ in range(2)]
                    zt1 = f("zt1", [128, 14, C])
                    twa = [f("tw", [65, C]) for d in range(2)]
                    ala = [f("al", [65, C]) for d in range(2)]
                    bW = Buf("rwtmp")
                    Ls = f("Ls", [64, 512])
                    gam = f("gam", [128, 4, C]); gamp = f("gamp", [128, 4, C]); ginv = f("ginv", [128, 4, C])
                    av = [f("av", [128, 4, C]) for d in range(2)]
                    kkr = f("kkr", [128, 4, C]); ksq = f("ksq", [128, 4, C]); kk = f("kk", [128, 4, C])
                    kdP = [[f("kd", [128, 4, C]) for d in range(2)] for p_ in range(2)]
                    tmp4 = f("tmp4", [128, 4, C]); tmp5 = f("tmp5", [128, 4, C])
                    Bn = {n_: Buf(n_) for n_ in ["zs", "zt1", "ala0", "ala1", "twa0", "twa1", "av0", "av1", "kkr", "ksq", "kk", "kd0", "kd1", "tmp4", "tmp5", "Ls", "gam", "gamp", "ginv", "AR", "KBt", "Gm1", "Gm2", "Nm0", "Nm1", "Am0", "Am1", "Vt", "Us0", "Us1", "KBT", "ysum", "ysq", "yst", "bon", "obt", "tmp6"] + [x + str(p_) for p_ in range(2) for x in ["zs", "AR", "KBt", "Vt", "Gm1", "Gm2", "Nmi", "gl", "kd0_", "kd1_"]]}
                    ARP = [f("AR", [128, 4, 2, C]) for p_ in range(2)]; KBtP = [f("KBt", [128, 4, 2, C]) for p_ in range(2)]
                    Gm1P = [f("Gm1", [64, 8, 128]) for p_ in range(2)]; Gm2P = [f("Gm2", [64, 8, 128]) for p_ in range(2)]; Nm = [f("Nm", [64, 8, 64]) for i in range(2)]
                    NmiP = [f("Nmi", [64, 8, 64]) for p_ in range(2)]; glP = [f("gl", [128, 4, 1]) for p_ in range(2)]; tmp6 = f("tmp6", [128, 4, C])
                    Am = [f("Am", [64, 8, 64]) for i in range(2)]
                    VtP = [f("Vt", [64, 512]) for p_ in range(2)]; Us = [f("Us", [64, 512]) for i in range(2)]
                    KBT = f("KBT", [64, 4, 2, 128])
                    ysum = f("ysum", [64, 8, 64]); ysq = f("ysq", [64, 8, 64]); yst = f("yst", [64, 16])
                    bon = f("bon", [128, 4, C]); obt = f("obt", [128, 4, C])
                    sto = f("sto", [64, 4, 128]); bSTO = Buf("sto")
                    sld = f("sld", [64, 8, 64]); bSLD = Buf("sld")
                    for d in range(2):
                        k.ms(twa[d][:], 1.0, [Bn["twa%d" % d]])
                        k.ms(ala[d][:], 1.0, [Bn["ala%d" % d]])

                    units = []
                    for s in range(min(nseq, KNS) if STOP >= 3 else 0):
                        for d in range(KND):
                            order = list(range(nchunk) if d == 0 else range(nchunk - 1, -1, -1))[:KNC]
                            for ci_, c in enumerate(order):
                                units.append((s, d, c, ci_ == 0, ci_ == len(order) - 1))
                    HO = [0, 2, 4, 6, 1, 3, 5, 7]
                    HOr = [1, 3, 5, 7, 0, 2, 4, 6]
                    rrA = [0]; rrB = [0]

                    def nbA():
                        rrA[0] += 1
                        return rrA[0] % 5

                    def nbB():
                        rrB[0] += 1
                        return 5 + rrB[0] % 3

                    def stageA(u, p):
                        s, d, c, first, last = u
                        tcol = s * TP + c * C
                        gt0 = s * T + c * C
                        yield
                        k.tt(zt1[:], zraw[:, :, tcol:tcol + C], zraw[:, :, tcol + 2:tcol + 2 + C], ALU.add, [bZ], [Bn["zt1"]], e="pool")
                        k.tt(zt1[:], zt1[:], evx[:, 0:14].unsqueeze(2).to_broadcast([128, 14, C]), ALU.mult, [Bn["zt1"], bP], [Bn["zt1"]], e="pool")
                        k.tt(zsP[p][:], zraw[:, :, tcol + 1:tcol + 1 + C], evx[:, 14:28].unsqueeze(2).to_broadcast([128, 14, C]),
                             ALU.mult, [bZ, bP], [Bn["zs%d" % p]])
                        k.tt(zsP[p][:], zsP[p][:], zt1[:], ALU.add, [Bn["zs%d" % p], Bn["zt1"]], [Bn["zs%d" % p]])
                        r_ = zsP[p][:, 0:4, :]; kraw = zsP[p][:, 4:8, :]; vv = zsP[p][:, 8:12, :]
                        dirs = [d] if d == 0 else [0, 1]
                        yield
                        for dd in dirs:
                            bal = Bn["ala%d" % dd]; bav = Bn["av%d" % dd]; bkd = Bn["kd%d_%d" % (dd, p)]
                            k.cp(ala[dd][0:64, :], zsP[p][dd * 64:(dd + 1) * 64, 13, :], [Bn["zs%d" % p]], [bal])
                            b2 = nbA(); p2 = bank(b2)
                            for cp in range(4):
                                k.mm(p2[:, cp * C:(cp + 1) * C], a2aug[:, dd, cp * 128:(cp + 1) * 128], ala[dd][:, :],
                                     [bal, bP], [bB[b2]])
                            k.act(av[dd][:].rearrange("p c t -> p (c t)"), p2[:, 0:4 * C], AF.Sigmoid, [bB[b2]], [bav])
                            k.tt(tmp4[:], av[dd][:], evs[:, 18:22].unsqueeze(2).to_broadcast([128, 4, C]), ALU.mult, [bav, bP], [Bn["tmp4"]], e="pool")
                            k.tt(tmp4[:], tmp4[:], evx[:, 28:32].unsqueeze(2).to_broadcast([128, 4, C]), ALU.add, [Bn["tmp4"], bP], [Bn["tmp4"]], e="pool")
                            k.tt(kdP[p][dd][:], kraw, tmp4[:], ALU.mult, [Bn["zs%d" % p], Bn["tmp4"]], [bkd], e="pool")
                        yield
                        k.tt(kkr[:], kraw, evs[:, 14:18].unsqueeze(2).to_broadcast([128, 4, C]), ALU.mult, [Bn["zs%d" % p], bP], [Bn["kkr"]])
                        k.tt(ksq[:], kkr[:], kkr[:], ALU.mult, [Bn["kkr"]], [Bn["ksq"]])
                        b2 = nbA(); p2 = bank(b2)
                        k.mm(p2[:, 0:4 * C], bdones[:, :], ksq[:].rearrange("p c t -> p (c t)"), [Bn["ksq"], bC], [bB[b2]])
                        k.ts(ksq[:].rearrange("p c t -> p (c t)"), p2[:, 0:4 * C], 1e-12, ALU.add, [bB[b2]], [Bn["ksq"]])
                        k.act(ksq[:], ksq[:], AF.Sqrt, [Bn["ksq"]], [Bn["ksq"]])
                        k.recip(ksq[:], ksq[:], [Bn["ksq"]], [Bn["ksq"]])
                        k.tt(kk[:], kkr[:], ksq[:], ALU.mult, [Bn["kkr"], Bn["ksq"]], [Bn["kk"]])
                        yield
                        btw = Bn["twa%d" % d]
                        k.act(twa[d][0:64, :], zsP[p][d * 64:(d + 1) * 64, 12, :], AF.Tanh, [Bn["zs%d" % p]], [btw])
                        b2 = nbA(); p2 = bank(b2)
                        k.mm(p2[0:64, :], twa[d][:, :], w2aug[:, d, :], [btw, bP], [bB[b2]])
                        k.act(Ls[:], p2[0:64, :], AF.Sigmoid, [bB[b2]], [Bn["Ls"]])
                        b2 = nbA(); p2 = bank(b2)
                        for cp in range(4):
                            k.mm(p2[:, cp * 128:(cp + 1) * 128], Ls[:, cp * 128:(cp + 1) * 128],
                                 tri64[:, d, :, :].rearrange("p a t -> p (a t)"), [Bn["Ls"], bC], [bB[b2]])
                        p4 = p2.rearrange("p (c a t) -> p c a t", c=4, a=2)
                        k.act(gamp[:], p4[:, :, 1, :], AF.Exp, [bB[b2]], [Bn["gamp"]])
                        k.act(ginv[:], p4[:, :, 0, :], AF.Exp, [bB[b2]], [Bn["ginv"]], scale=-1.0)
                        k.act(gam[:], p4[:, :, 0, :], AF.Exp, [bB[b2]], [Bn["gam"]])
                        k.cp(glP[p][:], (gam[:, :, C - 1:C] if d == 0 else gam[:, :, 0:1]), [Bn["gam"]], [Bn["gl%d" % p]])
                        yield
                        bav = Bn["av%d" % d]; bkd = Bn["kd%d_%d" % (d, p)]
                        k.stt(ARP[p][:, :, 0, :], kk[:], -1.0, gamp[:], ALU.mult, ALU.mult, [Bn["kk"], Bn["gamp"]], [Bn["AR%d" % p]])
                        k.tt(KBtP[p][:, :, 0, :], kdP[p][d][:], ginv[:], ALU.mult, [bkd, Bn["ginv"]], [Bn["KBt%d" % p]])
                        k.tt(tmp5[:], kk[:], av[d][:], ALU.mult, [Bn["kk"], bav], [Bn["tmp5"]])
                        k.tt(KBtP[p][:, :, 1, :], tmp5[:], ginv[:], ALU.mult, [Bn["tmp5"], Bn["ginv"]], [Bn["KBt%d" % p]])
                        k.tt(ARP[p][:, :, 1, :], r_, gam[:], ALU.mult, [Bn["zs%d" % p], Bn["gam"]], [Bn["AR%d" % p]])
                        yield
                        b2 = nbA(); p2 = bank(b2)
                        for cp in range(4):
                            k.tr(p2[0:64, cp * 128:(cp + 1) * 128], zsP[p][:, 8 + cp, :], ident[:], [Bn["zs%d" % p], bC], [bB[b2]])
                        k.act(VtP[p][:], p2[0:64, :], AF.Copy, [bB[b2]], [Bn["Vt%d" % p]])
                        yield
                        g1, g2, gn = 0, 2, 4
                        for h in HO:
                            cp = h // 2; hb = (h % 2) * 64
                            arh = ARP[p][hb:hb + 64, cp, :, :].rearrange("p a t -> p (a t)")
                            k.mm(bank(g1 + h // 4)[0:64, (h % 4) * 128:(h % 4 + 1) * 128], KBtP[p][hb:hb + 64, cp, 0, :], arh,
                                 [Bn["KBt%d" % p], Bn["AR%d" % p]], [bB[g1 + h // 4]])
                            k.mm(bank(g2 + h // 4)[0:64, (h % 4) * 128:(h % 4 + 1) * 128], KBtP[p][hb:hb + 64, cp, 1, :], arh,
                                 [Bn["KBt%d" % p], Bn["AR%d" % p]], [bB[g2 + h // 4]])
                            k.mm(bank(gn)[0:64, h * 64:(h + 1) * 64], ARP[p][hb:hb + 64, cp, 0, :], KBtP[p][hb:hb + 64, cp, 1, :],
                                 [Bn["KBt%d" % p], Bn["AR%d" % p]], [bB[gn]])
                        yield
                        m64b = mask64[:, d, :].unsqueeze(1).to_broadcast([64, 4, 128])
                        for hf in range(2):
                            k.tt(Gm1P[p][:, hf * 4:hf * 4 + 4, :], bank(g1 + hf)[0:64, :].rearrange("p (h t) -> p h t", h=4), m64b,
                                 ALU.mult, [bB[g1 + hf], bC], [Bn["Gm1%d" % p]])
                        for hf in range(2):
                            k.tt(Gm2P[p][:, hf * 4:hf * 4 + 4, :], bank(g2 + hf)[0:64, :].rearrange("p (h t) -> p h t", h=4), m64b,
                                 ALU.mult, [bB[g2 + hf], bC], [Bn["Gm2%d" % p]])
                        k.tt(NmiP[p][:], bank(gn)[0:64, :].rearrange("p (h t) -> p h t", h=8),
                             maskN[:, d, :].unsqueeze(1).to_broadcast([64, 8, 64]), ALU.mult, [bB[gn], bC], [Bn["Nmi%d" % p]])

                        yield

                    def stageB(u, p, pull):
                        s, d, c, first, last = u
                        tcol = s * TP + c * C
                        gt0 = s * T + c * C
                        r_ = zsP[p][:, 0:4, :]; vv = zsP[p][:, 8:12, :]
                        if first:
                            if g == 0:
                                k.ms(ST[d][:], 0.0, [bST[d]])
                            else:
                                k.dma("sp", sld[:], srs_d[d][j].rearrange("h v k -> v h k"), [], [bSLD])
                                b2 = nbB(); p2 = bank(b2)
                                for cp in range(4):
                                    k.tr(p2[:, cp * 64:(cp + 1) * 64], sld[:, 2 * cp:2 * cp + 2, :].rearrange("v h k -> v (h k)"),
                                         ident[0:64, 0:64], [bSLD, bC], [bB[b2]])
                                k.cp(ST[d][:], p2[:, 0:256].rearrange("p (c v) -> p c v", c=4), [bB[b2]], [bST[d]])

                        b2 = nbB(); p2 = bank(b2)
                        for h in HOr:
                            cp = h // 2; hb = (h % 2) * 64
                            k.mm(p2[0:64, h * 64:(h + 1) * 64], ARP[p][hb:hb + 64, cp, 0, :], ST[d][hb:hb + 64, cp, :],
                                 [Bn["AR%d" % p], bST[d]], [bB[b2]])
                        b2b = nbB(); p2b = bank(b2b)
                        for h in range(8):
                            k.mm(p2b[0:64, h * 64:(h + 1) * 64], Gm1P[p][:, h, 0:64], VtP[p][:, h * 64:(h + 1) * 64],
                                 [Bn["Gm1%d" % p], Bn["Vt%d" % p]], [bB[b2b]])
                        k.act(Us[0][:], p2[0:64, :], AF.Copy, [bB[b2]], [Bn["Us0"]])
                        k.tt(Us[0][:], Us[0][:], p2b[0:64, :], ALU.add, [Bn["Us0"], bB[b2b]], [Bn["Us0"]])
                        ui = 0; ai = 0
                        for jn in range(6):
                            bA = Bn["Am%d" % ai] if jn else Bn["Gm2%d" % p]; bN = Bn["Nm%d" % ai] if jn else Bn["Nmi%d" % p]
                            bA2 = Bn["Am%d" % (1 - ai)]; bN2 = Bn["Nm%d" % (1 - ai)]
                            Acur = (lambda h, ai=ai: Am[ai][:, h, :]) if jn else (lambda h: Gm2P[p][:, h, 0:64])
                            Ncur = (lambda h, ai=ai: Nm[ai][:, h, :]) if jn else (lambda h: NmiP[p][:, h, :])
                            pull()
                            bU = Bn["Us%d" % ui]; bU2 = Bn["Us%d" % (1 - ui)]
                            b2 = nbB(); p2 = bank(b2)
                            for h in range(8):
                                k.mm(p2[0:64, h * 64:(h + 1) * 64], Acur(h), Us[ui][:, h * 64:(h + 1) * 64], [bA, bU], [bB[b2]])
                            if jn < 5:
                                b3 = nbB(); p3 = bank(b3)
                                for h in range(8):
                                    k.mm(p3[0:64, h * 64:(h + 1) * 64], Ncur(h), Acur(h), [bN, bA], [bB[b3]])
                                if jn < 4:
                                    b4 = nbB(); p4_ = bank(b4)
                                    for h in range(8):
                                        k.mm(p4_[0:64, h * 64:(h + 1) * 64], Acur(h), Ncur(h), [bA, bN], [bB[b4]])
                            k.tt(Us[1 - ui][:], Us[ui][:], p2[0:64, :], ALU.add, [bU, bB[b2]], [bU2])
                            ui = 1 - ui
                            pull()
                            if jn < 5:
                                k.act(Am[1 - ai][:].rearrange("p h t -> p (h t)"), p3[0:64, :], AF.Copy, [bB[b3]], [bA2])
                                if jn < 4:
                                    k.act(Nm[1 - ai][:].rearrange("p h t -> p (h t)"), p4_[0:64, :], AF.Copy, [bB[b4]], [bN2])
                                ai = 1 - ai
                        U = Us[ui]; bU = Bn["Us%d" % ui]
                        by = nbB(); py = bank(by)
                        by0 = nbB(); py0 = bank(by0)
                        for h in range(8):
                            o_ = py[0:64, h * 64:(h + 1) * 64]
                            k.mm(o_, Gm2P[p][:, h, 64:128], U[:, h * 64:(h + 1) * 64], [Bn["Gm2%d" % p], bU], [bB[by]], start=True, stop=False)
                            k.mm(o_, Gm1P[p][:, h, 64:128], VtP[p][:, h * 64:(h + 1) * 64], [Bn["Gm1%d" % p], Bn["Vt%d" % p]], [bB[by]], start=False, stop=True)
                        for h in HO:
                            cp = h // 2; hb = (h % 2) * 64
                            k.mm(py0[0:64, h * 64:(h + 1) * 64], ARP[p][hb:hb + 64, cp, 1, :], ST[d][hb:hb + 64, cp, :], [Bn["AR%d" % p], bST[d]], [bB[by0]])
                        ci = s * nchunk + c
                        ys2 = ysum[:].rearrange("p h v -> p (h v)")
                        yfs = yf[(ci % 2) * 64:(ci % 2) * 64 + 64, ci // 2, :]
                        bys = Bn["ysum"]
                        if d == 0:
                            k.act(ys2, py0[0:64, :], AF.Copy, [bB[by0]], [bys])
                            k.tt(ys2, ys2, py[0:64, :], ALU.add, [bys, bB[by]], [bys])
                            k.cp(yfs, ys2, [bys], [bYF])
                        else:
                            k.tt(ys2, py[0:64, :], yfs, ALU.add, [bB[by], bYF], [bys])
                            k.tt(ys2, ys2, py0[0:64, :], ALU.add, [bys, bB[by0]], [bys])
                        pull()
                        bt1 = 6
                        for cp in range(4):
                            for a_ in range(2):
                                idx = cp * 2 + a_
                                k.tr(bank(bt1 + idx // 4)[0:64, (idx % 4) * 128:(idx % 4 + 1) * 128], KBtP[p][:, cp, a_, :], ident[:],
                                     [Bn["KBt%d" % p], bC], [bB[bt1 + idx // 4]])
                        k.act(KBT[:, 0:2, :, :].rearrange("p c a k -> p (c a k)"), bank(bt1)[0:64, :], AF.Copy, [bB[bt1]], [Bn["KBT"]])
                        k.cp(KBT[:, 2:4, :, :].rearrange("p c a k -> p (c a k)"), bank(bt1 + 1)[0:64, :], [bB[bt1 + 1]], [Bn["KBT"]])
                        bs = 5; psu = bank(bs)
                        for cp in range(4):
                            k.mm(psu[:, cp * 128:(cp + 1) * 128], KBT[:, cp, 0, :], VtP[p][:, cp * 128:(cp + 1) * 128], [Bn["KBT"], Bn["Vt%d" % p]], [bB[bs]],
                                 start=True, stop=False)
                            k.mm(psu[:, cp * 128:(cp + 1) * 128], KBT[:, cp, 1, :], U[:, cp * 128:(cp + 1) * 128], [Bn["KBT"], bU], [bB[bs]],
                                 start=False, stop=True)
                        ps4 = psu.rearrange("p (c x) -> p c x", c=4)
                        for hh in range(2):
                            hb = hh * 64
                            k.tt(ST[d][hb:hb + 64, :, :], ST[d][hb:hb + 64, :, :], ps4[hb:hb + 64, :, hb:hb + 64], ALU.add,
                                 [bST[d], bB[bs]], [bST[d]])
                            k.tt(ST[d][hb:hb + 64, :, :], ST[d][hb:hb + 64, :, :],
                                 glP[p][hb:hb + 64, :, :].to_broadcast([64, 4, 64]), ALU.mult, [bST[d], Bn["gl%d" % p]], [bST[d]])
                        pull()
                        if d == 1:
                            k.red(yst[:, 0:8], ysum[:], [bys], [Bn["yst"]])
                            k.ts(yst[:, 0:8], yst[:, 0:8], 1.0 / 64, ALU.mult, [Bn["yst"]], [Bn["yst"]])
                            k.tt(ysum[:], ysum[:], yst[:, 0:8].unsqueeze(2).to_broadcast([64, 8, 64]), ALU.subtract, [bys, Bn["yst"]], [bys])
                            k.tt(ysq[:], ysum[:], ysum[:], ALU.mult, [bys], [Bn["ysq"]])
                            k.red(yst[:, 8:16], ysq[:], [Bn["ysq"]], [Bn["yst"]])
                            k.ts(yst[:, 8:16], yst[:, 8:16], 1.0 / 64, ALU.mult, [Bn["yst"]], [Bn["yst"]], s2=GN_EPS, op1=ALU.add)
                            k.act(yst[:, 8:16], yst[:, 8:16], AF.Sqrt, [Bn["yst"]], [Bn["yst"]])
                            k.recip(yst[:, 8:16], yst[:, 8:16], [Bn["yst"]], [Bn["yst"]])
                            k.tt(ysum[:], ysum[:], yst[:, 8:16].unsqueeze(2).to_broadcast([64, 8, 64]), ALU.mult, [bys, Bn["yst"]], [bys])
                            k.tt(tmp6[:], kdP[p][0][:], kdP[p][1][:], ALU.add, [Bn["kd0_%d" % p], Bn["kd1_%d" % p]], [Bn["tmp6"]], e="pool")
                            k.tt(tmp6[:], tmp6[:], r_, ALU.mult, [Bn["tmp6"], Bn["zs%d" % p]], [Bn["tmp6"]], e="pool")
                            k.tt(tmp6[:], tmp6[:], evs[:, 22:26].unsqueeze(2).to_broadcast([128, 4, C]), ALU.mult, [Bn["tmp6"], bP], [Bn["tmp6"]], e="pool")
                            b2 = nbB(); p2 = bank(b2)
                            k.mm(p2[:, 0:4 * C], bdones[:, :], tmp6[:].rearrange("p c t -> p (c t)"), [Bn["tmp6"], bC], [bB[b2]])
                            k.tt(bon[:], p2[:, 0:4 * C].rearrange("p (c t) -> p c t", c=4), vv, ALU.mult, [bB[b2], Bn["zs%d" % p]], [Bn["bon"]])
                            b3 = nbB(); p3 = bank(b3)
                            for cp in range(4):
                                k.tr(p3[:, cp * C:(cp + 1) * C], ysum[:, 2 * cp:2 * cp + 2, :].rearrange("p h v -> p (h v)"),
                                     ident[0:64, 0:64], [bys, bC], [bB[b3]])
                            p3v = p3[:, 0:4 * C].rearrange("p (c t) -> p c t", c=4)
                            k.tt(obt[:], p3v, evs[:, 26:30].unsqueeze(2).to_broadcast([128, 4, C]), ALU.mult, [bB[b3], bP], [Bn["obt"]])
                            k.tt(obt[:], obt[:], evs[:, 30:34].unsqueeze(2).to_broadcast([128, 4, C]), ALU.add, [Bn["obt"], bP], [Bn["obt"]])
                            k.tt(obt[:], obt[:], bon[:], ALU.add, [Bn["obt"], Bn["bon"]], [Bn["obt"]])
                            k.tt(oT[:, 4:8, gt0:gt0 + C], obt[:], gbT[:, :, gt0:gt0 + C], ALU.mult, [Bn["obt"], bGB], [bO])

                        if last:
                            if g == 0:
                                b2 = nbB(); p2 = bank(b2)
                                for cp in range(4):
                                    k.tr(p2[0:64, cp * 128:(cp + 1) * 128], ST[d][:, cp, :], ident[:], [bST[d], bC], [bB[b2]])
                                k.cp(sto[:].rearrange("p c x -> p (c x)"), p2[0:64, :], [bB[b2]], [bSTO])
                                k.dma("pool", nrs_o[d][s, j].rearrange("h v k -> v h k"),
                                      sto[:].rearrange("p c (h k) -> p (c h) k", h=2), [bSTO], [])

                    gens = {}
                    if units:
                        for _ in stageA(units[0], 0):
                            pass
                    for ui_, u in enumerate(units):
                        nxt = stageA(units[ui_ + 1], (ui_ + 1) % 2) if ui_ + 1 < len(units) else None

                        def pull(nxt=nxt, n=2):
                            if nxt is None:
                                return
                            for _ in range(n):
                                try:
                                    next(nxt)
                                except StopIteration:
                                    return
                        stageB(u, ui_ % 2, pull)
                        if nxt is not None:
                            for _ in nxt:
                                pass
            if STOP >= 4:
                out_proj(evo[j])

        def odd_layer(g, j, L):
            nseq, T = (4, 256) if g == 0 else (1, 1024)
            C = 128
            nchunk = T // C
            k.dma("sp", gw2aug[:], gw2aug_d[j], [], [bP])
            k.dma("sp", ln256[:], ln256_d[j], [], [bP])
            with k.scope() as scL:
                qT = scL.sb(un("gqT"), [128, 4, GT], BF16); bQ = Buf("gqT")
                kT = scL.sb(un("gkT"), [128, 4, GT], BF16); bK = Buf("gkT")
                vt = scL.sb(un("gvt"), [128, 8, 1024], BF16); bV = Buf("gvt")
                gs = scL.sb(un("ggs"), [128, 8, GT], BF16); bG = Buf("ggs")
                glT = scL.sb(un("glT"), [17, 2, GT], F32); bGL = Buf("glT")
                of = scL.sb(un("gof"), [128, 8, 1024], BF16); bOF = Buf("gof")
                k.ms(glT[:], 1.0, [bGL])
                with k.scope() as sc:
                    hT, bHT = make_h(sc, L, g)
                    wblock = wblock_factory(sc)

                    def dest_q(cc, tt, pb, bp):
                        k.ts(qT[:, cc, tt * 512:(tt + 1) * 512], pb, float(128 ** -0.5), ALU.mult, [bp], [bQ])
                    proj_F(wblock, odw[j], [0, 1], hT, bHT, dest_q)

                    def dest_k(cc, tt, pb, bp):
                        k.cp(kT[:, cc, tt * 512:(tt + 1) * 512], pb, [bp], [bK])
                    proj_F(wblock, odw[j], [2, 3], hT, bHT, dest_k)

                    def dest_v(bi_, t8, pb, bp):
                        k.cp(vt[:, t8, bi_ * 256:(bi_ + 1) * 256], pb[:, 0:256], [bp], [bV])
                    proj_T(wblock, odw[j], [4, 5, 6, 7], hT, bHT, dest_v)

                    def dest_g(cc, tt, pb, bp):
                        k.act(gs[:, cc, tt * 512:(tt + 1) * 512], pb, AF.Silu, [bp], [bG])
                    proj_F(wblock, odw[j], [8, 9, 10, 11], hT, bHT, dest_g)

                    def dest_gl(cc, tt, pb, bp):
                        if cc < 2:
                            k.cp(glT[0:16, cc, tt * 512:(tt + 1) * 512], pb[0:16, :], [bp], [bGL])
                    proj_F(wblock, odw[j], [12], hT, bHT, dest_gl, msub=16, nsub=2)

                with k.scope() as sc:
                    f = lambda name, shape, dt=F32: sc.sb(un(name), shape, dt)
                    S = [f("gS", [128, 4, 256]) for d in range(2)]; bS = [Buf("gS0"), Buf("gS1")]
                    bW = Buf("glatmp")
                    G = {n_: Buf("g_" + n_) for n_ in ["Lg", "gam", "ginv", "qs", "ks", "Am", "kTt", "osum", "osq", "ost"]}
                    Lg = f("Lg", [128, 512])
                    gam = f("ggam", [128, 4, C]); ginv = f("gginv", [128, 4, C])
                    qs = f("gqs", [128, 4, C]); ks = f("gks", [128, 4, C])
                    Am = f("gAm", [128, 4, C], BF16); kTt = f("gkTt", [128, 4, C], BF16)
                    osum = f("gosum", [128, 4, 256]); osq = f("gosq", [128, 4, 256]); ost = f("gost", [128, 4])
                    print("GLA sbuf remaining", nc.sbuf_bytes_remaining)
                    for s in range(nseq):
                        for d in range(2):
                            if g == 0:
                                k.ms(S[d][:], 0.0, [bS[d]])
                            else:
                                k.dma("sp", S[d][:], sgs_d[d][j].rearrange("h k v -> k h v"), [], [bS[d]])
                            order = range(nchunk) if d == 0 else range(nchunk - 1, -1, -1)
                            for c in order:
                                gt0 = s * T + c * C
                                t8 = gt0 // 128
                                b2 = nb(); p2 = bank(b2)
                                k.mm(p2, glT[:, d, gt0:gt0 + C], gw2aug[:, d, :], [bGL, bP], [bB[b2]])
                                k.act(Lg[:], p2, AF.Sigmoid, [bB[b2]], [G["Lg"]])
                                k.act(Lg[:], Lg[:], AF.Ln, [G["Lg"]], [G["Lg"]])
                                b2 = nb(); p2 = bank(b2)
                                for h in range(4):
                                    k.mm(p2[:, h * C:(h + 1) * C], Lg[:, h * 128:(h + 1) * 128], tri128[:, d, :], [G["Lg"], bC], [bB[b2]])
                                k.act(gam[:].rearrange("p h t -> p (h t)"), p2, AF.Exp, [bB[b2]], [G["gam"]])
                                k.act(ginv[:].rearrange("p h t -> p (h t)"), p2, AF.Exp, [bB[b2]], [G["ginv"]], scale=-1.0)
                                glast = gam[:, :, C - 1:C] if d == 0 else gam[:, :, 0:1]
                                k.tt(qs[:], qT[:, :, gt0:gt0 + C], gam[:], ALU.mult, [bQ, G["gam"]], [G["qs"]])
                                k.tt(ks[:], kT[:, :, gt0:gt0 + C], ginv[:], ALU.mult, [bK, G["ginv"]], [G["ks"]])
                                b2 = nb(); p2 = bank(b2)
                                for h in range(4):
                                    k.mm(p2[:, h * C:(h + 1) * C], ks[:, h, :], qs[:, h, :], [G["ks"], G["qs"]], [bB[b2]])
                                k.tt(Am[:], p2.rearrange("p (h t) -> p h t", h=4), mask128[:, d, :].unsqueeze(1).to_broadcast([128, 4, C]),
                                     ALU.mult, [bB[b2], bC], [G["Am"]])
                                b2 = nb(); p2 = bank(b2)
                                for h in range(4):
                                    k.tr(p2[:, h * C:(h + 1) * C], ks[:, h, :], ident[:], [G["ks"], bC], [bB[b2]])
                                k.cp(kTt[:].rearrange("p h t -> p (h t)"), p2, [bB[b2]], [G["kTt"]])
                                by = nb2()
                                for h in range(4):
                                    o_ = bank(by + h // 2)[:, (h % 2) * 256:(h % 2 + 1) * 256]
                                    k.mm(o_, qs[:, h, :], S[d][:, h, :], [G["qs"], bS[d]], [bB[by + h // 2]], start=True, stop=False)
                                    k.mm(o_, Am[:, h, :], vt[:, t8, h * 256:(h + 1) * 256], [G["Am"], bV], [bB[by + h // 2]], start=False, stop=True)
                                bs = nb2()
                                for h in range(4):
                                    k.mm(bank(bs + h // 2)[:, (h % 2) * 256:(h % 2 + 1) * 256], kTt[:, h, :], vt[:, t8, h * 256:(h + 1) * 256],
                                         [G["kTt"], bV], [bB[bs + h // 2]])
                                for hf in range(2):
                                    sl = S[d][:, 2 * hf:2 * hf + 2, :]
                                    k.tt(sl, sl, bank(bs + hf).rearrange("p (h v) -> p h v", h=2), ALU.add, [bS[d], bB[bs + hf]], [bS[d]])
                                k.tt(S[d][:], S[d][:], glast.to_broadcast([128, 4, 256]), ALU.mult, [bS[d], G["gam"]], [bS[d]])
                                if d == 0:
                                    for hf in range(2):
                                        k.cp(of[:, t8, hf * 512:(hf + 1) * 512], bank(by + hf), [bB[by + hf]], [bOF])
                                else:
                                    for hf in range(2):
                                        k.tt(osum[:, 2 * hf:2 * hf + 2, :].rearrange("p h v -> p (h v)"), bank(by + hf),
                                             of[:, t8, hf * 512:(hf + 1) * 512], ALU.add, [bB[by + hf], bOF], [G["osum"]])
                                    k.act(osq[:], osum[:], AF.Square, [G["osum"]], [G["osq"]])
                                    k.red(ost[:], osq[:], [G["osq"]], [G["ost"]])
                                    k.ts(ost[:], ost[:], 1.0 / 256, ALU.mult, [G["ost"]], [G["ost"]], s2=EPS, op1=ALU.add)
                                    k.act(ost[:], ost[:], AF.Sqrt, [G["ost"]], [G["ost"]])
                                    k.recip(ost[:], ost[:], [G["ost"]], [G["ost"]])
                                    k.tt(osum[:], osum[:], ost[:].unsqueeze(2).to_broadcast([128, 4, 256]), ALU.mult, [G["osum"], G["ost"]], [G["osum"]])
                                    k.tt(osum[:], osum[:], ln256[:, :].unsqueeze(1).to_broadcast([128, 4, 256]), ALU.mult, [G["osum"], bP], [G["osum"]])
                                    bt = nb2()
                                    o2 = osum[:].rearrange("p h v -> p (h v)")
                                    for cc in range(8):
                                        k.tr(bank(bt + cc // 4)[:, (cc % 4) * 128:(cc % 4 + 1) * 128], o2[:, cc * 128:(cc + 1) * 128], ident[:],
                                             [G["osum"], bC], [bB[bt + cc // 4]])
                                    for hf in range(2):
                                        k.tt(oT[:, 4 * hf:4 * hf + 4, gt0:gt0 + C], bank(bt + hf).rearrange("p (c t) -> p c t", c=4),
                                             gs[:, 4 * hf:4 * hf + 4, gt0:gt0 + C], ALU.mult, [bB[bt + hf], bG], [bO])
                            if g == 0:
                                k.dma("pool", ngs_o[d][s, j].rearrange("h k v -> k h v"), S[d][:], [bS[d]], [])
            out_proj(odo[j])

        for g in groups:
            with k.scope() as sc:
                xin = [sc.sb(un("xin"), [128, D], F32) for i in range(2)]
                bxin = [Buf("xin0"), Buf("xin1")]
                for t8 in range(GT // 128):
                    xi = xin[t8 % 2]; bxi = bxin[t8 % 2]
                    k.dma("sp", xi[:], xg[g, t8 * 128:(t8 + 1) * 128, :], [], [bxi])
                    for half in range(2):
                        b2 = nb(); p2 = bank(b2)
                        for jj in range(4):
                            cch = half * 4 + jj
                            k.tr(p2[:, jj * 128:(jj + 1) * 128], xi[:, cch * 128:(cch + 1) * 128], ident[:], [bxi, bC], [bB[b2]])
                        k.cp(xT[:, half * 4:(half + 1) * 4, t8 * 128:(t8 + 1) * 128], p2.rearrange("p (j t) -> p j t", j=4), [bB[b2]], [bX])
            for L in range(nlayers):
                if L % 2 == 0:
                    even_layer(g, L // 2, L)
                else:
                    odd_layer(g, L // 2, L)
            with k.scope() as sc:
                rstd = compute_rstd(sc)
                yout = [sc.sb(un("yout"), [128, D], F32) for i in range(2)]
                byo = [Buf("yout0"), Buf("yout1")]
                tmpn = [sc.sb(un("tmpn"), [128, 512], F32) for i in range(2)]
                btn = [Buf("tmpn0"), Buf("tmpn1")]
                it = 0
                for t8 in range(GT // 128):
                    yo = yout[t8 % 2]; by_ = byo[t8 % 2]
                    for half in range(2):
                        b2 = nb(); p2 = bank(b2)
                        tn = tmpn[it % 2]; bt_ = btn[it % 2]; it += 1
                        for jj in range(4):
                            cch = half * 4 + jj
                            k.stt(tn[:, jj * 128:(jj + 1) * 128], xT[:, cch, t8 * 128:(t8 + 1) * 128], finalg_sb[:, cch:cch + 1],
                                  rstd[:, t8 * 128:(t8 + 1) * 128], ALU.mult, ALU.mult, [bX, bR, bC], [bt_])
                            k.tr(p2[:, jj * 128:(jj + 1) * 128], tn[:, jj * 128:(jj + 1) * 128], ident[:], [bt_, bC], [bB[b2]])
                        k.act(yo[:, half * 512:(half + 1) * 512], p2, AF.Copy, [bB[b2]], [by_])
                    k.dma("pool", yg[g, t8 * 128:(t8 + 1) * 128, :], yo[:], [by_], [])
        k.barrier()
    print("instructions:", k.ninstr, {e: c for e, c in k.cnt.items()})
    return nc


_CACHE = {}


def _consts():
    f32 = np.float32
    c = {}
    c["ident"] = np.eye(128, dtype=f32)
    s = np.arange(64)[:, None]; t = np.arange(64)[None, :]
    tri = np.zeros((64, 2, 2, 64), f32)
    tri[:, 0, 0] = (s <= t); tri[:, 0, 1] = (s < t); tri[:, 1, 0] = (s >= t); tri[:, 1, 1] = (s > t)
    c["tri64"] = tri * f32(-RWKV_DECAY_SCALE)
    s1 = np.arange(128)[:, None]; t1 = np.arange(128)[None, :]
    tri128 = np.zeros((128, 2, 128), f32)
    tri128[:, 0] = (s1 <= t1); tri128[:, 1] = (s1 >= t1)
    c["tri128"] = tri128 / f32(16.0)
    c["mask128"] = tri128.copy()
    m64 = np.zeros((64, 2, 128), f32)
    m64[:, 0, 0:64] = (s < t); m64[:, 0, 64:128] = (s <= t); m64[:, 1, 0:64] = (s > t); m64[:, 1, 64:128] = (s >= t)
    c["mask64"] = m64
    mN = np.zeros((64, 2, 64), f32)
    mN[:, 0] = (t < s); mN[:, 1] = (t > s)
    c["maskN"] = mN
    bd = np.zeros((128, 128), f32); bd[0:64, 0:64] = 1; bd[64:128, 64:128] = 1
    c["bdones"] = bd
    T = 1024
    row = np.repeat(np.arange(T // 64), 64).astype(f32); col = np.tile(np.arange(64), T // 64).astype(f32)
    inv = (f32(10000.0) ** (-np.arange(16, dtype=f32) / f32(16))).astype(f32)
    ang = np.stack([row[:, None] * inv, col[:, None] * inv], axis=1).astype(f32)
    cos = np.cos(ang).astype(f32); sin = np.sin(ang).astype(f32)
    cos64 = np.stack([cos, cos], axis=2).reshape(T, 64)
    sin64 = np.stack([-sin, sin], axis=2).reshape(T, 64)
    c["ropecos"] = np.ascontiguousarray(cos64.reshape(8, 128, 64).transpose(1, 0, 2))
    c["ropesin"] = np.ascontiguousarray(sin64.reshape(8, 128, 64).transpose(1, 0, 2))
    return c


def kernel(**inp):
    f32 = np.float32
    n = 8
    A = lambda name: np.asarray(inp[name], f32)
    x_prompt = A("x_prompt"); x_sample = A("x_sample"); c = A("c"); c_ctx = A("c_ctx")

    def pc(v, nchunk):
        v = np.asarray(v, f32)
        lead = v.shape[:-1]
        v = v.reshape(lead + (nchunk, 128))
        return np.ascontiguousarray(np.moveaxis(v, -1, 0))

    def wblk(w, nblk):
        Ln, rows, cols = w.shape
        if cols < nblk * WB:
            w = np.concatenate([w, np.zeros((Ln, rows, nblk * WB - cols), f32)], axis=2)
        return np.ascontiguousarray(w.reshape(Ln, NCH, 128, nblk, WB).transpose(0, 3, 2, 1, 4))

    shared = dict(_consts())
    shared["modw"] = wblk(A("mod_w"), 12).reshape(DEPTH * 12, 128, NCH, WB)
    shared["modbT"] = pc(A("mod_b"), 24)
    shared["normgT"] = pc(A("norm_g"), NCH)
    shared["finalgT"] = pc(A("final_g"), NCH)
    shared["evw"] = wblk(A("ev_w_in"), 14)
    shared["evo"] = wblk(A("ev_w_out"), 4)
    shared["odw"] = wblk(A("od_w_in"), 13)
    shared["odo"] = wblk(A("od_w_out"), 4)
    shared["w2aug"] = np.ascontiguousarray(np.concatenate([A("rw_w2"), A("rw_w0")[:, :, None, :]], axis=2).transpose(0, 2, 1, 3))
    shared["a2aug"] = np.ascontiguousarray(np.concatenate([A("rw_a2"), A("rw_a0")[:, :, None, :]], axis=2).transpose(0, 2, 1, 3))
    shared["gw2aug"] = np.ascontiguousarray(np.concatenate([A("gla_w2"), A("gla_b")[:, :, None, :]], axis=2).transpose(0, 2, 1, 3))
    shared["ln256"] = np.ascontiguousarray(np.broadcast_to(A("gla_ln_g")[:, None, :], (2, 128, 256)))
    qk = np.stack([A("ev_qn_g"), A("ev_kn_g")], axis=1)
    shared["qkg"] = np.ascontiguousarray(np.broadcast_to(qk[:, None], (2, 128, 2, 64)))
    evs = np.concatenate([pc(A("ev_shift_mu"), 14), pc(A("rw_kk"), 4), pc(A("rw_ka"), 4), pc(A("rw_rk").reshape(2, 512), 4),
                          pc(A("rw_ln_g"), 4), pc(A("rw_ln_b"), 4)], axis=2)
    shared["evs"] = np.ascontiguousarray(evs.transpose(1, 0, 2))
    in_maps = []
    for core in range(n):
        sb = core % 4
        m = dict(shared)
        m["xg"] = np.ascontiguousarray(np.stack([x_prompt[4 * core:4 * core + 4].reshape(GT, D), x_sample[sb]], axis=0))
        m["condT"] = np.ascontiguousarray(np.stack([pc(c_ctx, NCH), pc(c[sb], NCH)], axis=-1))
        m["cak"] = np.ascontiguousarray(A("cache_attn_k")[sb].reshape(2, 512, 128))
        m["cav"] = np.ascontiguousarray(A("cache_attn_v")[sb].reshape(2, 512, 128))
        m["srf"] = np.ascontiguousarray(A("state_rwkv_fwd")[sb]); m["srb"] = np.ascontiguousarray(A("state_rwkv_bwd")[sb])
        m["sgf"] = np.ascontiguousarray(A("state_gla_fwd")[sb]); m["sgb"] = np.ascontiguousarray(A("state_gla_bwd")[sb])
        in_maps.append(m)
    if "nc" not in _CACHE:
        _CACHE["nc"] = build_program()
    res = run_bass_kernel_spmd(_CACHE["nc"], in_maps, core_ids=list(range(n)))
    R = res.results
    cat = lambda name: np.concatenate([R[i][name] for i in range(n)], axis=0)
    y_prompt = np.concatenate([R[i]["yg"][0].reshape(4, 256, D) for i in range(n)], axis=0)
    y_sample = np.stack([R[i]["yg"][1] for i in range(4)], axis=0)
    new_k = cat("nk").reshape(32, 2, 256, 2, 64)
    new_v = cat("nv").reshape(32, 2, 256, 2, 64)
    return (y_prompt, y_sample, new_k, new_v, cat("nrf"), cat("nrb"), cat("ngf"), cat("ngb"))
```

```python
import contextlib
import os
STOP = int(os.environ.get('KSTOP', '9'))
SUB = float(os.environ.get('KSUB', '9'))
KNS = int(os.environ.get('KNS', '99'))
KNC = int(os.environ.get('KNC', '99'))
KND = int(os.environ.get('KND', '2'))
import numpy as np
import concourse.bass as bass
import concourse.mybir as mybir
from concourse.bass_utils import run_bass_kernel_spmd

F32 = mybir.dt.float32
BF16 = mybir.dt.bfloat16
AF = mybir.ActivationFunctionType
ALU = mybir.AluOpType
AX = mybir.AxisListType

D = 1024
NCH = 8
DEPTH = 4
GT = 1024
EV_COLS = 3584
OD_COLS = 3104
EPS = 1e-6
GN_EPS = 64e-5
RWKV_DECAY_SCALE = 0.606531
WB = 256


class Buf:
    __slots__ = ("name", "w", "r")

    def __init__(self, name):
        self.name = name
        self.w = None
        self.r = {}


class KB:
    SEM_WRAP = 20000

    def __init__(self, nc):
        self.nc = nc
        self.es = contextlib.ExitStack()
        self.engs = {"pe": nc.tensor, "act": nc.scalar, "dve": nc.vector, "pool": nc.gpsimd, "sp": nc.sync}
        self.cnt = {e: 0 for e in self.engs}
        self.esems = {e: [] for e in self.engs}
        self.seen = {e: {} for e in self.engs}
        self.semobj = {}
        self.ndma_sems = {"sp": 12, "pool": 6, "act": 4}
        self.dma_sems = {}
        self.dma_rr = {q: 0 for q in self.ndma_sems}
        self.dma_cnt = {}
        self.nsem = 0
        self.ninstr = 0

    def new_sem(self, name):
        s = self.es.enter_context(self.nc.semaphore(name))
        self.semobj[name] = s
        return name

    def sb(self, name, shape, dtype):
        return self.es.enter_context(self.nc.sbuf_tensor(name, list(shape), dtype))

    def ps(self, name, shape, dtype=F32):
        return self.es.enter_context(self.nc.psum_tensor(name, list(shape), dtype))

    def _cur_sem(self, e):
        idx = self.cnt[e] // self.SEM_WRAP
        while len(self.esems[e]) <= idx:
            self.esems[e].append(self.new_sem(f"s_{e}_{len(self.esems[e])}"))
        return self.esems[e][idx]

    def _wait(self, e, tok):
        if tok is None:
            return
        key, val = tok
        if self.seen[e].get(key, 0) >= val:
            return
        self.engs[e].wait_ge(self.semobj[key], val)
        self.seen[e][key] = val

    def _deps(self, e, reads, writes, pe_acc=False):
        for b in reads:
            if b.w is not None:
                self._wait(e, b.w)
        for b in writes:
            if b.w is not None and not (pe_acc and e == "pe"):
                self._wait(e, b.w)
            for e2, tok in b.r.items():
                if e2 == e and e == "pe":
                    continue
                self._wait(e, tok)

    def _mark(self, e, tok, reads, writes):
        for b in reads:
            b.r[e] = tok
        for b in writes:
            b.w = tok
            b.r = {}

    def op(self, e, reads, writes, fn, pe_acc=False):
        self._deps(e, reads, writes, pe_acc)
        sem = self._cur_sem(e)
        ins = fn(self.engs[e])
        ins.then_inc(self.semobj[sem], 1)
        self.cnt[e] += 1
        self.ninstr += 1
        val = self.cnt[e] - (self.cnt[e] - 1) // self.SEM_WRAP * self.SEM_WRAP
        tok = (sem, val)
        self._mark(e, tok, reads, writes)
        return tok

    def dma(self, q, out, in_, reads, writes):
        if q not in self.dma_sems:
            self.dma_sems[q] = [self.new_sem(f"d_{q}_{i}") for i in range(self.ndma_sems[q])]
            for s in self.dma_sems[q]:
                self.dma_cnt[s] = 0
        sems = self.dma_sems[q]
        s = sems[self.dma_rr[q] % len(sems)]
        self.dma_rr[q] += 1
        if self.dma_cnt[s] > 0:
            self._wait(q, (s, 16 * self.dma_cnt[s]))
        self._deps(q, reads, writes)
        self.engs[q].dma_start(out=out, in_=in_).then_inc(self.semobj[s], 16)
        self.dma_cnt[s] += 1
        self.ninstr += 1
        tok = (s, 16 * self.dma_cnt[s])
        self._mark(s, tok, reads, writes)
        return tok

    def finish(self, bufs):
        for b in bufs:
            if b.w is not None:
                self._wait("sp", b.w)
        for q, sems in self.dma_sems.items():
            for s in sems:
                if self.dma_cnt[s] > 0:
                    self._wait("sp", (s, 16 * self.dma_cnt[s]))


    def barrier(self):
        toks = []
        for e in self.engs:
            if self.cnt[e] > 0:
                sem = self.esems[e][(self.cnt[e] - 1) // self.SEM_WRAP]
                val = self.cnt[e] - (self.cnt[e] - 1) // self.SEM_WRAP * self.SEM_WRAP
                toks.append((sem, val))
        for q, sems in self.dma_sems.items():
            for s in sems:
                if self.dma_cnt[s] > 0:
                    toks.append((s, 16 * self.dma_cnt[s]))
        for e in self.engs:
            for t in toks:
                self._wait(e, t)

    @contextlib.contextmanager
    def scope(self):
        sc = _Scope(self)
        try:
            yield sc
        finally:
            self.barrier()
            sc.es.close()

    def _pe_rows(self, ap):
        base = ap.base_partition()
        n = ap.shape[0]
        grp = set(range(base // 32, (base + n - 1) // 32 + 1))
        last = getattr(self, "_pe_last_grp", None)
        if last is not None and not (grp & last) and self.cnt["pe"] > 0:
            for sem_i in range(max(0, (self.cnt["pe"] - 1) // self.SEM_WRAP - 1), (self.cnt["pe"] - 1) // self.SEM_WRAP + 1):
                sem = self.esems["pe"][sem_i]
                if sem_i == (self.cnt["pe"] - 1) // self.SEM_WRAP:
                    val = self.cnt["pe"] - sem_i * self.SEM_WRAP
                else:
                    val = self.SEM_WRAP
                self._wait("pe", (sem, val))
        self._pe_last_grp = grp

    def mm(self, out, lhsT, rhs, R, W, start=True, stop=True):
        self._pe_rows(lhsT)
        return self.op("pe", R, W, lambda e: e.matmul(out, lhsT, rhs, start=start, stop=stop), pe_acc=True)

    def tr(self, out, in_, ident, R, W):
        self._pe_rows(in_)
        return self.op("pe", R, W, lambda e: e.transpose(out, in_, ident), pe_acc=True)

    def tt(self, out, in0, in1, op, R, W, e="dve"):
        return self.op(e, R, W, lambda g: g.tensor_tensor(out=out, in0=in0, in1=in1, op=op))

    def ts(self, out, in0, s1, op0, R, W, s2=None, op1=None, e="dve"):
        if op1 is None:
            return self.op(e, R, W, lambda g: g.tensor_scalar(out=out, in0=in0, scalar1=s1, scalar2=None, op0=op0))
        return self.op(e, R, W, lambda g: g.tensor_scalar(out=out, in0=in0, scalar1=s1, scalar2=s2, op0=op0, op1=op1))

    def stt(self, out, in0, scalar, in1, op0, op1, R, W, e="dve"):
        return self.op(e, R, W, lambda g: g.scalar_tensor_tensor(out=out, in0=in0, scalar=scalar, in1=in1, op0=op0, op1=op1))

    def act(self, out, in_, func, R, W, **kw):
        return self.op("act", R, W, lambda g: g.activation(out=out, in_=in_, func=func, **kw))

    def cp(self, out, in_, R, W, e="dve"):
        return self.op(e, R, W, lambda g: g.tensor_copy(out=out, in_=in_))

    def red(self, out, in_, R, W, op=None):
        return self.op("dve", R, W, lambda g: g.tensor_reduce(out=out, in_=in_, axis=AX.X, op=(op or ALU.add)))

    def recip(self, out, in_, R, W):
        return self.op("dve", R, W, lambda g: g.reciprocal(out=out, in_=in_))

    def ms(self, ap, val, W, e="dve"):
        return self.op(e, [], W, lambda g: g.memset(ap, val))


class _Scope:
    def __init__(self, k):
        self.k = k
        self.es = contextlib.ExitStack()

    def sb(self, name, shape, dtype):
        return self.es.enter_context(self.k.nc.sbuf_tensor(name, list(shape), dtype))


def build_program(nlayers=DEPTH, groups=(0, 1)):
    nc = bass.Bass("TRN2", target_bir_lowering=False)
    k = KB(nc)
    uid = [0]

    def un(name):
        uid[0] += 1
        return f"{name}_{uid[0]}"

    def din(name, shape):
        return nc.dram_tensor(name, list(shape), F32, kind="ExternalInput").ap()

    def dout(name, shape):
        return nc.dram_tensor(name, list(shape), F32, kind="ExternalOutput").ap()

    xg = din("xg", [2, GT, D])
    condT = din("condT", [128, NCH, 2])
    modw = din("modw", [DEPTH * 12, 128, NCH, WB])
    modbT = din("modbT", [128, DEPTH, 24])
    normgT = din("normgT", [128, DEPTH, NCH])
    finalgT = din("finalgT", [128, NCH])
    ident_d = din("ident", [128, 128])
    evw = din("evw", [2, 14, 128, NCH, WB])
    evo = din("evo", [2, 4, 128, NCH, WB])
    odw = din("odw", [2, 13, 128, NCH, WB])
    odo = din("odo", [2, 4, 128, NCH, WB])
    tri64_d = din("tri64", [64, 2, 2, 64])
    tri128_d = din("tri128", [128, 2, 128])
    mask64_d = din("mask64", [64, 2, 128])
    maskN_d = din("maskN", [64, 2, 64])
    mask128_d = din("mask128", [128, 2, 128])
    bd_d = din("bdones", [128, 128])
    cos_d = din("ropecos", [128, 8, 64])
    sin_d = din("ropesin", [128, 8, 64])
    w2aug_d = din("w2aug", [2, 65, 2, 512])
    a2aug_d = din("a2aug", [2, 65, 2, 512])
    gw2aug_d = din("gw2aug", [2, 17, 2, 512])
    ln256_d = din("ln256", [2, 128, 256])
    qkg_d = din("qkg", [2, 128, 2, 64])
    evs_d = din("evs", [2, 128, 34])
    cak = din("cak", [2, 512, 128])
    cav = din("cav", [2, 512, 128])
    srs_d = [din("srf", [2, 8, 64, 64]), din("srb", [2, 8, 64, 64])]
    sgs_d = [din("sgf", [2, 4, 128, 256]), din("sgb", [2, 4, 128, 256])]
    yg = dout("yg", [2, GT, D])
    nk_o = dout("nk", [4, 2, 256, 128])
    nv_o = dout("nv", [4, 2, 256, 128])
    nrs_o = [dout("nrf", [4, 2, 8, 64, 64]), dout("nrb", [4, 2, 8, 64, 64])]
    ngs_o = [dout("ngf", [4, 2, 4, 128, 256]), dout("ngb", [4, 2, 4, 128, 256])]

    with k.es:
        bC = Buf("const")

        def cload(name, src, shape, dtype=F32):
            t = k.sb(name, shape, dtype)
            k.dma("sp", t[:], src, [], [bC])
            return t

        ident = cload("ident_sb", ident_d[:, :], [128, 128])
        tri64 = cload("tri64_sb", tri64_d[:, :, :, :], [64, 2, 2, 64])
        tri128 = cload("tri128_sb", tri128_d[:, :, :], [128, 2, 128])
        mask64 = cload("mask64_sb", mask64_d[:, :, :], [64, 2, 128])
        maskN = cload("maskN_sb", maskN_d[:, :, :], [64, 2, 64])
        mask128 = cload("mask128_sb", mask128_d[:, :, :], [128, 2, 128])
        bdones = cload("bd_sb", bd_d[:, :], [128, 128])
        rope_tabs = {}
        cond_sb = cload("cond_sb", condT[:, :, :], [128, NCH, 2])
        modb_sb = cload("modb_sb", modbT[:, :, :], [128, DEPTH, 24])
        normg_sb = cload("normg_sb", normgT[:, :, :], [128, DEPTH, NCH])
        finalg_sb = cload("finalg_sb", finalgT[:, :], [128, NCH])
        ones_f = k.sb("ones_f", [128, 128], F32)
        k.ms(ones_f[:], 1.0 / D, [bC])
        ones_bf = k.sb("ones_bf", [128, 64], BF16)
        k.ms(ones_bf[:], 1.0, [bC])
        bP = Buf("params")
        w2aug = k.sb("w2aug_sb", [65, 2, 512], F32)
        a2aug = k.sb("a2aug_sb", [65, 2, 512], F32)
        gw2aug = k.sb("gw2aug_sb", [17, 2, 512], F32)
        ln256 = k.sb("ln256_sb", [128, 256], F32)
        qkg = k.sb("qkg_sb", [128, 2, 64], F32)
        evs = k.sb("evs_sb", [128, 34], F32)
        evx = k.sb("evx_sb", [128, 32], F32)

        PT = [k.ps(f"P{i}", [128, 1024], F32) for i in range(4)]
        bB = [Buf(f"bank{i}") for i in range(8)]
        rr = [0]

        def bank(i):
            return PT[i // 2][:, (i % 2) * 512:(i % 2) * 512 + 512]

        def nb(lo=0, hi=8):
            i = lo + rr[0] % (hi - lo)
            rr[0] += 1
            return i

        def nb2():
            i = (rr[0] % 8 + 1) // 2 * 2 % 8
            rr[0] += (i - rr[0] % 8) % 8 + 2
            return i

        scond = k.sb("scond", [128, NCH, 2], F32)
        k.act(scond[:], cond_sb[:], AF.Silu, [bC], [bC])
        modT = k.sb("modT", [128, DEPTH, 24, 2], F32)
        bM = Buf("mod")
        xT = k.sb("xT", [128, NCH, GT], F32)
        bX = Buf("xT")
        oT = k.sb("oT", [128, NCH, GT], BF16)
        bO = Buf("oT")
        bR = Buf("rstd")
        hcol = k.sb("hcol", [128, 3, NCH], F32)
        bH = Buf("hcol")

        with k.scope() as sc:
            wst = [sc.sb(un("mwst"), [128, NCH, WB], F32) for i in range(4)]
            bws = [Buf("mwst%d" % i) for i in range(4)]
            mrow = [sc.sb(un("mrow"), [2, WB], F32) for i in range(2)]
            bmrow = [Buf("mrow0"), Buf("mrow1")]
            wi = 0
            for L in range(nlayers):
                for blk in range(12):
                    w = wst[wi % 4]; bw = bws[wi % 4]
                    k.dma("sp" if wi % 2 == 0 else "pool", w[:], modw[L * 12 + blk], [], [bw])
                    bi = nb(); pm = bank(bi); bp = bB[bi]
                    for kc in range(NCH):
                        k.mm(pm[0:2, 0:WB], scond[:, kc, :], w[:, kc, :], [bw, bC], [bp], start=(kc == 0), stop=(kc == NCH - 1))
                    rt = mrow[wi % 2]; brt = bmrow[wi % 2]
                    k.act(rt[:, :], pm[0:2, 0:WB], AF.Copy, [bp], [brt])
                    bi2 = nb(); pm2 = bank(bi2); bp2 = bB[bi2]
                    for sub in range(2):
                        k.tr(pm2[:, sub * 2:sub * 2 + 2], rt[0:2, sub * 128:(sub + 1) * 128], ident[0:2, 0:2], [brt, bC], [bp2])
                    for sub in range(2):
                        ch = blk * 2 + sub
                        k.ts(modT[:, L, ch, :], pm2[:, sub * 2:sub * 2 + 2], modb_sb[:, L, ch:ch + 1], ALU.add, [bp2, bC], [bM])
                    wi += 1

        def compute_rstd(sc):
            rstd = sc.sb(un("rstd"), [128, GT], F32)
            sq = [sc.sb(un("sq"), [128, 512], F32) for i in range(2)]
            bsq = [Buf("sq0"), Buf("sq1")]
            for tt in range(GT // 512):
                bi = nb(); pn = bank(bi); bp = bB[bi]
                for c in range(NCH):
                    s_ = sq[c % 2]; bs_ = bsq[c % 2]
                    k.act(s_[:], xT[:, c, tt * 512:(tt + 1) * 512], AF.Square, [bX], [bs_])
                    k.mm(pn, ones_f[:], s_[:], [bs_, bC], [bp], start=(c == 0), stop=(c == NCH - 1))
                sl = rstd[:, tt * 512:(tt + 1) * 512]
                k.ts(sl, pn, EPS, ALU.add, [bp], [bR])
                k.act(sl, sl, AF.Sqrt, [bR], [bR])
                k.recip(sl, sl, [bR], [bR])
            return rstd

        def wblock_factory(sc):
            wst2 = [sc.sb(un("wst"), [128, NCH, WB], F32) for i in range(2)]
            bws2 = [Buf("wst0"), Buf("wst1")]
            wbf = [sc.sb(un("wbf"), [128, NCH, WB], BF16) for i in range(2)]
            bwb = [Buf("wbf0"), Buf("wbf1")]
            cnt = [0]

            def wblock(src):
                i = cnt[0] % 2
                cnt[0] += 1
                wst = wst2[i]; bws = bws2[i]
                k.dma("sp", wst[:], src, [], [bws])
                k.cp(wbf[i][:], wst[:], [bws], [bwb[i]], e="pool")
                return wbf[i], bwb[i]
            return wblock

        def proj_F(wblock, wsrc, blocks, hT, bHT, dest, msub=128, nsub=None):
            nsub = nsub or WB // msub
            for bi_, blk in enumerate(blocks):
                w, bw = wblock(wsrc[blk])
                for sub in range(nsub):
                    for tt in range(GT // 512):
                        bi = nb(); pb = bank(bi); bp = bB[bi]
                        for kc in range(NCH):
                            k.mm(pb[0:msub, :], w[:, kc, sub * msub:(sub + 1) * msub], hT[:, kc, tt * 512:(tt + 1) * 512],
                                 [bw, bHT], [bp], start=(kc == 0), stop=(kc == NCH - 1))
                        dest(bi_ * nsub + sub, tt, pb, bp)

        def proj_T(wblock, wsrc, blocks, hT, bHT, dest):
            for bi_, blk in enumerate(blocks):
                w, bw = wblock(wsrc[blk])
                for t8 in range(GT // 128):
                    bi = nb(); pb = bank(bi); bp = bB[bi]
                    for kc in range(NCH):
                        k.mm(pb[:, 0:WB], hT[:, kc, t8 * 128:(t8 + 1) * 128], w[:, kc, :],
                             [bw, bHT], [bp], start=(kc == 0), stop=(kc == NCH - 1))
                    dest(bi_, t8, pb, bp)

        def make_h(sc, L, g):
            hT = sc.sb(un("hT"), [128, NCH, GT], BF16)
            bHT = Buf("hT")
            k.stt(hcol[:, 0, :], modT[:, L, 8:16, g], 1.0, normg_sb[:, L, :], ALU.add, ALU.mult, [bM, bC], [bH])
            k.cp(hcol[:, 1, :], modT[:, L, 0:8, g], [bM], [bH])
            k.cp(hcol[:, 2, :], modT[:, L, 16:24, g], [bM], [bH])
            rstd = compute_rstd(sc)
            tmp = [sc.sb(un("htmp"), [128, 512], F32) for i in range(2)]
            btmp = [Buf("htmp0"), Buf("htmp1")]
            i = 0
            for tt in range(GT // 512):
                for c in range(NCH):
                    t_ = tmp[i % 2]; bt_ = btmp[i % 2]; i += 1
                    k.tt(t_[:], xT[:, c, tt * 512:(tt + 1) * 512], rstd[:, tt * 512:(tt + 1) * 512], ALU.mult, [bX, bR], [bt_])
                    k.ts(hT[:, c, tt * 512:(tt + 1) * 512], t_[:], hcol[:, 0, c:c + 1], ALU.mult, [bt_, bH], [bHT],
                         s2=hcol[:, 1, c:c + 1], op1=ALU.add)
            return hT, bHT

        def out_proj(wsrc):
            with k.scope() as sc:
                wblock = wblock_factory(sc)

                def dest(cc, tt, pb, bp):
                    sl = xT[:, cc, tt * 512:(tt + 1) * 512]
                    k.stt(sl, pb, hcol[:, 2, cc:cc + 1], sl, ALU.mult, ALU.add, [bp, bH, bX], [bX])
                proj_F(wblock, wsrc, range(4), oT, bO, dest)

        def headnorm(sc_t, pb_view, nh, gidx, out3, R, W, bT=None):
            bT = bT or bT0
            sqt, ssq = sc_t
            k.act(sqt[:, 0:nh * 64], pb_view.rearrange("p h d -> p (h d)"), AF.Square, R, [bT])
            k.red(ssq[:, 0:nh], sqt[:, 0:nh * 64].rearrange("p (h d) -> p h d", h=nh), [bT], [bT])
            k.ts(ssq[:, 0:nh], ssq[:, 0:nh], 1.0 / 64, ALU.mult, [bT], [bT], s2=EPS, op1=ALU.add)
            k.act(ssq[:, 0:nh], ssq[:, 0:nh], AF.Sqrt, [bT], [bT])
            k.recip(ssq[:, 0:nh], ssq[:, 0:nh], [bT], [bT])
            k.tt(out3, pb_view, ssq[:, 0:nh].unsqueeze(2).to_broadcast([128, nh, 64]), ALU.mult, R + [bT], W)
            k.tt(out3, out3, qkg[:, gidx, :].unsqueeze(1).to_broadcast([128, nh, 64]), ALU.mult, W + [bP], W)

        bT0 = Buf("tmpT")

        def rope(x3, nh, t8, t1, t2, R, bT=None):
            bT = bT or bT0
            cosb = rope_tabs["cos"][:, t8, :].unsqueeze(1).to_broadcast([128, nh, 64])
            k.tt(t1[:, 0:nh, :], x3, cosb, ALU.mult, R + [bC], [bT])
            x5 = x3.rearrange("p h (a q f) -> p h a q f", a=2, q=2)
            t5 = t2[:, 0:nh, :].rearrange("p h (a q f) -> p h a q f", a=2, q=2)
            s4 = rope_tabs["sin"][:, t8, :].rearrange("p (a q f) -> p a q f", a=2, q=2)
            for q_ in range(2):
                k.tt(t5[:, :, :, q_, :], x5[:, :, :, 1 - q_, :],
                     s4[:, :, q_, :].unsqueeze(1).to_broadcast([128, nh, 2, 16]), ALU.mult, R + [bC], [bT])
            k.tt(x3, t1[:, 0:nh, :], t2[:, 0:nh, :], ALU.add, [bT], R)

        def even_layer(g, j, L):
            nseq, T = (4, 256) if g == 0 else (1, 1024)
            TP = T + 2
            koff = 0 if g == 0 else 512
            SK = GT + koff
            k.dma("sp", w2aug[:], w2aug_d[j], [], [bP])
            k.dma("sp", a2aug[:], a2aug_d[j], [], [bP])
            k.dma("sp", qkg[:], qkg_d[j], [], [bP])
            k.dma("sp", evs[:], evs_d[j], [], [bP])
            k.ts(evx[:, 0:14], evs[:, 0:14], 0.5, ALU.mult, [bP], [bP])
            k.ts(evx[:, 14:28], evs[:, 0:14], -1.0, ALU.mult, [bP], [bP], s2=1.0, op1=ALU.add)
            k.ts(evx[:, 28:32], evs[:, 18:22], -1.0, ALU.mult, [bP], [bP], s2=1.0, op1=ALU.add)
            with k.scope() as scL:
                gbT = scL.sb(un("gbT"), [128, 4, GT], BF16); bGB = Buf("gbT")
                zraw = scL.sb(un("zraw"), [128, 14, nseq * TP], BF16); bZ = Buf("zraw")
                k.ms(zraw[:], 0.0, [bZ])
                zr4 = zraw[:].rearrange("p c (s t) -> p c s t", s=nseq)
                with k.scope() as scA:
                    gaT = scA.sb(un("gaT"), [128, 4, GT], BF16); bGA = Buf("gaT")
                    qT = scA.sb(un("qT"), [64, 8, GT], BF16); bQ = Buf("qT")
                    kT = scA.sb(un("kT"), [64, 2, SK], BF16); bK = Buf("kT")
                    vtok = scA.sb(un("vtok"), [128, SK // 128, 128], BF16); bV = Buf("vtok")
                    if g == 1:
                      with k.scope() as sc:
                          ck = sc.sb(un("ck"), [128, 4, 128], F32); bCK = Buf("ck")
                          k.dma("sp", ck[:], cak[j].rearrange("(i p) f -> p i f", p=128), [], [bCK])
                          for i in range(4):
                              b2 = nb(); p2 = bank(b2)
                              for hh in range(2):
                                  k.tr(p2[0:64, hh * 128:(hh + 1) * 128], ck[:, i, hh * 64:(hh + 1) * 64], ident[:], [bCK, bC], [bB[b2]])
                              k.cp(kT[:, :, i * 128:(i + 1) * 128],
                                   p2[0:64, 0:256].rearrange("p (h t) -> p h t", h=2), [bB[b2]], [bK])
                          cv = sc.sb(un("cv"), [128, 4, 128], F32); bCV = Buf("cv")
                          k.dma("sp", cv[:], cav[j].rearrange("(i p) f -> p i f", p=128), [], [bCV])
                          k.cp(vtok[:, 0:4, :], cv[:], [bCV], [bV])

                    with k.scope() as sc:
                        hT, bHT = make_h(sc, L, g)
                        wblock = wblock_factory(sc)
                        if g == 1:
                            rope_tabs["cos"] = sc.sb(un("cos_sb"), [128, 8, 64], F32)
                            rope_tabs["sin"] = sc.sb(un("sin_sb"), [128, 8, 64], F32)
                            k.dma("sp", rope_tabs["cos"][:], cos_d[:, :, :], [], [bC])
                            k.dma("sp", rope_tabs["sin"][:], sin_d[:, :, :], [], [bC])
                        print("S1 sbuf remaining", nc.sbuf_bytes_remaining)
                        sqtL = [sc.sb(un("sqt"), [128, 256], F32) for _ in range(2)]
                        ssqL = [sc.sb(un("ssq"), [128, 4], F32) for _ in range(2)]
                        qnL = [sc.sb(un("qn"), [128, 4, 64], F32) for _ in range(2)]; bQNL = [Buf("qn0"), Buf("qn1")]
                        r1L = [sc.sb(un("r1"), [128, 4, 64], F32) for _ in range(2)]
                        r2L = [sc.sb(un("r2"), [128, 4, 64], F32) for _ in range(2)]
                        bTL = [Buf("tmpT0"), Buf("tmpT1")]
                        kvo = [sc.sb(un("kvo"), [128, 256], F32) for i in range(2)]
                        bKVO = [Buf("kvo0"), Buf("kvo1")]

                        def dest_q(bi_, t8, pb, bp):
                            ix = t8 % 2
                            sqt, ssq, qn, r1, r2, bQN, bTx = sqtL[ix], ssqL[ix], qnL[ix], r1L[ix], r2L[ix], bQNL[ix], bTL[ix]
                            headnorm((sqt, ssq), pb[:, 0:256].rearrange("p (h d) -> p h d", h=4), 4, 0, qn[:], [bp], [bQN], bT=bTx)
                            if g == 1:
                                rope(qn[:], 4, t8, r1, r2, [bQN], bT=bTx)
                            b2 = nb(); p2 = bank(b2)
                            for hh in range(4):
                                k.tr(p2[0:64, hh * 128:(hh + 1) * 128], qn[:, hh, :], ident[:], [bQN, bC], [bB[b2]])
                            k.cp(qT[:, bi_ * 4:bi_ * 4 + 4, t8 * 128:(t8 + 1) * 128],
                                 p2[0:64, :].rearrange("p (h t) -> p h t", h=4), [bB[b2]], [bQ])
                        proj_T(wblock, evw[j], [0, 1], hT, bHT, dest_q)

                        def dest_kv(bi_, t8, pb, bp):
                            ko = kvo[t8 % 2]; bko = bKVO[t8 % 2]
                            kn3 = ko[:, 0:128].rearrange("p (h d) -> p h d", h=2)
                            ix = t8 % 2
                            sqt, ssq, r1, r2, bTx = sqtL[ix], ssqL[ix], r1L[ix], r2L[ix], bTL[ix]
                            headnorm((sqt, ssq), pb[:, 0:128].rearrange("p (h d) -> p h d", h=2), 2, 1, kn3, [bp], [bko], bT=bTx)
                            k.cp(ko[:, 128:256], pb[:, 128:256], [bp], [bko])
                            k.cp(vtok[:, koff // 128 + t8, :], pb[:, 128:256], [bp], [bV])
                            if g == 0:
                                b_ = t8 // 2; t0 = (t8 % 2) * 128
                                k.dma("pool", nk_o[b_, j, t0:t0 + 128, :], ko[:, 0:128], [bko], [])
                                k.dma("pool", nv_o[b_, j, t0:t0 + 128, :], ko[:, 128:256], [bko], [])
                            else:
                                rope(kn3, 2, t8, r1, r2, [bko], bT=bTx)
                            b2 = nb(); p2 = bank(b2)
                            for hh in range(2):
                                k.tr(p2[0:64, hh * 128:(hh + 1) * 128], kn3[:, hh, :], ident[:], [bko, bC], [bB[b2]])
                            k.cp(kT[:, :, koff + t8 * 128:koff + (t8 + 1) * 128],
                                 p2[0:64, 0:256].rearrange("p (h t) -> p h t", h=2), [bB[b2]], [bK])
                        proj_T(wblock, evw[j], [2], hT, bHT, dest_kv)

                        def dest_ga(cc, tt, pb, bp):
                            k.act(gaT[:, cc, tt * 512:(tt + 1) * 512], pb, AF.Silu, [bp], [bGA])
                        proj_F(wblock, evw[j], [3, 4], hT, bHT, dest_ga)

                        def dest_zb(cc, tt, pb, bp):
                            if g == 0:
                                o_z = zr4[:, cc, 2 * tt:2 * tt + 2, 1:T + 1]; i_z = pb.rearrange("p (s t) -> p s t", s=2)
                            else:
                                o_z = zr4[:, cc, 0, 1 + tt * 512:1 + (tt + 1) * 512]; i_z = pb
                            if (cc + tt) % 2 == 0:
                                k.cp(o_z, i_z, [bp], [bZ])
                            else:
                                k.act(o_z, i_z, AF.Copy, [bp], [bZ])
                        proj_F(wblock, evw[j], range(5, 12), hT, bHT, dest_zb)

                        def dest_gb(cc, tt, pb, bp):
                            k.act(gbT[:, cc, tt * 512:(tt + 1) * 512], pb, AF.Silu, [bp], [bGB])
                        proj_F(wblock, evw[j], [12, 13], hT, bHT, dest_gb)

                    with (k.scope() if STOP >= 2 else contextlib.nullcontext()) as sc:
                      if STOP >= 2:
                            pexp = [sc.sb(un("pexp"), [128, 512], BF16) for i in range(4)]
                            bPE = [Buf("pexp%d" % i) for i in range(4)]
                            recL = [sc.sb(un("rec"), [64, 512], F32) for i in range(2)]; bRecL = [Buf("rec0"), Buf("rec1")]
                            oaL = [sc.sb(un("oa"), [128, 512], F32) for i in range(2)]; bOAL = [Buf("oa0"), Buf("oa1")]
                            QB = min(T, 512)
                            ie = 0; blk_i = 0
                            for s in range(nseq):
                                kbase = s * T if g == 0 else 0
                                nsc = (T + koff) // 128
                                for h in range(8):
                                    kv = h // 4
                                    hb = (h % 2) * 64
                                    for qb in range(T // QB):
                                        q0 = s * T + qb * QB
                                        bo_, bs_ = (6, 7) if blk_i % 2 == 0 else (4, 5)
                                        po = bank(bo_); psm = bank(bs_)
                                        rec = recL[blk_i % 2]; bRec = bRecL[blk_i % 2]; oa = oaL[blk_i % 2]; bOA = bOAL[blk_i % 2]
                                        blk_i += 1
                                        for sc_ in range(nsc):
                                            kpos = kbase + sc_ * 128
                                            bi = nb(0, 4); pb = bank(bi); bp = bB[bi]
                                            k.mm(pb[:, 0:QB], kT[:, kv, kpos:kpos + 128], qT[:, h, q0:q0 + QB], [bK, bQ], [bp])
                                            pe_ = pexp[ie % 4]; bpe = bPE[ie % 4]; ie += 1
                                            k.act(pe_[:, 0:QB], pb[:, 0:QB], AF.Exp, [bp], [bpe], scale=0.125)
                                            k.mm(po[0:64, 0:QB], vtok[:, kpos // 128, kv * 64:(kv + 1) * 64], pe_[:, 0:QB],
                                                 [bV, bpe], [bB[bo_]], start=(sc_ == 0), stop=(sc_ == nsc - 1))
                                            k.mm(psm[0:64, 0:QB], ones_bf[:, :], pe_[:, 0:QB],
                                                 [bC, bpe], [bB[bs_]], start=(sc_ == 0), stop=(sc_ == nsc - 1))
                                        k.recip(rec[:, 0:QB], psm[0:64, 0:QB], [bB[bs_]], [bRec])
                                        k.tt(oa[hb:hb + 64, 0:QB], po[0:64, 0:QB], rec[:, 0:QB], ALU.mult, [bB[bo_], bRec], [bOA])
                                        k.tt(oT[hb:hb + 64, h // 2, q0:q0 + QB], oa[hb:hb + 64, 0:QB],
                                             gaT[hb:hb + 64, h // 2, q0:q0 + QB], ALU.mult, [bOA, bGA], [bO])

                with k.scope() as sc:
                    C = 64
                    if STOP < 3:
                        raise_skip = True
                    else:
                        raise_skip = False
                    nchunk = T // C
                    f = lambda name, shape, dt=F32: sc.sb(un(name), shape, dt)
                    yf = f("yf", [128, nseq * nchunk // 2, 512], BF16); bYF = Buf("yf")
                    ST = [f("ST", [128, 4, 64]) for d in range(2)]; bST = [Buf("ST0"), Buf("ST1")]
                    zsP = [f("zs", [128, 14, C]) for p_ in range(2)]
                    zt1 = f("zt1", [128, 14, C])
                    twa = [f("tw", [65, C]) for d in range(2)]
                    ala = [f("al", [65, C]) for d in range(2)]
                    bW = Buf("rwtmp")
                    Ls = f("Ls", [64, 512])
                    gam = f("gam", [128, 4, C]); gamp = f("gamp", [128, 4, C]); ginv = f("ginv", [128, 4, C])
                    av = [f("av", [128, 4, C]) for d in range(2)]
                    kkr = f("kkr", [128, 4, C]); ksq = f("ksq", [128, 4, C]); kk = f("kk", [128, 4, C])
                    kdP = [[f("kd", [128, 4, C]) for d in range(2)] for p_ in range(2)]
                    tmp4 = f("tmp4", [128, 4, C]); tmp5 = f("tmp5", [128, 4, C])
                    Bn = {n_: Buf(n_) for n_ in ["zs", "zt1", "ala0", "ala1", "twa0", "twa1", "av0", "av1", "kkr", "ksq", "kk", "kd0", "kd1", "tmp4", "tmp5", "Ls", "gam", "gamp", "ginv", "AR", "KBt", "Gm1", "Gm2", "Nm0", "Nm1", "Am0", "Am1", "Vt", "Us0", "Us1", "KBT", "ysum", "ysq", "yst", "bon", "obt", "tmp6"] + [x + str(p_) for p_ in range(2) for x in ["zs", "AR", "KBt", "Vt", "Gm1", "Gm2", "Nmi", "gl", "kd0_", "kd1_"]]}
                    ARP = [f("AR", [128, 4, 2, C]) for p_ in range(2)]; KBtP = [f("KBt", [128, 4, 2, C]) for p_ in range(2)]
                    Gm1P = [f("Gm1", [64, 8, 128]) for p_ in range(2)]; Gm2P = [f("Gm2", [64, 8, 128]) for p_ in range(2)]; Nm = [f("Nm", [64, 8, 64]) for i in range(2)]
                    NmiP = [f("Nmi", [64, 8, 64]) for p_ in range(2)]; glP = [f("gl", [128, 4, 1]) for p_ in range(2)]; tmp6 = f("tmp6", [128, 4, C])
                    Am = [f("Am", [64, 8, 64]) for i in range(2)]
                    VtP = [f("Vt", [64, 512]) for p_ in range(2)]; Us = [f("Us", [64, 512]) for i in range(2)]
                    KBT = f("KBT", [64, 4, 2, 128])
                    ysum = f("ysum", [64, 8, 64]); ysq = f("ysq", [64, 8, 64]); yst = f("yst", [64, 16])
                    bon = f("bon", [128, 4, C]); obt = f("obt", [128, 4, C])
                    sto = f("sto", [64, 4, 128]); bSTO = Buf("sto")
                    sld = f("sld", [64, 8, 64]); bSLD = Buf("sld")
                    for d in range(2):
                        k.ms(twa[d][:], 1.0, [Bn["twa%d" % d]])
                        k.ms(ala[d][:], 1.0, [Bn["ala%d" % d]])

                    units = []
                    for s in range(min(nseq, KNS) if STOP >= 3 else 0):
                        for d in range(KND):
                            order = list(range(nchunk) if d == 0 else range(nchunk - 1, -1, -1))[:KNC]
                            for ci_, c in enumerate(order):
                                units.append((s, d, c, ci_ == 0, ci_ == len(order) - 1))
                    HO = [0, 2, 4, 6, 1, 3, 5, 7]
                    HOr = [1, 3, 5, 7, 0, 2, 4, 6]
                    rrA = [0]; rrB = [0]

                    def nbA():
                        rrA[0] += 1
                        return rrA[0] % 5

                    def nbB():
                        rrB[0] += 1
                        return 5 + rrB[0] % 3

                    def stageA(u, p):
                        s, d, c, first, last = u
                        tcol = s * TP + c * C
                        gt0 = s * T + c * C
                        yield
                        k.tt(zt1[:], zraw[:, :, tcol:tcol + C], zraw[:, :, tcol + 2:tcol + 2 + C], ALU.add, [bZ], [Bn["zt1"]], e="pool")
                        k.tt(zt1[:], zt1[:], evx[:, 0:14].unsqueeze(2).to_broadcast([128, 14, C]), ALU.mult, [Bn["zt1"], bP], [Bn["zt1"]], e="pool")
                        k.tt(zsP[p][:], zraw[:, :, tcol + 1:tcol + 1 + C], evx[:, 14:28].unsqueeze(2).to_broadcast([128, 14, C]),
                             ALU.mult, [bZ, bP], [Bn["zs%d" % p]])
                        k.tt(zsP[p][:], zsP[p][:], zt1[:], ALU.add, [Bn["zs%d" % p], Bn["zt1"]], [Bn["zs%d" % p]])
                        r_ = zsP[p][:, 0:4, :]; kraw = zsP[p][:, 4:8, :]; vv = zsP[p][:, 8:12, :]
                        dirs = [d] if d == 0 else [0, 1]
                        yield
                        for dd in dirs:
                            bal = Bn["ala%d" % dd]; bav = Bn["av%d" % dd]; bkd = Bn["kd%d_%d" % (dd, p)]
                            k.cp(ala[dd][0:64, :], zsP[p][dd * 64:(dd + 1) * 64, 13, :], [Bn["zs%d" % p]], [bal])
                            b2 = nbA(); p2 = bank(b2)
                            for cp in range(4):
                                k.mm(p2[:, cp * C:(cp + 1) * C], a2aug[:, dd, cp * 128:(cp + 1) * 128], ala[dd][:, :],
                                     [bal, bP], [bB[b2]])
                            k.act(av[dd][:].rearrange("p c t -> p (c t)"), p2[:, 0:4 * C], AF.Sigmoid, [bB[b2]], [bav])
                            k.tt(tmp4[:], av[dd][:], evs[:, 18:22].unsqueeze(2).to_broadcast([128, 4, C]), ALU.mult, [bav, bP], [Bn["tmp4"]], e="pool")
                            k.tt(tmp4[:], tmp4[:], evx[:, 28:32].unsqueeze(2).to_broadcast([128, 4, C]), ALU.add, [Bn["tmp4"], bP], [Bn["tmp4"]], e="pool")
                            k.tt(kdP[p][dd][:], kraw, tmp4[:], ALU.mult, [Bn["zs%d" % p], Bn["tmp4"]], [bkd], e="pool")
                        yield
                        k.tt(kkr[:], kraw, evs[:, 14:18].unsqueeze(2).to_broadcast([128, 4, C]), ALU.mult, [Bn["zs%d" % p], bP], [Bn["kkr"]])
                        k.tt(ksq[:], kkr[:], kkr[:], ALU.mult, [Bn["kkr"]], [Bn["ksq"]])
                        b2 = nbA(); p2 = bank(b2)
                        k.mm(p2[:, 0:4 * C], bdones[:, :], ksq[:].rearrange("p c t -> p (c t)"), [Bn["ksq"], bC], [bB[b2]])
                        k.ts(ksq[:].rearrange("p c t -> p (c t)"), p2[:, 0:4 * C], 1e-12, ALU.add, [bB[b2]], [Bn["ksq"]])
                        k.act(ksq[:], ksq[:], AF.Sqrt, [Bn["ksq"]], [Bn["ksq"]])
                        k.recip(ksq[:], ksq[:], [Bn["ksq"]], [Bn["ksq"]])
                        k.tt(kk[:], kkr[:], ksq[:], ALU.mult, [Bn["kkr"], Bn["ksq"]], [Bn["kk"]])
                        yield
                        btw = Bn["twa%d" % d]
                        k.act(twa[d][0:64, :], zsP[p][d * 64:(d + 1) * 64, 12, :], AF.Tanh, [Bn["zs%d" % p]], [btw])
                        b2 = nbA(); p2 = bank(b2)
                        k.mm(p2[0:64, :], twa[d][:, :], w2aug[:, d, :], [btw, bP], [bB[b2]])
                        k.act(Ls[:], p2[0:64, :], AF.Sigmoid, [bB[b2]], [Bn["Ls"]])
                        b2 = nbA(); p2 = bank(b2)
                        for cp in range(4):
                            k.mm(p2[:, cp * 128:(cp + 1) * 128], Ls[:, cp * 128:(cp + 1) * 128],
                                 tri64[:, d, :, :].rearrange("p a t -> p (a t)"), [Bn["Ls"], bC], [bB[b2]])
                        p4 = p2.rearrange("p (c a t) -> p c a t", c=4, a=2)
                        k.act(gamp[:], p4[:, :, 1, :], AF.Exp, [bB[b2]], [Bn["gamp"]])
                        k.act(ginv[:], p4[:, :, 0, :], AF.Exp, [bB[b2]], [Bn["ginv"]], scale=-1.0)
                        k.act(gam[:], p4[:, :, 0, :], AF.Exp, [bB[b2]], [Bn["gam"]])
                        k.cp(glP[p][:], (gam[:, :, C - 1:C] if d == 0 else gam[:, :, 0:1]), [Bn["gam"]], [Bn["gl%d" % p]])
                        yield
                        bav = Bn["av%d" % d]; bkd = Bn["kd%d_%d" % (d, p)]
                        k.stt(ARP[p][:, :, 0, :], kk[:], -1.0, gamp[:], ALU.mult, ALU.mult, [Bn["kk"], Bn["gamp"]], [Bn["AR%d" % p]])
                        k.tt(KBtP[p][:, :, 0, :], kdP[p][d][:], ginv[:], ALU.mult, [bkd, Bn["ginv"]], [Bn["KBt%d" % p]])
                        k.tt(tmp5[:], kk[:], av[d][:], ALU.mult, [Bn["kk"], bav], [Bn["tmp5"]])
                        k.tt(KBtP[p][:, :, 1, :], tmp5[:], ginv[:], ALU.mult, [Bn["tmp5"], Bn["ginv"]], [Bn["KBt%d" % p]])
                        k.tt(ARP[p][:, :, 1, :], r_, gam[:], ALU.mult, [Bn["zs%d" % p], Bn["gam"]], [Bn["AR%d" % p]])
                        yield
                        b2 = nbA(); p2 = bank(b2)
                        for cp in range(4):
                            k.tr(p2[0:64, cp * 128:(cp + 1) * 128], zsP[p][:, 8 + cp, :], ident[:], [Bn["zs%d" % p], bC], [bB[b2]])
                        k.act(VtP[p][:], p2[0:64, :], AF.Copy, [bB[b2]], [Bn["Vt%d" % p]])
                        yield
                        g1, g2, gn = 0, 2, 4
                        for h in HO:
                            cp = h // 2; hb = (h % 2) * 64
                            arh = ARP[p][hb:hb + 64, cp, :, :].rearrange("p a t -> p (a t)")
                            k.mm(bank(g1 + h // 4)[0:64, (h % 4) * 128:(h % 4 + 1) * 128], KBtP[p][hb:hb + 64, cp, 0, :], arh,
                                 [Bn["KBt%d" % p], Bn["AR%d" % p]], [bB[g1 + h // 4]])
                            k.mm(bank(g2 + h // 4)[0:64, (h % 4) * 128:(h % 4 + 1) * 128], KBtP[p][hb:hb + 64, cp, 1, :], arh,
                                 [Bn["KBt%d" % p], Bn["AR%d" % p]], [bB[g2 + h // 4]])
                            k.mm(bank(gn)[0:64, h * 64:(h + 1) * 64], ARP[p][hb:hb + 64, cp, 0, :], KBtP[p][hb:hb + 64, cp, 1, :],
                                 [Bn["KBt%d" % p], Bn["AR%d" % p]], [bB[gn]])
                        yield
                        m64b = mask64[:, d, :].unsqueeze(1).to_broadcast([64, 4, 128])
                        for hf in range(2):
                            k.tt(Gm1P[p][:, hf * 4:hf * 4 + 4, :], bank(g1 + hf)[0:64, :].rearrange("p (h t) -> p h t", h=4), m64b,
                                 ALU.mult, [bB[g1 + hf], bC], [Bn["Gm1%d" % p]])
                        for hf in range(2):
                            k.tt(Gm2P[p][:, hf * 4:hf * 4 + 4, :], bank(g2 + hf)[0:64, :].rearrange("p (h t) -> p h t", h=4), m64b,
                                 ALU.mult, [bB[g2 + hf], bC], [Bn["Gm2%d" % p]])
                        k.tt(NmiP[p][:], bank(gn)[0:64, :].rearrange("p (h t) -> p h t", h=8),
                             maskN[:, d, :].unsqueeze(1).to_broadcast([64, 8, 64]), ALU.mult, [bB[gn], bC], [Bn["Nmi%d" % p]])

                        yield

                    def stageB(u, p, pull):
                        s, d, c, first, last = u
                        tcol = s * TP + c * C
                        gt0 = s * T + c * C
                        r_ = zsP[p][:, 0:4, :]; vv = zsP[p][:, 8:12, :]
                        if first:
                            if g == 0:
                                k.ms(ST[d][:], 0.0, [bST[d]])
                            else:
                                k.dma("sp", sld[:], srs_d[d][j].rearrange("h v k -> v h k"), [], [bSLD])
                                b2 = nbB(); p2 = bank(b2)
                                for cp in range(4):
                                    k.tr(p2[:, cp * 64:(cp + 1) * 64], sld[:, 2 * cp:2 * cp + 2, :].rearrange("v h k -> v (h k)"),
                                         ident[0:64, 0:64], [bSLD, bC], [bB[b2]])
                                k.cp(ST[d][:], p2[:, 0:256].rearrange("p (c v) -> p c v", c=4), [bB[b2]], [bST[d]])

                        b2 = nbB(); p2 = bank(b2)
                        for h in HOr:
                            cp = h // 2; hb = (h % 2) * 64
                            k.mm(p2[0:64, h * 64:(h + 1) * 64], ARP[p][hb:hb + 64, cp, 0, :], ST[d][hb:hb + 64, cp, :],
                                 [Bn["AR%d" % p], bST[d]], [bB[b2]])
                        b2b = nbB(); p2b = bank(b2b)
                        for h in range(8):
                            k.mm(p2b[0:64, h * 64:(h + 1) * 64], Gm1P[p][:, h, 0:64], VtP[p][:, h * 64:(h + 1) * 64],
                                 [Bn["Gm1%d" % p], Bn["Vt%d" % p]], [bB[b2b]])
                        k.act(Us[0][:], p2[0:64, :], AF.Copy, [bB[b2]], [Bn["Us0"]])
                        k.tt(Us[0][:], Us[0][:], p2b[0:64, :], ALU.add, [Bn["Us0"], bB[b2b]], [Bn["Us0"]])
                        ui = 0; ai = 0
                        for jn in range(6):
                            bA = Bn["Am%d" % ai] if jn else Bn["Gm2%d" % p]; bN = Bn["Nm%d" % ai] if jn else Bn["Nmi%d" % p]
                            bA2 = Bn["Am%d" % (1 - ai)]; bN2 = Bn["Nm%d" % (1 - ai)]
                            Acur = (lambda h, ai=ai: Am[ai][:, h, :]) if jn else (lambda h: Gm2P[p][:, h, 0:64])
                            Ncur = (lambda h, ai=ai: Nm[ai][:, h, :]) if jn else (lambda h: NmiP[p][:, h, :])
                            pull()
                            bU = Bn["Us%d" % ui]; bU2 = Bn["Us%d" % (1 - ui)]
                            b2 = nbB(); p2 = bank(b2)
                            for h in range(8):
                                k.mm(p2[0:64, h * 64:(h + 1) * 64], Acur(h), Us[ui][:, h * 64:(h + 1) * 64], [bA, bU], [bB[b2]])
                            if jn < 5:
                                b3 = nbB(); p3 = bank(b3)
                                for h in range(8):
                                    k.mm(p3[0:64, h * 64:(h + 1) * 64], Ncur(h), Acur(h), [bN, bA], [bB[b3]])
                                if jn < 4:
                                    b4 = nbB(); p4_ = bank(b4)
                                    for h in range(8):
                                        k.mm(p4_[0:64, h * 64:(h + 1) * 64], Acur(h), Ncur(h), [bA, bN], [bB[b4]])
                            k.tt(Us[1 - ui][:], Us[ui][:], p2[0:64, :], ALU.add, [bU, bB[b2]], [bU2])
                            ui = 1 - ui
                            pull()
                            if jn < 5:
                                k.act(Am[1 - ai][:].rearrange("p h t -> p (h t)"), p3[0:64, :], AF.Copy, [bB[b3]], [bA2])
                                if jn < 4:
                                    k.act(Nm[1 - ai][:].rearrange("p h t -> p (h t)"), p4_[0:64, :], AF.Copy, [bB[b4]], [bN2])
                                ai = 1 - ai
                        U = Us[ui]; bU = Bn["Us%d" % ui]
                        by = nbB(); py = bank(by)
                        by0 = nbB(); py0 = bank(by0)
                        for h in range(8):
                            o_ = py[0:64, h * 64:(h + 1) * 64]
                            k.mm(o_, Gm2P[p][:, h, 64:128], U[:, h * 64:(h + 1) * 64], [Bn["Gm2%d" % p], bU], [bB[by]], start=True, stop=False)
                            k.mm(o_, Gm1P[p][:, h, 64:128], VtP[p][:, h * 64:(h + 1) * 64], [Bn["Gm1%d" % p], Bn["Vt%d" % p]], [bB[by]], start=False, stop=True)
                        for h in HO:
                            cp = h // 2; hb = (h % 2) * 64
                            k.mm(py0[0:64, h * 64:(h + 1) * 64], ARP[p][hb:hb + 64, cp, 1, :], ST[d][hb:hb + 64, cp, :], [Bn["AR%d" % p], bST[d]], [bB[by0]])
                        ci = s * nchunk + c
                        ys2 = ysum[:].rearrange("p h v -> p (h v)")
                        yfs = yf[(ci % 2) * 64:(ci % 2) * 64 + 64, ci // 2, :]
                        bys = Bn["ysum"]
                        if d == 0:
                            k.act(ys2, py0[0:64, :], AF.Copy, [bB[by0]], [bys])
                            k.tt(ys2, ys2, py[0:64, :], ALU.add, [bys, bB[by]], [bys])
                            k.cp(yfs, ys2, [bys], [bYF])
                        else:
                            k.tt(ys2, py[0:64, :], yfs, ALU.add, [bB[by], bYF], [bys])
                            k.tt(ys2, ys2, py0[0:64, :], ALU.add, [bys, bB[by0]], [bys])
                        pull()
                        bt1 = 6
                        for cp in range(4):
                            for a_ in range(2):
                                idx = cp * 2 + a_
                                k.tr(bank(bt1 + idx // 4)[0:64, (idx % 4) * 128:(idx % 4 + 1) * 128], KBtP[p][:, cp, a_, :], ident[:],
                                     [Bn["KBt%d" % p], bC], [bB[bt1 + idx // 4]])
                        k.act(KBT[:, 0:2, :, :].rearrange("p c a k -> p (c a k)"), bank(bt1)[0:64, :], AF.Copy, [bB[bt1]], [Bn["KBT"]])
                        k.cp(KBT[:, 2:4, :, :].rearrange("p c a k -> p (c a k)"), bank(bt1 + 1)[0:64, :], [bB[bt1 + 1]], [Bn["KBT"]])
                        bs = 5; psu = bank(bs)
                        for cp in range(4):
                            k.mm(psu[:, cp * 128:(cp + 1) * 128], KBT[:, cp, 0, :], VtP[p][:, cp * 128:(cp + 1) * 128], [Bn["KBT"], Bn["Vt%d" % p]], [bB[bs]],
                                 start=True, stop=False)
                            k.mm(psu[:, cp * 128:(cp + 1) * 128], KBT[:, cp, 1, :], U[:, cp * 128:(cp + 1) * 128], [Bn["KBT"], bU], [bB[bs]],
                                 start=False, stop=True)
                        ps4 = psu.rearrange("p (c x) -> p c x", c=4)
                        for hh in range(2):
                            hb = hh * 64
                            k.tt(ST[d][hb:hb + 64, :, :], ST[d][hb:hb + 64, :, :], ps4[hb:hb + 64, :, hb:hb + 64], ALU.add,
                                 [bST[d], bB[bs]], [bST[d]])
                            k.tt(ST[d][hb:hb + 64, :, :], ST[d][hb:hb + 64, :, :],
                                 glP[p][hb:hb + 64, :, :].to_broadcast([64, 4, 64]), ALU.mult, [bST[d], Bn["gl%d" % p]], [bST[d]])
                        pull()
                        if d == 1:
                            k.red(yst[:, 0:8], ysum[:], [bys], [Bn["yst"]])
                            k.ts(yst[:, 0:8], yst[:, 0:8], 1.0 / 64, ALU.mult, [Bn["yst"]], [Bn["yst"]])
                            k.tt(ysum[:], ysum[:], yst[:, 0:8].unsqueeze(2).to_broadcast([64, 8, 64]), ALU.subtract, [bys, Bn["yst"]], [bys])
                            k.tt(ysq[:], ysum[:], ysum[:], ALU.mult, [bys], [Bn["ysq"]])
                            k.red(yst[:, 8:16], ysq[:], [Bn["ysq"]], [Bn["yst"]])
                            k.ts(yst[:, 8:16], yst[:, 8:16], 1.0 / 64, ALU.mult, [Bn["yst"]], [Bn["yst"]], s2=GN_EPS, op1=ALU.add)
                            k.act(yst[:, 8:16], yst[:, 8:16], AF.Sqrt, [Bn["yst"]], [Bn["yst"]])
                            k.recip(yst[:, 8:16], yst[:, 8:16], [Bn["yst"]], [Bn["yst"]])
                            k.tt(ysum[:], ysum[:], yst[:, 8:16].unsqueeze(2).to_broadcast([64, 8, 64]), ALU.mult, [bys, Bn["yst"]], [bys])
                            k.tt(tmp6[:], kdP[p][0][:], kdP[p][1][:], ALU.add, [Bn["kd0_%d" % p], Bn["kd1_%d" % p]], [Bn["tmp6"]], e="pool")
                            k.tt(tmp6[:], tmp6[:], r_, ALU.mult, [Bn["tmp6"], Bn["zs%d" % p]], [Bn["tmp6"]], e="pool")
                            k.tt(tmp6[:], tmp6[:], evs[:, 22:26].unsqueeze(2).to_broadcast([128, 4, C]), ALU.mult, [Bn["tmp6"], bP], [Bn["tmp6"]], e="pool")
                            b2 = nbB(); p2 = bank(b2)
                            k.mm(p2[:, 0:4 * C], bdones[:, :], tmp6[:].rearrange("p c t -> p (c t)"), [Bn["tmp6"], bC], [bB[b2]])
                            k.tt(bon[:], p2[:, 0:4 * C].rearrange("p (c t) -> p c t", c=4), vv, ALU.mult, [bB[b2], Bn["zs%d" % p]], [Bn["bon"]])
                            b3 = nbB(); p3 = bank(b3)
                            for cp in range(4):
                                k.tr(p3[:, cp * C:(cp + 1) * C], ysum[:, 2 * cp:2 * cp + 2, :].rearrange("p h v -> p (h v)"),
                                     ident[0:64, 0:64], [bys, bC], [bB[b3]])
                            p3v = p3[:, 0:4 * C].rearrange("p (c t) -> p c t", c=4)
                            k.tt(obt[:], p3v, evs[:, 26:30].unsqueeze(2).to_broadcast([128, 4, C]), ALU.mult, [bB[b3], bP], [Bn["obt"]])
                            k.tt(obt[:], obt[:], evs[:, 30:34].unsqueeze(2).to_broadcast([128, 4, C]), ALU.add, [Bn["obt"], bP], [Bn["obt"]])
                            k.tt(obt[:], obt[:], bon[:], ALU.add, [Bn["obt"], Bn["bon"]], [Bn["obt"]])
                            k.tt(oT[:, 4:8, gt0:gt0 + C], obt[:], gbT[:, :, gt0:gt0 + C], ALU.mult, [Bn["obt"], bGB], [bO])

                        if last:
                            if g == 0:
                                b2 = nbB(); p2 = bank(b2)
                                for cp in range(4):
                                    k.tr(p2[0:64, cp * 128:(cp + 1) * 128], ST[d][:, cp, :], ident[:], [bST[d], bC], [bB[b2]])
                                k.cp(sto[:].rearrange("p c x -> p (c x)"), p2[0:64, :], [bB[b2]], [bSTO])
                                k.dma("pool", nrs_o[d][s, j].rearrange("h v k -> v h k"),
                                      sto[:].rearrange("p c (h k) -> p (c h) k", h=2), [bSTO], [])

                    gens = {}
                    if units:
                        for _ in stageA(units[0], 0):
                            pass
                    for ui_, u in enumerate(units):
                        nxt = stageA(units[ui_ + 1], (ui_ + 1) % 2) if ui_ + 1 < len(units) else None

                        def pull(nxt=nxt, n=2):
                            if nxt is None:
                                return
                            for _ in range(n):
                                try:
                                    next(nxt)
                                except StopIteration:
                                    return
                        stageB(u, ui_ % 2, pull)
                        if nxt is not None:
                            for _ in nxt:
                                pass
            if STOP >= 4:
                out_proj(evo[j])

        def odd_layer(g, j, L):
            nseq, T = (4, 256) if g == 0 else (1, 1024)
            C = 128
            nchunk = T // C
            k.dma("sp", gw2aug[:], gw2aug_d[j], [], [bP])
            k.dma("sp", ln256[:], ln256_d[j], [], [bP])
            with k.scope() as scL:
                qT = scL.sb(un("gqT"), [128, 4, GT], BF16); bQ = Buf("gqT")
                kT = scL.sb(un("gkT"), [128, 4, GT], BF16); bK = Buf("gkT")
                vt = scL.sb(un("gvt"), [128, 8, 1024], BF16); bV = Buf("gvt")
                gs = scL.sb(un("ggs"), [128, 8, GT], BF16); bG = Buf("ggs")
                glT = scL.sb(un("glT"), [17, 2, GT], F32); bGL = Buf("glT")
                of = scL.sb(un("gof"), [128, 8, 1024], BF16); bOF = Buf("gof")
                k.ms(glT[:], 1.0, [bGL])
                with k.scope() as sc:
                    hT, bHT = make_h(sc, L, g)
                    wblock = wblock_factory(sc)

                    def dest_q(cc, tt, pb, bp):
                        k.ts(qT[:, cc, tt * 512:(tt + 1) * 512], pb, float(128 ** -0.5), ALU.mult, [bp], [bQ])
                    proj_F(wblock, odw[j], [0, 1], hT, bHT, dest_q)

                    def dest_k(cc, tt, pb, bp):
                        k.act(kT[:, cc, tt * 512:(tt + 1) * 512], pb, AF.Copy, [bp], [bK])
                    proj_F(wblock, odw[j], [2, 3], hT, bHT, dest_k)

                    def dest_v(bi_, t8, pb, bp):
                        if t8 % 2 == 0:
                            k.cp(vt[:, t8, bi_ * 256:(bi_ + 1) * 256], pb[:, 0:256], [bp], [bV])
                        else:
                            k.act(vt[:, t8, bi_ * 256:(bi_ + 1) * 256], pb[:, 0:256], AF.Copy, [bp], [bV])
                    proj_T(wblock, odw[j], [4, 5, 6, 7], hT, bHT, dest_v)

                    def dest_g(cc, tt, pb, bp):
                        k.act(gs[:, cc, tt * 512:(tt + 1) * 512], pb, AF.Silu, [bp], [bG])
                    proj_F(wblock, odw[j], [8, 9, 10, 11], hT, bHT, dest_g)

                    def dest_gl(cc, tt, pb, bp):
                        if cc < 2:
                            k.cp(glT[0:16, cc, tt * 512:(tt + 1) * 512], pb[0:16, :], [bp], [bGL])
                    proj_F(wblock, odw[j], [12], hT, bHT, dest_gl, msub=16, nsub=2)

                with k.scope() as sc:
                    f = lambda name, shape, dt=F32: sc.sb(un(name), shape, dt)
                    S = [f("gS", [128, 4, 256]) for d in range(2)]; bS = [Buf("gS0"), Buf("gS1")]
                    bW = Buf("glatmp")
                    G = {n_: Buf("g_" + n_) for n_ in ["Lg", "gam", "ginv", "qs", "ks", "Am", "kTt", "osum", "osq", "ost"]}
                    Lg = f("Lg", [128, 512])
                    gam = f("ggam", [128, 4, C]); ginv = f("gginv", [128, 4, C])
                    qs = f("gqs", [128, 4, C]); ks = f("gks", [128, 4, C])
                    Am = f("gAm", [128, 4, C], BF16); kTt = f("gkTt", [128, 4, C], BF16)
                    osum = f("gosum", [128, 4, 256]); osq = f("gosq", [128, 4, 256]); ost = f("gost", [128, 4])
                    print("GLA sbuf remaining", nc.sbuf_bytes_remaining)
                    for s in range(nseq):
                        for d in range(2):
                            if g == 0:
                                k.ms(S[d][:], 0.0, [bS[d]])
                            else:
                                k.dma("sp", S[d][:], sgs_d[d][j].rearrange("h k v -> k h v"), [], [bS[d]])
                            order = range(nchunk) if d == 0 else range(nchunk - 1, -1, -1)
                            for c in order:
                                gt0 = s * T + c * C
                                t8 = gt0 // 128
                                b2 = nb(); p2 = bank(b2)
                                k.mm(p2, glT[:, d, gt0:gt0 + C], gw2aug[:, d, :], [bGL, bP], [bB[b2]])
                                k.act(Lg[:], p2, AF.Sigmoid, [bB[b2]], [G["Lg"]])
                                k.act(Lg[:], Lg[:], AF.Ln, [G["Lg"]], [G["Lg"]])
                                b2 = nb(); p2 = bank(b2)
                                for h in range(4):
                                    k.mm(p2[:, h * C:(h + 1) * C], Lg[:, h * 128:(h + 1) * 128], tri128[:, d, :], [G["Lg"], bC], [bB[b2]])
                                k.act(gam[:].rearrange("p h t -> p (h t)"), p2, AF.Exp, [bB[b2]], [G["gam"]])
                                k.act(ginv[:].rearrange("p h t -> p (h t)"), p2, AF.Exp, [bB[b2]], [G["ginv"]], scale=-1.0)
                                glast = gam[:, :, C - 1:C] if d == 0 else gam[:, :, 0:1]
                                k.tt(qs[:], qT[:, :, gt0:gt0 + C], gam[:], ALU.mult, [bQ, G["gam"]], [G["qs"]])
                                k.tt(ks[:], kT[:, :, gt0:gt0 + C], ginv[:], ALU.mult, [bK, G["ginv"]], [G["ks"]])
                                b2 = nb(); p2 = bank(b2)
                                for h in range(4):
                                    k.mm(p2[:, h * C:(h + 1) * C], ks[:, h, :], qs[:, h, :], [G["ks"], G["qs"]], [bB[b2]])
                                k.tt(Am[:], p2.rearrange("p (h t) -> p h t", h=4), mask128[:, d, :].unsqueeze(1).to_broadcast([128, 4, C]),
                                     ALU.mult, [bB[b2], bC], [G["Am"]])
                                b2 = nb(); p2 = bank(b2)
                                for h in range(4):
                                    k.tr(p2[:, h * C:(h + 1) * C], ks[:, h, :], ident[:], [G["ks"], bC], [bB[b2]])
                                k.cp(kTt[:].rearrange("p h t -> p (h t)"), p2, [bB[b2]], [G["kTt"]])
                                by = nb2()
                                for h in range(4):
                                    o_ = bank(by + h // 2)[:, (h % 2) * 256:(h % 2 + 1) * 256]
                                    k.mm(o_, qs[:, h, :], S[d][:, h, :], [G["qs"], bS[d]], [bB[by + h // 2]], start=True, stop=False)
                                    k.mm(o_, Am[:, h, :], vt[:, t8, h * 256:(h + 1) * 256], [G["Am"], bV], [bB[by + h // 2]], start=False, stop=True)
                                bs = nb2()
                                for h in range(4):
                                    k.mm(bank(bs + h // 2)[:, (h % 2) * 256:(h % 2 + 1) * 256], kTt[:, h, :], vt[:, t8, h * 256:(h + 1) * 256],
                                         [G["kTt"], bV], [bB[bs + h // 2]])
                                for hf in range(2):
                                    sl = S[d][:, 2 * hf:2 * hf + 2, :]
                                    k.tt(sl, sl, bank(bs + hf).rearrange("p (h v) -> p h v", h=2), ALU.add, [bS[d], bB[bs + hf]], [bS[d]])
                                k.tt(S[d][:], S[d][:], glast.to_broadcast([128, 4, 256]), ALU.mult, [bS[d], G["gam"]], [bS[d]])
                                if d == 0:
                                    for hf in range(2):
                                        k.cp(of[:, t8, hf * 512:(hf + 1) * 512], bank(by + hf), [bB[by + hf]], [bOF])
                                else:
                                    for hf in range(2):
                                        k.tt(osum[:, 2 * hf:2 * hf + 2, :].rearrange("p h v -> p (h v)"), bank(by + hf),
                                             of[:, t8, hf * 512:(hf + 1) * 512], ALU.add, [bB[by + hf], bOF], [G["osum"]])
                                    k.act(osq[:], osum[:], AF.Square, [G["osum"]], [G["osq"]])
                                    k.red(ost[:], osq[:], [G["osq"]], [G["ost"]])
                                    k.ts(ost[:], ost[:], 1.0 / 256, ALU.mult, [G["ost"]], [G["ost"]], s2=EPS, op1=ALU.add)
                                    k.act(ost[:], ost[:], AF.Sqrt, [G["ost"]], [G["ost"]])
                                    k.recip(ost[:], ost[:], [G["ost"]], [G["ost"]])
                                    k.tt(osum[:], osum[:], ost[:].unsqueeze(2).to_broadcast([128, 4, 256]), ALU.mult, [G["osum"], G["ost"]], [G["osum"]])
                                    k.tt(osum[:], osum[:], ln256[:, :].unsqueeze(1).to_broadcast([128, 4, 256]), ALU.mult, [G["osum"], bP], [G["osum"]])
                                    bt = nb2()
                                    o2 = osum[:].rearrange("p h v -> p (h v)")
                                    for cc in range(8):
                                        k.tr(bank(bt + cc // 4)[:, (cc % 4) * 128:(cc % 4 + 1) * 128], o2[:, cc * 128:(cc + 1) * 128], ident[:],
                                             [G["osum"], bC], [bB[bt + cc // 4]])
                                    for hf in range(2):
                                        k.tt(oT[:, 4 * hf:4 * hf + 4, gt0:gt0 + C], bank(bt + hf).rearrange("p (c t) -> p c t", c=4),
                                             gs[:, 4 * hf:4 * hf + 4, gt0:gt0 + C], ALU.mult, [bB[bt + hf], bG], [bO])
                            if g == 0:
                                k.dma("pool", ngs_o[d][s, j].rearrange("h k v -> k h v"), S[d][:], [bS[d]], [])
            out_proj(odo[j])

        for g in groups:
            with k.scope() as sc:
                xin = [sc.sb(un("xin"), [128, D], F32) for i in range(2)]
                bxin = [Buf("xin0"), Buf("xin1")]
                for t8 in range(GT // 128):
                    xi = xin[t8 % 2]; bxi = bxin[t8 % 2]
                    k.dma("sp", xi[:], xg[g, t8 * 128:(t8 + 1) * 128, :], [], [bxi])
                    for half in range(2):
                        b2 = nb(); p2 = bank(b2)
                        for jj in range(4):
                            cch = half * 4 + jj
                            k.tr(p2[:, jj * 128:(jj + 1) * 128], xi[:, cch * 128:(cch + 1) * 128], ident[:], [bxi, bC], [bB[b2]])
                        k.cp(xT[:, half * 4:(half + 1) * 4, t8 * 128:(t8 + 1) * 128], p2.rearrange("p (j t) -> p j t", j=4), [bB[b2]], [bX])
            for L in range(nlayers):
                if L % 2 == 0:
                    even_layer(g, L // 2, L)
                else:
                    odd_layer(g, L // 2, L)
            with k.scope() as sc:
                rstd = compute_rstd(sc)
                yout = [sc.sb(un("yout"), [128, D], F32) for i in range(2)]
                byo = [Buf("yout0"), Buf("yout1")]
                tmpn = [sc.sb(un("tmpn"), [128, 512], F32) for i in range(2)]
                btn = [Buf("tmpn0"), Buf("tmpn1")]
                it = 0
                for t8 in range(GT // 128):
                    yo = yout[t8 % 2]; by_ = byo[t8 % 2]
                    for half in range(2):
                        b2 = nb(); p2 = bank(b2)
                        tn = tmpn[it % 2]; bt_ = btn[it % 2]; it += 1
                        for jj in range(4):
                            cch = half * 4 + jj
                            k.stt(tn[:, jj * 128:(jj + 1) * 128], xT[:, cch, t8 * 128:(t8 + 1) * 128], finalg_sb[:, cch:cch + 1],
                                  rstd[:, t8 * 128:(t8 + 1) * 128], ALU.mult, ALU.mult, [bX, bR, bC], [bt_])
                            k.tr(p2[:, jj * 128:(jj + 1) * 128], tn[:, jj * 128:(jj + 1) * 128], ident[:], [bt_, bC], [bB[b2]])
                        k.act(yo[:, half * 512:(half + 1) * 512], p2, AF.Copy, [bB[b2]], [by_])
                    k.dma("pool", yg[g, t8 * 128:(t8 + 1) * 128, :], yo[:], [by_], [])
        k.barrier()
    print("instructions:", k.ninstr, {e: c for e, c in k.cnt.items()})
    return nc


_CACHE = {}


def _consts():
    f32 = np.float32
    c = {}
    c["ident"] = np.eye(128, dtype=f32)
    s = np.arange(64)[:, None]; t = np.arange(64)[None, :]
    tri = np.zeros((64, 2, 2, 64), f32)
    tri[:, 0, 0] = (s <= t); tri[:, 0, 1] = (s < t); tri[:, 1, 0] = (s >= t); tri[:, 1, 1] = (s > t)
    c["tri64"] = tri * f32(-RWKV_DECAY_SCALE)
    s1 = np.arange(128)[:, None]; t1 = np.arange(128)[None, :]
    tri128 = np.zeros((128, 2, 128), f32)
    tri128[:, 0] = (s1 <= t1); tri128[:, 1] = (s1 >= t1)
    c["tri128"] = tri128 / f32(16.0)
    c["mask128"] = tri128.copy()
    m64 = np.zeros((64, 2, 128), f32)
    m64[:, 0, 0:64] = (s < t); m64[:, 0, 64:128] = (s <= t); m64[:, 1, 0:64] = (s > t); m64[:, 1, 64:128] = (s >= t)
    c["mask64"] = m64
    mN = np.zeros((64, 2, 64), f32)
    mN[:, 0] = (t < s); mN[:, 1] = (t > s)
    c["maskN"] = mN
    bd = np.zeros((128, 128), f32); bd[0:64, 0:64] = 1; bd[64:128, 64:128] = 1
    c["bdones"] = bd
    T = 1024
    row = np.repeat(np.arange(T // 64), 64).astype(f32); col = np.tile(np.arange(64), T // 64).astype(f32)
    inv = (f32(10000.0) ** (-np.arange(16, dtype=f32) / f32(16))).astype(f32)
    ang = np.stack([row[:, None] * inv, col[:, None] * inv], axis=1).astype(f32)
    cos = np.cos(ang).astype(f32); sin = np.sin(ang).astype(f32)
    cos64 = np.stack([cos, cos], axis=2).reshape(T, 64)
    sin64 = np.stack([-sin, sin], axis=2).reshape(T, 64)
    c["ropecos"] = np.ascontiguousarray(cos64.reshape(8, 128, 64).transpose(1, 0, 2))
    c["ropesin"] = np.ascontiguousarray(sin64.reshape(8, 128, 64).transpose(1, 0, 2))
    return c


def kernel(**inp):
    f32 = np.float32
    n = 8
    A = lambda name: np.asarray(inp[name], f32)
    x_prompt = A("x_prompt"); x_sample = A("x_sample"); c = A("c"); c_ctx = A("c_ctx")

    def pc(v, nchunk):
        v = np.asarray(v, f32)
        lead = v.shape[:-1]
        v = v.reshape(lead + (nchunk, 128))
        return np.ascontiguousarray(np.moveaxis(v, -1, 0))

    def wblk(w, nblk):
        Ln, rows, cols = w.shape
        if cols < nblk * WB:
            w = np.concatenate([w, np.zeros((Ln, rows, nblk * WB - cols), f32)], axis=2)
        return np.ascontiguousarray(w.reshape(Ln, NCH, 128, nblk, WB).transpose(0, 3, 2, 1, 4))

    shared = dict(_consts())
    shared["modw"] = wblk(A("mod_w"), 12).reshape(DEPTH * 12, 128, NCH, WB)
    shared["modbT"] = pc(A("mod_b"), 24)
    shared["normgT"] = pc(A("norm_g"), NCH)
    shared["finalgT"] = pc(A("final_g"), NCH)
    shared["evw"] = wblk(A("ev_w_in"), 14)
    shared["evo"] = wblk(A("ev_w_out"), 4)
    shared["odw"] = wblk(A("od_w_in"), 13)
    shared["odo"] = wblk(A("od_w_out"), 4)
    shared["w2aug"] = np.ascontiguousarray(np.concatenate([A("rw_w2"), A("rw_w0")[:, :, None, :]], axis=2).transpose(0, 2, 1, 3))
    shared["a2aug"] = np.ascontiguousarray(np.concatenate([A("rw_a2"), A("rw_a0")[:, :, None, :]], axis=2).transpose(0, 2, 1, 3))
    shared["gw2aug"] = np.ascontiguousarray(np.concatenate([A("gla_w2"), A("gla_b")[:, :, None, :]], axis=2).transpose(0, 2, 1, 3))
    shared["ln256"] = np.ascontiguousarray(np.broadcast_to(A("gla_ln_g")[:, None, :], (2, 128, 256)))
    qk = np.stack([A("ev_qn_g"), A("ev_kn_g")], axis=1)
    shared["qkg"] = np.ascontiguousarray(np.broadcast_to(qk[:, None], (2, 128, 2, 64)))
    evs = np.concatenate([pc(A("ev_shift_mu"), 14), pc(A("rw_kk"), 4), pc(A("rw_ka"), 4), pc(A("rw_rk").reshape(2, 512), 4),
                          pc(A("rw_ln_g"), 4), pc(A("rw_ln_b"), 4)], axis=2)
    shared["evs"] = np.ascontiguousarray(evs.transpose(1, 0, 2))
    in_maps = []
    for core in range(n):
        sb = core % 4
        m = dict(shared)
        m["xg"] = np.ascontiguousarray(np.stack([x_prompt[4 * core:4 * core + 4].reshape(GT, D), x_sample[sb]], axis=0))
        m["condT"] = np.ascontiguousarray(np.stack([pc(c_ctx, NCH), pc(c[sb], NCH)], axis=-1))
        m["cak"] = np.ascontiguousarray(A("cache_attn_k")[sb].reshape(2, 512, 128))
        m["cav"] = np.ascontiguousarray(A("cache_attn_v")[sb].reshape(2, 512, 128))
        m["srf"] = np.ascontiguousarray(A("state_rwkv_fwd")[sb]); m["srb"] = np.ascontiguousarray(A("state_rwkv_bwd")[sb])
        m["sgf"] = np.ascontiguousarray(A("state_gla_fwd")[sb]); m["sgb"] = np.ascontiguousarray(A("state_gla_bwd")[sb])
        in_maps.append(m)
    if "nc" not in _CACHE:
        _CACHE["nc"] = build_program()
    res = run_bass_kernel_spmd(_CACHE["nc"], in_maps, core_ids=list(range(n)))
    R = res.results
    cat = lambda name: np.concatenate([R[i][name] for i in range(n)], axis=0)
    y_prompt = np.concatenate([R[i]["yg"][0].reshape(4, 256, D) for i in range(n)], axis=0)
    y_sample = np.stack([R[i]["yg"][1] for i in range(4)], axis=0)
    new_k = cat("nk").reshape(32, 2, 256, 2, 64)
    new_v = cat("nv").reshape(32, 2, 256, 2, 64)
    return (y_prompt, y_sample, new_k, new_v, cat("nrf"), cat("nrb"), cat("ngf"), cat("ngb"))
```

```python
import contextlib
import os
STOP = int(os.environ.get('KSTOP', '9'))
SUB = float(os.environ.get('KSUB', '9'))
KNS = int(os.environ.get('KNS', '99'))
KNC = int(os.environ.get('KNC', '99'))
KND = int(os.environ.get('KND', '2'))
import numpy as np
import concourse.bass as bass
import concourse.mybir as mybir
from concourse.bass_utils import run_bass_kernel_spmd

F32 = mybir.dt.float32
BF16 = mybir.dt.bfloat16
AF = mybir.ActivationFunctionType
ALU = mybir.AluOpType
AX = mybir.AxisListType

D = 1024
NCH = 8
DEPTH = 4
GT = 1024
EV_COLS = 3584
OD_COLS = 3104
EPS = 1e-6
GN_EPS = 64e-5
RWKV_DECAY_SCALE = 0.606531
WB = 256


class Buf:
    __slots__ = ("name", "w", "r")

    def __init__(self, name):
        self.name = name
        self.w = None
        self.r = {}


class KB:
    SEM_WRAP = 20000

    def __init__(self, nc):
        self.nc = nc
        self.es = contextlib.ExitStack()
        self.engs = {"pe": nc.tensor, "act": nc.scalar, "dve": nc.vector, "pool": nc.gpsimd, "sp": nc.sync}
        self.cnt = {e: 0 for e in self.engs}
        self.esems = {e: [] for e in self.engs}
        self.seen = {e: {} for e in self.engs}
        self.semobj = {}
        self.ndma_sems = {"sp": 12, "pool": 6, "act": 4}
        self.dma_sems = {}
        self.dma_rr = {q: 0 for q in self.ndma_sems}
        self.dma_cnt = {}
        self.nsem = 0
        self.ninstr = 0

    def new_sem(self, name):
        s = self.es.enter_context(self.nc.semaphore(name))
        self.semobj[name] = s
        return name

    def sb(self, name, shape, dtype):
        return self.es.enter_context(self.nc.sbuf_tensor(name, list(shape), dtype))

    def ps(self, name, shape, dtype=F32):
        return self.es.enter_context(self.nc.psum_tensor(name, list(shape), dtype))

    def _cur_sem(self, e):
        idx = self.cnt[e] // self.SEM_WRAP
        while len(self.esems[e]) <= idx:
            self.esems[e].append(self.new_sem(f"s_{e}_{len(self.esems[e])}"))
        return self.esems[e][idx]

    def _wait(self, e, tok):
        if tok is None:
            return
        key, val = tok
        if self.seen[e].get(key, 0) >= val:
            return
        self.engs[e].wait_ge(self.semobj[key], val)
        self.seen[e][key] = val

    def _deps(self, e, reads, writes, pe_acc=False):
        for b in reads:
            if b.w is not None:
                self._wait(e, b.w)
        for b in writes:
            if b.w is not None and not (pe_acc and e == "pe"):
                self._wait(e, b.w)
            for e2, tok in b.r.items():
                if e2 == e and e == "pe":
                    continue
                self._wait(e, tok)

    def _mark(self, e, tok, reads, writes):
        for b in reads:
            b.r[e] = tok
        for b in writes:
            b.w = tok
            b.r = {}

    def op(self, e, reads, writes, fn, pe_acc=False):
        self._deps(e, reads, writes, pe_acc)
        sem = self._cur_sem(e)
        ins = fn(self.engs[e])
        ins.then_inc(self.semobj[sem], 1)
        self.cnt[e] += 1
        self.ninstr += 1
        val = self.cnt[e] - (self.cnt[e] - 1) // self.SEM_WRAP * self.SEM_WRAP
        tok = (sem, val)
        self._mark(e, tok, reads, writes)
        return tok

    def dma(self, q, out, in_, reads, writes):
        if q not in self.dma_sems:
            self.dma_sems[q] = [self.new_sem(f"d_{q}_{i}") for i in range(self.ndma_sems[q])]
            for s in self.dma_sems[q]:
                self.dma_cnt[s] = 0
        sems = self.dma_sems[q]
        s = sems[self.dma_rr[q] % len(sems)]
        self.dma_rr[q] += 1
        if self.dma_cnt[s] > 0:
            self._wait(q, (s, 16 * self.dma_cnt[s]))
        self._deps(q, reads, writes)
        self.engs[q].dma_start(out=out, in_=in_).then_inc(self.semobj[s], 16)
        self.dma_cnt[s] += 1
        self.ninstr += 1
        tok = (s, 16 * self.dma_cnt[s])
        self._mark(s, tok, reads, writes)
        return tok

    def finish(self, bufs):
        for b in bufs:
            if b.w is not None:
                self._wait("sp", b.w)
        for q, sems in self.dma_sems.items():
            for s in sems:
                if self.dma_cnt[s] > 0:
                    self._wait("sp", (s, 16 * self.dma_cnt[s]))


    def barrier(self):
        toks = []
        for e in self.engs:
            if self.cnt[e] > 0:
                sem = self.esems[e][(self.cnt[e] - 1) // self.SEM_WRAP]
                val = self.cnt[e] - (self.cnt[e] - 1) // self.SEM_WRAP * self.SEM_WRAP
                toks.append((sem, val))
        for q, sems in self.dma_sems.items():
            for s in sems:
                if self.dma_cnt[s] > 0:
                    toks.append((s, 16 * self.dma_cnt[s]))
        for e in self.engs:
            for t in toks:
                self._wait(e, t)

    @contextlib.contextmanager
    def scope(self):
        sc = _Scope(self)
        try:
            yield sc
        finally:
            self.barrier()
            sc.es.close()

    def _pe_rows(self, ap):
        base = ap.base_partition()
        n = ap.shape[0]
        grp = set(range(base // 32, (base + n - 1) // 32 + 1))
        last = getattr(self, "_pe_last_grp", None)
        if last is not None and not (grp & last) and self.cnt["pe"] > 0:
            for sem_i in range(max(0, (self.cnt["pe"] - 1) // self.SEM_WRAP - 1), (self.cnt["pe"] - 1) // self.SEM_WRAP + 1):
                sem = self.esems["pe"][sem_i]
                if sem_i == (self.cnt["pe"] - 1) // self.SEM_WRAP:
                    val = self.cnt["pe"] - sem_i * self.SEM_WRAP
                else:
                    val = self.SEM_WRAP
                self._wait("pe", (sem, val))
        self._pe_last_grp = grp

    def mm(self, out, lhsT, rhs, R, W, start=True, stop=True):
        self._pe_rows(lhsT)
        return self.op("pe", R, W, lambda e: e.matmul(out, lhsT, rhs, start=start, stop=stop), pe_acc=True)

    def tr(self, out, in_, ident, R, W):
        self._pe_rows(in_)
        return self.op("pe", R, W, lambda e: e.transpose(out, in_, ident), pe_acc=True)

    def tt(self, out, in0, in1, op, R, W, e="dve"):
        return self.op(e, R, W, lambda g: g.tensor_tensor(out=out, in0=in0, in1=in1, op=op))

    def ts(self, out, in0, s1, op0, R, W, s2=None, op1=None, e="dve"):
        if op1 is None:
            return self.op(e, R, W, lambda g: g.tensor_scalar(out=out, in0=in0, scalar1=s1, scalar2=None, op0=op0))
        return self.op(e, R, W, lambda g: g.tensor_scalar(out=out, in0=in0, scalar1=s1, scalar2=s2, op0=op0, op1=op1))

    def stt(self, out, in0, scalar, in1, op0, op1, R, W, e="dve"):
        return self.op(e, R, W, lambda g: g.scalar_tensor_tensor(out=out, in0=in0, scalar=scalar, in1=in1, op0=op0, op1=op1))

    def act(self, out, in_, func, R, W, **kw):
        return self.op("act", R, W, lambda g: g.activation(out=out, in_=in_, func=func, **kw))

    def cp(self, out, in_, R, W, e="dve"):
        return self.op(e, R, W, lambda g: g.tensor_copy(out=out, in_=in_))

    def red(self, out, in_, R, W, op=None):
        return self.op("dve", R, W, lambda g: g.tensor_reduce(out=out, in_=in_, axis=AX.X, op=(op or ALU.add)))

    def recip(self, out, in_, R, W):
        return self.op("dve", R, W, lambda g: g.reciprocal(out=out, in_=in_))

    def ms(self, ap, val, W, e="dve"):
        return self.op(e, [], W, lambda g: g.memset(ap, val))


class _Scope:
    def __init__(self, k):
        self.k = k
        self.es = contextlib.ExitStack()

    def sb(self, name, shape, dtype):
        return self.es.enter_context(self.k.nc.sbuf_tensor(name, list(shape), dtype))


def build_program(nlayers=DEPTH, groups=(0, 1)):
    nc = bass.Bass("TRN2", target_bir_lowering=False)
    k = KB(nc)
    uid = [0]

    def un(name):
        uid[0] += 1
        return f"{name}_{uid[0]}"

    def din(name, shape):
        return nc.dram_tensor(name, list(shape), F32, kind="ExternalInput").ap()

    def dout(name, shape):
        return nc.dram_tensor(name, list(shape), F32, kind="ExternalOutput").ap()

    xg = din("xg", [2, GT, D])
    condT = din("condT", [128, NCH, 2])
    modw = din("modw", [DEPTH * 12, 128, NCH, WB])
    modbT = din("modbT", [128, DEPTH, 24])
    normgT = din("normgT", [128, DEPTH, NCH])
    finalgT = din("finalgT", [128, NCH])
    ident_d = din("ident", [128, 128])
    evw = din("evw", [2, 14, 128, NCH, WB])
    evo = din("evo", [2, 4, 128, NCH, WB])
    odw = din("odw", [2, 13, 128, NCH, WB])
    odo = din("odo", [2, 4, 128, NCH, WB])
    tri64_d = din("tri64", [64, 2, 2, 64])
    tri128_d = din("tri128", [128, 2, 128])
    mask64_d = din("mask64", [64, 2, 128])
    maskN_d = din("maskN", [64, 2, 64])
    mask128_d = din("mask128", [128, 2, 128])
    bd_d = din("bdones", [128, 128])
    cos_d = din("ropecos", [128, 8, 64])
    sin_d = din("ropesin", [128, 8, 64])
    w2aug_d = din("w2aug", [2, 65, 2, 512])
    a2aug_d = din("a2aug", [2, 65, 2, 512])
    gw2aug_d = din("gw2aug", [2, 17, 2, 512])
    ln256_d = din("ln256", [2, 128, 256])
    qkg_d = din("qkg", [2, 128, 2, 64])
    evs_d = din("evs", [2, 128, 34])
    cak = din("cak", [2, 512, 128])
    cav = din("cav", [2, 512, 128])
    srs_d = [din("srf", [2, 8, 64, 64]), din("srb", [2, 8, 64, 64])]
    sgs_d = [din("sgf", [2, 4, 128, 256]), din("sgb", [2, 4, 128, 256])]
    yg = dout("yg", [2, GT, D])
    nk_o = dout("nk", [4, 2, 256, 128])
    nv_o = dout("nv", [4, 2, 256, 128])
    nrs_o = [dout("nrf", [4, 2, 8, 64, 64]), dout("nrb", [4, 2, 8, 64, 64])]
    ngs_o = [dout("ngf", [4, 2, 4, 128, 256]), dout("ngb", [4, 2, 4, 128, 256])]

    with k.es:
        bC = Buf("const")

        def cload(name, src, shape, dtype=F32):
            t = k.sb(name, shape, dtype)
            k.dma("sp", t[:], src, [], [bC])
            return t

        ident = cload("ident_sb", ident_d[:, :], [128, 128])
        tri64 = cload("tri64_sb", tri64_d[:, :, :, :], [64, 2, 2, 64])
        tri128 = cload("tri128_sb", tri128_d[:, :, :], [128, 2, 128])
        mask64 = cload("mask64_sb", mask64_d[:, :, :], [64, 2, 128])
        maskN = cload("maskN_sb", maskN_d[:, :, :], [64, 2, 64])
        mask128 = cload("mask128_sb", mask128_d[:, :, :], [128, 2, 128])
        bdones = cload("bd_sb", bd_d[:, :], [128, 128])
        rope_tabs = {}
        cond_sb = cload("cond_sb", condT[:, :, :], [128, NCH, 2])
        modb_sb = cload("modb_sb", modbT[:, :, :], [128, DEPTH, 24])
        normg_sb = cload("normg_sb", normgT[:, :, :], [128, DEPTH, NCH])
        finalg_sb = cload("finalg_sb", finalgT[:, :], [128, NCH])
        ones_f = k.sb("ones_f", [128, 128], F32)
        k.ms(ones_f[:], 1.0 / D, [bC])
        ones_bf = k.sb("ones_bf", [128, 64], BF16)
        k.ms(ones_bf[:], 1.0, [bC])
        bP = Buf("params")
        w2aug = k.sb("w2aug_sb", [65, 2, 512], F32)
        a2aug = k.sb("a2aug_sb", [65, 2, 512], F32)
        gw2aug = k.sb("gw2aug_sb", [17, 2, 512], F32)
        ln256 = k.sb("ln256_sb", [128, 256], F32)
        qkg = k.sb("qkg_sb", [128, 2, 64], F32)
        evs = k.sb("evs_sb", [128, 34], F32)
        evx = k.sb("evx_sb", [128, 32], F32)

        PT = [k.ps(f"P{i}", [128, 1024], F32) for i in range(4)]
        bB = [Buf(f"bank{i}") for i in range(8)]
        rr = [0]

        def bank(i):
            return PT[i // 2][:, (i % 2) * 512:(i % 2) * 512 + 512]

        def nb(lo=0, hi=8):
            i = lo + rr[0] % (hi - lo)
            rr[0] += 1
            return i

        def nb2():
            i = (rr[0] % 8 + 1) // 2 * 2 % 8
            rr[0] += (i - rr[0] % 8) % 8 + 2
            return i

        scond = k.sb("scond", [128, NCH, 2], F32)
        k.act(scond[:], cond_sb[:], AF.Silu, [bC], [bC])
        modT = k.sb("modT", [128, DEPTH, 24, 2], F32)
        bM = Buf("mod")
        xT = k.sb("xT", [128, NCH, GT], F32)
        bX = Buf("xT")
        oT = k.sb("oT", [128, NCH, GT], BF16)
        bO = Buf("oT")
        bR = Buf("rstd")
        hcol = k.sb("hcol", [128, 3, NCH], F32)
        bH = Buf("hcol")

        with k.scope() as sc:
            wst = [sc.sb(un("mwst"), [128, NCH, WB], F32) for i in range(4)]
            bws = [Buf("mwst%d" % i) for i in range(4)]
            mrow = [sc.sb(un("mrow"), [2, WB], F32) for i in range(2)]
            bmrow = [Buf("mrow0"), Buf("mrow1")]
            wi = 0
            for L in range(nlayers):
                for blk in range(12):
                    w = wst[wi % 4]; bw = bws[wi % 4]
                    k.dma("sp" if wi % 2 == 0 else "pool", w[:], modw[L * 12 + blk], [], [bw])
                    bi = nb(); pm = bank(bi); bp = bB[bi]
                    for kc in range(NCH):
                        k.mm(pm[0:2, 0:WB], scond[:, kc, :], w[:, kc, :], [bw, bC], [bp], start=(kc == 0), stop=(kc == NCH - 1))
                    rt = mrow[wi % 2]; brt = bmrow[wi % 2]
                    k.act(rt[:, :], pm[0:2, 0:WB], AF.Copy, [bp], [brt])
                    bi2 = nb(); pm2 = bank(bi2); bp2 = bB[bi2]
                    for sub in range(2):
                        k.tr(pm2[:, sub * 2:sub * 2 + 2], rt[0:2, sub * 128:(sub + 1) * 128], ident[0:2, 0:2], [brt, bC], [bp2])
                    for sub in range(2):
                        ch = blk * 2 + sub
                        k.ts(modT[:, L, ch, :], pm2[:, sub * 2:sub * 2 + 2], modb_sb[:, L, ch:ch + 1], ALU.add, [bp2, bC], [bM])
                    wi += 1

        def compute_rstd(sc):
            rstd = sc.sb(un("rstd"), [128, GT], F32)
            sq = [sc.sb(un("sq"), [128, 512], F32) for i in range(2)]
            bsq = [Buf("sq0"), Buf("sq1")]
            for tt in range(GT // 512):
                bi = nb(); pn = bank(bi); bp = bB[bi]
                for c in range(NCH):
                    s_ = sq[c % 2]; bs_ = bsq[c % 2]
                    k.act(s_[:], xT[:, c, tt * 512:(tt + 1) * 512], AF.Square, [bX], [bs_])
                    k.mm(pn, ones_f[:], s_[:], [bs_, bC], [bp], start=(c == 0), stop=(c == NCH - 1))
                sl = rstd[:, tt * 512:(tt + 1) * 512]
                k.ts(sl, pn, EPS, ALU.add, [bp], [bR])
                k.act(sl, sl, AF.Sqrt, [bR], [bR])
                k.recip(sl, sl, [bR], [bR])
            return rstd

        def wblock_factory(sc):
            wst2 = [sc.sb(un("wst"), [128, NCH, WB], F32) for i in range(2)]
            bws2 = [Buf("wst0"), Buf("wst1")]
            wbf = [sc.sb(un("wbf"), [128, NCH, WB], BF16) for i in range(2)]
            bwb = [Buf("wbf0"), Buf("wbf1")]
            cnt = [0]

            def wblock(src):
                i = cnt[0] % 2
                cnt[0] += 1
                wst = wst2[i]; bws = bws2[i]
                k.dma("sp", wst[:], src, [], [bws])
                k.cp(wbf[i][:], wst[:], [bws], [bwb[i]], e="pool")
                return wbf[i], bwb[i]
            return wblock

        def proj_F(wblock, wsrc, blocks, hT, bHT, dest, msub=128, nsub=None):
            nsub = nsub or WB // msub
            for bi_, blk in enumerate(blocks):
                w, bw = wblock(wsrc[blk])
                for sub in range(nsub):
                    for tt in range(GT // 512):
                        bi = nb(); pb = bank(bi); bp = bB[bi]
                        for kc in range(NCH):
                            k.mm(pb[0:msub, :], w[:, kc, sub * msub:(sub + 1) * msub], hT[:, kc, tt * 512:(tt + 1) * 512],
                                 [bw, (bHT[tt] if isinstance(bHT, list) else bHT)], [bp], start=(kc == 0), stop=(kc == NCH - 1))
                        dest(bi_ * nsub + sub, tt, pb, bp)

        def proj_T(wblock, wsrc, blocks, hT, bHT, dest):
            for bi_, blk in enumerate(blocks):
                w, bw = wblock(wsrc[blk])
                for t8 in range(GT // 128):
                    bi = nb(); pb = bank(bi); bp = bB[bi]
                    for kc in range(NCH):
                        k.mm(pb[:, 0:WB], hT[:, kc, t8 * 128:(t8 + 1) * 128], w[:, kc, :],
                             [bw, (bHT[t8 // 4] if isinstance(bHT, list) else bHT)], [bp], start=(kc == 0), stop=(kc == NCH - 1))
                    dest(bi_, t8, pb, bp)

        def make_h(sc, L, g):
            hT = sc.sb(un("hT"), [128, NCH, GT], BF16)
            bHT = [Buf("hT0"), Buf("hT1")]
            k.stt(hcol[:, 0, :], modT[:, L, 8:16, g], 1.0, normg_sb[:, L, :], ALU.add, ALU.mult, [bM, bC], [bH])
            k.cp(hcol[:, 1, :], modT[:, L, 0:8, g], [bM], [bH])
            k.cp(hcol[:, 2, :], modT[:, L, 16:24, g], [bM], [bH])
            rstd = compute_rstd(sc)
            tmp = [sc.sb(un("htmp"), [128, 512], F32) for i in range(2)]
            btmp = [Buf("htmp0"), Buf("htmp1")]
            i = 0
            for tt in range(GT // 512):
                for c in range(NCH):
                    t_ = tmp[i % 2]; bt_ = btmp[i % 2]; i += 1
                    k.tt(t_[:], xT[:, c, tt * 512:(tt + 1) * 512], rstd[:, tt * 512:(tt + 1) * 512], ALU.mult, [bX, bR], [bt_])
                    k.ts(hT[:, c, tt * 512:(tt + 1) * 512], t_[:], hcol[:, 0, c:c + 1], ALU.mult, [bt_, bH], [bHT[tt]],
                         s2=hcol[:, 1, c:c + 1], op1=ALU.add)
            return hT, bHT

        def out_proj(wsrc):
            with k.scope() as sc:
                wblock = wblock_factory(sc)

                def dest(cc, tt, pb, bp):
                    sl = xT[:, cc, tt * 512:(tt + 1) * 512]
                    k.stt(sl, pb, hcol[:, 2, cc:cc + 1], sl, ALU.mult, ALU.add, [bp, bH, bX], [bX])
                proj_F(wblock, wsrc, range(4), oT, bO, dest)

        def headnorm(sc_t, pb_view, nh, gidx, out3, R, W, bT=None):
            bT = bT or bT0
            sqt, ssq = sc_t
            k.act(sqt[:, 0:nh * 64], pb_view.rearrange("p h d -> p (h d)"), AF.Square, R, [bT])
            k.red(ssq[:, 0:nh], sqt[:, 0:nh * 64].rearrange("p (h d) -> p h d", h=nh), [bT], [bT])
            k.ts(ssq[:, 0:nh], ssq[:, 0:nh], 1.0 / 64, ALU.mult, [bT], [bT], s2=EPS, op1=ALU.add)
            k.act(ssq[:, 0:nh], ssq[:, 0:nh], AF.Sqrt, [bT], [bT])
            k.recip(ssq[:, 0:nh], ssq[:, 0:nh], [bT], [bT])
            k.tt(out3, pb_view, ssq[:, 0:nh].unsqueeze(2).to_broadcast([128, nh, 64]), ALU.mult, R + [bT], W)
            k.tt(out3, out3, qkg[:, gidx, :].unsqueeze(1).to_broadcast([128, nh, 64]), ALU.mult, W + [bP], W)

        bT0 = Buf("tmpT")

        def rope(x3, nh, t8, t1, t2, R, bT=None):
            bT = bT or bT0
            cosb = rope_tabs["cos"][:, t8, :].unsqueeze(1).to_broadcast([128, nh, 64])
            k.tt(t1[:, 0:nh, :], x3, cosb, ALU.mult, R + [bC], [bT])
            x5 = x3.rearrange("p h (a q f) -> p h a q f", a=2, q=2)
            t5 = t2[:, 0:nh, :].rearrange("p h (a q f) -> p h a q f", a=2, q=2)
            s4 = rope_tabs["sin"][:, t8, :].rearrange("p (a q f) -> p a q f", a=2, q=2)
            for q_ in range(2):
                k.tt(t5[:, :, :, q_, :], x5[:, :, :, 1 - q_, :],
                     s4[:, :, q_, :].unsqueeze(1).to_broadcast([128, nh, 2, 16]), ALU.mult, R + [bC], [bT])
            k.tt(x3, t1[:, 0:nh, :], t2[:, 0:nh, :], ALU.add, [bT], R)

        def even_layer(g, j, L):
            nseq, T = (4, 256) if g == 0 else (1, 1024)
            TP = T + 2
            koff = 0 if g == 0 else 512
            SK = GT + koff
            k.dma("sp", w2aug[:], w2aug_d[j], [], [bP])
            k.dma("sp", a2aug[:], a2aug_d[j], [], [bP])
            k.dma("sp", qkg[:], qkg_d[j], [], [bP])
            k.dma("sp", evs[:], evs_d[j], [], [bP])
            k.ts(evx[:, 0:14], evs[:, 0:14], 0.5, ALU.mult, [bP], [bP])
            k.ts(evx[:, 14:28], evs[:, 0:14], -1.0, ALU.mult, [bP], [bP], s2=1.0, op1=ALU.add)
            k.ts(evx[:, 28:32], evs[:, 18:22], -1.0, ALU.mult, [bP], [bP], s2=1.0, op1=ALU.add)
            with k.scope() as scL:
                gbT = scL.sb(un("gbT"), [128, 4, GT], BF16); bGB = Buf("gbT")
                zraw = scL.sb(un("zraw"), [128, 14, nseq * TP], BF16); bZ = Buf("zraw")
                k.ms(zraw[:], 0.0, [bZ])
                zr4 = zraw[:].rearrange("p c (s t) -> p c s t", s=nseq)
                with k.scope() as scA:
                    gaT = scA.sb(un("gaT"), [128, 4, GT], BF16); bGA = Buf("gaT")
                    qT = scA.sb(un("qT"), [64, 8, GT], BF16); bQ = Buf("qT")
                    kT = scA.sb(un("kT"), [64, 2, SK], BF16); bK = Buf("kT")
                    vtok = scA.sb(un("vtok"), [128, SK // 128, 128], BF16); bV = Buf("vtok")
                    if g == 1:
                      with k.scope() as sc:
                          ck = sc.sb(un("ck"), [128, 4, 128], F32); bCK = Buf("ck")
                          k.dma("sp", ck[:], cak[j].rearrange("(i p) f -> p i f", p=128), [], [bCK])
                          for i in range(4):
                              b2 = nb(); p2 = bank(b2)
                              for hh in range(2):
                                  k.tr(p2[0:64, hh * 128:(hh + 1) * 128], ck[:, i, hh * 64:(hh + 1) * 64], ident[:], [bCK, bC], [bB[b2]])
                              k.cp(kT[:, :, i * 128:(i + 1) * 128],
                                   p2[0:64, 0:256].rearrange("p (h t) -> p h t", h=2), [bB[b2]], [bK])
                          cv = sc.sb(un("cv"), [128, 4, 128], F32); bCV = Buf("cv")
                          k.dma("sp", cv[:], cav[j].rearrange("(i p) f -> p i f", p=128), [], [bCV])
                          k.cp(vtok[:, 0:4, :], cv[:], [bCV], [bV])

                    with k.scope() as sc:
                        hT, bHT = make_h(sc, L, g)
                        wblock = wblock_factory(sc)
                        if g == 1:
                            rope_tabs["cos"] = sc.sb(un("cos_sb"), [128, 8, 64], F32)
                            rope_tabs["sin"] = sc.sb(un("sin_sb"), [128, 8, 64], F32)
                            k.dma("sp", rope_tabs["cos"][:], cos_d[:, :, :], [], [bC])
                            k.dma("sp", rope_tabs["sin"][:], sin_d[:, :, :], [], [bC])
                        print("S1 sbuf remaining", nc.sbuf_bytes_remaining)
                        sqtL = [sc.sb(un("sqt"), [128, 256], F32) for _ in range(2)]
                        ssqL = [sc.sb(un("ssq"), [128, 4], F32) for _ in range(2)]
                        qnL = [sc.sb(un("qn"), [128, 4, 64], F32) for _ in range(2)]; bQNL = [Buf("qn0"), Buf("qn1")]
                        r1L = [sc.sb(un("r1"), [128, 4, 64], F32) for _ in range(2)]
                        r2L = [sc.sb(un("r2"), [128, 4, 64], F32) for _ in range(2)]
                        bTL = [Buf("tmpT0"), Buf("tmpT1")]
                        kvo = [sc.sb(un("kvo"), [128, 256], F32) for i in range(2)]
                        bKVO = [Buf("kvo0"), Buf("kvo1")]

                        def dest_q(bi_, t8, pb, bp):
                            ix = t8 % 2
                            sqt, ssq, qn, r1, r2, bQN, bTx = sqtL[ix], ssqL[ix], qnL[ix], r1L[ix], r2L[ix], bQNL[ix], bTL[ix]
                            headnorm((sqt, ssq), pb[:, 0:256].rearrange("p (h d) -> p h d", h=4), 4, 0, qn[:], [bp], [bQN], bT=bTx)
                            if g == 1:
                                rope(qn[:], 4, t8, r1, r2, [bQN], bT=bTx)
                            b2 = nb(); p2 = bank(b2)
                            for hh in range(4):
                                k.tr(p2[0:64, hh * 128:(hh + 1) * 128], qn[:, hh, :], ident[:], [bQN, bC], [bB[b2]])
                            k.cp(qT[:, bi_ * 4:bi_ * 4 + 4, t8 * 128:(t8 + 1) * 128],
                                 p2[0:64, :].rearrange("p (h t) -> p h t", h=4), [bB[b2]], [bQ])
                        proj_T(wblock, evw[j], [0, 1], hT, bHT, dest_q)

                        def dest_kv(bi_, t8, pb, bp):
                            ko = kvo[t8 % 2]; bko = bKVO[t8 % 2]
                            kn3 = ko[:, 0:128].rearrange("p (h d) -> p h d", h=2)
                            ix = t8 % 2
                            sqt, ssq, r1, r2, bTx = sqtL[ix], ssqL[ix], r1L[ix], r2L[ix], bTL[ix]
                            headnorm((sqt, ssq), pb[:, 0:128].rearrange("p (h d) -> p h d", h=2), 2, 1, kn3, [bp], [bko], bT=bTx)
                            k.cp(ko[:, 128:256], pb[:, 128:256], [bp], [bko])
                            k.cp(vtok[:, koff // 128 + t8, :], pb[:, 128:256], [bp], [bV])
                            if g == 0:
                                b_ = t8 // 2; t0 = (t8 % 2) * 128
                                k.dma("pool", nk_o[b_, j, t0:t0 + 128, :], ko[:, 0:128], [bko], [])
                                k.dma("pool", nv_o[b_, j, t0:t0 + 128, :], ko[:, 128:256], [bko], [])
                            else:
                                rope(kn3, 2, t8, r1, r2, [bko], bT=bTx)
                            b2 = nb(); p2 = bank(b2)
                            for hh in range(2):
                                k.tr(p2[0:64, hh * 128:(hh + 1) * 128], kn3[:, hh, :], ident[:], [bko, bC], [bB[b2]])
                            k.cp(kT[:, :, koff + t8 * 128:koff + (t8 + 1) * 128],
                                 p2[0:64, 0:256].rearrange("p (h t) -> p h t", h=2), [bB[b2]], [bK])
                        proj_T(wblock, evw[j], [2], hT, bHT, dest_kv)

                        def dest_ga(cc, tt, pb, bp):
                            k.act(gaT[:, cc, tt * 512:(tt + 1) * 512], pb, AF.Silu, [bp], [bGA])
                        proj_F(wblock, evw[j], [3, 4], hT, bHT, dest_ga)

                        def dest_zb(cc, tt, pb, bp):
                            if g == 0:
                                k.cp(zr4[:, cc, 2 * tt:2 * tt + 2, 1:T + 1], pb.rearrange("p (s t) -> p s t", s=2), [bp], [bZ])
                            else:
                                k.cp(zr4[:, cc, 0, 1 + tt * 512:1 + (tt + 1) * 512], pb, [bp], [bZ])
                        proj_F(wblock, evw[j], range(5, 12), hT, bHT, dest_zb)

                        def dest_gb(cc, tt, pb, bp):
                            k.act(gbT[:, cc, tt * 512:(tt + 1) * 512], pb, AF.Silu, [bp], [bGB])
                        proj_F(wblock, evw[j], [12, 13], hT, bHT, dest_gb)

                    with (k.scope() if STOP >= 2 else contextlib.nullcontext()) as sc:
                      if STOP >= 2:
                            pexp = [sc.sb(un("pexp"), [128, 512], BF16) for i in range(4)]
                            bPE = [Buf("pexp%d" % i) for i in range(4)]
                            recL = [sc.sb(un("rec"), [64, 512], F32) for i in range(2)]; bRecL = [Buf("rec0"), Buf("rec1")]
                            oaL = [sc.sb(un("oa"), [128, 512], F32) for i in range(2)]; bOAL = [Buf("oa0"), Buf("oa1")]
                            QB = min(T, 512)
                            ie = 0; blk_i = 0
                            for s in range(nseq):
                                kbase = s * T if g == 0 else 0
                                nsc = (T + koff) // 128
                                for h in range(8):
                                    kv = h // 4
                                    hb = (h % 2) * 64
                                    for qb in range(T // QB):
                                        q0 = s * T + qb * QB
                                        bo_, bs_ = (6, 7) if blk_i % 2 == 0 else (4, 5)
                                        po = bank(bo_); psm = bank(bs_)
                                        rec = recL[blk_i % 2]; bRec = bRecL[blk_i % 2]; oa = oaL[blk_i % 2]; bOA = bOAL[blk_i % 2]
                                        blk_i += 1
                                        for sc_ in range(nsc):
                                            kpos = kbase + sc_ * 128
                                            bi = nb(0, 4); pb = bank(bi); bp = bB[bi]
                                            k.mm(pb[:, 0:QB], kT[:, kv, kpos:kpos + 128], qT[:, h, q0:q0 + QB], [bK, bQ], [bp])
                                            pe_ = pexp[ie % 4]; bpe = bPE[ie % 4]; ie += 1
                                            k.act(pe_[:, 0:QB], pb[:, 0:QB], AF.Exp, [bp], [bpe], scale=0.125)
                                            k.mm(po[0:64, 0:QB], vtok[:, kpos // 128, kv * 64:(kv + 1) * 64], pe_[:, 0:QB],
                                                 [bV, bpe], [bB[bo_]], start=(sc_ == 0), stop=(sc_ == nsc - 1))
                                            k.mm(psm[0:64, 0:QB], ones_bf[:, :], pe_[:, 0:QB],
                                                 [bC, bpe], [bB[bs_]], start=(sc_ == 0), stop=(sc_ == nsc - 1))
                                        k.recip(rec[:, 0:QB], psm[0:64, 0:QB], [bB[bs_]], [bRec])
                                        k.tt(oa[hb:hb + 64, 0:QB], po[0:64, 0:QB], rec[:, 0:QB], ALU.mult, [bB[bo_], bRec], [bOA])
                                        k.tt(oT[hb:hb + 64, h // 2, q0:q0 + QB], oa[hb:hb + 64, 0:QB],
                                             gaT[hb:hb + 64, h // 2, q0:q0 + QB], ALU.mult, [bOA, bGA], [bO])

                with k.scope() as sc:
                    C = 64
                    if STOP < 3:
                        raise_skip = True
                    else:
                        raise_skip = False
                    nchunk = T // C
                    f = lambda name, shape, dt=F32: sc.sb(un(name), shape, dt)
                    yf = f("yf", [128, nseq * nchunk // 2, 512], BF16); bYF = Buf("yf")
                    ST = [f("ST", [128, 4, 64]) for d in range(2)]; bST = [Buf("ST0"), Buf("ST1")]
                    zsP = [f("zs", [128, 14, C]) for p_ in range(2)]
                    zt1 = f("zt1", [128, 14, C])
                    twa = [f("tw", [65, C]) for d in range(2)]
                    ala = [f("al", [65, C]) for d in range(2)]
                    bW = Buf("rwtmp")
                    Ls = f("Ls", [64, 512])
                    gam = f("gam", [128, 4, C]); gamp = f("gamp", [128, 4, C]); ginv = f("ginv", [128, 4, C])
                    av = [f("av", [128, 4, C]) for d in range(2)]
                    kkr = f("kkr", [128, 4, C]); ksq = f("ksq", [128, 4, C]); kk = f("kk", [128, 4, C])
                    kdP = [[f("kd", [128, 4, C]) for d in range(2)] for p_ in range(2)]
                    tmp4 = f("tmp4", [128, 4, C]); tmp5 = f("tmp5", [128, 4, C])
                    Bn = {n_: Buf(n_) for n_ in ["zs", "zt1", "ala0", "ala1", "twa0", "twa1", "av0", "av1", "kkr", "ksq", "kk", "kd0", "kd1", "tmp4", "tmp5", "Ls", "gam", "gamp", "ginv", "AR", "KBt", "Gm1", "Gm2", "Nm0", "Nm1", "Am0", "Am1", "Vt", "Us0", "Us1", "KBT", "ysum", "ysq", "yst", "bon", "obt", "tmp6"] + [x + str(p_) for p_ in range(2) for x in ["zs", "AR", "KBt", "Vt", "Gm1", "Gm2", "Nmi", "gl", "kd0_", "kd1_"]]}
                    ARP = [f("AR", [128, 4, 2, C]) for p_ in range(2)]; KBtP = [f("KBt", [128, 4, 2, C]) for p_ in range(2)]
                    Gm1P = [f("Gm1", [64, 8, 128]) for p_ in range(2)]; Gm2P = [f("Gm2", [64, 8, 128]) for p_ in range(2)]; Nm = [f("Nm", [64, 8, 64]) for i in range(2)]
                    NmiP = [f("Nmi", [64, 8, 64]) for p_ in range(2)]; glP = [f("gl", [128, 4, 1]) for p_ in range(2)]; tmp6 = f("tmp6", [128, 4, C])
                    Am = [f("Am", [64, 8, 64]) for i in range(2)]
                    VtP = [f("Vt", [64, 512]) for p_ in range(2)]; Us = [f("Us", [64, 512]) for i in range(2)]
                    KBT = f("KBT", [64, 4, 2, 128])
                    ysum = f("ysum", [64, 8, 64]); ysq = f("ysq", [64, 8, 64]); yst = f("yst", [64, 16])
                    bon = f("bon", [128, 4, C]); obt = f("obt", [128, 4, C])
                    sto = f("sto", [64, 4, 128]); bSTO = Buf("sto")
                    sld = f("sld", [64, 8, 64]); bSLD = Buf("sld")
                    for d in range(2):
                        k.ms(twa[d][:], 1.0, [Bn["twa%d" % d]])
                        k.ms(ala[d][:], 1.0, [Bn["ala%d" % d]])

                    units = []
                    for s in range(min(nseq, KNS) if STOP >= 3 else 0):
                        for d in range(KND):
                            order = list(range(nchunk) if d == 0 else range(nchunk - 1, -1, -1))[:KNC]
                            for ci_, c in enumerate(order):
                                units.append((s, d, c, ci_ == 0, ci_ == len(order) - 1))
                    HO = [0, 2, 4, 6, 1, 3, 5, 7]
                    HOr = [1, 3, 5, 7, 0, 2, 4, 6]
                    rrA = [0]; rrB = [0]

                    def nbA():
                        rrA[0] += 1
                        return rrA[0] % 5

                    def nbB():
                        rrB[0] += 1
                        return 5 + rrB[0] % 3

                    def stageA(u, p):
                        s, d, c, first, last = u
                        tcol = s * TP + c * C
                        gt0 = s * T + c * C
                        yield
                        k.tt(zt1[:], zraw[:, :, tcol:tcol + C], zraw[:, :, tcol + 2:tcol + 2 + C], ALU.add, [bZ], [Bn["zt1"]], e="pool")
                        k.tt(zt1[:], zt1[:], evx[:, 0:14].unsqueeze(2).to_broadcast([128, 14, C]), ALU.mult, [Bn["zt1"], bP], [Bn["zt1"]], e="pool")
                        k.tt(zsP[p][:], zraw[:, :, tcol + 1:tcol + 1 + C], evx[:, 14:28].unsqueeze(2).to_broadcast([128, 14, C]),
                             ALU.mult, [bZ, bP], [Bn["zs%d" % p]])
                        k.tt(zsP[p][:], zsP[p][:], zt1[:], ALU.add, [Bn["zs%d" % p], Bn["zt1"]], [Bn["zs%d" % p]])
                        r_ = zsP[p][:, 0:4, :]; kraw = zsP[p][:, 4:8, :]; vv = zsP[p][:, 8:12, :]
                        dirs = [d] if d == 0 else [0, 1]
                        yield
                        for dd in dirs:
                            bal = Bn["ala%d" % dd]; bav = Bn["av%d" % dd]; bkd = Bn["kd%d_%d" % (dd, p)]
                            k.cp(ala[dd][0:64, :], zsP[p][dd * 64:(dd + 1) * 64, 13, :], [Bn["zs%d" % p]], [bal])
                            b2 = nbA(); p2 = bank(b2)
                            for cp in range(4):
                                k.mm(p2[:, cp * C:(cp + 1) * C], a2aug[:, dd, cp * 128:(cp + 1) * 128], ala[dd][:, :],
                                     [bal, bP], [bB[b2]])
                            k.act(av[dd][:].rearrange("p c t -> p (c t)"), p2[:, 0:4 * C], AF.Sigmoid, [bB[b2]], [bav])
                            k.tt(tmp4[:], av[dd][:], evs[:, 18:22].unsqueeze(2).to_broadcast([128, 4, C]), ALU.mult, [bav, bP], [Bn["tmp4"]], e="pool")
                            k.tt(tmp4[:], tmp4[:], evx[:, 28:32].unsqueeze(2).to_broadcast([128, 4, C]), ALU.add, [Bn["tmp4"], bP], [Bn["tmp4"]], e="pool")
                            k.tt(kdP[p][dd][:], kraw, tmp4[:], ALU.mult, [Bn["zs%d" % p], Bn["tmp4"]], [bkd], e="pool")
                        yield
                        k.tt(kkr[:], kraw, evs[:, 14:18].unsqueeze(2).to_broadcast([128, 4, C]), ALU.mult, [Bn["zs%d" % p], bP], [Bn["kkr"]])
                        k.tt(ksq[:], kkr[:], kkr[:], ALU.mult, [Bn["kkr"]], [Bn["ksq"]])
                        b2 = nbA(); p2 = bank(b2)
                        k.mm(p2[:, 0:4 * C], bdones[:, :], ksq[:].rearrange("p c t -> p (c t)"), [Bn["ksq"], bC], [bB[b2]])
                        k.ts(ksq[:].rearrange("p c t -> p (c t)"), p2[:, 0:4 * C], 1e-12, ALU.add, [bB[b2]], [Bn["ksq"]])
                        k.act(ksq[:], ksq[:], AF.Sqrt, [Bn["ksq"]], [Bn["ksq"]])
                        k.recip(ksq[:], ksq[:], [Bn["ksq"]], [Bn["ksq"]])
                        k.tt(kk[:], kkr[:], ksq[:], ALU.mult, [Bn["kkr"], Bn["ksq"]], [Bn["kk"]])
                        yield
                        btw = Bn["twa%d" % d]
                        k.act(twa[d][0:64, :], zsP[p][d * 64:(d + 1) * 64, 12, :], AF.Tanh, [Bn["zs%d" % p]], [btw])
                        b2 = nbA(); p2 = bank(b2)
                        k.mm(p2[0:64, :], twa[d][:, :], w2aug[:, d, :], [btw, bP], [bB[b2]])
                        k.act(Ls[:], p2[0:64, :], AF.Sigmoid, [bB[b2]], [Bn["Ls"]])
                        b2 = nbA(); p2 = bank(b2)
                        for cp in range(4):
                            k.mm(p2[:, cp * 128:(cp + 1) * 128], Ls[:, cp * 128:(cp + 1) * 128],
                                 tri64[:, d, :, :].rearrange("p a t -> p (a t)"), [Bn["Ls"], bC], [bB[b2]])
                        p4 = p2.rearrange("p (c a t) -> p c a t", c=4, a=2)
                        k.act(gamp[:], p4[:, :, 1, :], AF.Exp, [bB[b2]], [Bn["gamp"]])
                        k.act(ginv[:], p4[:, :, 0, :], AF.Exp, [bB[b2]], [Bn["ginv"]], scale=-1.0)
                        k.act(gam[:], p4[:, :, 0, :], AF.Exp, [bB[b2]], [Bn["gam"]])
                        k.cp(glP[p][:], (gam[:, :, C - 1:C] if d == 0 else gam[:, :, 0:1]), [Bn["gam"]], [Bn["gl%d" % p]])
                        yield
                        bav = Bn["av%d" % d]; bkd = Bn["kd%d_%d" % (d, p)]
                        k.stt(ARP[p][:, :, 0, :], kk[:], -1.0, gamp[:], ALU.mult, ALU.mult, [Bn["kk"], Bn["gamp"]], [Bn["AR%d" % p]])
                        k.tt(KBtP[p][:, :, 0, :], kdP[p][d][:], ginv[:], ALU.mult, [bkd, Bn["ginv"]], [Bn["KBt%d" % p]])
                        k.tt(tmp5[:], kk[:], av[d][:], ALU.mult, [Bn["kk"], bav], [Bn["tmp5"]])
                        k.tt(KBtP[p][:, :, 1, :], tmp5[:], ginv[:], ALU.mult, [Bn["tmp5"], Bn["ginv"]], [Bn["KBt%d" % p]])
                        k.tt(ARP[p][:, :, 1, :], r_, gam[:], ALU.mult, [Bn["zs%d" % p], Bn["gam"]], [Bn["AR%d" % p]])
                        yield
                        b2 = nbA(); p2 = bank(b2)
                        for cp in range(4):
                            k.tr(p2[0:64, cp * 128:(cp + 1) * 128], zsP[p][:, 8 + cp, :], ident[:], [Bn["zs%d" % p], bC], [bB[b2]])
                        k.act(VtP[p][:], p2[0:64, :], AF.Copy, [bB[b2]], [Bn["Vt%d" % p]])
                        yield
                        g1, g2, gn = 0, 2, 4
                        for h in HO:
                            cp = h // 2; hb = (h % 2) * 64
                            arh = ARP[p][hb:hb + 64, cp, :, :].rearrange("p a t -> p (a t)")
                            k.mm(bank(g1 + h // 4)[0:64, (h % 4) * 128:(h % 4 + 1) * 128], KBtP[p][hb:hb + 64, cp, 0, :], arh,
                                 [Bn["KBt%d" % p], Bn["AR%d" % p]], [bB[g1 + h // 4]])
                            k.mm(bank(g2 + h // 4)[0:64, (h % 4) * 128:(h % 4 + 1) * 128], KBtP[p][hb:hb + 64, cp, 1, :], arh,
                                 [Bn["KBt%d" % p], Bn["AR%d" % p]], [bB[g2 + h // 4]])
                            k.mm(bank(gn)[0:64, h * 64:(h + 1) * 64], ARP[p][hb:hb + 64, cp, 0, :], KBtP[p][hb:hb + 64, cp, 1, :],
                                 [Bn["KBt%d" % p], Bn["AR%d" % p]], [bB[gn]])
                        yield
                        m64b = mask64[:, d, :].unsqueeze(1).to_broadcast([64, 4, 128])
                        for hf in range(2):
                            k.tt(Gm1P[p][:, hf * 4:hf * 4 + 4, :], bank(g1 + hf)[0:64, :].rearrange("p (h t) -> p h t", h=4), m64b,
                                 ALU.mult, [bB[g1 + hf], bC], [Bn["Gm1%d" % p]])
                        for hf in range(2):
                            k.tt(Gm2P[p][:, hf * 4:hf * 4 + 4, :], bank(g2 + hf)[0:64, :].rearrange("p (h t) -> p h t", h=4), m64b,
                                 ALU.mult, [bB[g2 + hf], bC], [Bn["Gm2%d" % p]])
                        k.tt(NmiP[p][:], bank(gn)[0:64, :].rearrange("p (h t) -> p h t", h=8),
                             maskN[:, d, :].unsqueeze(1).to_broadcast([64, 8, 64]), ALU.mult, [bB[gn], bC], [Bn["Nmi%d" % p]])

                        yield

                    def stageB(u, p, pull):
                        s, d, c, first, last = u
                        tcol = s * TP + c * C
                        gt0 = s * T + c * C
                        r_ = zsP[p][:, 0:4, :]; vv = zsP[p][:, 8:12, :]
                        if first:
                            if g == 0:
                                k.ms(ST[d][:], 0.0, [bST[d]])
                            else:
                                k.dma("sp", sld[:], srs_d[d][j].rearrange("h v k -> v h k"), [], [bSLD])
                                b2 = nbB(); p2 = bank(b2)
                                for cp in range(4):
                                    k.tr(p2[:, cp * 64:(cp + 1) * 64], sld[:, 2 * cp:2 * cp + 2, :].rearrange("v h k -> v (h k)"),
                                         ident[0:64, 0:64], [bSLD, bC], [bB[b2]])
                                k.cp(ST[d][:], p2[:, 0:256].rearrange("p (c v) -> p c v", c=4), [bB[b2]], [bST[d]])

                        b2 = nbB(); p2 = bank(b2)
                        for h in HOr:
                            cp = h // 2; hb = (h % 2) * 64
                            k.mm(p2[0:64, h * 64:(h + 1) * 64], ARP[p][hb:hb + 64, cp, 0, :], ST[d][hb:hb + 64, cp, :],
                                 [Bn["AR%d" % p], bST[d]], [bB[b2]])
                        b2b = nbB(); p2b = bank(b2b)
                        for h in range(8):
                            k.mm(p2b[0:64, h * 64:(h + 1) * 64], Gm1P[p][:, h, 0:64], VtP[p][:, h * 64:(h + 1) * 64],
                                 [Bn["Gm1%d" % p], Bn["Vt%d" % p]], [bB[b2b]])
                        k.act(Us[0][:], p2[0:64, :], AF.Copy, [bB[b2]], [Bn["Us0"]])
                        k.tt(Us[0][:], Us[0][:], p2b[0:64, :], ALU.add, [Bn["Us0"], bB[b2b]], [Bn["Us0"]])
                        ui = 0; ai = 0
                        for jn in range(6):
                            bA = Bn["Am%d" % ai] if jn else Bn["Gm2%d" % p]; bN = Bn["Nm%d" % ai] if jn else Bn["Nmi%d" % p]
                            bA2 = Bn["Am%d" % (1 - ai)]; bN2 = Bn["Nm%d" % (1 - ai)]
                            Acur = (lambda h, ai=ai: Am[ai][:, h, :]) if jn else (lambda h: Gm2P[p][:, h, 0:64])
                            Ncur = (lambda h, ai=ai: Nm[ai][:, h, :]) if jn else (lambda h: NmiP[p][:, h, :])
                            pull()
                            bU = Bn["Us%d" % ui]; bU2 = Bn["Us%d" % (1 - ui)]
                            b2 = nbB(); p2 = bank(b2)
                            for h in range(8):
                                k.mm(p2[0:64, h * 64:(h + 1) * 64], Acur(h), Us[ui][:, h * 64:(h + 1) * 64], [bA, bU], [bB[b2]])
                            if jn < 5:
                                b3 = nbB(); p3 = bank(b3)
                                for h in range(8):
                                    k.mm(p3[0:64, h * 64:(h + 1) * 64], Ncur(h), Acur(h), [bN, bA], [bB[b3]])
                                if jn < 4:
                                    b4 = nbB(); p4_ = bank(b4)
                                    for h in range(8):
                                        k.mm(p4_[0:64, h * 64:(h + 1) * 64], Acur(h), Ncur(h), [bA, bN], [bB[b4]])
                            k.tt(Us[1 - ui][:], Us[ui][:], p2[0:64, :], ALU.add, [bU, bB[b2]], [bU2])
                            ui = 1 - ui
                            pull()
                            if jn < 5:
                                k.act(Am[1 - ai][:].rearrange("p h t -> p (h t)"), p3[0:64, :], AF.Copy, [bB[b3]], [bA2])
                                if jn < 4:
                                    k.act(Nm[1 - ai][:].rearrange("p h t -> p (h t)"), p4_[0:64, :], AF.Copy, [bB[b4]], [bN2])
                                ai = 1 - ai
                        U = Us[ui]; bU = Bn["Us%d" % ui]
                        by = nbB(); py = bank(by)
                        by0 = nbB(); py0 = bank(by0)
                        for h in range(8):
                            o_ = py[0:64, h * 64:(h + 1) * 64]
                            k.mm(o_, Gm2P[p][:, h, 64:128], U[:, h * 64:(h + 1) * 64], [Bn["Gm2%d" % p], bU], [bB[by]], start=True, stop=False)
                            k.mm(o_, Gm1P[p][:, h, 64:128], VtP[p][:, h * 64:(h + 1) * 64], [Bn["Gm1%d" % p], Bn["Vt%d" % p]], [bB[by]], start=False, stop=True)
                        for h in HO:
                            cp = h // 2; hb = (h % 2) * 64
                            k.mm(py0[0:64, h * 64:(h + 1) * 64], ARP[p][hb:hb + 64, cp, 1, :], ST[d][hb:hb + 64, cp, :], [Bn["AR%d" % p], bST[d]], [bB[by0]])
                        ci = s * nchunk + c
                        ys2 = ysum[:].rearrange("p h v -> p (h v)")
                        yfs = yf[(ci % 2) * 64:(ci % 2) * 64 + 64, ci // 2, :]
                        bys = Bn["ysum"]
                        if d == 0:
                            k.act(ys2, py0[0:64, :], AF.Copy, [bB[by0]], [bys])
                            k.tt(ys2, ys2, py[0:64, :], ALU.add, [bys, bB[by]], [bys])
                            k.cp(yfs, ys2, [bys], [bYF])
                        else:
                            k.tt(ys2, py[0:64, :], yfs, ALU.add, [bB[by], bYF], [bys])
                            k.tt(ys2, ys2, py0[0:64, :], ALU.add, [bys, bB[by0]], [bys])
                        pull()
                        bt1 = 6
                        for cp in range(4):
                            for a_ in range(2):
                                idx = cp * 2 + a_
                                k.tr(bank(bt1 + idx // 4)[0:64, (idx % 4) * 128:(idx % 4 + 1) * 128], KBtP[p][:, cp, a_, :], ident[:],
                                     [Bn["KBt%d" % p], bC], [bB[bt1 + idx // 4]])
                        k.act(KBT[:, 0:2, :, :].rearrange("p c a k -> p (c a k)"), bank(bt1)[0:64, :], AF.Copy, [bB[bt1]], [Bn["KBT"]])
                        k.cp(KBT[:, 2:4, :, :].rearrange("p c a k -> p (c a k)"), bank(bt1 + 1)[0:64, :], [bB[bt1 + 1]], [Bn["KBT"]])
                        bs = 5; psu = bank(bs)
                        for cp in range(4):
                            k.mm(psu[:, cp * 128:(cp + 1) * 128], KBT[:, cp, 0, :], VtP[p][:, cp * 128:(cp + 1) * 128], [Bn["KBT"], Bn["Vt%d" % p]], [bB[bs]],
                                 start=True, stop=False)
                            k.mm(psu[:, cp * 128:(cp + 1) * 128], KBT[:, cp, 1, :], U[:, cp * 128:(cp + 1) * 128], [Bn["KBT"], bU], [bB[bs]],
                                 start=False, stop=True)
                        ps4 = psu.rearrange("p (c x) -> p c x", c=4)
                        for hh in range(2):
                            hb = hh * 64
                            k.tt(ST[d][hb:hb + 64, :, :], ST[d][hb:hb + 64, :, :], ps4[hb:hb + 64, :, hb:hb + 64], ALU.add,
                                 [bST[d], bB[bs]], [bST[d]])
                            k.tt(ST[d][hb:hb + 64, :, :], ST[d][hb:hb + 64, :, :],
                                 glP[p][hb:hb + 64, :, :].to_broadcast([64, 4, 64]), ALU.mult, [bST[d], Bn["gl%d" % p]], [bST[d]])
                        pull()
                        if d == 1:
                            k.red(yst[:, 0:8], ysum[:], [bys], [Bn["yst"]])
                            k.ts(yst[:, 0:8], yst[:, 0:8], 1.0 / 64, ALU.mult, [Bn["yst"]], [Bn["yst"]])
                            k.tt(ysum[:], ysum[:], yst[:, 0:8].unsqueeze(2).to_broadcast([64, 8, 64]), ALU.subtract, [bys, Bn["yst"]], [bys])
                            k.tt(ysq[:], ysum[:], ysum[:], ALU.mult, [bys], [Bn["ysq"]])
                            k.red(yst[:, 8:16], ysq[:], [Bn["ysq"]], [Bn["yst"]])
                            k.ts(yst[:, 8:16], yst[:, 8:16], 1.0 / 64, ALU.mult, [Bn["yst"]], [Bn["yst"]], s2=GN_EPS, op1=ALU.add)
                            k.act(yst[:, 8:16], yst[:, 8:16], AF.Sqrt, [Bn["yst"]], [Bn["yst"]])
                            k.recip(yst[:, 8:16], yst[:, 8:16], [Bn["yst"]], [Bn["yst"]])
                            k.tt(ysum[:], ysum[:], yst[:, 8:16].unsqueeze(2).to_broadcast([64, 8, 64]), ALU.mult, [bys, Bn["yst"]], [bys])
                            k.tt(tmp6[:], kdP[p][0][:], kdP[p][1][:], ALU.add, [Bn["kd0_%d" % p], Bn["kd1_%d" % p]], [Bn["tmp6"]], e="pool")
                            k.tt(tmp6[:], tmp6[:], r_, ALU.mult, [Bn["tmp6"], Bn["zs%d" % p]], [Bn["tmp6"]], e="pool")
                            k.tt(tmp6[:], tmp6[:], evs[:, 22:26].unsqueeze(2).to_broadcast([128, 4, C]), ALU.mult, [Bn["tmp6"], bP], [Bn["tmp6"]], e="pool")
                            b2 = nbB(); p2 = bank(b2)
                            k.mm(p2[:, 0:4 * C], bdones[:, :], tmp6[:].rearrange("p c t -> p (c t)"), [Bn["tmp6"], bC], [bB[b2]])
                            k.tt(bon[:], p2[:, 0:4 * C].rearrange("p (c t) -> p c t", c=4), vv, ALU.mult, [bB[b2], Bn["zs%d" % p]], [Bn["bon"]])
                            b3 = nbB(); p3 = bank(b3)
                            for cp in range(4):
                                k.tr(p3[:, cp * C:(cp + 1) * C], ysum[:, 2 * cp:2 * cp + 2, :].rearrange("p h v -> p (h v)"),
                                     ident[0:64, 0:64], [bys, bC], [bB[b3]])
                            p3v = p3[:, 0:4 * C].rearrange("p (c t) -> p c t", c=4)
                            k.tt(obt[:], p3v, evs[:, 26:30].unsqueeze(2).to_broadcast([128, 4, C]), ALU.mult, [bB[b3], bP], [Bn["obt"]])
                            k.tt(obt[:], obt[:], evs[:, 30:34].unsqueeze(2).to_broadcast([128, 4, C]), ALU.add, [Bn["obt"], bP], [Bn["obt"]])
                            k.tt(obt[:], obt[:], bon[:], ALU.add, [Bn["obt"], Bn["bon"]], [Bn["obt"]])
                            k.tt(oT[:, 4:8, gt0:gt0 + C], obt[:], gbT[:, :, gt0:gt0 + C], ALU.mult, [Bn["obt"], bGB], [bO])

                        if last:
                            if g == 0:
                                b2 = nbB(); p2 = bank(b2)
                                for cp in range(4):
                                    k.tr(p2[0:64, cp * 128:(cp + 1) * 128], ST[d][:, cp, :], ident[:], [bST[d], bC], [bB[b2]])
                                k.cp(sto[:].rearrange("p c x -> p (c x)"), p2[0:64, :], [bB[b2]], [bSTO])
                                k.dma("pool", nrs_o[d][s, j].rearrange("h v k -> v h k"),
                                      sto[:].rearrange("p c (h k) -> p (c h) k", h=2), [bSTO], [])

                    gens = {}
                    if units:
                        for _ in stageA(units[0], 0):
                            pass
                    for ui_, u in enumerate(units):
                        nxt = stageA(units[ui_ + 1], (ui_ + 1) % 2) if ui_ + 1 < len(units) else None

                        def pull(nxt=nxt, n=2):
                            if nxt is None:
                                return
                            for _ in range(n):
                                try:
                                    next(nxt)
                                except StopIteration:
                                    return
                        stageB(u, ui_ % 2, pull)
                        if nxt is not None:
                            for _ in nxt:
                                pass
            if STOP >= 4:
                out_proj(evo[j])

        def odd_layer(g, j, L):
            nseq, T = (4, 256) if g == 0 else (1, 1024)
            C = 128
            nchunk = T // C
            k.dma("sp", gw2aug[:], gw2aug_d[j], [], [bP])
            k.dma("sp", ln256[:], ln256_d[j], [], [bP])
            with k.scope() as scL:
                qT = scL.sb(un("gqT"), [128, 4, GT], BF16); bQ = Buf("gqT")
                kT = scL.sb(un("gkT"), [128, 4, GT], BF16); bK = Buf("gkT")
                vt = scL.sb(un("gvt"), [128, 8, 1024], BF16); bV = Buf("gvt")
                gs = scL.sb(un("ggs"), [128, 8, GT], BF16); bG = Buf("ggs")
                glT = scL.sb(un("glT"), [17, 2, GT], F32); bGL = Buf("glT")
                of = scL.sb(un("gof"), [128, 8, 1024], BF16); bOF = Buf("gof")
                k.ms(glT[:], 1.0, [bGL])
                with k.scope() as sc:
                    hT, bHT = make_h(sc, L, g)
                    wblock = wblock_factory(sc)

                    def dest_q(cc, tt, pb, bp):
                        k.ts(qT[:, cc, tt * 512:(tt + 1) * 512], pb, float(128 ** -0.5), ALU.mult, [bp], [bQ])
                    proj_F(wblock, odw[j], [0, 1], hT, bHT, dest_q)

                    def dest_k(cc, tt, pb, bp):
                        k.cp(kT[:, cc, tt * 512:(tt + 1) * 512], pb, [bp], [bK])
                    proj_F(wblock, odw[j], [2, 3], hT, bHT, dest_k)

                    def dest_v(bi_, t8, pb, bp):
                        k.cp(vt[:, t8, bi_ * 256:(bi_ + 1) * 256], pb[:, 0:256], [bp], [bV])
                    proj_T(wblock, odw[j], [4, 5, 6, 7], hT, bHT, dest_v)

                    def dest_g(cc, tt, pb, bp):
                        k.act(gs[:, cc, tt * 512:(tt + 1) * 512], pb, AF.Silu, [bp], [bG])
                    proj_F(wblock, odw[j], [8, 9, 10, 11], hT, bHT, dest_g)

                    def dest_gl(cc, tt, pb, bp):
                        if cc < 2:
                            k.cp(glT[0:16, cc, tt * 512:(tt + 1) * 512], pb[0:16, :], [bp], [bGL])
                    proj_F(wblock, odw[j], [12], hT, bHT, dest_gl, msub=16, nsub=2)

                with k.scope() as sc:
                    f = lambda name, shape, dt=F32: sc.sb(un(name), shape, dt)
                    S = [f("gS", [128, 4, 256]) for d in range(2)]; bS = [Buf("gS0"), Buf("gS1")]
                    bW = Buf("glatmp")
                    G = {n_: Buf("g_" + n_) for n_ in ["Lg", "gam", "ginv", "qs", "ks", "Am", "kTt", "osum", "osq", "ost"]}
                    Lg = f("Lg", [128, 512])
                    gam = f("ggam", [128, 4, C]); ginv = f("gginv", [128, 4, C])
                    qs = f("gqs", [128, 4, C]); ks = f("gks", [128, 4, C])
                    Am = f("gAm", [128, 4, C], BF16); kTt = f("gkTt", [128, 4, C], BF16)
                    osum = f("gosum", [128, 4, 256]); osq = f("gosq", [128, 4, 256]); ost = f("gost", [128, 4])
                    print("GLA sbuf remaining", nc.sbuf_bytes_remaining)
                    for s in range(nseq):
                        for d in range(2):
                            if g == 0:
                                k.ms(S[d][:], 0.0, [bS[d]])
                            else:
                                k.dma("sp", S[d][:], sgs_d[d][j].rearrange("h k v -> k h v"), [], [bS[d]])
                            order = range(nchunk) if d == 0 else range(nchunk - 1, -1, -1)
                            for c in order:
                                gt0 = s * T + c * C
                                t8 = gt0 // 128
                                b2 = nb(); p2 = bank(b2)
                                k.mm(p2, glT[:, d, gt0:gt0 + C], gw2aug[:, d, :], [bGL, bP], [bB[b2]])
                                k.act(Lg[:], p2, AF.Sigmoid, [bB[b2]], [G["Lg"]])
                                k.act(Lg[:], Lg[:], AF.Ln, [G["Lg"]], [G["Lg"]])
                                b2 = nb(); p2 = bank(b2)
                                for h in range(4):
                                    k.mm(p2[:, h * C:(h + 1) * C], Lg[:, h * 128:(h + 1) * 128], tri128[:, d, :], [G["Lg"], bC], [bB[b2]])
                                k.act(gam[:].rearrange("p h t -> p (h t)"), p2, AF.Exp, [bB[b2]], [G["gam"]])
                                k.act(ginv[:].rearrange("p h t -> p (h t)"), p2, AF.Exp, [bB[b2]], [G["ginv"]], scale=-1.0)
                                glast = gam[:, :, C - 1:C] if d == 0 else gam[:, :, 0:1]
                                k.tt(qs[:], qT[:, :, gt0:gt0 + C], gam[:], ALU.mult, [bQ, G["gam"]], [G["qs"]])
                                k.tt(ks[:], kT[:, :, gt0:gt0 + C], ginv[:], ALU.mult, [bK, G["ginv"]], [G["ks"]])
                                b2 = nb(); p2 = bank(b2)
                                for h in range(4):
                                    k.mm(p2[:, h * C:(h + 1) * C], ks[:, h, :], qs[:, h, :], [G["ks"], G["qs"]], [bB[b2]])
                                k.tt(Am[:], p2.rearrange("p (h t) -> p h t", h=4), mask128[:, d, :].unsqueeze(1).to_broadcast([128, 4, C]),
                                     ALU.mult, [bB[b2], bC], [G["Am"]])
                                b2 = nb(); p2 = bank(b2)
                                for h in range(4):
                                    k.tr(p2[:, h * C:(h + 1) * C], ks[:, h, :], ident[:], [G["ks"], bC], [bB[b2]])
                                k.cp(kTt[:].rearrange("p h t -> p (h t)"), p2, [bB[b2]], [G["kTt"]])
                                by = nb2()
                                for h in range(4):
                                    o_ = bank(by + h // 2)[:, (h % 2) * 256:(h % 2 + 1) * 256]
                                    k.mm(o_, qs[:, h, :], S[d][:, h, :], [G["qs"], bS[d]], [bB[by + h // 2]], start=True, stop=False)
                                    k.mm(o_, Am[:, h, :], vt[:, t8, h * 256:(h + 1) * 256], [G["Am"], bV], [bB[by + h // 2]], start=False, stop=True)
                                bs = nb2()
                                for h in range(4):
                                    k.mm(bank(bs + h // 2)[:, (h % 2) * 256:(h % 2 + 1) * 256], kTt[:, h, :], vt[:, t8, h * 256:(h + 1) * 256],
                                         [G["kTt"], bV], [bB[bs + h // 2]])
                                for hf in range(2):
                                    sl = S[d][:, 2 * hf:2 * hf + 2, :]
                                    k.tt(sl, sl, bank(bs + hf).rearrange("p (h v) -> p h v", h=2), ALU.add, [bS[d], bB[bs + hf]], [bS[d]])
                                k.tt(S[d][:], S[d][:], glast.to_broadcast([128, 4, 256]), ALU.mult, [bS[d], G["gam"]], [bS[d]])
                                if d == 0:
                                    for hf in range(2):
                                        k.cp(of[:, t8, hf * 512:(hf + 1) * 512], bank(by + hf), [bB[by + hf]], [bOF])
                                else:
                                    for hf in range(2):
                                        k.tt(osum[:, 2 * hf:2 * hf + 2, :].rearrange("p h v -> p (h v)"), bank(by + hf),
                                             of[:, t8, hf * 512:(hf + 1) * 512], ALU.add, [bB[by + hf], bOF], [G["osum"]])
                                    k.act(osq[:], osum[:], AF.Square, [G["osum"]], [G["osq"]])
                                    k.red(ost[:], osq[:], [G["osq"]], [G["ost"]])
                                    k.ts(ost[:], ost[:], 1.0 / 256, ALU.mult, [G["ost"]], [G["ost"]], s2=EPS, op1=ALU.add)
                                    k.act(ost[:], ost[:], AF.Sqrt, [G["ost"]], [G["ost"]])
                                    k.recip(ost[:], ost[:], [G["ost"]], [G["ost"]])
                                    k.tt(osum[:], osum[:], ost[:].unsqueeze(2).to_broadcast([128, 4, 256]), ALU.mult, [G["osum"], G["ost"]], [G["osum"]])
                                    k.tt(osum[:], osum[:], ln256[:, :].unsqueeze(1).to_broadcast([128, 4, 256]), ALU.mult, [G["osum"], bP], [G["osum"]])
                                    bt = nb2()
                                    o2 = osum[:].rearrange("p h v -> p (h v)")
                                    for cc in range(8):
                                        k.tr(bank(bt + cc // 4)[:, (cc % 4) * 128:(cc % 4 + 1) * 128], o2[:, cc * 128:(cc + 1) * 128], ident[:],
                                             [G["osum"], bC], [bB[bt + cc // 4]])
                                    for hf in range(2):
                                        k.tt(oT[:, 4 * hf:4 * hf + 4, gt0:gt0 + C], bank(bt + hf).rearrange("p (c t) -> p c t", c=4),
                                             gs[:, 4 * hf:4 * hf + 4, gt0:gt0 + C], ALU.mult, [bB[bt + hf], bG], [bO])
                            if g == 0:
                                k.dma("pool", ngs_o[d][s, j].rearrange("h k v -> k h v"), S[d][:], [bS[d]], [])
            out_proj(odo[j])

        for g in groups:
            with k.scope() as sc:
                xin = [sc.sb(un("xin"), [128, D], F32) for i in range(2)]
                bxin = [Buf("xin0"), Buf("xin1")]
                for t8 in range(GT // 128):
                    xi = xin[t8 % 2]; bxi = bxin[t8 % 2]
                    k.dma("sp", xi[:], xg[g, t8 * 128:(t8 + 1) * 128, :], [], [bxi])
                    for half in range(2):
                        b2 = nb(); p2 = bank(b2)
                        for jj in range(4):
                            cch = half * 4 + jj
                            k.tr(p2[:, jj * 128:(jj + 1) * 128], xi[:, cch * 128:(cch + 1) * 128], ident[:], [bxi, bC], [bB[b2]])
                        k.cp(xT[:, half * 4:(half + 1) * 4, t8 * 128:(t8 + 1) * 128], p2.rearrange("p (j t) -> p j t", j=4), [bB[b2]], [bX])
            for L in range(nlayers):
                if L % 2 == 0:
                    even_layer(g, L // 2, L)
                else:
                    odd_layer(g, L // 2, L)
            with k.scope() as sc:
                rstd = compute_rstd(sc)
                yout = [sc.sb(un("yout"), [128, D], F32) for i in range(2)]
                byo = [Buf("yout0"), Buf("yout1")]
                tmpn = [sc.sb(un("tmpn"), [128, 512], F32) for i in range(2)]
                btn = [Buf("tmpn0"), Buf("tmpn1")]
                it = 0
                for t8 in range(GT // 128):
                    yo = yout[t8 % 2]; by_ = byo[t8 % 2]
                    for half in range(2):
                        b2 = nb(); p2 = bank(b2)
                        tn = tmpn[it % 2]; bt_ = btn[it % 2]; it += 1
                        for jj in range(4):
                            cch = half * 4 + jj
                            k.stt(tn[:, jj * 128:(jj + 1) * 128], xT[:, cch, t8 * 128:(t8 + 1) * 128], finalg_sb[:, cch:cch + 1],
                                  rstd[:, t8 * 128:(t8 + 1) * 128], ALU.mult, ALU.mult, [bX, bR, bC], [bt_])
                            k.tr(p2[:, jj * 128:(jj + 1) * 128], tn[:, jj * 128:(jj + 1) * 128], ident[:], [bt_, bC], [bB[b2]])
                        k.act(yo[:, half * 512:(half + 1) * 512], p2, AF.Copy, [bB[b2]], [by_])
                    k.dma("pool", yg[g, t8 * 128:(t8 + 1) * 128, :], yo[:], [by_], [])
        k.barrier()
    print("instructions:", k.ninstr, {e: c for e, c in k.cnt.items()})
    return nc


_CACHE = {}


def _consts():
    f32 = np.float32
    c = {}
    c["ident"] = np.eye(128, dtype=f32)
    s = np.arange(64)[:, None]; t = np.arange(64)[None, :]
    tri = np.zeros((64, 2, 2, 64), f32)
    tri[:, 0, 0] = (s <= t); tri[:, 0, 1] = (s < t); tri[:, 1, 0] = (s >= t); tri[:, 1, 1] = (s > t)
    c["tri64"] = tri * f32(-RWKV_DECAY_SCALE)
    s1 = np.arange(128)[:, None]; t1 = np.arange(128)[None, :]
    tri128 = np.zeros((128, 2, 128), f32)
    tri128[:, 0] = (s1 <= t1); tri128[:, 1] = (s1 >= t1)
    c["tri128"] = tri128 / f32(16.0)
    c["mask128"] = tri128.copy()
    m64 = np.zeros((64, 2, 128), f32)
    m64[:, 0, 0:64] = (s < t); m64[:, 0, 64:128] = (s <= t); m64[:, 1, 0:64] = (s > t); m64[:, 1, 64:128] = (s >= t)
    c["mask64"] = m64
    mN = np.zeros((64, 2, 64), f32)
    mN[:, 0] = (t < s); mN[:, 1] = (t > s)
    c["maskN"] = mN
    bd = np.zeros((128, 128), f32); bd[0:64, 0:64] = 1; bd[64:128, 64:128] = 1
    c["bdones"] = bd
    T = 1024
    row = np.repeat(np.arange(T // 64), 64).astype(f32); col = np.tile(np.arange(64), T // 64).astype(f32)
    inv = (f32(10000.0) ** (-np.arange(16, dtype=f32) / f32(16))).astype(f32)
    ang = np.stack([row[:, None] * inv, col[:, None] * inv], axis=1).astype(f32)
    cos = np.cos(ang).astype(f32); sin = np.sin(ang).astype(f32)
    cos64 = np.stack([cos, cos], axis=2).reshape(T, 64)
    sin64 = np.stack([-sin, sin], axis=2).reshape(T, 64)
    c["ropecos"] = np.ascontiguousarray(cos64.reshape(8, 128, 64).transpose(1, 0, 2))
    c["ropesin"] = np.ascontiguousarray(sin64.reshape(8, 128, 64).transpose(1, 0, 2))
    return c


def kernel(**inp):
    f32 = np.float32
    n = 8
    A = lambda name: np.asarray(inp[name], f32)
    x_prompt = A("x_prompt"); x_sample = A("x_sample"); c = A("c"); c_ctx = A("c_ctx")

    def pc(v, nchunk):
        v = np.asarray(v, f32)
        lead = v.shape[:-1]
        v = v.reshape(lead + (nchunk, 128))
        return np.ascontiguousarray(np.moveaxis(v, -1, 0))

    def wblk(w, nblk):
        Ln, rows, cols = w.shape
        if cols < nblk * WB:
            w = np.concatenate([w, np.zeros((Ln, rows, nblk * WB - cols), f32)], axis=2)
        return np.ascontiguousarray(w.reshape(Ln, NCH, 128, nblk, WB).transpose(0, 3, 2, 1, 4))

    shared = dict(_consts())
    shared["modw"] = wblk(A("mod_w"), 12).reshape(DEPTH * 12, 128, NCH, WB)
    shared["modbT"] = pc(A("mod_b"), 24)
    shared["normgT"] = pc(A("norm_g"), NCH)
    shared["finalgT"] = pc(A("final_g"), NCH)
    shared["evw"] = wblk(A("ev_w_in"), 14)
    shared["evo"] = wblk(A("ev_w_out"), 4)
    shared["odw"] = wblk(A("od_w_in"), 13)
    shared["odo"] = wblk(A("od_w_out"), 4)
    shared["w2aug"] = np.ascontiguousarray(np.concatenate([A("rw_w2"), A("rw_w0")[:, :, None, :]], axis=2).transpose(0, 2, 1, 3))
    shared["a2aug"] = np.ascontiguousarray(np.concatenate([A("rw_a2"), A("rw_a0")[:, :, None, :]], axis=2).transpose(0, 2, 1, 3))
    shared["gw2aug"] = np.ascontiguousarray(np.concatenate([A("gla_w2"), A("gla_b")[:, :, None, :]], axis=2).transpose(0, 2, 1, 3))
    shared["ln256"] = np.ascontiguousarray(np.broadcast_to(A("gla_ln_g")[:, None, :], (2, 128, 256)))
    qk = np.stack([A("ev_qn_g"), A("ev_kn_g")], axis=1)
    shared["qkg"] = np.ascontiguousarray(np.broadcast_to(qk[:, None], (2, 128, 2, 64)))
    evs = np.concatenate([pc(A("ev_shift_mu"), 14), pc(A("rw_kk"), 4), pc(A("rw_ka"), 4), pc(A("rw_rk").reshape(2, 512), 4),
                          pc(A("rw_ln_g"), 4), pc(A("rw_ln_b"), 4)], axis=2)
    shared["evs"] = np.ascontiguousarray(evs.transpose(1, 0, 2))
    in_maps = []
    for core in range(n):
        sb = core % 4
        m = dict(shared)
        m["xg"] = np.ascontiguousarray(np.stack([x_prompt[4 * core:4 * core + 4].reshape(GT, D), x_sample[sb]], axis=0))
        m["condT"] = np.ascontiguousarray(np.stack([pc(c_ctx, NCH), pc(c[sb], NCH)], axis=-1))
        m["cak"] = np.ascontiguousarray(A("cache_attn_k")[sb].reshape(2, 512, 128))
        m["cav"] = np.ascontiguousarray(A("cache_attn_v")[sb].reshape(2, 512, 128))
        m["srf"] = np.ascontiguousarray(A("state_rwkv_fwd")[sb]); m["srb"] = np.ascontiguousarray(A("state_rwkv_bwd")[sb])
        m["sgf"] = np.ascontiguousarray(A("state_gla_fwd")[sb]); m["sgb"] = np.ascontiguousarray(A("state_gla_bwd")[sb])
        in_maps.append(m)
    if "nc" not in _CACHE:
        _CACHE["nc"] = build_program()
    res = run_bass_kernel_spmd(_CACHE["nc"], in_maps, core_ids=list(range(n)))
    R = res.results
    cat = lambda name: np.concatenate([R[i][name] for i in range(n)], axis=0)
    y_prompt = np.concatenate([R[i]["yg"][0].reshape(4, 256, D) for i in range(n)], axis=0)
    y_sample = np.stack([R[i]["yg"][1] for i in range(4)], axis=0)
    new_k = cat("nk").reshape(32, 2, 256, 2, 64)
    new_v = cat("nv").reshape(32, 2, 256, 2, 64)
    return (y_prompt, y_sample, new_k, new_v, cat("nrf"), cat("nrb"), cat("ngf"), cat("ngb"))
```
